# Optimizing a Trainium2 kernel written in Bass

```python
import math
import jax, jax.numpy as jnp
from jax import lax
import numpy as np

D_MODEL = 1024
BATCH = 8
SEQ = 2048
DEPTH = 1
DEC_BATCH = 128
DEC_SEQ = 8
PAST_LEN = 16384
PAGE_SIZE = 128

SSD_EXPAND = 2
SSD_WIDTH = SSD_EXPAND * D_MODEL
SSD_HEAD_DIM = 64
SSD_HEADS = SSD_WIDTH // SSD_HEAD_DIM
SSD_GROUPS = 8
SSD_HEADS_PER_GROUP = SSD_HEADS // SSD_GROUPS
SSD_STATE = 128
SSD_CONV = 4
SSD_CHUNK = 128
SSD_CONV_DIM = SSD_WIDTH + 2 * SSD_GROUPS * SSD_STATE
S5_WIDTH = D_MODEL
S5_GROUP_SIZE = 16
S5_GROUPS = S5_WIDTH // S5_GROUP_SIZE
S5_STATE = 64
DT_MIN = 0.001
DT_MAX = 0.1
NORM_EPS = 1e-6
PROJ_SIZES = (SSD_WIDTH, SSD_CONV_DIM, SSD_HEADS, S5_WIDTH, S5_WIDTH, D_MODEL, D_MODEL)
PROJ_DIM = sum(PROJ_SIZES)

kernel_name = 'hybrid_ssd_s5_gated_decode_step'


def rmsnorm(x, w):
    xf = x.astype(jnp.float32)
    xf = xf * lax.rsqrt(jnp.mean(xf * xf, axis=-1, keepdims=True) + NORM_EPS)
    return (xf * w.astype(jnp.float32)).astype(x.dtype)


def causal_conv(xbc, conv_state, w, b):
    full = jnp.concatenate([conv_state.astype(xbc.dtype), xbc], axis=1)
    y = lax.conv_general_dilated(full, w.astype(xbc.dtype)[:, None, :], window_strides=(1,),
                                 padding='VALID', dimension_numbers=('NWC', 'WIO', 'NWC'),
                                 feature_group_count=xbc.shape[-1])
    return y + b.astype(xbc.dtype), full[:, -(SSD_CONV - 1):]


def ssd_chunked(x, dt, A, B, C, init_state):
    b, L = x.shape[0], x.shape[1]
    G, R, P, N = SSD_GROUPS, SSD_HEADS_PER_GROUP, SSD_HEAD_DIM, SSD_STATE
    Q = min(SSD_CHUNK, L)
    nc = -(-L // Q)
    pad = nc * Q - L
    X = x.astype(jnp.float32) * dt[..., None]
    dA = dt * A
    Bf = B.astype(jnp.float32)
    Cf = C.astype(jnp.float32)
    if pad:
        X = jnp.pad(X, ((0, 0), (0, pad), (0, 0), (0, 0)))
        dA = jnp.pad(dA, ((0, 0), (0, pad), (0, 0)))
        Bf = jnp.pad(Bf, ((0, 0), (0, pad), (0, 0), (0, 0)))
        Cf = jnp.pad(Cf, ((0, 0), (0, pad), (0, 0), (0, 0)))
    X = X.reshape(b, nc, Q, G, R, P)
    dA = dA.reshape(b, nc, Q, G, R).transpose(0, 3, 4, 1, 2)
    Bc = Bf.reshape(b, nc, Q, G, N)
    Cc = Cf.reshape(b, nc, Q, G, N)
    A_cs = jnp.cumsum(dA, axis=-1)
    causal = jnp.tril(jnp.ones((Q, Q), dtype=bool))
    Lmat = jnp.exp(jnp.where(causal, A_cs[..., :, None] - A_cs[..., None, :], -jnp.inf))
    CB = jnp.einsum('bclgn,bcsgn->bgcls', Cc, Bc)
    Y_diag = jnp.einsum('bgcls,bgrcls,bcsgrp->bclgrp', CB, Lmat, X)
    decay_states = jnp.exp(A_cs[..., -1:] - A_cs)
    states = jnp.einsum('bclgn,bgrcl,bclgrp->bcgrpn', Bc, decay_states, X)
    s0 = init_state.astype(jnp.float32).reshape(b, 1, G, R, P, N)
    states = jnp.concatenate([s0, states], axis=1)
    cz = jnp.cumsum(jnp.pad(A_cs[..., -1], ((0, 0), (0, 0), (0, 0), (1, 0))), axis=-1)
    causal_c = jnp.tril(jnp.ones((nc + 1, nc + 1), dtype=bool))
    decay_chunk = jnp.exp(jnp.where(causal_c, cz[..., :, None] - cz[..., None, :], -jnp.inf))
    new_states = jnp.einsum('bgrzc,bcgrpn->bzgrpn', decay_chunk, states)
    Y_off = jnp.einsum('bclgn,bcgrpn,bgrcl->bclgrp', Cc, new_states[:, :-1], jnp.exp(A_cs))
    Y = (Y_diag + Y_off).reshape(b, nc * Q, SSD_HEADS, P)[:, :L]
    return Y, new_states[:, -1].reshape(b, SSD_HEADS, P, N)


def s5_scan(u, lam_re, lam_im, log_dt, B_re, B_im, C_re, C_im, D, s0_re, s0_im):
    b, L = u.shape[0], u.shape[1]
    f32 = jnp.float32
    lam = lax.complex(lam_re.astype(f32), lam_im.astype(f32))
    step = jnp.exp(log_dt.astype(f32))[:, None]
    A_bar = jnp.exp(lam * step)
    B_bar = ((A_bar - 1.0) / lam)[..., None] * lax.complex(B_re.astype(f32), B_im.astype(f32))
    uf = u.astype(f32)
    Bu = jnp.einsum('gpk,blgk->blgp', B_bar, uf.reshape(b, L, S5_GROUPS, S5_GROUP_SIZE))
    s0 = lax.complex(s0_re.astype(f32), s0_im.astype(f32))
    Bu = Bu.at[:, 0].add(A_bar * s0)
    A_seq = jnp.broadcast_to(A_bar, Bu.shape)

    def combine(e1, e2):
        a1, b1 = e1
        a2, b2 = e2
        return a2 * a1, a2 * b1 + b2

    _, states = lax.associative_scan(combine, (A_seq, Bu), axis=1)
    Cmat = lax.complex(C_re.astype(f32), C_im.astype(f32))
    y = jnp.einsum('gkp,blgp->blgk', Cmat, states).real.reshape(b, L, S5_WIDTH) + D.astype(f32) * uf
    final = states[:, -1]
    return y.astype(u.dtype), final.real, final.imag


def hybrid_layer(x, conv0, ssm0, re0, im0, norm_w, w_in, conv_w, conv_b, dt_bias, A_log, D_ssd,
                 ssd_norm_w, w_down_ssd, lam_re, lam_im, log_dt, B_re, B_im, C_re, C_im, D_s5,
                 w_glu, b_glu, w_down_s5, w_out):
    b, L = x.shape[0], x.shape[1]
    h = rmsnorm(x, norm_w)
    proj = h @ w_in.astype(h.dtype)
    splits = [int(i) for i in np.cumsum(PROJ_SIZES)[:-1]]
    z_ssd, xbc, dt_raw, u5, z5, g_a, g_b = jnp.split(proj, splits, axis=-1)
    xbc, new_conv = causal_conv(xbc, conv0, conv_w, conv_b)
    xbc = jax.nn.silu(xbc)
    xs, Bs, Cs = jnp.split(xbc, [SSD_WIDTH, SSD_WIDTH + SSD_GROUPS * SSD_STATE], axis=-1)
    xs = xs.reshape(b, L, SSD_HEADS, SSD_HEAD_DIM)
    Bs = Bs.reshape(b, L, SSD_GROUPS, SSD_STATE)
    Cs = Cs.reshape(b, L, SSD_GROUPS, SSD_STATE)
    dt = jax.nn.softplus(dt_raw.astype(jnp.float32) + dt_bias.astype(jnp.float32))
    A = -jnp.exp(A_log.astype(jnp.float32))
    y_ssd, new_ssm = ssd_chunked(xs, dt, A, Bs, Cs, ssm0)
    y_ssd = y_ssd + D_ssd.astype(jnp.float32)[:, None] * xs.astype(jnp.float32)
    yz = (y_ssd.reshape(b, L, SSD_WIDTH) * jax.nn.silu(z_ssd.astype(jnp.float32)))
    yz = yz.reshape(b, L, SSD_GROUPS, SSD_WIDTH // SSD_GROUPS)
    yz = yz * lax.rsqrt(jnp.mean(yz * yz, axis=-1, keepdims=True) + NORM_EPS)
    yz = (yz.reshape(b, L, SSD_WIDTH) * ssd_norm_w.astype(jnp.float32)).astype(x.dtype)
    y_a = yz @ w_down_ssd.astype(x.dtype)
    y5, new_re, new_im = s5_scan(u5, lam_re, lam_im, log_dt, B_re, B_im, C_re, C_im, D_s5, re0, im0)
    y5 = jax.nn.gelu(y5)
    y5 = y5 * jax.nn.sigmoid(y5 @ w_glu.astype(y5.dtype) + b_glu.astype(y5.dtype))
    y5 = y5 * jax.nn.silu(z5)
    y_b = y5 @ w_down_s5.astype(x.dtype)
    mix = jax.nn.sigmoid(g_a) * y_a + jax.nn.sigmoid(g_b) * y_b
    out = x + mix @ w_out.astype(x.dtype)
    return out, new_conv, new_ssm, new_re, new_im


def setup_inputs(seed: int = 0) -> dict:
    key = jax.random.key(seed)
    ks = iter(jax.random.split(key, 40))
    f32 = jnp.float32

    def nrm(shape, scale):
        return jax.random.normal(next(ks), shape, f32) * scale

    def unif(shape, lo, hi):
        return jax.random.uniform(next(ks), shape, f32, lo, hi)

    dt0 = jnp.exp(unif((DEPTH, SSD_HEADS), math.log(DT_MIN), math.log(DT_MAX)))
    return {
        'x_prompt': nrm((BATCH, SEQ, D_MODEL), 1.0),
        'x_sample': nrm((DEC_BATCH, DEC_SEQ, D_MODEL), 1.0),
        'state_conv': nrm((DEPTH, DEC_BATCH, SSD_CONV - 1, SSD_CONV_DIM), 1.0),
        'state_ssm': nrm((DEPTH, DEC_BATCH, SSD_HEADS, SSD_HEAD_DIM, SSD_STATE), 0.5),
        'state_s5_re': nrm((DEPTH, DEC_BATCH, S5_GROUPS, S5_STATE), 0.3),
        'state_s5_im': nrm((DEPTH, DEC_BATCH, S5_GROUPS, S5_STATE), 0.3),
        'norm_w': 1.0 + nrm((DEPTH, D_MODEL), 0.02),
        'w_in': nrm((DEPTH, D_MODEL, PROJ_DIM), D_MODEL ** -0.5),
        'conv_w': nrm((DEPTH, SSD_CONV, SSD_CONV_DIM), SSD_CONV ** -0.5),
        'conv_b': nrm((DEPTH, SSD_CONV_DIM), 0.02),
        'dt_bias': dt0 + jnp.log(-jnp.expm1(-dt0)),
        'A_log': jnp.log(unif((DEPTH, SSD_HEADS), 1.0, 16.0)),
        'D_ssd': 1.0 + nrm((DEPTH, SSD_HEADS), 0.02),
        'ssd_norm_w': 1.0 + nrm((DEPTH, SSD_WIDTH), 0.02),
        'w_down_ssd': nrm((DEPTH, SSD_WIDTH, D_MODEL), SSD_WIDTH ** -0.5),
        'lam_re': -0.5 + nrm((DEPTH, S5_GROUPS, S5_STATE), 0.01),
        'lam_im': math.pi * jnp.arange(S5_STATE, dtype=f32) + nrm((DEPTH, S5_GROUPS, S5_STATE), 0.01),
        'log_dt': unif((DEPTH, S5_GROUPS), math.log(DT_MIN), math.log(DT_MAX)),
        'B_re': nrm((DEPTH, S5_GROUPS, S5_STATE, S5_GROUP_SIZE), (2 * S5_GROUP_SIZE) ** -0.5),
        'B_im': nrm((DEPTH, S5_GROUPS, S5_STATE, S5_GROUP_SIZE), (2 * S5_GROUP_SIZE) ** -0.5),
        'C_re': nrm((DEPTH, S5_GROUPS, S5_GROUP_SIZE, S5_STATE), S5_STATE ** -0.5),
        'C_im': nrm((DEPTH, S5_GROUPS, S5_GROUP_SIZE, S5_STATE), S5_STATE ** -0.5),
        'D_s5': nrm((DEPTH, S5_WIDTH), 1.0),
        'w_glu': nrm((DEPTH, S5_WIDTH, S5_WIDTH), S5_WIDTH ** -0.5),
        'b_glu': nrm((DEPTH, S5_WIDTH), 0.02),
        'w_down_s5': nrm((DEPTH, S5_WIDTH, D_MODEL), S5_WIDTH ** -0.5),
        'w_out': nrm((DEPTH, D_MODEL, D_MODEL), D_MODEL ** -0.5),
        'final_norm_w': 1.0 + nrm((D_MODEL,), 0.02),
    }


def reference(x_prompt, x_sample, state_conv, state_ssm, state_s5_re, state_s5_im, norm_w, w_in,
              conv_w, conv_b, dt_bias, A_log, D_ssd, ssd_norm_w, w_down_ssd, lam_re, lam_im, log_dt,
              B_re, B_im, C_re, C_im, D_s5, w_glu, b_glu, w_down_s5, w_out, final_norm_w):
    def run(x, conv0, ssm0, re0, im0):
        convs, ssms, res, ims = [], [], [], []
        for l in range(DEPTH):
            x, c, s, r, i = hybrid_layer(
                x, conv0[l], ssm0[l], re0[l], im0[l], norm_w[l], w_in[l], conv_w[l], conv_b[l],
                dt_bias[l], A_log[l], D_ssd[l], ssd_norm_w[l], w_down_ssd[l], lam_re[l], lam_im[l],
                log_dt[l], B_re[l], B_im[l], C_re[l], C_im[l], D_s5[l], w_glu[l], b_glu[l],
                w_down_s5[l], w_out[l])
            convs.append(c)
            ssms.append(s)
            res.append(r)
            ims.append(i)
        return (rmsnorm(x, final_norm_w), jnp.stack(convs), jnp.stack(ssms),
                jnp.stack(res), jnp.stack(ims))

    bp = x_prompt.shape[0]
    zc = jnp.zeros((DEPTH, bp, SSD_CONV - 1, SSD_CONV_DIM), x_prompt.dtype)
    zs = jnp.zeros((DEPTH, bp, SSD_HEADS, SSD_HEAD_DIM, SSD_STATE), jnp.float32)
    z5 = jnp.zeros((DEPTH, bp, S5_GROUPS, S5_STATE), jnp.float32)
    y_prompt, conv_p, ssm_p, re_p, im_p = run(x_prompt, zc, zs, z5, z5)
    y_sample, conv_s, ssm_s, re_s, im_s = run(x_sample, state_conv, state_ssm, state_s5_re, state_s5_im)
    return (y_prompt, y_sample, conv_p, ssm_p, re_p, im_p, conv_s, ssm_s, re_s, im_s)
```

```python
import contextlib
import math
import numpy as np
import concourse.bass as bass
import concourse.mybir as mybir
from concourse.bass_utils import run_bass_kernel_spmd

F32 = mybir.dt.float32
BF16 = mybir.dt.bfloat16
I32 = mybir.dt.int32
ALU = mybir.AluOpType
AF = mybir.ActivationFunctionType

ENGS = ("pe", "act", "dve", "pool", "sp")
N_DMA_SEM = 24
NCORES = 8
D = 1024
PROJ = 10272
Z0, XBC0, DT0, U0, Z50, GA0, GB0 = 0, 2048, 6144, 6176, 7200, 8224, 9248
EPS = 1e-6
SIN_SCALE = 6.28318


class Prog:
    def __init__(self):
        self.ops = []
        self.last_write = {}
        self.readers = {}
        self.dma_count = {e: 0 for e in ENGS}
        self.slot_last = {}
        self.last_op = {}
        self.pending = {}

    def barrier(self):
        deps = set(self.last_op.values()) | set(self.slot_last.values())
        for e in ENGS:
            self.pending[e] = set(deps) | self.pending.get(e, set())

    def op(self, eng, fn, reads=(), writes=(), dma=False):
        writes = list(writes) + [k for k in reads if k.startswith("pb")]
        reads = [k for k in reads if not k.startswith("pb")]
        oid = len(self.ops)
        deps = set()
        for k in reads:
            if k in self.last_write:
                deps.add(self.last_write[k])
        for k in writes:
            if k in self.last_write:
                deps.add(self.last_write[k])
            deps.update(self.readers.get(k, ()))
        if eng in self.pending:
            deps |= self.pending.pop(eng)
        rec = dict(eng=eng, fn=fn, deps=deps, dma=dma, sig=False, seq=None)
        if dma:
            i = self.dma_count[eng]
            self.dma_count[eng] += 1
            slot = i % N_DMA_SEM
            rec["slot"] = slot
            rec["val"] = 16 * (i // N_DMA_SEM + 1)
            prev = self.slot_last.get((eng, slot))
            if prev is not None:
                deps.add(prev)
            self.slot_last[(eng, slot)] = oid
        else:
            self.last_op[eng] = oid
        deps.discard(oid)
        self.ops.append(rec)
        for k in writes:
            self.last_write[k] = oid
            self.readers[k] = []
        for k in reads:
            self.readers.setdefault(k, []).append(oid)
        return oid

    def emit(self, nc):
        ops = self.ops

        def skip(dd, o):
            return (not dd["dma"]) and dd["eng"] == o["eng"] == "pe" and not o["dma"]
        for o in ops:
            for d in o["deps"]:
                dd = ops[d]
                if dd["dma"] or skip(dd, o):
                    continue
                dd["sig"] = True
        cnt = {e: 0 for e in ENGS}
        for o in ops:
            if o["sig"]:
                cnt[o["eng"]] += 1
                o["seq"] = cnt[o["eng"]]
        with contextlib.ExitStack() as st:
            csem = {e: st.enter_context(nc.semaphore("c_" + e)) for e in ENGS}
            dsem = {(e, s): st.enter_context(nc.semaphore(f"d_{e}{s}"))
                    for e in ENGS if self.dma_count[e] > 0
                    for s in range(min(N_DMA_SEM, self.dma_count[e]))}
            block = st.enter_context(nc.Block())

            def run(eng_name, eng):
                waited = {}
                for o in ops:
                    if o["eng"] != eng_name:
                        continue
                    for d in sorted(o["deps"]):
                        dd = ops[d]
                        if dd["dma"]:
                            key = ("d", dd["eng"], dd["slot"])
                            sem = dsem[(dd["eng"], dd["slot"])]
                            val = dd["val"]
                        else:
                            if skip(dd, o):
                                continue
                            key = ("c", dd["eng"])
                            sem = csem[dd["eng"]]
                            val = dd["seq"]
                        if waited.get(key, 0) >= val:
                            continue
                        waited[key] = val
                        eng.wait_ge(sem, val)
                    ins = o["fn"](eng)
                    if o["dma"]:
                        ins.then_inc(dsem[(eng_name, o["slot"])], 16)
                    elif o["sig"]:
                        ins.then_inc(csem[eng_name], 1)
                if eng_name == "sp":
                    for (e, s), oid in self.slot_last.items():
                        eng.wait_ge(dsem[(e, s)], ops[oid]["val"])

            block.tensor(lambda e: run("pe", e))
            block.scalar(lambda e: run("act", e))
            block.vector(lambda e: run("dve", e))
            block.gpsimd(lambda e: run("pool", e))
            block.sync(lambda e: run("sp", e))


def host_consts():
    c = {}
    c["ident"] = np.eye(128, dtype=np.float32)
    s = np.arange(128)
    c["negm_p"] = np.where(s[None, :] >= s[:, None], 0.0, -30000.0).astype(np.float32)
    same = (s[None, :] // 8) == (s[:, None] // 8)
    c["negm_s"] = np.where((s[None, :] >= s[:, None]) & same, 0.0, -30000.0).astype(np.float32)
    rp = np.ones((32, 128), np.float32); rp[:, 0] = 0
    rs = np.ones((32, 128), np.float32); rs[:, ::8] = 0
    c["rm_p"] = rp
    c["rm_s"] = rs
    r128 = np.ones((128, 128), np.float32); r128[:, ::8] = 0
    c["rm128"] = r128
    c["seqmask"] = (s[:, None] // 8 == np.arange(16)[None, :]).astype(np.float32)
    c["pairmask"] = (s[:, None] // 32 == np.arange(4)[None, :]).astype(np.float32)
    c["iota"] = np.broadcast_to(np.arange(128, dtype=np.float32)[None, :], (128, 128)).copy()
    return c


IN_SPECS = [
    ("xp", (2048, 1024)), ("xs", (128, 1024)), ("conv0", (16, 3, 4096)), ("ssm0", (16, 32, 64, 128)),
    ("re0", (16, 64, 64)), ("im0", (16, 64, 64)), ("norm_w", (1024,)), ("w_in", (1024, PROJ)),
    ("conv_w", (4, 4096)), ("conv_b", (4096,)), ("dt_bias", (32,)), ("A_log", (32,)), ("D_ssd", (32,)),
    ("ssd_norm_w", (2048,)), ("w_down_ssd", (2048, 1024)), ("lam_re", (64, 64)), ("lam_im", (64, 64)),
    ("log_dt", (64,)), ("B_re", (64, 64, 16)), ("B_im", (64, 64, 16)), ("C_re", (64, 16, 64)),
    ("C_im", (64, 16, 64)), ("D_s5", (1024,)), ("w_glu", (1024, 1024)), ("b_glu", (1024,)),
    ("w_down_s5", (1024, 1024)), ("w_out", (1024, 1024)), ("final_norm_w", (1024,)),
    ("ident", (128, 128)), ("negm_p", (128, 128)), ("negm_s", (128, 128)), ("rm_p", (32, 128)),
    ("rm_s", (32, 128)), ("rm128", (128, 128)), ("seqmask", (128, 16)), ("pairmask", (128, 4)),
    ("iota", (128, 128)),
]
OUT_SPECS = [
    ("y_p", (2048, 1024)), ("y_s", (128, 1024)), ("conv_p", (3, 4096)), ("ssm_p", (32, 64, 128)),
    ("re_p", (64, 64)), ("im_p", (64, 64)), ("conv_s", (16, 3, 4096)), ("ssm_s", (16, 32, 64, 128)),
    ("re_s", (16, 64, 64)), ("im_s", (16, 64, 64)),
]


STOP = [None]
BLK = [None]


class _Stop(Exception):
    pass


def ck(n):
    if STOP[0] is not None and n >= STOP[0]:
        raise _Stop()


def build(tiles=None):
    nc = bass.Bass("TRN2", target_bir_lowering=False)
    I = {n: nc.dram_tensor(n, list(s), F32, kind="ExternalInput").ap() for n, s in IN_SPECS}
    O = {n: nc.dram_tensor(n, list(s), F32, kind="ExternalOutput").ap() for n, s in OUT_SPECS}
    WS = {
        "w_in": nc.dram_tensor("ws_w_in", [1024, PROJ], BF16).ap(),
        "w_down_ssd": nc.dram_tensor("ws_wds", [2048, 1024], BF16).ap(),
        "w_glu": nc.dram_tensor("ws_wglu", [1024, 1024], BF16).ap(),
        "w_down_s5": nc.dram_tensor("ws_wd5", [1024, 1024], BF16).ap(),
        "w_out": nc.dram_tensor("ws_wout", [1024, 1024], BF16).ap(),
    }
    P = Prog()

    def V(fn, r=(), w=()):
        P.op("dve", fn, r, w)

    def G(fn, r=(), w=()):
        P.op("pool", fn, r, w)

    def A(fn, r=(), w=()):
        P.op("act", fn, r, w)

    def T(fn, r=(), w=()):
        P.op("pe", fn, r, w)

    def DMA(fn, r=(), w=()):
        P.op("sp", fn, r, w, dma=True)

    st = contextlib.ExitStack()
    with st:
        def sb(name, shape, dt=F32, stack=None):
            return (stack or st).enter_context(nc.sbuf_tensor("s_" + name, list(shape), dt))

        def kn(t):
            n_ = getattr(t, "name")
            return n_[2:] if n_.startswith("s_") else n_
        pb = [st.enter_context(nc.psum_tensor(f"pb{i}", [128, 512], F32)) for i in range(8)]
        pbb = [p.bitcast(BF16) for p in pb]

        identf = sb("identf", [128, 128]); identb = sb("identb", [128, 128], BF16)
        negm_p = sb("negm_p", [128, 128]); negm_s = sb("negm_s", [128, 128])
        rm_p = sb("rm_p", [32, 128]); rm_s = sb("rm_s", [32, 128]); rm128 = sb("rm128", [128, 128])
        seqmask = sb("seqmask", [128, 16]); pairmask = sb("pairmask", [128, 4])
        ones32 = sb("ones32", [32, 128]); onesb = sb("onesb", [128, 128], BF16)
        onecol = sb("onecol", [128, 1]); epscol = sb("epscol", [128, 1])
        for nm, t in [("ident", identf), ("negm_p", negm_p), ("negm_s", negm_s), ("rm_p", rm_p), ("rm_s", rm_s),
                      ("rm128", rm128), ("seqmask", seqmask), ("pairmask", pairmask)]:
            DMA(lambda e, nm=nm, t=t: e.dma_start(out=t[:], in_=I[nm]), w=[kn(t)])
        V(lambda e: e.tensor_copy(out=identb[:], in_=identf[:]), ["identf"], ["identb"])
        G(lambda e: e.memset(ones32[:], 1.0), w=["ones32"])
        G(lambda e: e.memset(onesb[:], 1.0), w=["onesb"])
        G(lambda e: e.memset(onecol[:], 1.0), w=["onecol"])
        G(lambda e: e.memset(epscol[:], EPS), w=["epscol"])

        def colvec(name, src, nt):
            t = sb(name, [128, nt])
            DMA(lambda e: e.dma_start(out=t[:], in_=src.rearrange("(j p) -> p j", p=128), allow_slow_non_contiguous=True), w=[name])
            return t
        normw = colvec("normw", I["norm_w"], 8)
        convb = colvec("convb", I["conv_b"], 32)
        ssdnw = colvec("ssdnw", I["ssd_norm_w"], 16)
        ds5 = colvec("ds5", I["D_s5"], 8)
        bglu = colvec("bglu", I["b_glu"], 8)
        convw = sb("convw", [128, 32, 4])
        for k in range(4):
            DMA(lambda e, k=k: e.dma_start(out=convw[:, :, k], in_=I["conv_w"][k].rearrange("(j p) -> p j", p=128), allow_slow_non_contiguous=True), w=["convw"])
        fnw = sb("fnw", [128, 1024])
        DMA(lambda e: e.dma_start(out=fnw[:], in_=I["final_norm_w"].partition_broadcast(128)), w=["fnw"])
        dtb = sb("dtb", [32, 1]); aneg = sb("aneg", [32, 1])
        DMA(lambda e: e.dma_start(out=dtb[:], in_=I["dt_bias"].rearrange("(h o) -> h o", o=1)), w=["dtb"])
        DMA(lambda e: e.dma_start(out=aneg[:], in_=I["A_log"].rearrange("(h o) -> h o", o=1)), w=["aneg"])
        A(lambda e: e.activation(out=aneg[:], in_=aneg[:], func=AF.Exp), ["aneg"], ["aneg"])
        V(lambda e: e.tensor_scalar(out=aneg[:], in0=aneg[:], scalar1=-1.0, scalar2=None, op0=ALU.mult), ["aneg"], ["aneg"])
        dexp = sb("dexp", [128, 16])
        dv = I["D_ssd"].rearrange("(pr h2) -> h2 pr", h2=2)
        for h2 in range(2):
            DMA(lambda e, h2=h2: e.dma_start(out=dexp[64 * h2:64 * h2 + 64, :], in_=dv[h2].partition_broadcast(64), allow_slow_non_contiguous=True), w=["dexp"])
        are = sb("are", [128, 32]); aim = sb("aim", [128, 32]); rdec = sb("rdec", [128, 32])
        aere = sb("aere", [128, 32]); aeim = sb("aeim", [128, 32])
        ecos = sb("ecos", [128, 32, 64]); esin = sb("esin", [128, 32, 64])
        wb5 = sb("wb5", [128, 8, 2, 128], BF16); wd5 = sb("wd5", [128, 8, 2, 128], BF16)
        halo = sb("halo", [128, 32, 3])
        STt = sb("STt", [128, 32, 64])
        SA = sb("SA", [128, 16, 128], BF16); SB = sb("SB", [128, 16, 128], BF16)
        XA = sb("XA", [128, 16, 128], BF16); XB = sb("XB", [128, 16, 128], BF16)
        sgc = sb("sgc", [128, 32, 2])
        for t_ in (halo, STt, SA, SB, XA, XB, sgc, wd5):
            G(lambda e, t_=t_: e.memset(t_[:], 0.0), w=[kn(t_)])

        pst = contextlib.ExitStack()
        with pst:
            def psb(name, shape, dt=F32):
                return sb(name, shape, dt, stack=pst)
            stg = [psb(f"stg{i}", [128, 1024]) for i in range(3)]
            stgb = [psb(f"stgb{i}", [128, 1024], BF16) for i in range(3)]
            cnt = [0]

            def cast_rows(src, dst, ncols):
                for c0 in range(0, ncols, 1024):
                    w_ = min(1024, ncols - c0)
                    i = cnt[0] % 3
                    cnt[0] += 1
                    DMA(lambda e, i=i, c0=c0, w_=w_: e.dma_start(out=stg[i][:, 0:w_], in_=src[:, c0:c0 + w_]), w=[f"stg{i}"])
                    if cnt[0] % 2:
                        G(lambda e, i=i, w_=w_: e.tensor_copy(out=stgb[i][:, 0:w_], in_=stg[i][:, 0:w_]), [f"stg{i}"], [f"stgb{i}"])
                    else:
                        A(lambda e, i=i, w_=w_: e.activation(out=stgb[i][:, 0:w_], in_=stg[i][:, 0:w_], func=AF.Copy), [f"stg{i}"], [f"stgb{i}"])
                    DMA(lambda e, i=i, c0=c0, w_=w_: e.dma_start(out=dst[:, c0:c0 + w_], in_=stgb[i][:, 0:w_]), [f"stgb{i}"], ["ws"])
            for nm, rows in [("w_in", 1024), ("w_down_ssd", 2048), ("w_glu", 1024), ("w_down_s5", 1024), ("w_out", 1024)]:
                nco = PROJ if nm == "w_in" else 1024
                for r0 in range(0, rows, 128):
                    cast_rows(I[nm][r0:r0 + 128, :], WS[nm][r0:r0 + 128, :], nco)

            iota = psb("iota", [128, 128])
            DMA(lambda e: e.dma_start(out=iota[:], in_=I["iota"]), w=["iota"])
            lre = psb("lre", [128, 32]); lim = psb("lim", [128, 32]); stp = psb("stp", [128, 32]); th = psb("th", [128, 32])
            DMA(lambda e: e.dma_start(out=lre[:], in_=I["lam_re"].rearrange("(gp g2) p -> (g2 p) gp", g2=2), allow_slow_non_contiguous=True), w=["lre"])
            DMA(lambda e: e.dma_start(out=lim[:], in_=I["lam_im"].rearrange("(gp g2) p -> (g2 p) gp", g2=2), allow_slow_non_contiguous=True), w=["lim"])
            ldv = I["log_dt"].rearrange("(gp g2) -> g2 gp", g2=2)
            for g2 in range(2):
                DMA(lambda e, g2=g2: e.dma_start(out=stp[64 * g2:64 * g2 + 64, :], in_=ldv[g2].partition_broadcast(64), allow_slow_non_contiguous=True), w=["stp"])
            A(lambda e: e.activation(out=stp[:], in_=stp[:], func=AF.Exp), ["stp"], ["stp"])
            V(lambda e: e.tensor_tensor(out=rdec[:], in0=lre[:], in1=stp[:], op=ALU.mult), ["lre", "stp"], ["rdec"])
            A(lambda e: e.activation(out=rdec[:], in_=rdec[:], func=AF.Exp), ["rdec"], ["rdec"])
            V(lambda e: e.tensor_tensor(out=th[:], in0=lim[:], in1=stp[:], op=ALU.mult), ["lim", "stp"], ["th"])
            V(lambda e: e.tensor_scalar(out=th[:], in0=th[:], scalar1=1.0 / (2.0 * math.pi), scalar2=None, op0=ALU.mult), ["th"], ["th"])
            tmpx = psb("tmpx", [128, 2048]); tmpf = psb("tmpf", [128, 2048]); tmpi = psb("tmpi", [128, 2048], I32)

            def sin_turns(out_ap, mk_x, n, off, rkeys, wkeys):
                V(lambda e: mk_x(e, tmpx[:, 0:n]), rkeys, ["tmpx"])
                if off:
                    V(lambda e: e.tensor_scalar(out=tmpx[:, 0:n], in0=tmpx[:, 0:n], scalar1=float(off), scalar2=None, op0=ALU.add), ["tmpx"], ["tmpx"])
                V(lambda e: e.tensor_copy(out=tmpi[:, 0:n], in_=tmpx[:, 0:n]), ["tmpx"], ["tmpi"])
                V(lambda e: e.tensor_copy(out=tmpf[:, 0:n], in_=tmpi[:, 0:n]), ["tmpi"], ["tmpf"])
                V(lambda e: e.tensor_tensor(out=tmpx[:, 0:n], in0=tmpx[:, 0:n], in1=tmpf[:, 0:n], op=ALU.subtract), ["tmpx", "tmpf"], ["tmpx"])
                A(lambda e: e.activation(out=out_ap, in_=tmpx[:, 0:n], func=AF.Sin, scale=SIN_SCALE), ["tmpx"], wkeys)
            cs1 = psb("cs1", [128, 32]); sn1 = psb("sn1", [128, 32])
            x1 = lambda e, dst: e.tensor_copy(out=dst, in_=th[:])
            sin_turns(sn1[:], x1, 32, 0.0, ["th"], ["sn1"])
            sin_turns(cs1[:], x1, 32, 0.25, ["th"], ["cs1"])
            V(lambda e: e.tensor_tensor(out=are[:], in0=rdec[:], in1=cs1[:], op=ALU.mult), ["rdec", "cs1"], ["are"])
            V(lambda e: e.tensor_tensor(out=aim[:], in0=rdec[:], in1=sn1[:], op=ALU.mult), ["rdec", "sn1"], ["aim"])
            x64 = lambda e, dst: e.tensor_scalar(out=dst, in0=th[:], scalar1=64.0, scalar2=None, op0=ALU.mult)
            sin_turns(sn1[:], x64, 32, 0.0, ["th"], ["sn1"])
            sin_turns(cs1[:], x64, 32, 0.25, ["th"], ["cs1"])
            V(lambda e: e.tensor_tensor(out=aere[:], in0=rdec[:], in1=cs1[:], op=ALU.mult), ["rdec", "cs1"], ["aere"])
            V(lambda e: e.tensor_tensor(out=aeim[:], in0=rdec[:], in1=sn1[:], op=ALU.mult), ["rdec", "sn1"], ["aeim"])
            xt_ = lambda e, dst: e.tensor_tensor(out=dst.rearrange("p (a b) -> p a b", b=64), in0=th[:].unsqueeze(2).broadcast_to([128, 32, 64]),
                                                 in1=iota[:, 0:64].unsqueeze(1).broadcast_to([128, 32, 64]), op=ALU.mult)
            sin_turns(esin[:].rearrange("p a b -> p (a b)"), xt_, 2048, 0.0, ["th", "iota"], ["esin"])
            sin_turns(ecos[:].rearrange("p a b -> p (a b)"), xt_, 2048, 0.25, ["th", "iota"], ["ecos"])
            gre = psb("gre", [128, 32]); gim = psb("gim", [128, 32]); den = psb("den", [128, 32])
            t1 = psb("t1s", [128, 32]); t2 = psb("t2s", [128, 32]); am1 = psb("am1", [128, 32])
            V(lambda e: e.tensor_scalar(out=am1[:], in0=are[:], scalar1=-1.0, scalar2=None, op0=ALU.add), ["are"], ["am1"])
            V(lambda e: e.tensor_tensor(out=den[:], in0=lre[:], in1=lre[:], op=ALU.mult), ["lre"], ["den"])
            V(lambda e: e.tensor_tensor(out=t1[:], in0=lim[:], in1=lim[:], op=ALU.mult), ["lim"], ["t1s"])
            V(lambda e: e.tensor_tensor(out=den[:], in0=den[:], in1=t1[:], op=ALU.add), ["den", "t1s"], ["den"])
            V(lambda e: e.reciprocal(out=den[:], in_=den[:]), ["den"], ["den"])
            V(lambda e: e.tensor_tensor(out=t1[:], in0=am1[:], in1=lre[:], op=ALU.mult), ["am1", "lre"], ["t1s"])
            V(lambda e: e.tensor_tensor(out=t2[:], in0=aim[:], in1=lim[:], op=ALU.mult), ["aim", "lim"], ["t2s"])
            V(lambda e: e.tensor_tensor(out=gre[:], in0=t1[:], in1=t2[:], op=ALU.add), ["t1s", "t2s"], ["gre"])
            V(lambda e: e.tensor_tensor(out=gre[:], in0=gre[:], in1=den[:], op=ALU.mult), ["gre", "den"], ["gre"])
            V(lambda e: e.tensor_tensor(out=t1[:], in0=aim[:], in1=lre[:], op=ALU.mult), ["aim", "lre"], ["t1s"])
            V(lambda e: e.tensor_tensor(out=t2[:], in0=am1[:], in1=lim[:], op=ALU.mult), ["am1", "lim"], ["t2s"])
            V(lambda e: e.tensor_tensor(out=gim[:], in0=t1[:], in1=t2[:], op=ALU.subtract), ["t1s", "t2s"], ["gim"])
            V(lambda e: e.tensor_tensor(out=gim[:], in0=gim[:], in1=den[:], op=ALU.mult), ["gim", "den"], ["gim"])
            bre = psb("bre", [128, 32, 16]); bim = psb("bim", [128, 32, 16]); bbr = psb("bbr", [128, 32, 16]); bbi = psb("bbi", [128, 32, 16]); tb = psb("tbs", [128, 32, 16])
            DMA(lambda e: e.dma_start(out=bre[:], in_=I["B_re"].rearrange("(gp g2) p k -> (g2 p) gp k", g2=2)), w=["bre"])
            DMA(lambda e: e.dma_start(out=bim[:], in_=I["B_im"].rearrange("(gp g2) p k -> (g2 p) gp k", g2=2)), w=["bim"])
            gb_ = lambda t: t[:].unsqueeze(2).broadcast_to([128, 32, 16])
            V(lambda e: e.tensor_tensor(out=bbr[:], in0=bre[:], in1=gb_(gre), op=ALU.mult), ["bre", "gre"], ["bbr"])
            V(lambda e: e.tensor_tensor(out=tb[:], in0=bim[:], in1=gb_(gim), op=ALU.mult), ["bim", "gim"], ["tbs"])
            V(lambda e: e.tensor_tensor(out=bbr[:], in0=bbr[:], in1=tb[:], op=ALU.subtract), ["bbr", "tbs"], ["bbr"])
            V(lambda e: e.tensor_tensor(out=bbi[:], in0=bim[:], in1=gb_(gre), op=ALU.mult), ["bim", "gre"], ["bbi"])
            V(lambda e: e.tensor_tensor(out=tb[:], in0=bre[:], in1=gb_(gim), op=ALU.mult), ["bre", "gim"], ["tbs"])
            V(lambda e: e.tensor_tensor(out=bbi[:], in0=bbi[:], in1=tb[:], op=ALU.add), ["bbi", "tbs"], ["bbi"])
            cin = psb("cin", [128, 4, 2, 64])
            cst = [psb("cstr", [128, 32, 16]), psb("csti", [128, 32, 16])]
            for ri, nm in enumerate(["C_re", "C_im"]):
                cv = I[nm].rearrange("(gh gq g2) k p -> gq k gh g2 p", gq=8, g2=2)
                for gq in range(8):
                    for gh in range(4):
                        DMA(lambda e, gq=gq, gh=gh, cv=cv: e.dma_start(out=cin[16 * gq:16 * gq + 16, gh, :, :], in_=cv[gq][:, gh, :, :]), w=["cin"])
                for gh in range(4):
                    T(lambda e, gh=gh: e.transpose(out=pb[7][:, 0:128], in_=cin[:, gh, :, :].rearrange("p a b -> p (a b)"), identity=identf[:]), ["cin", "identf"], ["pb7"])
                    V(lambda e, gh=gh, ri=ri: e.tensor_copy(out=cst[ri][:, 8 * gh:8 * gh + 8, :], in_=pb[7][:, 0:128].rearrange("p (a b) -> p a b", b=16)), ["pb7"], [kn(cst[ri])])
            zst = psb("zst", [128, 4, 2, 16])
            G(lambda e: e.memset(zst[:], 0.0), w=["zst"])
            for ft in range(8):
                for ri in range(2):
                    src = [bbr, bbi][ri]
                    for h2 in range(2):
                        V(lambda e, ft=ft, h2=h2, src=src: e.tensor_copy(out=zst[64 * h2:64 * h2 + 64, :, h2, :], in_=src[64 * h2:64 * h2 + 64, 4 * ft:4 * ft + 4, :]), [kn(src)], ["zst"])
                    T(lambda e: e.transpose(out=pb[6][:, 0:128], in_=zst[:].rearrange("p a b c -> p (a b c)"), identity=identf[:]), ["zst", "identf"], ["pb6"])
                    V(lambda e, ft=ft, ri=ri: e.tensor_copy(out=wb5[:, ft, ri, :], in_=pb[6][:, 0:128]), ["pb6"], ["wb5"])
                    for h2 in range(2):
                        V(lambda e, ft=ft, ri=ri, h2=h2: e.tensor_scalar(
                            out=wd5[64 * h2:64 * h2 + 64, ft, ri, :].rearrange("p (q g k) -> p q g k", g=2, k=16)[:, :, h2, :],
                            in0=cst[ri][64 * h2:64 * h2 + 64, 4 * ft:4 * ft + 4, :], scalar1=(1.0 if ri == 0 else -1.0), scalar2=None, op0=ALU.mult),
                          [kn(cst[ri])], ["wd5"])
        P.barrier()

        xt = sb("xt", [128, 1024]); ot = sb("ot", [128, 1024]); ss = sb("ss", [128, 1]); rstd = sb("rstd", [128, 1])
        xsn = sb("xsn", [128, 1024], BF16); hT = sb("hT", [128, 8, 128], BF16)
        wbuf = [sb(f"wbuf{i}", [128, 8, 256], BF16) for i in range(3)]
        zs = sb("zs", [128, 16, 128], BF16)
        xc = sb("xc", [128, 32, 128], BF16)
        raw = [sb(f"raw{i}", [128, 176]) for i in range(2)]
        acc = [sb(f"acc{i}", [128, 128]) for i in range(2)]
        dts = sb("dts", [32, 128]); dAs = sb("dAs", [32, 128]); acs = sb("acs", [32, 128]); dec = sb("dec", [32, 128]); wdt = sb("wdt", [32, 128])
        tokm = sb("tokm", [128, 3, 32])
        u5 = sb("u5", [128, 8, 128], BF16); z5s = sb("z5s", [128, 8, 128], BF16)
        gas = sb("gas", [128, 8, 128], BF16); gbs = sb("gbs", [128, 8, 128], BF16)
        Xd = sb("Xd", [128, 32, 64], BF16); Btok = sb("Btok", [128, 8, 128], BF16); Bm = sb("Bm", [128, 8, 128], BF16)
        rhsR = sb("rhsR", [32, 4, 128]); Eg = sb("Eg", [128, 4, 128]); Dm = sb("Dm", [128, 4, 128])
        Lg = sb("Lg", [128, 4, 128], BF16); Mg = sb("Mg", [128, 4, 128], BF16)
        CEall = sb("CEall", [128, 32, 128], BF16)
        dch = sb("dch", [128, 32, 16])
        S0q = sb("S0q", [128, 4, 128]); Snq = sb("Snq", [128, 4, 128])
        yv = sb("yv", [128, 16, 128]); ysq = sb("ysq", [128, 16, 128], BF16); rstg = sb("rstg", [128, 8, 128]); ynT = sb("ynT", [128, 16, 128], BF16)
        yaT = sb("yaT", [128, 8, 128], BF16); ybT = sb("ybT", [128, 8, 128], BF16); mixT = sb("mixT", [128, 8, 128], BF16)
        u5m = sb("u5m", [128, 4, 128], BF16)
        bu = sb("bu", [128, 4, 2, 128]); sgm = sb("sgm", [128, 4, 2, 128]); rts = sb("rts", [128, 4, 128])
        bt1 = sb("bt1", [128, 4, 128]); bt2 = sb("bt2", [128, 4, 128])
        s5s = sb("s5s", [128, 4, 2, 128], BF16)
        cin_ = sb("cinn", [128, 4, 2, 16])
        ctm = sb("ctm", [128, 4, 2, 16])
        fin = sb("fin", [128, 32, 2, 16]); s0all = sb("s0all", [128, 32, 2, 16])
        y5t = sb("y5t", [128, 128]); y5u = sb("y5u", [128, 128]); y5v = sb("y5v", [128, 128])
        y5g = sb("y5g", [128, 8, 128], BF16); y5f = sb("y5f", [128, 8, 128], BF16); glus = sb("glus", [128, 128])
        s5io = sb("s5io", [128, 4, 2, 64]); s5tr = sb("s5tr", [128, 16, 8])
        cvo = sb("cvo", [48, 1024]); cst3 = sb("cst3", [128, 48])

        wcnt = [0]

        def wload(src_ap):
            i = wcnt[0] % 3
            wcnt[0] += 1
            kt = src_ap.shape[0] // 128
            nco = src_ap.shape[1]
            DMA(lambda e: e.dma_start(out=wbuf[i][:, 0:kt, 0:nco], in_=src_ap.rearrange("(j p) f -> p j f", p=128)), ["ws"], [f"wbuf{i}"])
            return wbuf[i], f"wbuf{i}"

        pslot = [0]

        def mm_slot():
            s = pslot[0] % 2
            pslot[0] += 1
            return pb[s][:, 0:128], f"pb{s}"

        def proj_ft(wt, wk, c0, nrows=128):
            ps, pk = mm_slot()
            for j in range(8):
                T(lambda e, j=j: e.matmul(ps[0:nrows, :], lhsT=wt[:, j, c0:c0 + nrows], rhs=hT[:, j, :], start=(j == 0), stop=(j == 7)), [wk, "hT"], [pk])
            return ps, pk

        def dense_T(wname, kt, rhs_fn, rkeys_fn, evac):
            for cb in range(4):
                halves = [wload(WS[wname][128 * k0:128 * (k0 + 8), 256 * cb:256 * cb + 256]) for k0 in range(0, kt, 8)]
                for s in range(2):
                    m = 2 * cb + s
                    ps, pk = mm_slot()
                    for k in range(kt):
                        wt, wk = halves[k // 8]
                        T(lambda e, wt=wt, k=k, s=s, ps=ps: e.matmul(ps, lhsT=wt[:, k % 8, 128 * s:128 * s + 128], rhs=rhs_fn(k), start=(k == 0), stop=(k == kt - 1)), [wk] + rkeys_fn(k), [pk])
                    evac(m, ps, pk)

        def cmul(out_re, out_im, ar, ai, br, bi, keys_r, keys_w, t1_, t2_, tk):
            V(lambda e: e.tensor_tensor(out=t1_, in0=ar, in1=br, op=ALU.mult), keys_r, [tk[0]])
            G(lambda e: e.tensor_tensor(out=t2_, in0=ai, in1=bi, op=ALU.mult), keys_r, [tk[1]])
            V(lambda e: e.tensor_tensor(out=out_re, in0=t1_, in1=t2_, op=ALU.subtract), tk, keys_w[0:1])
            V(lambda e: e.tensor_tensor(out=t1_, in0=ar, in1=bi, op=ALU.mult), keys_r + keys_w[0:1], [tk[0]])
            G(lambda e: e.tensor_tensor(out=t2_, in0=ai, in1=br, op=ALU.mult), keys_r + keys_w[0:1], [tk[1]])
            V(lambda e: e.tensor_tensor(out=out_im, in0=t1_, in1=t2_, op=ALU.add), tk, keys_w[1:2])

        def tile_body(ti):
            smp = (ti == 16)
            lastp = (ti == 15)
            xsrc = I["xs"] if smp else I["xp"][128 * ti:128 * ti + 128, :]
            ydst = O["y_s"] if smp else O["y_p"][128 * ti:128 * ti + 128, :]
            negm = negm_s if smp else negm_p
            rmk = rm_s if smp else rm_p
            nseq, L = (16, 8) if smp else (1, 128)
            DMA(lambda e: e.dma_start(out=xt[:], in_=xsrc), w=["xt"])
            G(lambda e: e.memset(ss[:], 0.0), w=["ss"])
            A(lambda e: e.activation(out=ot[:], in_=xt[:], func=AF.Square, accum_out=ss[:]), ["xt", "ss"], ["ot", "ss"])
            A(lambda e: e.activation(out=rstd[:], in_=ss[:], func=AF.Sqrt, scale=1.0 / D, bias=epscol[:]), ["ss", "epscol"], ["rstd"])
            V(lambda e: e.reciprocal(out=rstd[:], in_=rstd[:]), ["rstd"], ["rstd"])
            V(lambda e: e.tensor_scalar(out=xsn[:], in0=xt[:], scalar1=rstd[:, 0:1], scalar2=None, op0=ALU.mult), ["xt", "rstd"], ["xsn"])
            for j in range(8):
                T(lambda e, j=j: e.transpose(out=pbb[5][:, 128 * j:128 * j + 128], in_=xsn[:, 128 * j:128 * j + 128], identity=identb[:]), ["xsn", "identb"], ["pb5"])
            for j in range(8):
                V(lambda e, j=j: e.tensor_scalar(out=hT[:, j, :], in0=pbb[5][:, 128 * j:128 * j + 128], scalar1=normw[:, j:j + 1], scalar2=None, op0=ALU.mult), ["pb5", "normw"], ["hT"])
            ck(1)
            blocks = [("xbc", c0, 256) for c0 in range(XBC0, DT0, 256)] + [("dt", DT0, 32)]
            blocks += [("z", c0, 256) for c0 in range(Z0, XBC0, 256)]
            for nm, b0 in (("u5", U0), ("z5", Z50), ("ga", GA0), ("gb", GB0)):
                blocks += [(nm, c0, 256) for c0 in range(b0, b0 + 1024, 256)]
            base = {"xbc": XBC0, "z": Z0, "u5": U0, "z5": Z50, "ga": GA0, "gb": GB0}
            for nm, c0, nco in blocks:
                if BLK[0] is not None and nm not in BLK[0]:
                    continue
                wt, wk = wload(WS["w_in"][:, c0:c0 + nco])
                if nm == "dt":
                    ps, pk = proj_ft(wt, wk, 0, 32)
                    A(lambda e, ps=ps: e.activation(out=dts[:], in_=ps[0:32, :], func=AF.Exp, bias=dtb[:]), [pk, "dtb"], ["dts"])
                    A(lambda e: e.activation(out=dts[:], in_=dts[:], func=AF.Ln, bias=onecol[0:32, :]), ["dts", "onecol"], ["dts"])
                    continue
                for s in range(2):
                    ft = (c0 - base[nm]) // 128 + s
                    ps, pk = proj_ft(wt, wk, 128 * s)
                    if nm == "z":
                        A(lambda e, ps=ps, ft=ft: e.activation(out=zs[:, ft, :], in_=ps, func=AF.Silu), [pk], [f"zs{ft}"])
                    elif nm == "u5":
                        A(lambda e, ps=ps, ft=ft: e.activation(out=u5[:, ft, :], in_=ps, func=AF.Copy), [pk], [f"u5{ft}"])
                    elif nm == "z5":
                        A(lambda e, ps=ps, ft=ft: e.activation(out=z5s[:, ft, :], in_=ps, func=AF.Silu), [pk], [f"z5{ft}"])
                    elif nm == "ga":
                        A(lambda e, ps=ps, ft=ft: e.activation(out=gas[:, ft, :], in_=ps, func=AF.Sigmoid), [pk], [f"ga{ft}"])
                    elif nm == "gb":
                        A(lambda e, ps=ps, ft=ft: e.activation(out=gbs[:, ft, :], in_=ps, func=AF.Sigmoid), [pk], [f"gb{ft}"])
                    else:
                        r_ = raw[ft % 2]; rk = f"raw{ft % 2}"; ac = acc[ft % 2]; ak = f"acc{ft % 2}"
                        EV = V
                        if smp:
                            rv = r_[:, 0:176].rearrange("p (s r) -> p s r", r=11)
                            if ft % 8 == 0:
                                DMA(lambda e, ft=ft: e.dma_start(out=cvo[:], in_=I["conv0"].rearrange("s r c -> (s r) c")[:, 128 * ft:128 * ft + 1024]), w=["cvo"])
                            T(lambda e, ft=ft: e.transpose(out=pb[6][:, 0:48], in_=cvo[:, 128 * (ft % 8):128 * (ft % 8) + 128], identity=identf[0:48, 0:48]), ["cvo", "identf"], ["pb6"])
                            V(lambda e, rv=rv: e.tensor_copy(out=rv[:, :, 0:3], in_=pb[6][:, 0:48].rearrange("p (s r) -> p s r", r=3)), ["pb6"], [rk])
                        else:
                            rv = r_[:, 0:131].unsqueeze(1)
                            V(lambda e, rv=rv, ft=ft: e.tensor_copy(out=rv[:, 0, 0:3], in_=halo[:, ft, :]), ["halo"], [rk])
                        A(lambda e, rv=rv, ps=ps: e.activation(out=rv[:, :, 3:3 + L], in_=ps.rearrange("p (s l) -> p s l", l=L), func=AF.Copy), [pk], [rk])
                        av = ac[:].rearrange("p (s l) -> p s l", l=L)
                        EV(lambda e, rv=rv, av=av, ft=ft: e.tensor_scalar(out=av, in0=rv[:, :, 0:L], scalar1=convw[:, ft, 0:1], scalar2=convb[:, ft:ft + 1], op0=ALU.mult, op1=ALU.add), [rk, "convw", "convb"], [ak])
                        for k in range(1, 4):
                            EV(lambda e, rv=rv, av=av, ft=ft, k=k: e.scalar_tensor_tensor(out=av, in0=rv[:, :, k:k + L], scalar=convw[:, ft, k:k + 1], in1=av, op0=ALU.mult, op1=ALU.add), [rk, ak, "convw"], [ak])
                        A(lambda e, ac=ac, ft=ft: e.activation(out=xc[:, ft, :], in_=ac[:], func=AF.Silu), [ak], [f"xc{ft}"])
                        if not smp:
                            EV(lambda e, rv=rv, ft=ft: e.tensor_copy(out=halo[:, ft, :], in_=rv[:, 0, 128:131]), [rk], ["halo"])
                        if smp or lastp:
                            nr = 3 * nseq
                            EV(lambda e, rv=rv, nr=nr: e.tensor_copy(out=cst3[:, 0:nr].rearrange("p (s r) -> p s r", r=3), in_=rv[:, :, L:L + 3]), [rk], ["cst3"])
                            T(lambda e, nr=nr: e.transpose(out=pb[6][0:nr, 128:256], in_=cst3[:, 0:nr], identity=identf[:]), ["cst3", "identf"], ["pb6"])
                            V(lambda e, ft=ft, nr=nr: e.tensor_copy(out=cvo[0:nr, 128 * (ft % 8):128 * (ft % 8) + 128], in_=pb[6][0:nr, 128:256]), ["pb6"], ["cvo"])
                            if ft % 8 == 7:
                                cdst = (O["conv_s"].rearrange("s r c -> (s r) c") if smp else O["conv_p"])[:, 128 * (ft - 7):128 * (ft - 7) + 1024]
                                DMA(lambda e, cdst=cdst, nr=nr: e.dma_start(out=cdst, in_=cvo[0:nr, :]), ["cvo"], [])
            ck(2)
            V(lambda e: e.tensor_scalar(out=dAs[:], in0=dts[:], scalar1=aneg[:, 0:1], scalar2=None, op0=ALU.mult), ["dts", "aneg"], ["dAs"])
            V(lambda e: e.tensor_tensor_scan(out=acs[:], data0=rmk[:], data1=dAs[:], initial=0.0, op0=ALU.mult, op1=ALU.add), [kn(rmk), "dAs"], ["acs"])
            a3 = acs[:].rearrange("h (s l) -> h s l", l=L)
            V(lambda e: e.tensor_tensor(out=dec[:].rearrange("h (s l) -> h s l", l=L), in0=a3[:, :, L - 1:L].broadcast_to([32, nseq, L]), in1=a3, op=ALU.subtract), ["acs"], ["dec"])
            A(lambda e: e.activation(out=dec[:], in_=dec[:], func=AF.Exp), ["dec"], ["dec"])
            V(lambda e: e.tensor_tensor(out=wdt[:], in0=dts[:], in1=dec[:], op=ALU.mult), ["dts", "dec"], ["wdt"])
            for i_, src in enumerate((acs, dts, wdt)):
                T(lambda e, i_=i_, src=src: e.transpose(out=pb[6][:, 256 + 32 * i_:256 + 32 * i_ + 32], in_=src[:], identity=identf[0:32, 0:32]), [kn(src), "identf"], ["pb6"])
            V(lambda e: e.tensor_copy(out=tokm[:].rearrange("p a b -> p (a b)"), in_=pb[6][:, 256:352]), ["pb6"], ["tokm"])
            for half in range(2):
                bk = f"pb{2 + half}"
                for j in range(8):
                    ft = 8 * half + j
                    T(lambda e, ft=ft, j=j, half=half: e.transpose(out=pbb[2 + half][:, 128 * j:128 * j + 128], in_=xc[:, ft, :], identity=identb[:]), [f"xc{ft}", "identb"], [bk])
                pv = pbb[2 + half][:, :].rearrange("p (a h2 q) -> p a h2 q", h2=2, q=64)
                prs = slice(8 * half, 8 * half + 8)
                dtT = tokm[:, 1, 16 * half:16 * half + 16].rearrange("p (a h2) -> p a h2", h2=2)
                wT = tokm[:, 2, 16 * half:16 * half + 16]
                V(lambda e, pv=pv, prs=prs, dtT=dtT: e.tensor_tensor(out=XA[:, prs, 0:64], in0=pv[:, :, 0, :], in1=dtT[:, :, 0:1].broadcast_to([128, 8, 64]), op=ALU.mult), [bk, "tokm"], ["XA"])
                V(lambda e, pv=pv, prs=prs, dtT=dtT: e.tensor_tensor(out=XB[:, prs, 64:128], in0=pv[:, :, 1, :], in1=dtT[:, :, 1:2].broadcast_to([128, 8, 64]), op=ALU.mult), [bk, "tokm"], ["XB"])
                V(lambda e, half=half, wT=wT: e.tensor_tensor(out=Xd[:, 16 * half:16 * half + 16, :], in0=pbb[2 + half][:, :].rearrange("p (h q) -> p h q", q=64), in1=wT.unsqueeze(2).broadcast_to([128, 16, 64]), op=ALU.mult), [bk, "tokm"], ["Xd"])
            for j in range(8):
                T(lambda e, j=j: e.transpose(out=pbb[4][:, 128 * j:128 * j + 128], in_=xc[:, 16 + j, :], identity=identb[:]), [f"xc{16 + j}", "identb"], ["pb4"])
            A(lambda e: e.activation(out=Btok[:].rearrange("p a b -> p (a b)"), in_=pbb[4][:, :], func=AF.Copy), ["pb4"], ["Btok"])
            ck(3)
            ypk = lambda pr: f"pb{2 + pr // 4}"
            yps = lambda pr: pb[2 + pr // 4][:, 128 * (pr % 4):128 * (pr % 4) + 128]
            for g in range(8):
                V(lambda e, g=g: e.tensor_tensor(out=rhsR[:], in0=acs[:].unsqueeze(1).broadcast_to([32, 4, 128]), in1=identf[0:32, 4 * g:4 * g + 4].unsqueeze(2).broadcast_to([32, 4, 128]), op=ALU.mult), ["acs", "identf"], ["rhsR"])
                T(lambda e: e.matmul(pb[6][:, :], lhsT=ones32[:], rhs=rhsR[:].rearrange("p a b -> p (a b)"), start=True, stop=True), ["ones32", "rhsR"], ["pb6"])
                A(lambda e: e.activation(out=Eg[:].rearrange("p a b -> p (a b)"), in_=pb[6][:, :], func=AF.Exp), ["pb6"], ["Eg"])
                G(lambda e, g=g: e.tensor_tensor(out=CEall[:, 4 * g:4 * g + 4, :], in0=Eg[:], in1=xc[:, 24 + g, :].unsqueeze(1).broadcast_to([128, 4, 128]), op=ALU.mult), ["Eg", f"xc{24 + g}"], ["CEall"])
                for h4 in range(4):
                    V(lambda e, g=g, h4=h4: e.scalar_tensor_tensor(out=Dm[:, h4, :], in0=pb[6][:, 128 * h4:128 * h4 + 128], scalar=tokm[:, 0, 4 * g + h4:4 * g + h4 + 1], in1=negm[:], op0=ALU.subtract, op1=ALU.min), ["pb6", "tokm", kn(negm)], ["Dm"])
                A(lambda e: e.activation(out=Lg[:].rearrange("p a b -> p (a b)"), in_=Dm[:].rearrange("p a b -> p (a b)"), func=AF.Exp), ["Dm"], ["Lg"])
                T(lambda e, g=g: e.matmul(pb[7][:, 0:128], lhsT=xc[:, 16 + g, :], rhs=xc[:, 24 + g, :], start=True, stop=True), [f"xc{16 + g}", f"xc{24 + g}"], ["pb7"])
                V(lambda e: e.tensor_tensor(out=Mg[:], in0=Lg[:], in1=pb[7][:, 0:128].unsqueeze(1).broadcast_to([128, 4, 128]), op=ALU.mult), ["Lg", "pb7"], ["Mg"])
                V(lambda e, g=g: e.tensor_copy(out=dch[:, 4 * g:4 * g + 4, 0:nseq], in_=Eg[:].rearrange("p a (s l) -> p a s l", l=L)[:, :, :, L - 1]), ["Eg"], ["dch"])
                for j in range(2):
                    pr = 2 * g + j
                    T(lambda e, pr=pr, j=j: e.matmul(yps(pr), lhsT=XA[:, pr, :], rhs=Mg[:, 2 * j, :], start=(pr % 4 == 0), stop=False, skip_group_check=True), ["XA", "Mg"], [ypk(pr)])
                    T(lambda e, pr=pr, j=j: e.matmul(yps(pr), lhsT=XB[:, pr, :], rhs=Mg[:, 2 * j + 1, :], start=False, stop=False, skip_group_check=True), ["XB", "Mg"], [ypk(pr)])
            for s in range(nseq):
                cs = slice(L * s, L * s + L)
                last = (s == nseq - 1)
                if smp:
                    G(lambda e, s=s: e.tensor_scalar(out=Bm[:].rearrange("p a b -> p (a b)"), in0=Btok[:].rearrange("p a b -> p (a b)"), scalar1=seqmask[:, s:s + 1], scalar2=None, op0=ALU.mult), ["Btok", "seqmask"], ["Bm"])
                for q4 in range(4 if smp else 1):
                    prs_ = range(4 * q4, 4 * q4 + 4) if smp else range(16)
                    if smp:
                        sview = I["ssm0"][s].rearrange("(pr h2) p n -> (h2 p) pr n", h2=2)[:, 4 * q4:4 * q4 + 4, :]
                        DMA(lambda e, sview=sview: e.dma_start(out=S0q[:], in_=sview), w=["S0q"])
                        for j in range(4):
                            T(lambda e, j=j: e.transpose(out=pb[7][:, 128 * j:128 * j + 128], in_=S0q[:, j, :], identity=identf[:]), ["S0q", "identf"], ["pb7"])
                        pv = pb[7][:, :].rearrange("p (a h2 q) -> p a h2 q", h2=2, q=64)
                        A(lambda e, q4=q4, pv=pv: e.activation(out=SA[:, 4 * q4:4 * q4 + 4, 0:64], in_=pv[:, :, 0, :], func=AF.Copy), ["pb7"], ["SA"])
                        A(lambda e, q4=q4, pv=pv: e.activation(out=SB[:, 4 * q4:4 * q4 + 4, 64:128], in_=pv[:, :, 1, :], func=AF.Copy), ["pb7"], ["SB"])
                    for pr in prs_:
                        T(lambda e, pr=pr, cs=cs: e.matmul(yps(pr)[:, cs], lhsT=SA[:, pr, :], rhs=CEall[:, 2 * pr, cs], start=False, stop=False, skip_group_check=True), ["SA", "CEall"], [ypk(pr)])
                        T(lambda e, pr=pr, cs=cs, last=last: e.matmul(yps(pr)[:, cs], lhsT=SB[:, pr, :], rhs=CEall[:, 2 * pr + 1, cs], start=False, stop=last, skip_group_check=True), ["SB", "CEall"], [ypk(pr)])
                    if smp:
                        for j in range(4):
                            pr = 4 * q4 + j
                            T(lambda e, pr=pr, j=j: e.matmul(pb[6][:, 128 * j:128 * j + 128], lhsT=Xd[:, 2 * pr:2 * pr + 2, :].rearrange("p a b -> p (a b)"), rhs=Bm[:, pr // 2, :], start=True, stop=True), ["Xd", "Bm"], ["pb6"])
                        for j in range(4):
                            pr = 4 * q4 + j
                            for h2 in range(2):
                                hs = slice(64 * h2, 64 * h2 + 64)
                                V(lambda e, pr=pr, j=j, hs=hs, s=s, h2=h2: e.scalar_tensor_tensor(out=Snq[hs, j, :], in0=S0q[hs, j, :], scalar=dch[hs, 2 * pr + h2, s:s + 1], in1=pb[6][hs, 128 * j:128 * j + 128], op0=ALU.mult, op1=ALU.add), ["S0q", "dch", "pb6"], ["Snq"])
                        oview = O["ssm_s"][s].rearrange("(pr h2) p n -> (h2 p) pr n", h2=2)[:, 4 * q4:4 * q4 + 4, :]
                        DMA(lambda e, oview=oview: e.dma_start(out=oview, in_=Snq[:]), ["Snq"], [])
            ck(4)
            for pr in range(16):
                V(lambda e, pr=pr: e.scalar_tensor_tensor(out=yv[:, pr, :], in0=xc[:, pr, :], scalar=dexp[:, pr:pr + 1], in1=yps(pr), op0=ALU.mult, op1=ALU.add), [f"xc{pr}", "dexp", ypk(pr)], [f"yv{pr}"])
                G(lambda e, pr=pr: e.tensor_tensor(out=yv[:, pr, :], in0=yv[:, pr, :], in1=zs[:, pr, :], op=ALU.mult), [f"yv{pr}", f"zs{pr}"], [f"yv{pr}"])
                A(lambda e, pr=pr: e.activation(out=ysq[:, pr, :], in_=yv[:, pr, :], func=AF.Square), [f"yv{pr}"], [f"ysq{pr}"])
            if not smp:
                for g in range(8):
                    T(lambda e, g=g: e.matmul(pb[2 + g // 2][:, 256 * (g % 2):256 * (g % 2) + 256], lhsT=Btok[:, g, :], rhs=Xd[:, 4 * g:4 * g + 4, :].rearrange("p a b -> p (a b)"), start=True, stop=True), ["Btok", "Xd"], [f"pb{2 + g // 2}"])
                V(lambda e: e.tensor_tensor(out=STt[:], in0=STt[:], in1=dch[:, :, 0:1].broadcast_to([128, 32, 64]), op=ALU.mult), ["STt", "dch"], ["STt"])
                for b4 in range(4):
                    V(lambda e, b4=b4: e.tensor_tensor(out=STt[:, 8 * b4:8 * b4 + 8, :], in0=STt[:, 8 * b4:8 * b4 + 8, :], in1=pb[2 + b4][:, :].rearrange("p (h q) -> p h q", q=64), op=ALU.add), ["STt", f"pb{2 + b4}"], ["STt"])
                sv = STt[:].rearrange("p (pr h2) q -> p pr h2 q", h2=2)
                A(lambda e, sv=sv: e.activation(out=SA[:, :, 0:64], in_=sv[:, :, 0, :], func=AF.Copy), ["STt"], ["SA"])
                A(lambda e, sv=sv: e.activation(out=SB[:, :, 64:128], in_=sv[:, :, 1, :], func=AF.Copy), ["STt"], ["SB"])
                if lastp:
                    for q4 in range(4):
                        for j in range(4):
                            pr = 4 * q4 + j
                            T(lambda e, pr=pr, j=j: e.transpose(out=pb[6][:, 128 * j:128 * j + 128], in_=STt[:, 2 * pr:2 * pr + 2, :].rearrange("p a b -> p (a b)"), identity=identf[:]), ["STt", "identf"], ["pb6"])
                        V(lambda e: e.tensor_copy(out=Snq[:].rearrange("p a b -> p (a b)"), in_=pb[6][:, :]), ["pb6"], ["Snq"])
                        oview = O["ssm_p"].rearrange("(pr h2) p n -> (h2 p) pr n", h2=2)[:, 4 * q4:4 * q4 + 4, :]
                        DMA(lambda e, oview=oview: e.dma_start(out=oview, in_=Snq[:]), ["Snq"], [])
            for gq in range(8):
                ps = pb[6 + gq // 4][:, 128 * (gq % 4):128 * (gq % 4) + 128]
                pk = f"pb{6 + gq // 4}"
                for j in range(2):
                    T(lambda e, ps=ps, gq=gq, j=j: e.matmul(ps, lhsT=onesb[:], rhs=ysq[:, 2 * gq + j, :], start=(j == 0), stop=(j == 1)), ["onesb", f"ysq{2 * gq + j}"], [pk])
                A(lambda e, ps=ps, gq=gq: e.activation(out=rstg[:, gq, :], in_=ps, func=AF.Sqrt, scale=1.0 / 256, bias=epscol[:]), [pk, "epscol"], [f"rstg{gq}"])
                V(lambda e, gq=gq: e.reciprocal(out=rstg[:, gq, :], in_=rstg[:, gq, :]), [f"rstg{gq}"], [f"rstg{gq}"])
                for j in range(2):
                    pr = 2 * gq + j
                    V(lambda e, pr=pr, gq=gq: e.scalar_tensor_tensor(out=ynT[:, pr, :], in0=yv[:, pr, :], scalar=ssdnw[:, pr:pr + 1], in1=rstg[:, gq, :], op0=ALU.mult, op1=ALU.mult), [f"yv{pr}", "ssdnw", f"rstg{gq}"], [f"ynT{pr}"])
            dense_T("w_down_ssd", 16, lambda k: ynT[:, k, :], lambda k: [f"ynT{k}"],
                    lambda m, ps, pk: V(lambda e: e.tensor_tensor(out=yaT[:, m, :], in0=ps, in1=gas[:, m, :], op=ALU.mult), [pk, f"ga{m}"], [f"yaT{m}"]))

            ck(5)
            if smp:
                for ri, nm in enumerate(["re0", "im0"]):
                    sv_ = I[nm].rearrange("s (gh gq g2) p -> s gq gh g2 p", gq=8, g2=2)
                    for s in range(16):
                        for gh in range(4):
                            DMA(lambda e, s=s, gh=gh, sv_=sv_: e.dma_start(out=s5io[8 * s:8 * s + 8, gh, :, :], in_=sv_[s][:, gh, :, :]), w=["s5io"])
                    for gh in range(4):
                        T(lambda e, gh=gh: e.transpose(out=pb[7][:, 0:128], in_=s5io[:, gh, :, :].rearrange("p a b -> p (a b)"), identity=identf[:]), ["s5io", "identf"], ["pb7"])
                        V(lambda e, gh=gh, ri=ri: e.tensor_copy(out=s0all[:, 8 * gh:8 * gh + 8, ri, :], in_=pb[7][:, 0:128].rearrange("p (s g) -> p g s", g=8)), ["pb7"], ["s0all"])
            Ls, nsg = (8, 16) if smp else (64, 2)
            for ft in range(8):
                gsl = slice(4 * ft, 4 * ft + 4)
                ecv = ecos[:, gsl, 0:Ls].unsqueeze(2).broadcast_to([128, 4, nsg, Ls])
                esv = esin[:, gsl, 0:Ls].unsqueeze(2).broadcast_to([128, 4, nsg, Ls])
                v4 = lambda ap: ap.rearrange("p q (s l) -> p q s l", l=Ls)
                for q in range(4):
                    G(lambda e, ft=ft, q=q: e.tensor_scalar(out=u5m[:, q, :], in0=u5[:, ft, :], scalar1=pairmask[:, q:q + 1], scalar2=None, op0=ALU.mult), [f"u5{ft}", "pairmask"], ["u5m"])
                for ri in range(2):
                    pbk = f"pb{6 + ri}"
                    for q in range(4):
                        T(lambda e, ft=ft, q=q, ri=ri: e.matmul(pb[6 + ri][:, 128 * q:128 * q + 128], lhsT=wb5[:, ft, ri, :], rhs=u5m[:, q, :], start=True, stop=True), ["wb5", "u5m"], [pbk])
                    A(lambda e, ri=ri: e.activation(out=bu[:, :, ri, :], in_=pb[6 + ri][:, :].rearrange("p (q t) -> p q t", t=128), func=AF.Copy), [pbk], ["bu"])
                bre_, bim_ = v4(bu[:, :, 0, :]), v4(bu[:, :, 1, :])
                V(lambda e, ecv=ecv: e.tensor_tensor(out=v4(bt1[:]), in0=ecv, in1=bre_, op=ALU.mult), ["ecos", "bu"], ["bt1"])
                G(lambda e, esv=esv: e.tensor_tensor(out=v4(bt2[:]), in0=esv, in1=bim_, op=ALU.mult), ["esin", "bu"], ["bt2"])
                V(lambda e: e.tensor_tensor(out=sgm[:, :, 0, :], in0=bt1[:], in1=bt2[:], op=ALU.add), ["bt1", "bt2"], ["sgm"])
                V(lambda e, ecv=ecv: e.tensor_tensor(out=v4(bt1[:]), in0=ecv, in1=bim_, op=ALU.mult), ["ecos", "bu", "sgm"], ["bt1"])
                G(lambda e, esv=esv: e.tensor_tensor(out=v4(bt2[:]), in0=esv, in1=bre_, op=ALU.mult), ["esin", "bu", "sgm"], ["bt2"])
                V(lambda e: e.tensor_tensor(out=sgm[:, :, 1, :], in0=bt1[:], in1=bt2[:], op=ALU.subtract), ["bt1", "bt2"], ["sgm"])
                if smp:
                    V(lambda e, gsl=gsl: e.tensor_tensor(out=rts[:], in0=rdec[:, gsl].unsqueeze(2).broadcast_to([128, 4, 128]), in1=rm128[:].unsqueeze(1).broadcast_to([128, 4, 128]), op=ALU.mult), ["rdec", "rm128"], ["rts"])
                else:
                    V(lambda e, gsl=gsl: e.tensor_copy(out=rts[:], in_=rdec[:, gsl].unsqueeze(2).broadcast_to([128, 4, 128])), ["rdec"], ["rts"])
                if smp:
                    n_ = 16
                    ab = lambda t: t[:, gsl].unsqueeze(2).broadcast_to([128, 4, n_])
                    cmul(cin_[:, :, 0, :], cin_[:, :, 1, :], ab(are), ab(aim), s0all[:, gsl, 0, :], s0all[:, gsl, 1, :],
                         ["are", "aim", "s0all"], ["cinr", "cini"], ctm[:, :, 0, :], ctm[:, :, 1, :], ["ctm0", "ctm1"])
                    for ri in range(2):
                        tgt = v4(sgm[:, :, ri, :])[:, :, :, 0]
                        V(lambda e, tgt=tgt, ri=ri: e.tensor_tensor(out=tgt, in0=tgt, in1=cin_[:, :, ri, :], op=ALU.add), ["sgm", "cinr", "cini"], ["sgm"])
                    for q in range(4):
                        for ri in range(2):
                            EV = V
                            EV(lambda e, q=q, ri=ri: e.tensor_tensor_scan(out=bu[:, q, ri, :], data0=rts[:, q, :], data1=sgm[:, q, ri, :], initial=0.0, op0=ALU.mult, op1=ALU.add), ["rts", "sgm"], ["bu"])
                else:
                    for sg in range(2):
                        c0 = 64 * sg
                        src_re = sgc[:, gsl, 0:1] if sg == 0 else bu[:, :, 0, 63:64]
                        src_im = sgc[:, gsl, 1:2] if sg == 0 else bu[:, :, 1, 63:64]
                        ab = lambda t: t[:, gsl].unsqueeze(2)
                        cmul(cin_[:, :, 0, 0:1], cin_[:, :, 1, 0:1], ab(aere), ab(aeim), src_re, src_im,
                             ["aere", "aeim", "sgc", "bu"], ["cinr", "cini"], ctm[:, :, 0, 0:1], ctm[:, :, 1, 0:1], ["ctm0", "ctm1"])
                        for ri in range(2):
                            tgt = sgm[:, :, ri, c0:c0 + 1]
                            V(lambda e, tgt=tgt, ri=ri: e.tensor_tensor(out=tgt, in0=tgt, in1=cin_[:, :, ri, 0:1], op=ALU.add), ["sgm", "cinr", "cini"], ["sgm"])
                        for q in range(4):
                            for ri in range(2):
                                EV = V
                                EV(lambda e, q=q, ri=ri, c0=c0: e.tensor_tensor_scan(out=bu[:, q, ri, c0:c0 + 64], data0=rts[:, q, c0:c0 + 64], data1=sgm[:, q, ri, c0:c0 + 64], initial=0.0, op0=ALU.mult, op1=ALU.add), ["rts", "sgm"], ["bu"])
                    V(lambda e, gsl=gsl: e.tensor_copy(out=sgc[:, gsl, :], in_=bu[:, :, :, 127]), ["bu"], ["sgc"])
                sre_, sim_ = v4(bu[:, :, 0, :]), v4(bu[:, :, 1, :])
                V(lambda e, ecv=ecv: e.tensor_tensor(out=v4(bt1[:]), in0=ecv, in1=sre_, op=ALU.mult), ["ecos", "bu"], ["bt1"])
                G(lambda e, esv=esv: e.tensor_tensor(out=v4(bt2[:]), in0=esv, in1=sim_, op=ALU.mult), ["esin", "bu"], ["bt2"])
                V(lambda e: e.tensor_tensor(out=s5s[:, :, 0, :], in0=bt1[:], in1=bt2[:], op=ALU.subtract), ["bt1", "bt2"], ["s5s"])
                if smp or lastp:
                    fsrc = lambda t: (v4(t[:])[:, :, :, Ls - 1] if smp else t[:, :, 127:128])
                    nf = 16 if smp else 1
                    V(lambda e, gsl=gsl, fsrc=fsrc, nf=nf: e.tensor_tensor(out=fin[:, gsl, 0, 0:nf], in0=fsrc(bt1), in1=fsrc(bt2), op=ALU.subtract), ["bt1", "bt2"], ["fin"])
                V(lambda e, ecv=ecv: e.tensor_tensor(out=v4(bt1[:]), in0=ecv, in1=sim_, op=ALU.mult), ["ecos", "bu", "s5s", "fin"], ["bt1"])
                G(lambda e, esv=esv: e.tensor_tensor(out=v4(bt2[:]), in0=esv, in1=sre_, op=ALU.mult), ["esin", "bu", "s5s", "fin"], ["bt2"])
                V(lambda e: e.tensor_tensor(out=s5s[:, :, 1, :], in0=bt1[:], in1=bt2[:], op=ALU.add), ["bt1", "bt2"], ["s5s"])
                if smp or lastp:
                    V(lambda e, gsl=gsl, fsrc=fsrc, nf=nf: e.tensor_tensor(out=fin[:, gsl, 1, 0:nf], in0=fsrc(bt1), in1=fsrc(bt2), op=ALU.add), ["bt1", "bt2"], ["fin"])
                ps4 = [pb[6][:, 128 * q:128 * q + 128] for q in range(4)]
                for q in range(4):
                    for ri in range(2):
                        T(lambda e, ft=ft, q=q, ri=ri: e.matmul(ps4[q], lhsT=wd5[:, ft, ri, :], rhs=s5s[:, q, ri, :], start=(ri == 0), stop=(ri == 1)), ["wd5", "s5s"], ["pb6"])
                for q in range(4):
                    rs_ = slice(32 * q, 32 * q + 32)
                    V(lambda e, ft=ft, q=q, rs_=rs_: e.scalar_tensor_tensor(out=y5t[rs_, :], in0=u5[rs_, ft, :], scalar=ds5[rs_, ft:ft + 1], in1=ps4[q][rs_, :], op0=ALU.mult, op1=ALU.add), [f"u5{ft}", "ds5", "pb6"], ["y5t"])
                V(lambda e: e.tensor_tensor(out=y5u[:], in0=y5t[:], in1=y5t[:], op=ALU.mult), ["y5t"], ["y5u"])
                V(lambda e: e.tensor_scalar(out=y5u[:], in0=y5u[:], scalar1=0.044715, scalar2=1.0, op0=ALU.mult, op1=ALU.add), ["y5u"], ["y5u"])
                V(lambda e: e.tensor_tensor(out=y5u[:], in0=y5u[:], in1=y5t[:], op=ALU.mult), ["y5u", "y5t"], ["y5u"])
                A(lambda e: e.activation(out=y5v[:], in_=y5u[:], func=AF.Sigmoid, scale=1.5957691216057308), ["y5u"], ["y5v"])
                V(lambda e, ft=ft: e.tensor_tensor(out=y5g[:, ft, :], in0=y5v[:], in1=y5t[:], op=ALU.mult), ["y5v", "y5t"], [f"y5g{ft}"])
            if lastp:
                for ri, nm in enumerate(["re_p", "im_p"]):
                    DMA(lambda e, ri=ri, nm=nm: e.dma_start(out=O[nm].rearrange("(gp g2) p -> (g2 p) gp", g2=2), in_=fin[:, :, ri, 0], allow_slow_non_contiguous=True), ["fin"], [])
            if smp:
                for ri, nm in enumerate(["re_s", "im_s"]):
                    for gh in range(4):
                        V(lambda e, gh=gh, ri=ri: e.tensor_copy(out=s5tr[:], in_=fin[:, 8 * gh:8 * gh + 8, ri, :].rearrange("p g s -> p s g")), ["fin"], ["s5tr"])
                        T(lambda e: e.transpose(out=pb[7][:, 0:128], in_=s5tr[:].rearrange("p a b -> p (a b)"), identity=identf[:]), ["s5tr", "identf"], ["pb7"])
                        V(lambda e, gh=gh: e.tensor_copy(out=s5io[:, gh, :, :].rearrange("p a b -> p (a b)"), in_=pb[7][:, 0:128]), ["pb7"], ["s5io"])
                    ov = O[nm].rearrange("s (gh gq g2) p -> s gq gh g2 p", gq=8, g2=2)
                    for s in range(16):
                        for gh in range(4):
                            DMA(lambda e, s=s, gh=gh, ov=ov: e.dma_start(out=ov[s][:, gh, :, :], in_=s5io[8 * s:8 * s + 8, gh, :, :]), ["s5io"], [])

            ck(6)
            def glu_evac(m, ps, pk):
                A(lambda e: e.activation(out=glus[:], in_=ps, func=AF.Sigmoid, bias=bglu[:, m:m + 1]), [pk, "bglu"], ["glus"])
                V(lambda e: e.tensor_tensor(out=glus[:], in0=glus[:], in1=y5g[:, m, :], op=ALU.mult), ["glus", f"y5g{m}"], ["glus"])
                V(lambda e: e.tensor_tensor(out=y5f[:, m, :], in0=glus[:], in1=z5s[:, m, :], op=ALU.mult), ["glus", f"z5{m}"], [f"y5f{m}"])
            dense_T("w_glu", 8, lambda k: y5g[:, k, :], lambda k: [f"y5g{k}"], glu_evac)
            dense_T("w_down_s5", 8, lambda k: y5f[:, k, :], lambda k: [f"y5f{k}"],
                    lambda m, ps, pk: V(lambda e: e.tensor_tensor(out=ybT[:, m, :], in0=ps, in1=gbs[:, m, :], op=ALU.mult), [pk, f"gb{m}"], [f"ybT{m}"]))
            for m in range(8):
                V(lambda e, m=m: e.tensor_tensor(out=mixT[:, m, :], in0=yaT[:, m, :], in1=ybT[:, m, :], op=ALU.add), [f"yaT{m}", f"ybT{m}"], [f"mixT{m}"])
            for cb in range(4):
                wt, wk = wload(WS["w_out"][:, 256 * cb:256 * cb + 256])
                hb = cb % 2
                ps = pb[hb][:, 0:256]
                pks = [f"pb{hb}"]
                for k in range(8):
                    T(lambda e, wt=wt, k=k, ps=ps: e.matmul(ps, lhsT=mixT[:, k, :], rhs=wt[:, k, 0:256], start=(k == 0), stop=(k == 7)), [wk, f"mixT{k}"], pks)
                V(lambda e, cb=cb, ps=ps: e.tensor_tensor(out=ot[:, 256 * cb:256 * cb + 256], in0=ps, in1=xt[:, 256 * cb:256 * cb + 256], op=ALU.add), pks + ["xt"], ["ot"])
            G(lambda e: e.memset(ss[:], 0.0), w=["ss"])
            A(lambda e: e.activation(out=xt[:], in_=ot[:], func=AF.Square, accum_out=ss[:]), ["ot", "ss"], ["xt", "ss"])
            A(lambda e: e.activation(out=rstd[:], in_=ss[:], func=AF.Sqrt, scale=1.0 / D, bias=epscol[:]), ["ss", "epscol"], ["rstd"])
            V(lambda e: e.reciprocal(out=rstd[:], in_=rstd[:]), ["rstd"], ["rstd"])
            V(lambda e: e.scalar_tensor_tensor(out=ot[:], in0=ot[:], scalar=rstd[:, 0:1], in1=fnw[:], op0=ALU.mult, op1=ALU.mult), ["ot", "rstd", "fnw"], ["ot"])
            DMA(lambda e: e.dma_start(out=ydst, in_=ot[:]), ["ot"], [])

        for ti in (tiles if tiles is not None else range(17)):
            try:
                tile_body(ti)
            except _Stop:
                pass
        P.emit(nc)
    return nc


_NC = {}


def _shard(inputs, c):
    d = {}
    d["xp"] = np.ascontiguousarray(inputs["x_prompt"][c])
    d["xs"] = np.ascontiguousarray(inputs["x_sample"][16 * c:16 * c + 16].reshape(128, 1024))
    d["conv0"] = np.ascontiguousarray(inputs["state_conv"][0, 16 * c:16 * c + 16])
    d["ssm0"] = np.ascontiguousarray(inputs["state_ssm"][0, 16 * c:16 * c + 16])
    d["re0"] = np.ascontiguousarray(inputs["state_s5_re"][0, 16 * c:16 * c + 16])
    d["im0"] = np.ascontiguousarray(inputs["state_s5_im"][0, 16 * c:16 * c + 16])
    for k in ("norm_w", "w_in", "conv_w", "conv_b", "dt_bias", "A_log", "D_ssd", "ssd_norm_w", "w_down_ssd", "lam_re",
              "lam_im", "log_dt", "B_re", "B_im", "C_re", "C_im", "D_s5", "w_glu", "b_glu", "w_down_s5", "w_out"):
        d[k] = np.ascontiguousarray(inputs[k][0])
    d["final_norm_w"] = np.ascontiguousarray(inputs["final_norm_w"])
    return d


def kernel(**inputs):
    inputs = {k: np.asarray(v, dtype=np.float32) for k, v in inputs.items()}
    if "nc" not in _NC:
        _NC["nc"] = build()
    nc = _NC["nc"]
    consts = host_consts()
    in_maps = []
    for c in range(NCORES):
        d = _shard(inputs, c)
        d.update(consts)
        in_maps.append(d)
    res = run_bass_kernel_spmd(nc, in_maps, core_ids=list(range(NCORES))).results
    cat = lambda n: np.concatenate([r[n] for r in res], axis=0)
    y_p = np.stack([r["y_p"] for r in res], 0)
    y_s = cat("y_s").reshape(128, 8, 1024)
    conv_p = np.stack([r["conv_p"] for r in res], 0)[None]
    ssm_p = np.stack([r["ssm_p"] for r in res], 0)[None]
    re_p = np.stack([r["re_p"] for r in res], 0)[None]
    im_p = np.stack([r["im_p"] for r in res], 0)[None]
    conv_s = cat("conv_s")[None]
    ssm_s = cat("ssm_s")[None]
    re_s = cat("re_s")[None]
    im_s = cat("im_s")[None]
    return tuple(np.ascontiguousarray(a, dtype=np.float32) for a in
                 (y_p, y_s, conv_p, ssm_p, re_p, im_p, conv_s, ssm_s, re_s, im_s))
```

```python
import contextlib
import math
import numpy as np
import concourse.bass as bass
import concourse.mybir as mybir
from concourse.bass_utils import run_bass_kernel_spmd

F32 = mybir.dt.float32
BF16 = mybir.dt.bfloat16
I32 = mybir.dt.int32
ALU = mybir.AluOpType
AF = mybir.ActivationFunctionType

ENGS = ("pe", "act", "dve", "pool", "sp")
N_DMA_SEM = 24
NCORES = 8
D = 1024
PROJ = 10272
Z0, XBC0, DT0, U0, Z50, GA0, GB0 = 0, 2048, 6144, 6176, 7200, 8224, 9248
EPS = 1e-6
SIN_SCALE = 6.28318


class Prog:
    def __init__(self):
        self.ops = []
        self.last_write = {}
        self.readers = {}
        self.dma_count = {e: 0 for e in ENGS}
        self.slot_last = {}
        self.last_op = {}
        self.pending = {}

    def barrier(self):
        deps = set(self.last_op.values()) | set(self.slot_last.values())
        for e in ENGS:
            self.pending[e] = set(deps) | self.pending.get(e, set())

    def op(self, eng, fn, reads=(), writes=(), dma=False):
        writes = list(writes) + [k for k in reads if k.startswith("pb")]
        reads = [k for k in reads if not k.startswith("pb")]
        oid = len(self.ops)
        deps = set()
        for k in reads:
            if k in self.last_write:
                deps.add(self.last_write[k])
        for k in writes:
            if k in self.last_write:
                deps.add(self.last_write[k])
            deps.update(self.readers.get(k, ()))
        if eng in self.pending:
            deps |= self.pending.pop(eng)
        rec = dict(eng=eng, fn=fn, deps=deps, dma=dma, sig=False, seq=None)
        if dma:
            i = self.dma_count[eng]
            self.dma_count[eng] += 1
            slot = i % N_DMA_SEM
            rec["slot"] = slot
            rec["val"] = 16 * (i // N_DMA_SEM + 1)
            prev = self.slot_last.get((eng, slot))
            if prev is not None:
                deps.add(prev)
            self.slot_last[(eng, slot)] = oid
        else:
            self.last_op[eng] = oid
        deps.discard(oid)
        self.ops.append(rec)
        for k in writes:
            self.last_write[k] = oid
            self.readers[k] = []
        for k in reads:
            self.readers.setdefault(k, []).append(oid)
        return oid

    def emit(self, nc):
        ops = self.ops

        def skip(dd, o):
            return (not dd["dma"]) and dd["eng"] == o["eng"] == "pe" and not o["dma"]
        for o in ops:
            for d in o["deps"]:
                dd = ops[d]
                if dd["dma"] or skip(dd, o):
                    continue
                dd["sig"] = True
        cnt = {e: 0 for e in ENGS}
        for o in ops:
            if o["sig"]:
                cnt[o["eng"]] += 1
                o["seq"] = cnt[o["eng"]]
        with contextlib.ExitStack() as st:
            csem = {e: st.enter_context(nc.semaphore("c_" + e)) for e in ENGS}
            dsem = {(e, s): st.enter_context(nc.semaphore(f"d_{e}{s}"))
                    for e in ENGS if self.dma_count[e] > 0
                    for s in range(min(N_DMA_SEM, self.dma_count[e]))}
            block = st.enter_context(nc.Block())

            def run(eng_name, eng):
                waited = {}
                for o in ops:
                    if o["eng"] != eng_name:
                        continue
                    for d in sorted(o["deps"]):
                        dd = ops[d]
                        if dd["dma"]:
                            key = ("d", dd["eng"], dd["slot"])
                            sem = dsem[(dd["eng"], dd["slot"])]
                            val = dd["val"]
                        else:
                            if skip(dd, o):
                                continue
                            key = ("c", dd["eng"])
                            sem = csem[dd["eng"]]
                            val = dd["seq"]
                        if waited.get(key, 0) >= val:
                            continue
                        waited[key] = val
                        eng.wait_ge(sem, val)
                    ins = o["fn"](eng)
                    if o["dma"]:
                        ins.then_inc(dsem[(eng_name, o["slot"])], 16)
                    elif o["sig"]:
                        ins.then_inc(csem[eng_name], 1)
                if eng_name == "sp":
                    for (e, s), oid in self.slot_last.items():
                        eng.wait_ge(dsem[(e, s)], ops[oid]["val"])

            block.tensor(lambda e: run("pe", e))
            block.scalar(lambda e: run("act", e))
            block.vector(lambda e: run("dve", e))
            block.gpsimd(lambda e: run("pool", e))
            block.sync(lambda e: run("sp", e))


def host_consts():
    c = {}
    c["ident"] = np.eye(128, dtype=np.float32)
    s = np.arange(128)
    c["negm_p"] = np.where(s[None, :] >= s[:, None], 0.0, -30000.0).astype(np.float32)
    same = (s[None, :] // 8) == (s[:, None] // 8)
    c["negm_s"] = np.where((s[None, :] >= s[:, None]) & same, 0.0, -30000.0).astype(np.float32)
    rp = np.ones((32, 128), np.float32); rp[:, 0] = 0
    rs = np.ones((32, 128), np.float32); rs[:, ::8] = 0
    c["rm_p"] = rp
    c["rm_s"] = rs
    r128 = np.ones((128, 128), np.float32); r128[:, ::8] = 0
    c["rm128"] = r128
    c["seqmask"] = (s[:, None] // 8 == np.arange(16)[None, :]).astype(np.float32)
    c["pairmask"] = (s[:, None] // 32 == np.arange(4)[None, :]).astype(np.float32)
    c["iota"] = np.broadcast_to(np.arange(128, dtype=np.float32)[None, :], (128, 128)).copy()
    return c


IN_SPECS = [
    ("xp", (2048, 1024)), ("xs", (128, 1024)), ("conv0", (16, 3, 4096)), ("ssm0", (16, 32, 64, 128)),
    ("re0", (16, 64, 64)), ("im0", (16, 64, 64)), ("norm_w", (1024,)), ("w_in", (1024, PROJ)),
    ("conv_w", (4, 4096)), ("conv_b", (4096,)), ("dt_bias", (32,)), ("A_log", (32,)), ("D_ssd", (32,)),
    ("ssd_norm_w", (2048,)), ("w_down_ssd", (2048, 1024)), ("lam_re", (64, 64)), ("lam_im", (64, 64)),
    ("log_dt", (64,)), ("B_re", (64, 64, 16)), ("B_im", (64, 64, 16)), ("C_re", (64, 16, 64)),
    ("C_im", (64, 16, 64)), ("D_s5", (1024,)), ("w_glu", (1024, 1024)), ("b_glu", (1024,)),
    ("w_down_s5", (1024, 1024)), ("w_out", (1024, 1024)), ("final_norm_w", (1024,)),
    ("ident", (128, 128)), ("negm_p", (128, 128)), ("negm_s", (128, 128)), ("rm_p", (32, 128)),
    ("rm_s", (32, 128)), ("rm128", (128, 128)), ("seqmask", (128, 16)), ("pairmask", (128, 4)),
    ("iota", (128, 128)),
]
OUT_SPECS = [
    ("y_p", (2048, 1024)), ("y_s", (128, 1024)), ("conv_p", (3, 4096)), ("ssm_p", (32, 64, 128)),
    ("re_p", (64, 64)), ("im_p", (64, 64)), ("conv_s", (16, 3, 4096)), ("ssm_s", (16, 32, 64, 128)),
    ("re_s", (16, 64, 64)), ("im_s", (16, 64, 64)),
]


STOP = [None]
BLK = [None]


class _Stop(Exception):
    pass


def ck(n):
    if STOP[0] is not None and n >= STOP[0]:
        raise _Stop()


def build(tiles=None):
    nc = bass.Bass("TRN2", target_bir_lowering=False)
    I = {n: nc.dram_tensor(n, list(s), F32, kind="ExternalInput").ap() for n, s in IN_SPECS}
    O = {n: nc.dram_tensor(n, list(s), F32, kind="ExternalOutput").ap() for n, s in OUT_SPECS}
    WS = {
        "w_in": nc.dram_tensor("ws_w_in", [1024, PROJ], BF16).ap(),
        "w_down_ssd": nc.dram_tensor("ws_wds", [2048, 1024], BF16).ap(),
        "w_glu": nc.dram_tensor("ws_wglu", [1024, 1024], BF16).ap(),
        "w_down_s5": nc.dram_tensor("ws_wd5", [1024, 1024], BF16).ap(),
        "w_out": nc.dram_tensor("ws_wout", [1024, 1024], BF16).ap(),
    }
    P = Prog()

    def V(fn, r=(), w=()):
        P.op("dve", fn, r, w)

    def G(fn, r=(), w=()):
        P.op("pool", fn, r, w)

    def A(fn, r=(), w=()):
        P.op("act", fn, r, w)

    def T(fn, r=(), w=()):
        P.op("pe", fn, r, w)

    def DMA(fn, r=(), w=()):
        P.op("sp", fn, r, w, dma=True)

    def DMAO(fn, r=(), w=()):
        P.op("pool", fn, r, w, dma=True)

    st = contextlib.ExitStack()
    with st:
        def sb(name, shape, dt=F32, stack=None):
            return (stack or st).enter_context(nc.sbuf_tensor("s_" + name, list(shape), dt))

        def kn(t):
            n_ = getattr(t, "name")
            return n_[2:] if n_.startswith("s_") else n_
        pb = [st.enter_context(nc.psum_tensor(f"pb{i}", [128, 512], F32)) for i in range(8)]
        pbb = [p.bitcast(BF16) for p in pb]

        identf = sb("identf", [128, 128]); identb = sb("identb", [128, 128], BF16)
        negm_p = sb("negm_p", [128, 128]); negm_s = sb("negm_s", [128, 128])
        rm_p = sb("rm_p", [32, 128]); rm_s = sb("rm_s", [32, 128]); rm128 = sb("rm128", [128, 128])
        seqmask = sb("seqmask", [128, 16]); pairmask = sb("pairmask", [128, 4])
        ones32 = sb("ones32", [32, 128]); onesb = sb("onesb", [128, 128], BF16)
        onecol = sb("onecol", [128, 1]); epscol = sb("epscol", [128, 1])
        for nm, t in [("ident", identf), ("negm_p", negm_p), ("negm_s", negm_s), ("rm_p", rm_p), ("rm_s", rm_s),
                      ("rm128", rm128), ("seqmask", seqmask), ("pairmask", pairmask)]:
            DMA(lambda e, nm=nm, t=t: e.dma_start(out=t[:], in_=I[nm]), w=[kn(t)])
        V(lambda e: e.tensor_copy(out=identb[:], in_=identf[:]), ["identf"], ["identb"])
        G(lambda e: e.memset(ones32[:], 1.0), w=["ones32"])
        G(lambda e: e.memset(onesb[:], 1.0), w=["onesb"])
        G(lambda e: e.memset(onecol[:], 1.0), w=["onecol"])
        G(lambda e: e.memset(epscol[:], EPS), w=["epscol"])

        def colvec(name, src, nt):
            t = sb(name, [128, nt])
            DMA(lambda e: e.dma_start(out=t[:], in_=src.rearrange("(j p) -> p j", p=128), allow_slow_non_contiguous=True), w=[name])
            return t
        normw = colvec("normw", I["norm_w"], 8)
        convb = colvec("convb", I["conv_b"], 32)
        ssdnw = colvec("ssdnw", I["ssd_norm_w"], 16)
        ds5 = colvec("ds5", I["D_s5"], 8)
        bglu = colvec("bglu", I["b_glu"], 8)
        convw = sb("convw", [128, 32, 4])
        for k in range(4):
            DMA(lambda e, k=k: e.dma_start(out=convw[:, :, k], in_=I["conv_w"][k].rearrange("(j p) -> p j", p=128), allow_slow_non_contiguous=True), w=["convw"])
        fnw = sb("fnw", [128, 1024])
        DMA(lambda e: e.dma_start(out=fnw[:], in_=I["final_norm_w"].partition_broadcast(128)), w=["fnw"])
        dtb = sb("dtb", [32, 1]); aneg = sb("aneg", [32, 1])
        DMA(lambda e: e.dma_start(out=dtb[:], in_=I["dt_bias"].rearrange("(h o) -> h o", o=1)), w=["dtb"])
        DMA(lambda e: e.dma_start(out=aneg[:], in_=I["A_log"].rearrange("(h o) -> h o", o=1)), w=["aneg"])
        A(lambda e: e.activation(out=aneg[:], in_=aneg[:], func=AF.Exp), ["aneg"], ["aneg"])
        V(lambda e: e.tensor_scalar(out=aneg[:], in0=aneg[:], scalar1=-1.0, scalar2=None, op0=ALU.mult), ["aneg"], ["aneg"])
        dexp = sb("dexp", [128, 16])
        dv = I["D_ssd"].rearrange("(pr h2) -> h2 pr", h2=2)
        for h2 in range(2):
            DMA(lambda e, h2=h2: e.dma_start(out=dexp[64 * h2:64 * h2 + 64, :], in_=dv[h2].partition_broadcast(64), allow_slow_non_contiguous=True), w=["dexp"])
        are = sb("are", [128, 32]); aim = sb("aim", [128, 32]); rdec = sb("rdec", [128, 32])
        aere = sb("aere", [128, 32]); aeim = sb("aeim", [128, 32])
        ecos = sb("ecos", [128, 32, 64]); esin = sb("esin", [128, 32, 64])
        wb5 = sb("wb5", [128, 8, 2, 128], BF16); wd5 = sb("wd5", [128, 8, 2, 128], BF16)
        halo = sb("halo", [128, 32, 3])
        STt = sb("STt", [128, 32, 64])
        SA = sb("SA", [128, 16, 128], BF16); SB = sb("SB", [128, 16, 128], BF16)
        XA = sb("XA", [128, 16, 128], BF16); XB = sb("XB", [128, 16, 128], BF16)
        sgc = sb("sgc", [128, 32, 2])
        for t_ in (halo, STt, SA, SB, XA, XB, sgc, wd5):
            G(lambda e, t_=t_: e.memset(t_[:], 0.0), w=([kn(t_)] if kn(t_) not in ("SA", "SB") else [f"{kn(t_)}{i}" for i in range(4)]))

        pst = contextlib.ExitStack()
        with pst:
            def psb(name, shape, dt=F32):
                return sb(name, shape, dt, stack=pst)
            iota = psb("iota", [128, 128])
            DMA(lambda e: e.dma_start(out=iota[:], in_=I["iota"]), w=["iota"])
            lre = psb("lre", [128, 32]); lim = psb("lim", [128, 32]); stp = psb("stp", [128, 32]); th = psb("th", [128, 32])
            DMA(lambda e: e.dma_start(out=lre[:], in_=I["lam_re"].rearrange("(gp g2) p -> (g2 p) gp", g2=2), allow_slow_non_contiguous=True), w=["lre"])
            DMA(lambda e: e.dma_start(out=lim[:], in_=I["lam_im"].rearrange("(gp g2) p -> (g2 p) gp", g2=2), allow_slow_non_contiguous=True), w=["lim"])
            ldv = I["log_dt"].rearrange("(gp g2) -> g2 gp", g2=2)
            for g2 in range(2):
                DMA(lambda e, g2=g2: e.dma_start(out=stp[64 * g2:64 * g2 + 64, :], in_=ldv[g2].partition_broadcast(64), allow_slow_non_contiguous=True), w=["stp"])
            A(lambda e: e.activation(out=stp[:], in_=stp[:], func=AF.Exp), ["stp"], ["stp"])
            V(lambda e: e.tensor_tensor(out=rdec[:], in0=lre[:], in1=stp[:], op=ALU.mult), ["lre", "stp"], ["rdec"])
            A(lambda e: e.activation(out=rdec[:], in_=rdec[:], func=AF.Exp), ["rdec"], ["rdec"])
            V(lambda e: e.tensor_tensor(out=th[:], in0=lim[:], in1=stp[:], op=ALU.mult), ["lim", "stp"], ["th"])
            V(lambda e: e.tensor_scalar(out=th[:], in0=th[:], scalar1=1.0 / (2.0 * math.pi), scalar2=None, op0=ALU.mult), ["th"], ["th"])
            tmpx = psb("tmpx", [128, 2048]); tmpf = psb("tmpf", [128, 2048]); tmpi = psb("tmpi", [128, 2048], I32)

            def sin_turns(out_ap, mk_x, n, off, rkeys, wkeys):
                V(lambda e: mk_x(e, tmpx[:, 0:n]), rkeys, ["tmpx"])
                if off:
                    V(lambda e: e.tensor_scalar(out=tmpx[:, 0:n], in0=tmpx[:, 0:n], scalar1=float(off), scalar2=None, op0=ALU.add), ["tmpx"], ["tmpx"])
                V(lambda e: e.tensor_copy(out=tmpi[:, 0:n], in_=tmpx[:, 0:n]), ["tmpx"], ["tmpi"])
                V(lambda e: e.tensor_copy(out=tmpf[:, 0:n], in_=tmpi[:, 0:n]), ["tmpi"], ["tmpf"])
                V(lambda e: e.tensor_tensor(out=tmpx[:, 0:n], in0=tmpx[:, 0:n], in1=tmpf[:, 0:n], op=ALU.subtract), ["tmpx", "tmpf"], ["tmpx"])
                A(lambda e: e.activation(out=out_ap, in_=tmpx[:, 0:n], func=AF.Sin, scale=SIN_SCALE), ["tmpx"], wkeys)
            cs1 = psb("cs1", [128, 32]); sn1 = psb("sn1", [128, 32])
            x1 = lambda e, dst: e.tensor_copy(out=dst, in_=th[:])
            sin_turns(sn1[:], x1, 32, 0.0, ["th"], ["sn1"])
            sin_turns(cs1[:], x1, 32, 0.25, ["th"], ["cs1"])
            V(lambda e: e.tensor_tensor(out=are[:], in0=rdec[:], in1=cs1[:], op=ALU.mult), ["rdec", "cs1"], ["are"])
            V(lambda e: e.tensor_tensor(out=aim[:], in0=rdec[:], in1=sn1[:], op=ALU.mult), ["rdec", "sn1"], ["aim"])
            x64 = lambda e, dst: e.tensor_scalar(out=dst, in0=th[:], scalar1=64.0, scalar2=None, op0=ALU.mult)
            sin_turns(sn1[:], x64, 32, 0.0, ["th"], ["sn1"])
            sin_turns(cs1[:], x64, 32, 0.25, ["th"], ["cs1"])
            V(lambda e: e.tensor_tensor(out=aere[:], in0=rdec[:], in1=cs1[:], op=ALU.mult), ["rdec", "cs1"], ["aere"])
            V(lambda e: e.tensor_tensor(out=aeim[:], in0=rdec[:], in1=sn1[:], op=ALU.mult), ["rdec", "sn1"], ["aeim"])
            xt_ = lambda e, dst: e.tensor_tensor(out=dst.rearrange("p (a b) -> p a b", b=64), in0=th[:].unsqueeze(2).broadcast_to([128, 32, 64]),
                                                 in1=iota[:, 0:64].unsqueeze(1).broadcast_to([128, 32, 64]), op=ALU.mult)
            sin_turns(esin[:].rearrange("p a b -> p (a b)"), xt_, 2048, 0.0, ["th", "iota"], ["esin"])
            sin_turns(ecos[:].rearrange("p a b -> p (a b)"), xt_, 2048, 0.25, ["th", "iota"], ["ecos"])
            gre = psb("gre", [128, 32]); gim = psb("gim", [128, 32]); den = psb("den", [128, 32])
            t1 = psb("t1s", [128, 32]); t2 = psb("t2s", [128, 32]); am1 = psb("am1", [128, 32])
            V(lambda e: e.tensor_scalar(out=am1[:], in0=are[:], scalar1=-1.0, scalar2=None, op0=ALU.add), ["are"], ["am1"])
            V(lambda e: e.tensor_tensor(out=den[:], in0=lre[:], in1=lre[:], op=ALU.mult), ["lre"], ["den"])
            V(lambda e: e.tensor_tensor(out=t1[:], in0=lim[:], in1=lim[:], op=ALU.mult), ["lim"], ["t1s"])
            V(lambda e: e.tensor_tensor(out=den[:], in0=den[:], in1=t1[:], op=ALU.add), ["den", "t1s"], ["den"])
            V(lambda e: e.reciprocal(out=den[:], in_=den[:]), ["den"], ["den"])
            V(lambda e: e.tensor_tensor(out=t1[:], in0=am1[:], in1=lre[:], op=ALU.mult), ["am1", "lre"], ["t1s"])
            V(lambda e: e.tensor_tensor(out=t2[:], in0=aim[:], in1=lim[:], op=ALU.mult), ["aim", "lim"], ["t2s"])
            V(lambda e: e.tensor_tensor(out=gre[:], in0=t1[:], in1=t2[:], op=ALU.add), ["t1s", "t2s"], ["gre"])
            V(lambda e: e.tensor_tensor(out=gre[:], in0=gre[:], in1=den[:], op=ALU.mult), ["gre", "den"], ["gre"])
            V(lambda e: e.tensor_tensor(out=t1[:], in0=aim[:], in1=lre[:], op=ALU.mult), ["aim", "lre"], ["t1s"])
            V(lambda e: e.tensor_tensor(out=t2[:], in0=am1[:], in1=lim[:], op=ALU.mult), ["am1", "lim"], ["t2s"])
            V(lambda e: e.tensor_tensor(out=gim[:], in0=t1[:], in1=t2[:], op=ALU.subtract), ["t1s", "t2s"], ["gim"])
            V(lambda e: e.tensor_tensor(out=gim[:], in0=gim[:], in1=den[:], op=ALU.mult), ["gim", "den"], ["gim"])
            bre = psb("bre", [128, 32, 16]); bim = psb("bim", [128, 32, 16]); bbr = psb("bbr", [128, 32, 16]); bbi = psb("bbi", [128, 32, 16]); tb = psb("tbs", [128, 32, 16])
            DMA(lambda e: e.dma_start(out=bre[:], in_=I["B_re"].rearrange("(gp g2) p k -> (g2 p) gp k", g2=2)), w=["bre"])
            DMA(lambda e: e.dma_start(out=bim[:], in_=I["B_im"].rearrange("(gp g2) p k -> (g2 p) gp k", g2=2)), w=["bim"])
            gb_ = lambda t: t[:].unsqueeze(2).broadcast_to([128, 32, 16])
            V(lambda e: e.tensor_tensor(out=bbr[:], in0=bre[:], in1=gb_(gre), op=ALU.mult), ["bre", "gre"], ["bbr"])
            V(lambda e: e.tensor_tensor(out=tb[:], in0=bim[:], in1=gb_(gim), op=ALU.mult), ["bim", "gim"], ["tbs"])
            V(lambda e: e.tensor_tensor(out=bbr[:], in0=bbr[:], in1=tb[:], op=ALU.subtract), ["bbr", "tbs"], ["bbr"])
            V(lambda e: e.tensor_tensor(out=bbi[:], in0=bim[:], in1=gb_(gre), op=ALU.mult), ["bim", "gre"], ["bbi"])
            V(lambda e: e.tensor_tensor(out=tb[:], in0=bre[:], in1=gb_(gim), op=ALU.mult), ["bre", "gim"], ["tbs"])
            V(lambda e: e.tensor_tensor(out=bbi[:], in0=bbi[:], in1=tb[:], op=ALU.add), ["bbi", "tbs"], ["bbi"])
            cin = psb("cin", [128, 4, 2, 64])
            cst = [psb("cstr", [128, 32, 16]), psb("csti", [128, 32, 16])]
            for ri, nm in enumerate(["C_re", "C_im"]):
                cv = I[nm].rearrange("(gh gq g2) k p -> gq k gh g2 p", gq=8, g2=2)
                for gq in range(8):
                    for gh in range(4):
                        DMA(lambda e, gq=gq, gh=gh, cv=cv: e.dma_start(out=cin[16 * gq:16 * gq + 16, gh, :, :], in_=cv[gq][:, gh, :, :]), w=["cin"])
                for gh in range(4):
                    T(lambda e, gh=gh: e.transpose(out=pb[7][:, 0:128], in_=cin[:, gh, :, :].rearrange("p a b -> p (a b)"), identity=identf[:]), ["cin", "identf"], ["pb7"])
                    V(lambda e, gh=gh, ri=ri: e.tensor_copy(out=cst[ri][:, 8 * gh:8 * gh + 8, :], in_=pb[7][:, 0:128].rearrange("p (a b) -> p a b", b=16)), ["pb7"], [kn(cst[ri])])
            zst = psb("zst", [128, 4, 2, 16])
            G(lambda e: e.memset(zst[:], 0.0), w=["zst"])
            for ft in range(8):
                for ri in range(2):
                    src = [bbr, bbi][ri]
                    for h2 in range(2):
                        V(lambda e, ft=ft, h2=h2, src=src: e.tensor_copy(out=zst[64 * h2:64 * h2 + 64, :, h2, :], in_=src[64 * h2:64 * h2 + 64, 4 * ft:4 * ft + 4, :]), [kn(src)], ["zst"])
                    T(lambda e: e.transpose(out=pb[6][:, 0:128], in_=zst[:].rearrange("p a b c -> p (a b c)"), identity=identf[:]), ["zst", "identf"], ["pb6"])
                    V(lambda e, ft=ft, ri=ri: e.tensor_copy(out=wb5[:, ft, ri, :], in_=pb[6][:, 0:128]), ["pb6"], ["wb5"])
                    for h2 in range(2):
                        V(lambda e, ft=ft, ri=ri, h2=h2: e.tensor_scalar(
                            out=wd5[64 * h2:64 * h2 + 64, ft, ri, :].rearrange("p (q g k) -> p q g k", g=2, k=16)[:, :, h2, :],
                            in0=cst[ri][64 * h2:64 * h2 + 64, 4 * ft:4 * ft + 4, :], scalar1=(1.0 if ri == 0 else -1.0), scalar2=None, op0=ALU.mult),
                          [kn(cst[ri])], ["wd5"])
            NST = 6
            stg = [psb(f"stg{i}", [128, 2048]) for i in range(NST)]
            stgb = [psb(f"stgb{i}", [128, 2048], BF16) for i in range(NST)]
            cnt = [0]

            def cast_rows(src, dst, ncols):
                for c0 in range(0, ncols, 2048):
                    w_ = min(2048, ncols - c0)
                    i = cnt[0] % NST
                    cnt[0] += 1
                    DMA(lambda e, i=i, c0=c0, w_=w_: e.dma_start(out=stg[i][:, 0:w_], in_=src[:, c0:c0 + w_]), w=[f"stg{i}"])
                    sel = cnt[0] % 3
                    if sel == 0:
                        G(lambda e, i=i, w_=w_: e.tensor_copy(out=stgb[i][:, 0:w_], in_=stg[i][:, 0:w_]), [f"stg{i}"], [f"stgb{i}"])
                    elif sel == 1:
                        A(lambda e, i=i, w_=w_: e.activation(out=stgb[i][:, 0:w_], in_=stg[i][:, 0:w_], func=AF.Copy), [f"stg{i}"], [f"stgb{i}"])
                    else:
                        V(lambda e, i=i, w_=w_: e.tensor_copy(out=stgb[i][:, 0:w_], in_=stg[i][:, 0:w_]), [f"stg{i}"], [f"stgb{i}"])
                    DMAO(lambda e, i=i, c0=c0, w_=w_: e.dma_start(out=dst[:, c0:c0 + w_], in_=stgb[i][:, 0:w_]), [f"stgb{i}"], ["ws"])
            for nm, rows in [("w_in", 1024), ("w_down_ssd", 2048), ("w_glu", 1024), ("w_down_s5", 1024), ("w_out", 1024)]:
                nco = PROJ if nm == "w_in" else 1024
                for r0 in range(0, rows, 128):
                    cast_rows(I[nm][r0:r0 + 128, :], WS[nm][r0:r0 + 128, :], nco)

        P.barrier()

        xt = sb("xt", [128, 1024]); ot = sb("ot", [128, 1024]); ss = sb("ss", [128, 1]); rstd = sb("rstd", [128, 1])
        xsn = sb("xsn", [128, 1024], BF16); hT = sb("hT", [128, 8, 128], BF16)
        wbuf = [sb(f"wbuf{i}", [128, 8, 256], BF16) for i in range(3)]
        zs = sb("zs", [128, 16, 128], BF16)
        xc = sb("xc", [128, 32, 128], BF16)
        raw = [sb(f"raw{i}", [128, 176]) for i in range(2)]
        acc = [sb(f"acc{i}", [128, 128]) for i in range(2)]
        dts = sb("dts", [32, 128]); dAs = sb("dAs", [32, 128]); acs = sb("acs", [32, 128]); dec = sb("dec", [32, 128]); wdt = sb("wdt", [32, 128])
        tokm = sb("tokm", [128, 3, 32])
        u5 = sb("u5", [128, 8, 128], BF16); z5s = sb("z5s", [128, 8, 128], BF16)
        gas = sb("gas", [128, 8, 128], BF16); gbs = sb("gbs", [128, 8, 128], BF16)
        Xd = sb("Xd", [128, 32, 64], BF16); Btok = sb("Btok", [128, 8, 128], BF16)
        rhsR = sb("rhsR", [32, 4, 128]); Eg = sb("Eg", [128, 4, 128]); Dm = sb("Dm", [128, 4, 128])
        Lg = sb("Lg", [128, 4, 128], BF16); Mg = sb("Mg", [128, 4, 128], BF16)
        CEall = sb("CEall", [128, 32, 128], BF16)
        dch = sb("dch", [128, 32, 16])
        S0qs = [sb(f"S0q{i}", [128, 4, 128]) for i in range(2)]; Snqs = [sb(f"Snq{i}", [128, 4, 128]) for i in range(2)]
        Bms = [sb("Bm0", [128, 8, 128], BF16)] * 2
        yv = sb("yv", [128, 16, 128]); ysq = sb("ysq", [128, 16, 128], BF16); rstg = sb("rstg", [128, 8, 128]); ynT = sb("ynT", [128, 16, 128], BF16)
        yaT = sb("yaT", [128, 8, 128], BF16); ybT = sb("ybT", [128, 8, 128], BF16); mixT = sb("mixT", [128, 8, 128], BF16)
        u5m = sb("u5m", [128, 4, 128], BF16)
        bu = sb("bu", [128, 4, 2, 128]); sgm = sb("sgm", [128, 4, 2, 128]); rts = sb("rts", [128, 4, 128])
        bt1 = sb("bt1", [128, 4, 128]); bt2 = sb("bt2", [128, 4, 128])
        s5s = sb("s5s", [128, 4, 2, 128], BF16)
        cin_ = sb("cinn", [128, 4, 2, 16])
        ctm = sb("ctm", [128, 4, 2, 16])
        fin = sb("fin", [128, 32, 2, 16]); s0all = sb("s0all", [128, 32, 2, 16])
        y5t = sb("y5t", [128, 128]); y5u = sb("y5u", [128, 128]); y5v = sb("y5v", [128, 128])
        y5g = sb("y5g", [128, 8, 128], BF16); y5f = sb("y5f", [128, 8, 128], BF16); glus = sb("glus", [128, 128])
        s5io = sb("s5io", [128, 4, 2, 64]); s5tr = sb("s5tr", [128, 16, 8])
        cvo = sb("cvo", [48, 1024]); cst3 = sb("cst3", [128, 48])

        wcnt = [0]

        def wload(src_ap):
            i = wcnt[0] % 3
            wcnt[0] += 1
            kt = src_ap.shape[0] // 128
            nco = src_ap.shape[1]
            DMA(lambda e: e.dma_start(out=wbuf[i][:, 0:kt, 0:nco], in_=src_ap.rearrange("(j p) f -> p j f", p=128)), ["ws"], [f"wbuf{i}"])
            return wbuf[i], f"wbuf{i}"

        pslot = [0]

        def mm_slot():
            s = pslot[0] % 2
            pslot[0] += 1
            return pb[s][:, 0:128], f"pb{s}"

        def proj_ft(wt, wk, c0, nrows=128):
            ps, pk = mm_slot()
            for j in range(8):
                T(lambda e, j=j: e.matmul(ps[0:nrows, :], lhsT=wt[:, j, c0:c0 + nrows], rhs=hT[:, j, :], start=(j == 0), stop=(j == 7)), [wk, "hT"], [pk])
            return ps, pk

        def dense_T(wname, kt, rhs_fn, rkeys_fn, evac):
            for cb in range(4):
                halves = [wload(WS[wname][128 * k0:128 * (k0 + 8), 256 * cb:256 * cb + 256]) for k0 in range(0, kt, 8)]
                for s in range(2):
                    m = 2 * cb + s
                    ps, pk = mm_slot()
                    for k in range(kt):
                        wt, wk = halves[k // 8]
                        T(lambda e, wt=wt, k=k, s=s, ps=ps: e.matmul(ps, lhsT=wt[:, k % 8, 128 * s:128 * s + 128], rhs=rhs_fn(k), start=(k == 0), stop=(k == kt - 1)), [wk] + rkeys_fn(k), [pk])
                    evac(m, ps, pk)

        def cmul(out_re, out_im, ar, ai, br, bi, keys_r, keys_w, t1_, t2_, tk):
            V(lambda e: e.tensor_tensor(out=t1_, in0=ar, in1=br, op=ALU.mult), keys_r, [tk[0]])
            G(lambda e: e.tensor_tensor(out=t2_, in0=ai, in1=bi, op=ALU.mult), keys_r, [tk[1]])
            V(lambda e: e.tensor_tensor(out=out_re, in0=t1_, in1=t2_, op=ALU.subtract), tk, keys_w[0:1])
            V(lambda e: e.tensor_tensor(out=t1_, in0=ar, in1=bi, op=ALU.mult), keys_r + keys_w[0:1], [tk[0]])
            G(lambda e: e.tensor_tensor(out=t2_, in0=ai, in1=br, op=ALU.mult), keys_r + keys_w[0:1], [tk[1]])
            V(lambda e: e.tensor_tensor(out=out_im, in0=t1_, in1=t2_, op=ALU.add), tk, keys_w[1:2])

        def tile_body(ti):
            smp = (ti == 16)
            lastp = (ti == 15)
            xsrc = I["xs"] if smp else I["xp"][128 * ti:128 * ti + 128, :]
            ydst = O["y_s"] if smp else O["y_p"][128 * ti:128 * ti + 128, :]
            negm = negm_s if smp else negm_p
            rmk = rm_s if smp else rm_p
            nseq, L = (16, 8) if smp else (1, 128)
            DMA(lambda e: e.dma_start(out=xt[:], in_=xsrc), w=["xt"])
            G(lambda e: e.memset(ss[:], 0.0), w=["ss"])
            A(lambda e: e.activation(out=ot[:], in_=xt[:], func=AF.Square, accum_out=ss[:]), ["xt", "ss"], ["ot", "ss"])
            A(lambda e: e.activation(out=rstd[:], in_=ss[:], func=AF.Sqrt, scale=1.0 / D, bias=epscol[:]), ["ss", "epscol"], ["rstd"])
            V(lambda e: e.reciprocal(out=rstd[:], in_=rstd[:]), ["rstd"], ["rstd"])
            V(lambda e: e.tensor_scalar(out=xsn[:], in0=xt[:], scalar1=rstd[:, 0:1], scalar2=None, op0=ALU.mult), ["xt", "rstd"], ["xsn"])
            for j in range(8):
                T(lambda e, j=j: e.transpose(out=pbb[5][:, 128 * j:128 * j + 128], in_=xsn[:, 128 * j:128 * j + 128], identity=identb[:]), ["xsn", "identb"], ["pb5"])
            for j in range(8):
                V(lambda e, j=j: e.tensor_scalar(out=hT[:, j, :], in0=pbb[5][:, 128 * j:128 * j + 128], scalar1=normw[:, j:j + 1], scalar2=None, op0=ALU.mult), ["pb5", "normw"], ["hT"])
            ck(1)
            blocks = [("xbc", c0, 256) for c0 in range(XBC0, DT0, 256)] + [("dt", DT0, 32)]
            blocks += [("z", c0, 256) for c0 in range(Z0, XBC0, 256)]
            for nm, b0 in (("u5", U0), ("z5", Z50), ("ga", GA0), ("gb", GB0)):
                blocks += [(nm, c0, 256) for c0 in range(b0, b0 + 1024, 256)]
            base = {"xbc": XBC0, "z": Z0, "u5": U0, "z5": Z50, "ga": GA0, "gb": GB0}
            for nm, c0, nco in blocks:
                if BLK[0] is not None and nm not in BLK[0]:
                    continue
                wt, wk = wload(WS["w_in"][:, c0:c0 + nco])
                if nm == "dt":
                    ps, pk = proj_ft(wt, wk, 0, 32)
                    A(lambda e, ps=ps: e.activation(out=dts[:], in_=ps[0:32, :], func=AF.Exp, bias=dtb[:]), [pk, "dtb"], ["dts"])
                    A(lambda e: e.activation(out=dts[:], in_=dts[:], func=AF.Ln, bias=onecol[0:32, :]), ["dts", "onecol"], ["dts"])
                    continue
                for s in range(2):
                    ft = (c0 - base[nm]) // 128 + s
                    ps, pk = proj_ft(wt, wk, 128 * s)
                    if nm == "z":
                        A(lambda e, ps=ps, ft=ft: e.activation(out=zs[:, ft, :], in_=ps, func=AF.Silu), [pk], [f"zs{ft}"])
                    elif nm == "u5":
                        A(lambda e, ps=ps, ft=ft: e.activation(out=u5[:, ft, :], in_=ps, func=AF.Copy), [pk], [f"u5{ft}"])
                    elif nm == "z5":
                        A(lambda e, ps=ps, ft=ft: e.activation(out=z5s[:, ft, :], in_=ps, func=AF.Silu), [pk], [f"z5{ft}"])
                    elif nm == "ga":
                        A(lambda e, ps=ps, ft=ft: e.activation(out=gas[:, ft, :], in_=ps, func=AF.Sigmoid), [pk], [f"ga{ft}"])
                    elif nm == "gb":
                        A(lambda e, ps=ps, ft=ft: e.activation(out=gbs[:, ft, :], in_=ps, func=AF.Sigmoid), [pk], [f"gb{ft}"])
                    else:
                        r_ = raw[ft % 2]; rk = f"raw{ft % 2}"; ac = acc[ft % 2]; ak = f"acc{ft % 2}"
                        EV = V
                        if smp:
                            rv = r_[:, 0:176].rearrange("p (s r) -> p s r", r=11)
                            if ft % 8 == 0:
                                DMA(lambda e, ft=ft: e.dma_start(out=cvo[:], in_=I["conv0"].rearrange("s r c -> (s r) c")[:, 128 * ft:128 * ft + 1024]), w=["cvo"])
                            T(lambda e, ft=ft: e.transpose(out=pb[6][:, 0:48], in_=cvo[:, 128 * (ft % 8):128 * (ft % 8) + 128], identity=identf[0:48, 0:48]), ["cvo", "identf"], ["pb6"])
                            V(lambda e, rv=rv: e.tensor_copy(out=rv[:, :, 0:3], in_=pb[6][:, 0:48].rearrange("p (s r) -> p s r", r=3)), ["pb6"], [rk])
                        else:
                            rv = r_[:, 0:131].unsqueeze(1)
                            V(lambda e, rv=rv, ft=ft: e.tensor_copy(out=rv[:, 0, 0:3], in_=halo[:, ft, :]), ["halo"], [rk])
                        A(lambda e, rv=rv, ps=ps: e.activation(out=rv[:, :, 3:3 + L], in_=ps.rearrange("p (s l) -> p s l", l=L), func=AF.Copy), [pk], [rk])
                        av = ac[:].rearrange("p (s l) -> p s l", l=L)
                        EV(lambda e, rv=rv, av=av, ft=ft: e.tensor_scalar(out=av, in0=rv[:, :, 0:L], scalar1=convw[:, ft, 0:1], scalar2=convb[:, ft:ft + 1], op0=ALU.mult, op1=ALU.add), [rk, "convw", "convb"], [ak])
                        for k in range(1, 4):
                            EV(lambda e, rv=rv, av=av, ft=ft, k=k: e.scalar_tensor_tensor(out=av, in0=rv[:, :, k:k + L], scalar=convw[:, ft, k:k + 1], in1=av, op0=ALU.mult, op1=ALU.add), [rk, ak, "convw"], [ak])
                        A(lambda e, ac=ac, ft=ft: e.activation(out=xc[:, ft, :], in_=ac[:], func=AF.Silu), [ak], [f"xc{ft}"])
                        if not smp:
                            EV(lambda e, rv=rv, ft=ft: e.tensor_copy(out=halo[:, ft, :], in_=rv[:, 0, 128:131]), [rk], ["halo"])
                        if smp or lastp:
                            nr = 3 * nseq
                            EV(lambda e, rv=rv, nr=nr: e.tensor_copy(out=cst3[:, 0:nr].rearrange("p (s r) -> p s r", r=3), in_=rv[:, :, L:L + 3]), [rk], ["cst3"])
                            T(lambda e, nr=nr: e.transpose(out=pb[6][0:nr, 128:256], in_=cst3[:, 0:nr], identity=identf[:]), ["cst3", "identf"], ["pb6"])
                            V(lambda e, ft=ft, nr=nr: e.tensor_copy(out=cvo[0:nr, 128 * (ft % 8):128 * (ft % 8) + 128], in_=pb[6][0:nr, 128:256]), ["pb6"], ["cvo"])
                            if ft % 8 == 7:
                                cdst = (O["conv_s"].rearrange("s r c -> (s r) c") if smp else O["conv_p"])[:, 128 * (ft - 7):128 * (ft - 7) + 1024]
                                DMAO(lambda e, cdst=cdst, nr=nr: e.dma_start(out=cdst, in_=cvo[0:nr, :]), ["cvo"], [])
            ck(2)
            V(lambda e: e.tensor_scalar(out=dAs[:], in0=dts[:], scalar1=aneg[:, 0:1], scalar2=None, op0=ALU.mult), ["dts", "aneg"], ["dAs"])
            V(lambda e: e.tensor_tensor_scan(out=acs[:], data0=rmk[:], data1=dAs[:], initial=0.0, op0=ALU.mult, op1=ALU.add), [kn(rmk), "dAs"], ["acs"])
            a3 = acs[:].rearrange("h (s l) -> h s l", l=L)
            V(lambda e: e.tensor_tensor(out=dec[:].rearrange("h (s l) -> h s l", l=L), in0=a3[:, :, L - 1:L].broadcast_to([32, nseq, L]), in1=a3, op=ALU.subtract), ["acs"], ["dec"])
            A(lambda e: e.activation(out=dec[:], in_=dec[:], func=AF.Exp), ["dec"], ["dec"])
            V(lambda e: e.tensor_tensor(out=wdt[:], in0=dts[:], in1=dec[:], op=ALU.mult), ["dts", "dec"], ["wdt"])
            for i_, src in enumerate((acs, dts, wdt)):
                T(lambda e, i_=i_, src=src: e.transpose(out=pb[6][:, 256 + 32 * i_:256 + 32 * i_ + 32], in_=src[:], identity=identf[0:32, 0:32]), [kn(src), "identf"], ["pb6"])
            V(lambda e: e.tensor_copy(out=tokm[:].rearrange("p a b -> p (a b)"), in_=pb[6][:, 256:352]), ["pb6"], ["tokm"])
            for half in range(2):
                bk = f"pb{2 + half}"
                for j in range(8):
                    ft = 8 * half + j
                    T(lambda e, ft=ft, j=j, half=half: e.transpose(out=pbb[2 + half][:, 128 * j:128 * j + 128], in_=xc[:, ft, :], identity=identb[:]), [f"xc{ft}", "identb"], [bk])
                pv = pbb[2 + half][:, :].rearrange("p (a h2 q) -> p a h2 q", h2=2, q=64)
                prs = slice(8 * half, 8 * half + 8)
                dtT = tokm[:, 1, 16 * half:16 * half + 16].rearrange("p (a h2) -> p a h2", h2=2)
                wT = tokm[:, 2, 16 * half:16 * half + 16]
                V(lambda e, pv=pv, prs=prs, dtT=dtT: e.tensor_tensor(out=XA[:, prs, 0:64], in0=pv[:, :, 0, :], in1=dtT[:, :, 0:1].broadcast_to([128, 8, 64]), op=ALU.mult), [bk, "tokm"], ["XA"])
                V(lambda e, pv=pv, prs=prs, dtT=dtT: e.tensor_tensor(out=XB[:, prs, 64:128], in0=pv[:, :, 1, :], in1=dtT[:, :, 1:2].broadcast_to([128, 8, 64]), op=ALU.mult), [bk, "tokm"], ["XB"])
                V(lambda e, half=half, wT=wT: e.tensor_tensor(out=Xd[:, 16 * half:16 * half + 16, :], in0=pbb[2 + half][:, :].rearrange("p (h q) -> p h q", q=64), in1=wT.unsqueeze(2).broadcast_to([128, 16, 64]), op=ALU.mult), [bk, "tokm"], ["Xd"])
            for j in range(8):
                T(lambda e, j=j: e.transpose(out=pbb[4][:, 128 * j:128 * j + 128], in_=xc[:, 16 + j, :], identity=identb[:]), [f"xc{16 + j}", "identb"], ["pb4"])
            A(lambda e: e.activation(out=Btok[:].rearrange("p a b -> p (a b)"), in_=pbb[4][:, :], func=AF.Copy), ["pb4"], ["Btok"])
            ck(3)
            ypk = lambda pr: f"pb{2 + pr // 4}"
            yps = lambda pr: pb[2 + pr // 4][:, 128 * (pr % 4):128 * (pr % 4) + 128]
            for g in range(8):
                V(lambda e, g=g: e.tensor_tensor(out=rhsR[:], in0=acs[:].unsqueeze(1).broadcast_to([32, 4, 128]), in1=identf[0:32, 4 * g:4 * g + 4].unsqueeze(2).broadcast_to([32, 4, 128]), op=ALU.mult), ["acs", "identf"], ["rhsR"])
                T(lambda e: e.matmul(pb[6][:, :], lhsT=ones32[:], rhs=rhsR[:].rearrange("p a b -> p (a b)"), start=True, stop=True), ["ones32", "rhsR"], ["pb6"])
                A(lambda e: e.activation(out=Eg[:].rearrange("p a b -> p (a b)"), in_=pb[6][:, :], func=AF.Exp), ["pb6"], ["Eg"])
                G(lambda e, g=g: e.tensor_tensor(out=CEall[:, 4 * g:4 * g + 4, :], in0=Eg[:], in1=xc[:, 24 + g, :].unsqueeze(1).broadcast_to([128, 4, 128]), op=ALU.mult), ["Eg", f"xc{24 + g}"], ["CEall"])
                for h4 in range(4):
                    V(lambda e, g=g, h4=h4: e.scalar_tensor_tensor(out=Dm[:, h4, :], in0=pb[6][:, 128 * h4:128 * h4 + 128], scalar=tokm[:, 0, 4 * g + h4:4 * g + h4 + 1], in1=negm[:], op0=ALU.subtract, op1=ALU.min), ["pb6", "tokm", kn(negm)], ["Dm"])
                A(lambda e: e.activation(out=Lg[:].rearrange("p a b -> p (a b)"), in_=Dm[:].rearrange("p a b -> p (a b)"), func=AF.Exp), ["Dm"], ["Lg"])
                T(lambda e, g=g: e.matmul(pb[7][:, 0:128], lhsT=xc[:, 16 + g, :], rhs=xc[:, 24 + g, :], start=True, stop=True), [f"xc{16 + g}", f"xc{24 + g}"], ["pb7"])
                V(lambda e: e.tensor_tensor(out=Mg[:], in0=Lg[:], in1=pb[7][:, 0:128].unsqueeze(1).broadcast_to([128, 4, 128]), op=ALU.mult), ["Lg", "pb7"], ["Mg"])
                V(lambda e, g=g: e.tensor_copy(out=dch[:, 4 * g:4 * g + 4, 0:nseq], in_=Eg[:].rearrange("p a (s l) -> p a s l", l=L)[:, :, :, L - 1]), ["Eg"], ["dch"])
                for j in range(2):
                    pr = 2 * g + j
                    T(lambda e, pr=pr, j=j: e.matmul(yps(pr), lhsT=XA[:, pr, :], rhs=Mg[:, 2 * j, :], start=(pr % 4 == 0), stop=False, skip_group_check=True), ["XA", "Mg"], [ypk(pr)])
                    T(lambda e, pr=pr, j=j: e.matmul(yps(pr), lhsT=XB[:, pr, :], rhs=Mg[:, 2 * j + 1, :], start=False, stop=False, skip_group_check=True), ["XB", "Mg"], [ypk(pr)])
            it = 0
            for s in range(nseq):
                cs = slice(L * s, L * s + L)
                last = (s == nseq - 1)
                if smp:
                    Bm = Bms[0]; bmk = "Bm0"
                    G(lambda e, s=s, Bm=Bm: e.tensor_scalar(out=Bm[:].rearrange("p a b -> p (a b)"), in0=Btok[:].rearrange("p a b -> p (a b)"), scalar1=seqmask[:, s:s + 1], scalar2=None, op0=ALU.mult), ["Btok", "seqmask"], [bmk])
                for q4 in range(4 if smp else 1):
                    prs_ = range(4 * q4, 4 * q4 + 4) if smp else range(16)
                    sak = [f"SA{q4}", f"SB{q4}"] if smp else [f"SA{i}" for i in range(4)] + [f"SB{i}" for i in range(4)]
                    if smp:
                        S0q = S0qs[it % 2]; s0k = f"S0q{it % 2}"; Snq = Snqs[it % 2]; snk = f"Snq{it % 2}"
                        tb_ = 7 if it % 2 == 0 else 0
                        nb_ = 6 if it % 2 == 0 else 1
                        it += 1
                        sview = I["ssm0"][s].rearrange("(pr h2) p n -> (h2 p) pr n", h2=2)[:, 4 * q4:4 * q4 + 4, :]
                        DMA(lambda e, sview=sview, S0q=S0q: e.dma_start(out=S0q[:], in_=sview), w=[s0k])
                        for j in range(4):
                            T(lambda e, j=j, S0q=S0q, tb_=tb_: e.transpose(out=pb[tb_][:, 128 * j:128 * j + 128], in_=S0q[:, j, :], identity=identf[:]), [s0k, "identf"], [f"pb{tb_}"])
                        pv = pb[tb_][:, :].rearrange("p (a h2 q) -> p a h2 q", h2=2, q=64)
                        A(lambda e, q4=q4, pv=pv: e.activation(out=SA[:, 4 * q4:4 * q4 + 4, 0:64], in_=pv[:, :, 0, :], func=AF.Copy), [f"pb{tb_}"], [f"SA{q4}"])
                        A(lambda e, q4=q4, pv=pv: e.activation(out=SB[:, 4 * q4:4 * q4 + 4, 64:128], in_=pv[:, :, 1, :], func=AF.Copy), [f"pb{tb_}"], [f"SB{q4}"])
                    for pr in prs_:
                        T(lambda e, pr=pr, cs=cs: e.matmul(yps(pr)[:, cs], lhsT=SA[:, pr, :], rhs=CEall[:, 2 * pr, cs], start=False, stop=False, skip_group_check=True), sak + ["CEall"], [ypk(pr)])
                        T(lambda e, pr=pr, cs=cs, last=last: e.matmul(yps(pr)[:, cs], lhsT=SB[:, pr, :], rhs=CEall[:, 2 * pr + 1, cs], start=False, stop=last, skip_group_check=True), sak + ["CEall"], [ypk(pr)])
                    if smp:
                        for j in range(4):
                            pr = 4 * q4 + j
                            T(lambda e, pr=pr, j=j, Bm=Bm, nb_=nb_: e.matmul(pb[nb_][:, 128 * j:128 * j + 128], lhsT=Xd[:, 2 * pr:2 * pr + 2, :].rearrange("p a b -> p (a b)"), rhs=Bm[:, pr // 2, :], start=True, stop=True), ["Xd", bmk], [f"pb{nb_}"])
                        for j in range(4):
                            pr = 4 * q4 + j
                            for h2 in range(2):
                                hs = slice(64 * h2, 64 * h2 + 64)
                                V(lambda e, pr=pr, j=j, hs=hs, s=s, h2=h2, S0q=S0q, Snq=Snq, nb_=nb_: e.scalar_tensor_tensor(out=Snq[hs, j, :], in0=S0q[hs, j, :], scalar=dch[hs, 2 * pr + h2, s:s + 1], in1=pb[nb_][hs, 128 * j:128 * j + 128], op0=ALU.mult, op1=ALU.add), [s0k, "dch", f"pb{nb_}"], [snk])
                        oview = O["ssm_s"][s].rearrange("(pr h2) p n -> (h2 p) pr n", h2=2)[:, 4 * q4:4 * q4 + 4, :]
                        DMAO(lambda e, oview=oview, Snq=Snq: e.dma_start(out=oview, in_=Snq[:]), [snk], [])
            ck(4)
            for pr in range(16):
                V(lambda e, pr=pr: e.scalar_tensor_tensor(out=yv[:, pr, :], in0=xc[:, pr, :], scalar=dexp[:, pr:pr + 1], in1=yps(pr), op0=ALU.mult, op1=ALU.add), [f"xc{pr}", "dexp", ypk(pr)], [f"yv{pr}"])
                G(lambda e, pr=pr: e.tensor_tensor(out=yv[:, pr, :], in0=yv[:, pr, :], in1=zs[:, pr, :], op=ALU.mult), [f"yv{pr}", f"zs{pr}"], [f"yv{pr}"])
                A(lambda e, pr=pr: e.activation(out=ysq[:, pr, :], in_=yv[:, pr, :], func=AF.Square), [f"yv{pr}"], [f"ysq{pr}"])
            if not smp:
                for g in range(8):
                    T(lambda e, g=g: e.matmul(pb[2 + g // 2][:, 256 * (g % 2):256 * (g % 2) + 256], lhsT=Btok[:, g, :], rhs=Xd[:, 4 * g:4 * g + 4, :].rearrange("p a b -> p (a b)"), start=True, stop=True), ["Btok", "Xd"], [f"pb{2 + g // 2}"])
                V(lambda e: e.tensor_tensor(out=STt[:], in0=STt[:], in1=dch[:, :, 0:1].broadcast_to([128, 32, 64]), op=ALU.mult), ["STt", "dch"], ["STt"])
                for b4 in range(4):
                    V(lambda e, b4=b4: e.tensor_tensor(out=STt[:, 8 * b4:8 * b4 + 8, :], in0=STt[:, 8 * b4:8 * b4 + 8, :], in1=pb[2 + b4][:, :].rearrange("p (h q) -> p h q", q=64), op=ALU.add), ["STt", f"pb{2 + b4}"], ["STt"])
                sv = STt[:].rearrange("p (pr h2) q -> p pr h2 q", h2=2)
                A(lambda e, sv=sv: e.activation(out=SA[:, :, 0:64], in_=sv[:, :, 0, :], func=AF.Copy), ["STt"], [f"SA{i}" for i in range(4)])
                A(lambda e, sv=sv: e.activation(out=SB[:, :, 64:128], in_=sv[:, :, 1, :], func=AF.Copy), ["STt"], [f"SB{i}" for i in range(4)])
                if lastp:
                    for q4 in range(4):
                        for j in range(4):
                            pr = 4 * q4 + j
                            T(lambda e, pr=pr, j=j: e.transpose(out=pb[6][:, 128 * j:128 * j + 128], in_=STt[:, 2 * pr:2 * pr + 2, :].rearrange("p a b -> p (a b)"), identity=identf[:]), ["STt", "identf"], ["pb6"])
                        Snq = Snqs[q4 % 2]; snk = f"Snq{q4 % 2}"
                        V(lambda e, Snq=Snq: e.tensor_copy(out=Snq[:].rearrange("p a b -> p (a b)"), in_=pb[6][:, :]), ["pb6"], [snk])
                        oview = O["ssm_p"].rearrange("(pr h2) p n -> (h2 p) pr n", h2=2)[:, 4 * q4:4 * q4 + 4, :]
                        DMAO(lambda e, oview=oview, Snq=Snq: e.dma_start(out=oview, in_=Snq[:]), [snk], [])
            for gq in range(8):
                ps = pb[6 + gq // 4][:, 128 * (gq % 4):128 * (gq % 4) + 128]
                pk = f"pb{6 + gq // 4}"
                for j in range(2):
                    T(lambda e, ps=ps, gq=gq, j=j: e.matmul(ps, lhsT=onesb[:], rhs=ysq[:, 2 * gq + j, :], start=(j == 0), stop=(j == 1)), ["onesb", f"ysq{2 * gq + j}"], [pk])
                A(lambda e, ps=ps, gq=gq: e.activation(out=rstg[:, gq, :], in_=ps, func=AF.Sqrt, scale=1.0 / 256, bias=epscol[:]), [pk, "epscol"], [f"rstg{gq}"])
                V(lambda e, gq=gq: e.reciprocal(out=rstg[:, gq, :], in_=rstg[:, gq, :]), [f"rstg{gq}"], [f"rstg{gq}"])
                for j in range(2):
                    pr = 2 * gq + j
                    V(lambda e, pr=pr, gq=gq: e.scalar_tensor_tensor(out=ynT[:, pr, :], in0=yv[:, pr, :], scalar=ssdnw[:, pr:pr + 1], in1=rstg[:, gq, :], op0=ALU.mult, op1=ALU.mult), [f"yv{pr}", "ssdnw", f"rstg{gq}"], [f"ynT{pr}"])
            dense_T("w_down_ssd", 16, lambda k: ynT[:, k, :], lambda k: [f"ynT{k}"],
                    lambda m, ps, pk: V(lambda e: e.tensor_tensor(out=yaT[:, m, :], in0=ps, in1=gas[:, m, :], op=ALU.mult), [pk, f"ga{m}"], [f"yaT{m}"]))

            ck(5)
            if smp:
                for ri, nm in enumerate(["re0", "im0"]):
                    sv_ = I[nm].rearrange("s (gh gq g2) p -> s gq gh (g2 p)", gq=8, g2=2)
                    for s in range(16):
                        DMA(lambda e, s=s, sv_=sv_: e.dma_start(out=s5io[8 * s:8 * s + 8, :, :, :].rearrange("p a b c -> p a (b c)"), in_=sv_[s]), w=["s5io"])
                    for gh in range(4):
                        T(lambda e, gh=gh: e.transpose(out=pb[7][:, 0:128], in_=s5io[:, gh, :, :].rearrange("p a b -> p (a b)"), identity=identf[:]), ["s5io", "identf"], ["pb7"])
                        V(lambda e, gh=gh, ri=ri: e.tensor_copy(out=s0all[:, 8 * gh:8 * gh + 8, ri, :], in_=pb[7][:, 0:128].rearrange("p (s g) -> p g s", g=8)), ["pb7"], ["s0all"])
            Ls, nsg = (8, 16) if smp else (64, 2)
            for ft in range(8):
                gsl = slice(4 * ft, 4 * ft + 4)
                ecv = ecos[:, gsl, 0:Ls].unsqueeze(2).broadcast_to([128, 4, nsg, Ls])
                esv = esin[:, gsl, 0:Ls].unsqueeze(2).broadcast_to([128, 4, nsg, Ls])
                v4 = lambda ap: ap.rearrange("p q (s l) -> p q s l", l=Ls)
                for q in range(4):
                    G(lambda e, ft=ft, q=q: e.tensor_scalar(out=u5m[:, q, :], in0=u5[:, ft, :], scalar1=pairmask[:, q:q + 1], scalar2=None, op0=ALU.mult), [f"u5{ft}", "pairmask"], ["u5m"])
                for ri in range(2):
                    pbk = f"pb{6 + ri}"
                    for q in range(4):
                        T(lambda e, ft=ft, q=q, ri=ri: e.matmul(pb[6 + ri][:, 128 * q:128 * q + 128], lhsT=wb5[:, ft, ri, :], rhs=u5m[:, q, :], start=True, stop=True), ["wb5", "u5m"], [pbk])
                    A(lambda e, ri=ri: e.activation(out=bu[:, :, ri, :], in_=pb[6 + ri][:, :].rearrange("p (q t) -> p q t", t=128), func=AF.Copy), [pbk], ["bu"])
                bre_, bim_ = v4(bu[:, :, 0, :]), v4(bu[:, :, 1, :])
                V(lambda e, ecv=ecv: e.tensor_tensor(out=v4(bt1[:]), in0=ecv, in1=bre_, op=ALU.mult), ["ecos", "bu"], ["bt1"])
                G(lambda e, esv=esv: e.tensor_tensor(out=v4(bt2[:]), in0=esv, in1=bim_, op=ALU.mult), ["esin", "bu"], ["bt2"])
                V(lambda e: e.tensor_tensor(out=sgm[:, :, 0, :], in0=bt1[:], in1=bt2[:], op=ALU.add), ["bt1", "bt2"], ["sgm"])
                V(lambda e, ecv=ecv: e.tensor_tensor(out=v4(bt1[:]), in0=ecv, in1=bim_, op=ALU.mult), ["ecos", "bu", "sgm"], ["bt1"])
                G(lambda e, esv=esv: e.tensor_tensor(out=v4(bt2[:]), in0=esv, in1=bre_, op=ALU.mult), ["esin", "bu", "sgm"], ["bt2"])
                V(lambda e: e.tensor_tensor(out=sgm[:, :, 1, :], in0=bt1[:], in1=bt2[:], op=ALU.subtract), ["bt1", "bt2"], ["sgm"])
                if smp:
                    V(lambda e, gsl=gsl: e.tensor_tensor(out=rts[:], in0=rdec[:, gsl].unsqueeze(2).broadcast_to([128, 4, 128]), in1=rm128[:].unsqueeze(1).broadcast_to([128, 4, 128]), op=ALU.mult), ["rdec", "rm128"], ["rts"])
                else:
                    V(lambda e, gsl=gsl: e.tensor_copy(out=rts[:], in_=rdec[:, gsl].unsqueeze(2).broadcast_to([128, 4, 128])), ["rdec"], ["rts"])
                if smp:
                    n_ = 16
                    ab = lambda t: t[:, gsl].unsqueeze(2).broadcast_to([128, 4, n_])
                    cmul(cin_[:, :, 0, :], cin_[:, :, 1, :], ab(are), ab(aim), s0all[:, gsl, 0, :], s0all[:, gsl, 1, :],
                         ["are", "aim", "s0all"], ["cinr", "cini"], ctm[:, :, 0, :], ctm[:, :, 1, :], ["ctm0", "ctm1"])
                    for ri in range(2):
                        tgt = v4(sgm[:, :, ri, :])[:, :, :, 0]
                        V(lambda e, tgt=tgt, ri=ri: e.tensor_tensor(out=tgt, in0=tgt, in1=cin_[:, :, ri, :], op=ALU.add), ["sgm", "cinr", "cini"], ["sgm"])
                    for q in range(4):
                        for ri in range(2):
                            EV = V
                            EV(lambda e, q=q, ri=ri: e.tensor_tensor_scan(out=bu[:, q, ri, :], data0=rts[:, q, :], data1=sgm[:, q, ri, :], initial=0.0, op0=ALU.mult, op1=ALU.add), ["rts", "sgm"], ["bu"])
                else:
                    for sg in range(2):
                        c0 = 64 * sg
                        src_re = sgc[:, gsl, 0:1] if sg == 0 else bu[:, :, 0, 63:64]
                        src_im = sgc[:, gsl, 1:2] if sg == 0 else bu[:, :, 1, 63:64]
                        ab = lambda t: t[:, gsl].unsqueeze(2)
                        cmul(cin_[:, :, 0, 0:1], cin_[:, :, 1, 0:1], ab(aere), ab(aeim), src_re, src_im,
                             ["aere", "aeim", "sgc", "bu"], ["cinr", "cini"], ctm[:, :, 0, 0:1], ctm[:, :, 1, 0:1], ["ctm0", "ctm1"])
                        for ri in range(2):
                            tgt = sgm[:, :, ri, c0:c0 + 1]
                            V(lambda e, tgt=tgt, ri=ri: e.tensor_tensor(out=tgt, in0=tgt, in1=cin_[:, :, ri, 0:1], op=ALU.add), ["sgm", "cinr", "cini"], ["sgm"])
                        for q in range(4):
                            for ri in range(2):
                                EV = V
                                EV(lambda e, q=q, ri=ri, c0=c0: e.tensor_tensor_scan(out=bu[:, q, ri, c0:c0 + 64], data0=rts[:, q, c0:c0 + 64], data1=sgm[:, q, ri, c0:c0 + 64], initial=0.0, op0=ALU.mult, op1=ALU.add), ["rts", "sgm"], ["bu"])
                    V(lambda e, gsl=gsl: e.tensor_copy(out=sgc[:, gsl, :], in_=bu[:, :, :, 127]), ["bu"], ["sgc"])
                sre_, sim_ = v4(bu[:, :, 0, :]), v4(bu[:, :, 1, :])
                V(lambda e, ecv=ecv: e.tensor_tensor(out=v4(bt1[:]), in0=ecv, in1=sre_, op=ALU.mult), ["ecos", "bu"], ["bt1"])
                G(lambda e, esv=esv: e.tensor_tensor(out=v4(bt2[:]), in0=esv, in1=sim_, op=ALU.mult), ["esin", "bu"], ["bt2"])
                V(lambda e: e.tensor_tensor(out=s5s[:, :, 0, :], in0=bt1[:], in1=bt2[:], op=ALU.subtract), ["bt1", "bt2"], ["s5s"])
                if smp or lastp:
                    fsrc = lambda t: (v4(t[:])[:, :, :, Ls - 1] if smp else t[:, :, 127:128])
                    nf = 16 if smp else 1
                    V(lambda e, gsl=gsl, fsrc=fsrc, nf=nf: e.tensor_tensor(out=fin[:, gsl, 0, 0:nf], in0=fsrc(bt1), in1=fsrc(bt2), op=ALU.subtract), ["bt1", "bt2"], ["fin"])
                V(lambda e, ecv=ecv: e.tensor_tensor(out=v4(bt1[:]), in0=ecv, in1=sim_, op=ALU.mult), ["ecos", "bu", "s5s", "fin"], ["bt1"])
                G(lambda e, esv=esv: e.tensor_tensor(out=v4(bt2[:]), in0=esv, in1=sre_, op=ALU.mult), ["esin", "bu", "s5s", "fin"], ["bt2"])
                V(lambda e: e.tensor_tensor(out=s5s[:, :, 1, :], in0=bt1[:], in1=bt2[:], op=ALU.add), ["bt1", "bt2"], ["s5s"])
                if smp or lastp:
                    V(lambda e, gsl=gsl, fsrc=fsrc, nf=nf: e.tensor_tensor(out=fin[:, gsl, 1, 0:nf], in0=fsrc(bt1), in1=fsrc(bt2), op=ALU.add), ["bt1", "bt2"], ["fin"])
                ps4 = [pb[6][:, 128 * q:128 * q + 128] for q in range(4)]
                for q in range(4):
                    for ri in range(2):
                        T(lambda e, ft=ft, q=q, ri=ri: e.matmul(ps4[q], lhsT=wd5[:, ft, ri, :], rhs=s5s[:, q, ri, :], start=(ri == 0), stop=(ri == 1)), ["wd5", "s5s"], ["pb6"])
                for q in range(4):
                    rs_ = slice(32 * q, 32 * q + 32)
                    V(lambda e, ft=ft, q=q, rs_=rs_: e.scalar_tensor_tensor(out=y5t[rs_, :], in0=u5[rs_, ft, :], scalar=ds5[rs_, ft:ft + 1], in1=ps4[q][rs_, :], op0=ALU.mult, op1=ALU.add), [f"u5{ft}", "ds5", "pb6"], ["y5t"])
                V(lambda e: e.tensor_tensor(out=y5u[:], in0=y5t[:], in1=y5t[:], op=ALU.mult), ["y5t"], ["y5u"])
                V(lambda e: e.tensor_scalar(out=y5u[:], in0=y5u[:], scalar1=0.044715, scalar2=1.0, op0=ALU.mult, op1=ALU.add), ["y5u"], ["y5u"])
                V(lambda e: e.tensor_tensor(out=y5u[:], in0=y5u[:], in1=y5t[:], op=ALU.mult), ["y5u", "y5t"], ["y5u"])
                A(lambda e: e.activation(out=y5v[:], in_=y5u[:], func=AF.Sigmoid, scale=1.5957691216057308), ["y5u"], ["y5v"])
                V(lambda e, ft=ft: e.tensor_tensor(out=y5g[:, ft, :], in0=y5v[:], in1=y5t[:], op=ALU.mult), ["y5v", "y5t"], [f"y5g{ft}"])
            if lastp:
                for ri, nm in enumerate(["re_p", "im_p"]):
                    DMAO(lambda e, ri=ri, nm=nm: e.dma_start(out=O[nm].rearrange("(gp g2) p -> (g2 p) gp", g2=2), in_=fin[:, :, ri, 0], allow_slow_non_contiguous=True), ["fin"], [])
            if smp:
                for ri, nm in enumerate(["re_s", "im_s"]):
                    for gh in range(4):
                        V(lambda e, gh=gh, ri=ri: e.tensor_copy(out=s5tr[:], in_=fin[:, 8 * gh:8 * gh + 8, ri, :].rearrange("p g s -> p s g")), ["fin"], ["s5tr"])
                        T(lambda e: e.transpose(out=pb[7][:, 0:128], in_=s5tr[:].rearrange("p a b -> p (a b)"), identity=identf[:]), ["s5tr", "identf"], ["pb7"])
                        V(lambda e, gh=gh: e.tensor_copy(out=s5io[:, gh, :, :].rearrange("p a b -> p (a b)"), in_=pb[7][:, 0:128]), ["pb7"], ["s5io"])
                    ov = O[nm].rearrange("s (gh gq g2) p -> s gq gh (g2 p)", gq=8, g2=2)
                    for s in range(16):
                        DMAO(lambda e, s=s, ov=ov: e.dma_start(out=ov[s], in_=s5io[8 * s:8 * s + 8, :, :, :].rearrange("p a b c -> p a (b c)")), ["s5io"], [])

            ck(6)
            def glu_evac(m, ps, pk):
                A(lambda e: e.activation(out=glus[:], in_=ps, func=AF.Sigmoid, bias=bglu[:, m:m + 1]), [pk, "bglu"], ["glus"])
                V(lambda e: e.tensor_tensor(out=glus[:], in0=glus[:], in1=y5g[:, m, :], op=ALU.mult), ["glus", f"y5g{m}"], ["glus"])
                V(lambda e: e.tensor_tensor(out=y5f[:, m, :], in0=glus[:], in1=z5s[:, m, :], op=ALU.mult), ["glus", f"z5{m}"], [f"y5f{m}"])
            dense_T("w_glu", 8, lambda k: y5g[:, k, :], lambda k: [f"y5g{k}"], glu_evac)
            dense_T("w_down_s5", 8, lambda k: y5f[:, k, :], lambda k: [f"y5f{k}"],
                    lambda m, ps, pk: V(lambda e: e.tensor_tensor(out=ybT[:, m, :], in0=ps, in1=gbs[:, m, :], op=ALU.mult), [pk, f"gb{m}"], [f"ybT{m}"]))
            for m in range(8):
                V(lambda e, m=m: e.tensor_tensor(out=mixT[:, m, :], in0=yaT[:, m, :], in1=ybT[:, m, :], op=ALU.add), [f"yaT{m}", f"ybT{m}"], [f"mixT{m}"])
            for cb in range(4):
                wt, wk = wload(WS["w_out"][:, 256 * cb:256 * cb + 256])
                hb = cb % 2
                ps = pb[hb][:, 0:256]
                pks = [f"pb{hb}"]
                for k in range(8):
                    T(lambda e, wt=wt, k=k, ps=ps: e.matmul(ps, lhsT=mixT[:, k, :], rhs=wt[:, k, 0:256], start=(k == 0), stop=(k == 7)), [wk, f"mixT{k}"], pks)
                V(lambda e, cb=cb, ps=ps: e.tensor_tensor(out=ot[:, 256 * cb:256 * cb + 256], in0=ps, in1=xt[:, 256 * cb:256 * cb + 256], op=ALU.add), pks + ["xt"], ["ot"])
            G(lambda e: e.memset(ss[:], 0.0), w=["ss"])
            A(lambda e: e.activation(out=xt[:], in_=ot[:], func=AF.Square, accum_out=ss[:]), ["ot", "ss"], ["xt", "ss"])
            A(lambda e: e.activation(out=rstd[:], in_=ss[:], func=AF.Sqrt, scale=1.0 / D, bias=epscol[:]), ["ss", "epscol"], ["rstd"])
            V(lambda e: e.reciprocal(out=rstd[:], in_=rstd[:]), ["rstd"], ["rstd"])
            V(lambda e: e.scalar_tensor_tensor(out=ot[:], in0=ot[:], scalar=rstd[:, 0:1], in1=fnw[:], op0=ALU.mult, op1=ALU.mult), ["ot", "rstd", "fnw"], ["ot"])
            DMAO(lambda e: e.dma_start(out=ydst, in_=ot[:]), ["ot"], [])

        for ti in (tiles if tiles is not None else range(17)):
            try:
                tile_body(ti)
            except _Stop:
                pass
        P.emit(nc)
    return nc


_NC = {}


def _shard(inputs, c):
    d = {}
    d["xp"] = np.ascontiguousarray(inputs["x_prompt"][c])
    d["xs"] = np.ascontiguousarray(inputs["x_sample"][16 * c:16 * c + 16].reshape(128, 1024))
    d["conv0"] = np.ascontiguousarray(inputs["state_conv"][0, 16 * c:16 * c + 16])
    d["ssm0"] = np.ascontiguousarray(inputs["state_ssm"][0, 16 * c:16 * c + 16])
    d["re0"] = np.ascontiguousarray(inputs["state_s5_re"][0, 16 * c:16 * c + 16])
    d["im0"] = np.ascontiguousarray(inputs["state_s5_im"][0, 16 * c:16 * c + 16])
    for k in ("norm_w", "w_in", "conv_w", "conv_b", "dt_bias", "A_log", "D_ssd", "ssd_norm_w", "w_down_ssd", "lam_re",
              "lam_im", "log_dt", "B_re", "B_im", "C_re", "C_im", "D_s5", "w_glu", "b_glu", "w_down_s5", "w_out"):
        d[k] = np.ascontiguousarray(inputs[k][0])
    d["final_norm_w"] = np.ascontiguousarray(inputs["final_norm_w"])
    return d


def kernel(**inputs):
    inputs = {k: np.asarray(v, dtype=np.float32) for k, v in inputs.items()}
    if "nc" not in _NC:
        _NC["nc"] = build()
    nc = _NC["nc"]
    consts = host_consts()
    in_maps = []
    for c in range(NCORES):
        d = _shard(inputs, c)
        d.update(consts)
        in_maps.append(d)
    res = run_bass_kernel_spmd(nc, in_maps, core_ids=list(range(NCORES))).results
    cat = lambda n: np.concatenate([r[n] for r in res], axis=0)
    y_p = np.stack([r["y_p"] for r in res], 0)
    y_s = cat("y_s").reshape(128, 8, 1024)
    conv_p = np.stack([r["conv_p"] for r in res], 0)[None]
    ssm_p = np.stack([r["ssm_p"] for r in res], 0)[None]
    re_p = np.stack([r["re_p"] for r in res], 0)[None]
    im_p = np.stack([r["im_p"] for r in res], 0)[None]
    conv_s = cat("conv_s")[None]
    ssm_s = cat("ssm_s")[None]
    re_s = cat("re_s")[None]
    im_s = cat("im_s")[None]
    return tuple(np.ascontiguousarray(a, dtype=np.float32) for a in
                 (y_p, y_s, conv_p, ssm_p, re_p, im_p, conv_s, ssm_s, re_s, im_s))
```

```python
import contextlib
import math
import numpy as np
import concourse.bass as bass
import concourse.mybir as mybir
from concourse.bass_utils import run_bass_kernel_spmd

F32 = mybir.dt.float32
BF16 = mybir.dt.bfloat16
I32 = mybir.dt.int32
ALU = mybir.AluOpType
AF = mybir.ActivationFunctionType

ENGS = ("pe", "act", "dve", "pool", "sp")
N_DMA_SEM = 24
NCORES = 8
D = 1024
PROJ = 10272
Z0, XBC0, DT0, U0, Z50, GA0, GB0 = 0, 2048, 6144, 6176, 7200, 8224, 9248
EPS = 1e-6
SIN_SCALE = 6.28318


class Prog:
    def __init__(self):
        self.ops = []
        self.last_write = {}
        self.readers = {}
        self.dma_count = {e: 0 for e in ENGS}
        self.slot_last = {}
        self.last_op = {}
        self.pending = {}

    def barrier(self):
        deps = set(self.last_op.values()) | set(self.slot_last.values())
        for e in ENGS:
            self.pending[e] = set(deps) | self.pending.get(e, set())

    def op(self, eng, fn, reads=(), writes=(), dma=False):
        writes = list(writes) + [k for k in reads if k.startswith("pb")]
        reads = [k for k in reads if not k.startswith("pb")]
        oid = len(self.ops)
        deps = set()
        for k in reads:
            if k in self.last_write:
                deps.add(self.last_write[k])
        for k in writes:
            if k in self.last_write:
                deps.add(self.last_write[k])
            deps.update(self.readers.get(k, ()))
        if eng in self.pending:
            deps |= self.pending.pop(eng)
        rec = dict(eng=eng, fn=fn, deps=deps, dma=dma, sig=False, seq=None)
        if dma:
            i = self.dma_count[eng]
            self.dma_count[eng] += 1
            slot = i % N_DMA_SEM
            rec["slot"] = slot
            rec["val"] = 16 * (i // N_DMA_SEM + 1)
            prev = self.slot_last.get((eng, slot))
            if prev is not None:
                deps.add(prev)
            self.slot_last[(eng, slot)] = oid
        else:
            self.last_op[eng] = oid
        deps.discard(oid)
        self.ops.append(rec)
        for k in writes:
            self.last_write[k] = oid
            self.readers[k] = []
        for k in reads:
            self.readers.setdefault(k, []).append(oid)
        return oid

    def emit(self, nc):
        ops = self.ops

        def skip(dd, o):
            return (not dd["dma"]) and dd["eng"] == o["eng"] == "pe" and not o["dma"]
        for o in ops:
            for d in o["deps"]:
                dd = ops[d]
                if dd["dma"] or skip(dd, o):
                    continue
                dd["sig"] = True
        cnt = {e: 0 for e in ENGS}
        for o in ops:
            if o["sig"]:
                cnt[o["eng"]] += 1
                o["seq"] = cnt[o["eng"]]
        with contextlib.ExitStack() as st:
            csem = {e: st.enter_context(nc.semaphore("c_" + e)) for e in ENGS}
            dsem = {(e, s): st.enter_context(nc.semaphore(f"d_{e}{s}"))
                    for e in ENGS if self.dma_count[e] > 0
                    for s in range(min(N_DMA_SEM, self.dma_count[e]))}
            block = st.enter_context(nc.Block())

            def run(eng_name, eng):
                waited = {}
                for o in ops:
                    if o["eng"] != eng_name:
                        continue
                    for d in sorted(o["deps"]):
                        dd = ops[d]
                        if dd["dma"]:
                            key = ("d", dd["eng"], dd["slot"])
                            sem = dsem[(dd["eng"], dd["slot"])]
                            val = dd["val"]
                        else:
                            if skip(dd, o):
                                continue
                            key = ("c", dd["eng"])
                            sem = csem[dd["eng"]]
                            val = dd["seq"]
                        if waited.get(key, 0) >= val:
                            continue
                        waited[key] = val
                        eng.wait_ge(sem, val)
                    ins = o["fn"](eng)
                    if o["dma"]:
                        ins.then_inc(dsem[(eng_name, o["slot"])], 16)
                    elif o["sig"]:
                        ins.then_inc(csem[eng_name], 1)
                if eng_name == "sp":
                    for (e, s), oid in self.slot_last.items():
                        eng.wait_ge(dsem[(e, s)], ops[oid]["val"])

            block.tensor(lambda e: run("pe", e))
            block.scalar(lambda e: run("act", e))
            block.vector(lambda e: run("dve", e))
            block.gpsimd(lambda e: run("pool", e))
            block.sync(lambda e: run("sp", e))


def host_consts():
    c = {}
    c["ident"] = np.eye(128, dtype=np.float32)
    s = np.arange(128)
    c["negm_p"] = np.where(s[None, :] >= s[:, None], 0.0, -30000.0).astype(np.float32)
    same = (s[None, :] // 8) == (s[:, None] // 8)
    c["negm_s"] = np.where((s[None, :] >= s[:, None]) & same, 0.0, -30000.0).astype(np.float32)
    rp = np.ones((32, 128), np.float32); rp[:, 0] = 0
    rs = np.ones((32, 128), np.float32); rs[:, ::8] = 0
    c["rm_p"] = rp
    c["rm_s"] = rs
    r128 = np.ones((128, 128), np.float32); r128[:, ::8] = 0
    c["rm128"] = r128
    c["seqmask"] = (s[:, None] // 8 == np.arange(16)[None, :]).astype(np.float32)
    c["pairmask"] = (s[:, None] // 32 == np.arange(4)[None, :]).astype(np.float32)
    c["iota"] = np.broadcast_to(np.arange(128, dtype=np.float32)[None, :], (128, 128)).copy()
    return c


IN_SPECS = [
    ("xp", (2048, 1024)), ("xs", (128, 1024)), ("conv0", (16, 3, 4096)), ("ssm0", (16, 32, 64, 128)),
    ("re0", (16, 64, 64)), ("im0", (16, 64, 64)), ("norm_w", (1024,)), ("w_in", (1024, PROJ)),
    ("conv_w", (4, 4096)), ("conv_b", (4096,)), ("dt_bias", (32,)), ("A_log", (32,)), ("D_ssd", (32,)),
    ("ssd_norm_w", (2048,)), ("w_down_ssd", (2048, 1024)), ("lam_re", (64, 64)), ("lam_im", (64, 64)),
    ("log_dt", (64,)), ("B_re", (64, 64, 16)), ("B_im", (64, 64, 16)), ("C_re", (64, 16, 64)),
    ("C_im", (64, 16, 64)), ("D_s5", (1024,)), ("w_glu", (1024, 1024)), ("b_glu", (1024,)),
    ("w_down_s5", (1024, 1024)), ("w_out", (1024, 1024)), ("final_norm_w", (1024,)),
    ("ident", (128, 128)), ("negm_p", (128, 128)), ("negm_s", (128, 128)), ("rm_p", (32, 128)),
    ("rm_s", (32, 128)), ("rm128", (128, 128)), ("seqmask", (128, 16)), ("pairmask", (128, 4)),
    ("iota", (128, 128)),
]
OUT_SPECS = [
    ("y_p", (2048, 1024)), ("y_s", (128, 1024)), ("conv_p", (3, 4096)), ("ssm_p", (32, 64, 128)),
    ("re_p", (64, 64)), ("im_p", (64, 64)), ("conv_s", (16, 3, 4096)), ("ssm_s", (16, 32, 64, 128)),
    ("re_s", (16, 64, 64)), ("im_s", (16, 64, 64)),
]


STOP = [None]
INTERLEAVE = [True]
BLK = [None]


class _Stop(Exception):
    pass


def ck(n):
    if STOP[0] is not None and n >= STOP[0]:
        raise _Stop()


def build(tiles=None):
    nc = bass.Bass("TRN2", target_bir_lowering=False)
    I = {n: nc.dram_tensor(n, list(s), F32, kind="ExternalInput").ap() for n, s in IN_SPECS}
    O = {n: nc.dram_tensor(n, list(s), F32, kind="ExternalOutput").ap() for n, s in OUT_SPECS}
    WS = {
        "w_in": nc.dram_tensor("ws_w_in", [1024, PROJ], BF16).ap(),
        "w_down_ssd": nc.dram_tensor("ws_wds", [2048, 1024], BF16).ap(),
        "w_glu": nc.dram_tensor("ws_wglu", [1024, 1024], BF16).ap(),
        "w_down_s5": nc.dram_tensor("ws_wd5", [1024, 1024], BF16).ap(),
        "w_out": nc.dram_tensor("ws_wout", [1024, 1024], BF16).ap(),
    }
    P = Prog()

    CUR = [None]
    MMB = [(0, 1)]

    def _rec(eng, fn, r, w, dma=False):
        if CUR[0] is None:
            P.op(eng, fn, r, w, dma=dma)
        else:
            CUR[0].append((eng, fn, tuple(r), tuple(w), dma))

    def V(fn, r=(), w=()):
        _rec("dve", fn, r, w)

    def G(fn, r=(), w=()):
        _rec("pool", fn, r, w)

    def A(fn, r=(), w=()):
        _rec("act", fn, r, w)

    def T(fn, r=(), w=()):
        _rec("pe", fn, r, w)

    def DMA(fn, r=(), w=()):
        _rec("sp", fn, r, w, dma=True)

    def DMAO(fn, r=(), w=()):
        _rec("pool", fn, r, w, dma=True)

    st = contextlib.ExitStack()
    with st:
        def sb(name, shape, dt=F32, stack=None):
            return (stack or st).enter_context(nc.sbuf_tensor("s_" + name, list(shape), dt))

        def kn(t):
            n_ = getattr(t, "name")
            return n_[2:] if n_.startswith("s_") else n_
        pb = [st.enter_context(nc.psum_tensor(f"pb{i}", [128, 512], F32)) for i in range(8)]
        pbb = [p.bitcast(BF16) for p in pb]

        identf = sb("identf", [128, 128]); identb = sb("identb", [128, 128], BF16)
        negm_p = sb("negm_p", [128, 128]); negm_s = sb("negm_s", [128, 128])
        rm_p = sb("rm_p", [32, 128]); rm_s = sb("rm_s", [32, 128]); rm128 = sb("rm128", [128, 128])
        seqmask = sb("seqmask", [128, 16]); pairmask = sb("pairmask", [128, 4])
        ones32 = sb("ones32", [32, 128]); onesb = sb("onesb", [128, 128], BF16)
        onecol = sb("onecol", [128, 1]); epscol = sb("epscol", [128, 1])
        for nm, t in [("ident", identf), ("negm_p", negm_p), ("negm_s", negm_s), ("rm_p", rm_p), ("rm_s", rm_s),
                      ("rm128", rm128), ("seqmask", seqmask), ("pairmask", pairmask)]:
            DMA(lambda e, nm=nm, t=t: e.dma_start(out=t[:], in_=I[nm]), w=[kn(t)])
        V(lambda e: e.tensor_copy(out=identb[:], in_=identf[:]), ["identf"], ["identb"])
        G(lambda e: e.memset(ones32[:], 1.0), w=["ones32"])
        G(lambda e: e.memset(onesb[:], 1.0), w=["onesb"])
        G(lambda e: e.memset(onecol[:], 1.0), w=["onecol"])
        G(lambda e: e.memset(epscol[:], EPS), w=["epscol"])

        def colvec(name, src, nt):
            t = sb(name, [128, nt])
            DMA(lambda e: e.dma_start(out=t[:], in_=src.rearrange("(j p) -> p j", p=128), allow_slow_non_contiguous=True), w=[name])
            return t
        normw = colvec("normw", I["norm_w"], 8)
        convb = colvec("convb", I["conv_b"], 32)
        ssdnw = colvec("ssdnw", I["ssd_norm_w"], 16)
        ds5 = colvec("ds5", I["D_s5"], 8)
        bglu = colvec("bglu", I["b_glu"], 8)
        convw = sb("convw", [128, 32, 4])
        for k in range(4):
            DMA(lambda e, k=k: e.dma_start(out=convw[:, :, k], in_=I["conv_w"][k].rearrange("(j p) -> p j", p=128), allow_slow_non_contiguous=True), w=["convw"])
        fnw = sb("fnw", [128, 1024])
        DMA(lambda e: e.dma_start(out=fnw[:], in_=I["final_norm_w"].partition_broadcast(128)), w=["fnw"])
        dtb = sb("dtb", [32, 1]); aneg = sb("aneg", [32, 1])
        DMA(lambda e: e.dma_start(out=dtb[:], in_=I["dt_bias"].rearrange("(h o) -> h o", o=1)), w=["dtb"])
        DMA(lambda e: e.dma_start(out=aneg[:], in_=I["A_log"].rearrange("(h o) -> h o", o=1)), w=["aneg"])
        A(lambda e: e.activation(out=aneg[:], in_=aneg[:], func=AF.Exp), ["aneg"], ["aneg"])
        V(lambda e: e.tensor_scalar(out=aneg[:], in0=aneg[:], scalar1=-1.0, scalar2=None, op0=ALU.mult), ["aneg"], ["aneg"])
        dexp = sb("dexp", [128, 16])
        dv = I["D_ssd"].rearrange("(pr h2) -> h2 pr", h2=2)
        for h2 in range(2):
            DMA(lambda e, h2=h2: e.dma_start(out=dexp[64 * h2:64 * h2 + 64, :], in_=dv[h2].partition_broadcast(64), allow_slow_non_contiguous=True), w=["dexp"])
        are = sb("are", [128, 32]); aim = sb("aim", [128, 32]); rdec = sb("rdec", [128, 32])
        aere = sb("aere", [128, 32]); aeim = sb("aeim", [128, 32])
        ecos = sb("ecos", [128, 32, 64]); esin = sb("esin", [128, 32, 64])
        wb5 = sb("wb5", [128, 8, 2, 128], BF16); wd5 = sb("wd5", [128, 8, 2, 128], BF16)
        halo = sb("halo", [128, 32, 3])
        STt = sb("STt", [128, 32, 64])
        SA = sb("SA", [128, 16, 128], BF16); SB = sb("SB", [128, 16, 128], BF16)
        XA = sb("XA", [128, 16, 128], BF16); XB = sb("XB", [128, 16, 128], BF16)
        sgc = sb("sgc", [128, 32, 2])
        for t_ in (halo, STt, SA, SB, XA, XB, sgc, wd5):
            G(lambda e, t_=t_: e.memset(t_[:], 0.0), w=([kn(t_)] if kn(t_) not in ("SA", "SB") else [f"{kn(t_)}{i}" for i in range(4)]))

        pst = contextlib.ExitStack()
        with pst:
            def psb(name, shape, dt=F32):
                return sb(name, shape, dt, stack=pst)
            iota = psb("iota", [128, 128])
            DMA(lambda e: e.dma_start(out=iota[:], in_=I["iota"]), w=["iota"])
            lre = psb("lre", [128, 32]); lim = psb("lim", [128, 32]); stp = psb("stp", [128, 32]); th = psb("th", [128, 32])
            DMA(lambda e: e.dma_start(out=lre[:], in_=I["lam_re"].rearrange("(gp g2) p -> (g2 p) gp", g2=2), allow_slow_non_contiguous=True), w=["lre"])
            DMA(lambda e: e.dma_start(out=lim[:], in_=I["lam_im"].rearrange("(gp g2) p -> (g2 p) gp", g2=2), allow_slow_non_contiguous=True), w=["lim"])
            ldv = I["log_dt"].rearrange("(gp g2) -> g2 gp", g2=2)
            for g2 in range(2):
                DMA(lambda e, g2=g2: e.dma_start(out=stp[64 * g2:64 * g2 + 64, :], in_=ldv[g2].partition_broadcast(64), allow_slow_non_contiguous=True), w=["stp"])
            A(lambda e: e.activation(out=stp[:], in_=stp[:], func=AF.Exp), ["stp"], ["stp"])
            V(lambda e: e.tensor_tensor(out=rdec[:], in0=lre[:], in1=stp[:], op=ALU.mult), ["lre", "stp"], ["rdec"])
            A(lambda e: e.activation(out=rdec[:], in_=rdec[:], func=AF.Exp), ["rdec"], ["rdec"])
            V(lambda e: e.tensor_tensor(out=th[:], in0=lim[:], in1=stp[:], op=ALU.mult), ["lim", "stp"], ["th"])
            V(lambda e: e.tensor_scalar(out=th[:], in0=th[:], scalar1=1.0 / (2.0 * math.pi), scalar2=None, op0=ALU.mult), ["th"], ["th"])
            tmpx = psb("tmpx", [128, 2048]); tmpf = psb("tmpf", [128, 2048]); tmpi = psb("tmpi", [128, 2048], I32)

            def sin_turns(out_ap, mk_x, n, off, rkeys, wkeys):
                V(lambda e: mk_x(e, tmpx[:, 0:n]), rkeys, ["tmpx"])
                if off:
                    V(lambda e: e.tensor_scalar(out=tmpx[:, 0:n], in0=tmpx[:, 0:n], scalar1=float(off), scalar2=None, op0=ALU.add), ["tmpx"], ["tmpx"])
                V(lambda e: e.tensor_copy(out=tmpi[:, 0:n], in_=tmpx[:, 0:n]), ["tmpx"], ["tmpi"])
                V(lambda e: e.tensor_copy(out=tmpf[:, 0:n], in_=tmpi[:, 0:n]), ["tmpi"], ["tmpf"])
                V(lambda e: e.tensor_tensor(out=tmpx[:, 0:n], in0=tmpx[:, 0:n], in1=tmpf[:, 0:n], op=ALU.subtract), ["tmpx", "tmpf"], ["tmpx"])
                A(lambda e: e.activation(out=out_ap, in_=tmpx[:, 0:n], func=AF.Sin, scale=SIN_SCALE), ["tmpx"], wkeys)
            cs1 = psb("cs1", [128, 32]); sn1 = psb("sn1", [128, 32])
            x1 = lambda e, dst: e.tensor_copy(out=dst, in_=th[:])
            sin_turns(sn1[:], x1, 32, 0.0, ["th"], ["sn1"])
            sin_turns(cs1[:], x1, 32, 0.25, ["th"], ["cs1"])
            V(lambda e: e.tensor_tensor(out=are[:], in0=rdec[:], in1=cs1[:], op=ALU.mult), ["rdec", "cs1"], ["are"])
            V(lambda e: e.tensor_tensor(out=aim[:], in0=rdec[:], in1=sn1[:], op=ALU.mult), ["rdec", "sn1"], ["aim"])
            x64 = lambda e, dst: e.tensor_scalar(out=dst, in0=th[:], scalar1=64.0, scalar2=None, op0=ALU.mult)
            sin_turns(sn1[:], x64, 32, 0.0, ["th"], ["sn1"])
            sin_turns(cs1[:], x64, 32, 0.25, ["th"], ["cs1"])
            V(lambda e: e.tensor_tensor(out=aere[:], in0=rdec[:], in1=cs1[:], op=ALU.mult), ["rdec", "cs1"], ["aere"])
            V(lambda e: e.tensor_tensor(out=aeim[:], in0=rdec[:], in1=sn1[:], op=ALU.mult), ["rdec", "sn1"], ["aeim"])
            xt_ = lambda e, dst: e.tensor_tensor(out=dst.rearrange("p (a b) -> p a b", b=64), in0=th[:].unsqueeze(2).broadcast_to([128, 32, 64]),
                                                 in1=iota[:, 0:64].unsqueeze(1).broadcast_to([128, 32, 64]), op=ALU.mult)
            sin_turns(esin[:].rearrange("p a b -> p (a b)"), xt_, 2048, 0.0, ["th", "iota"], ["esin"])
            sin_turns(ecos[:].rearrange("p a b -> p (a b)"), xt_, 2048, 0.25, ["th", "iota"], ["ecos"])
            gre = psb("gre", [128, 32]); gim = psb("gim", [128, 32]); den = psb("den", [128, 32])
            t1 = psb("t1s", [128, 32]); t2 = psb("t2s", [128, 32]); am1 = psb("am1", [128, 32])
            V(lambda e: e.tensor_scalar(out=am1[:], in0=are[:], scalar1=-1.0, scalar2=None, op0=ALU.add), ["are"], ["am1"])
            V(lambda e: e.tensor_tensor(out=den[:], in0=lre[:], in1=lre[:], op=ALU.mult), ["lre"], ["den"])
            V(lambda e: e.tensor_tensor(out=t1[:], in0=lim[:], in1=lim[:], op=ALU.mult), ["lim"], ["t1s"])
            V(lambda e: e.tensor_tensor(out=den[:], in0=den[:], in1=t1[:], op=ALU.add), ["den", "t1s"], ["den"])
            V(lambda e: e.reciprocal(out=den[:], in_=den[:]), ["den"], ["den"])
            V(lambda e: e.tensor_tensor(out=t1[:], in0=am1[:], in1=lre[:], op=ALU.mult), ["am1", "lre"], ["t1s"])
            V(lambda e: e.tensor_tensor(out=t2[:], in0=aim[:], in1=lim[:], op=ALU.mult), ["aim", "lim"], ["t2s"])
            V(lambda e: e.tensor_tensor(out=gre[:], in0=t1[:], in1=t2[:], op=ALU.add), ["t1s", "t2s"], ["gre"])
            V(lambda e: e.tensor_tensor(out=gre[:], in0=gre[:], in1=den[:], op=ALU.mult), ["gre", "den"], ["gre"])
            V(lambda e: e.tensor_tensor(out=t1[:], in0=aim[:], in1=lre[:], op=ALU.mult), ["aim", "lre"], ["t1s"])
            V(lambda e: e.tensor_tensor(out=t2[:], in0=am1[:], in1=lim[:], op=ALU.mult), ["am1", "lim"], ["t2s"])
            V(lambda e: e.tensor_tensor(out=gim[:], in0=t1[:], in1=t2[:], op=ALU.subtract), ["t1s", "t2s"], ["gim"])
            V(lambda e: e.tensor_tensor(out=gim[:], in0=gim[:], in1=den[:], op=ALU.mult), ["gim", "den"], ["gim"])
            bre = psb("bre", [128, 32, 16]); bim = psb("bim", [128, 32, 16]); bbr = psb("bbr", [128, 32, 16]); bbi = psb("bbi", [128, 32, 16]); tb = psb("tbs", [128, 32, 16])
            DMA(lambda e: e.dma_start(out=bre[:], in_=I["B_re"].rearrange("(gp g2) p k -> (g2 p) gp k", g2=2)), w=["bre"])
            DMA(lambda e: e.dma_start(out=bim[:], in_=I["B_im"].rearrange("(gp g2) p k -> (g2 p) gp k", g2=2)), w=["bim"])
            gb_ = lambda t: t[:].unsqueeze(2).broadcast_to([128, 32, 16])
            V(lambda e: e.tensor_tensor(out=bbr[:], in0=bre[:], in1=gb_(gre), op=ALU.mult), ["bre", "gre"], ["bbr"])
            V(lambda e: e.tensor_tensor(out=tb[:], in0=bim[:], in1=gb_(gim), op=ALU.mult), ["bim", "gim"], ["tbs"])
            V(lambda e: e.tensor_tensor(out=bbr[:], in0=bbr[:], in1=tb[:], op=ALU.subtract), ["bbr", "tbs"], ["bbr"])
            V(lambda e: e.tensor_tensor(out=bbi[:], in0=bim[:], in1=gb_(gre), op=ALU.mult), ["bim", "gre"], ["bbi"])
            V(lambda e: e.tensor_tensor(out=tb[:], in0=bre[:], in1=gb_(gim), op=ALU.mult), ["bre", "gim"], ["tbs"])
            V(lambda e: e.tensor_tensor(out=bbi[:], in0=bbi[:], in1=tb[:], op=ALU.add), ["bbi", "tbs"], ["bbi"])
            cin = psb("cin", [128, 4, 2, 64])
            cst = [psb("cstr", [128, 32, 16]), psb("csti", [128, 32, 16])]
            for ri, nm in enumerate(["C_re", "C_im"]):
                cv = I[nm].rearrange("(gh gq g2) k p -> gq k gh g2 p", gq=8, g2=2)
                for gq in range(8):
                    for gh in range(4):
                        DMA(lambda e, gq=gq, gh=gh, cv=cv: e.dma_start(out=cin[16 * gq:16 * gq + 16, gh, :, :], in_=cv[gq][:, gh, :, :]), w=["cin"])
                for gh in range(4):
                    T(lambda e, gh=gh: e.transpose(out=pb[7][:, 0:128], in_=cin[:, gh, :, :].rearrange("p a b -> p (a b)"), identity=identf[:]), ["cin", "identf"], ["pb7"])
                    V(lambda e, gh=gh, ri=ri: e.tensor_copy(out=cst[ri][:, 8 * gh:8 * gh + 8, :], in_=pb[7][:, 0:128].rearrange("p (a b) -> p a b", b=16)), ["pb7"], [kn(cst[ri])])
            zst = psb("zst", [128, 4, 2, 16])
            G(lambda e: e.memset(zst[:], 0.0), w=["zst"])
            for ft in range(8):
                for ri in range(2):
                    src = [bbr, bbi][ri]
                    for h2 in range(2):
                        V(lambda e, ft=ft, h2=h2, src=src: e.tensor_copy(out=zst[64 * h2:64 * h2 + 64, :, h2, :], in_=src[64 * h2:64 * h2 + 64, 4 * ft:4 * ft + 4, :]), [kn(src)], ["zst"])
                    T(lambda e: e.transpose(out=pb[6][:, 0:128], in_=zst[:].rearrange("p a b c -> p (a b c)"), identity=identf[:]), ["zst", "identf"], ["pb6"])
                    V(lambda e, ft=ft, ri=ri: e.tensor_copy(out=wb5[:, ft, ri, :], in_=pb[6][:, 0:128]), ["pb6"], ["wb5"])
                    for h2 in range(2):
                        V(lambda e, ft=ft, ri=ri, h2=h2: e.tensor_scalar(
                            out=wd5[64 * h2:64 * h2 + 64, ft, ri, :].rearrange("p (q g k) -> p q g k", g=2, k=16)[:, :, h2, :],
                            in0=cst[ri][64 * h2:64 * h2 + 64, 4 * ft:4 * ft + 4, :], scalar1=(1.0 if ri == 0 else -1.0), scalar2=None, op0=ALU.mult),
                          [kn(cst[ri])], ["wd5"])
            NST = 6
            stg = [psb(f"stg{i}", [128, 2048]) for i in range(NST)]
            stgb = [psb(f"stgb{i}", [128, 2048], BF16) for i in range(NST)]
            cnt = [0]

            def cast_rows(src, dst, ncols):
                for c0 in range(0, ncols, 2048):
                    w_ = min(2048, ncols - c0)
                    i = cnt[0] % NST
                    cnt[0] += 1
                    DMA(lambda e, i=i, c0=c0, w_=w_: e.dma_start(out=stg[i][:, 0:w_], in_=src[:, c0:c0 + w_]), w=[f"stg{i}"])
                    sel = cnt[0] % 3
                    if sel == 0:
                        G(lambda e, i=i, w_=w_: e.tensor_copy(out=stgb[i][:, 0:w_], in_=stg[i][:, 0:w_]), [f"stg{i}"], [f"stgb{i}"])
                    elif sel == 1:
                        A(lambda e, i=i, w_=w_: e.activation(out=stgb[i][:, 0:w_], in_=stg[i][:, 0:w_], func=AF.Copy), [f"stg{i}"], [f"stgb{i}"])
                    else:
                        V(lambda e, i=i, w_=w_: e.tensor_copy(out=stgb[i][:, 0:w_], in_=stg[i][:, 0:w_]), [f"stg{i}"], [f"stgb{i}"])
                    DMAO(lambda e, i=i, c0=c0, w_=w_: e.dma_start(out=dst[:, c0:c0 + w_], in_=stgb[i][:, 0:w_]), [f"stgb{i}"], ["ws"])
            for nm, rows in [("w_in", 1024), ("w_down_ssd", 2048), ("w_glu", 1024), ("w_down_s5", 1024), ("w_out", 1024)]:
                nco = PROJ if nm == "w_in" else 1024
                for r0 in range(0, rows, 128):
                    cast_rows(I[nm][r0:r0 + 128, :], WS[nm][r0:r0 + 128, :], nco)

        P.barrier()

        xt = sb("xt", [128, 1024]); ot = sb("ot", [128, 1024]); ss = sb("ss", [128, 1]); rstd = sb("rstd", [128, 1])
        ssB = sb("ssB", [128, 1]); rstdB = sb("rstdB", [128, 1])
        xsn = sb("xsn", [128, 1024], BF16); hT = sb("hT", [128, 8, 128], BF16)
        wbuf = [sb(f"wbuf{i}", [128, 8, 256], BF16) for i in range(4)]
        zs = sb("zs", [128, 16, 128], BF16)
        xc = sb("xc", [128, 32, 128], BF16)
        raw = [sb(f"raw{i}", [128, 176]) for i in range(2)]
        acc = [sb(f"acc{i}", [128, 128]) for i in range(2)]
        dts = sb("dts", [32, 128]); dAs = sb("dAs", [32, 128]); acs = sb("acs", [32, 128]); dec = sb("dec", [32, 128]); wdt = sb("wdt", [32, 128])
        tokm = sb("tokm", [128, 3, 32])
        u5_ = [sb(f"u5_{i}", [128, 8, 128], BF16) for i in range(2)]; z5s_ = [sb(f"z5s_{i}", [128, 8, 128], BF16) for i in range(2)]
        gas = sb("gas", [128, 8, 128], BF16); gbs_ = [sb(f"gbs_{i}", [128, 8, 128], BF16) for i in range(2)]
        Xd = sb("Xd", [128, 32, 64], BF16); Btok = sb("Btok", [128, 8, 128], BF16)
        rhsR = sb("rhsR", [32, 4, 128]); Eg = sb("Eg", [128, 4, 128]); Dm = sb("Dm", [128, 4, 128])
        Lg = sb("Lg", [128, 4, 128], BF16); Mg = sb("Mg", [128, 4, 128], BF16)
        CEall = sb("CEall", [128, 32, 128], BF16)
        dch = sb("dch", [128, 32, 16])
        S0qs = [sb(f"S0q{i}", [128, 4, 128]) for i in range(2)]; Snqs = [sb(f"Snq{i}", [128, 4, 128]) for i in range(2)]
        Bms = [sb("Bm0", [128, 8, 128], BF16)] * 2
        yv2 = [sb(f"yv2_{i}", [128, 2, 128]) for i in range(2)]; ysq2 = [sb(f"ysq2_{i}", [128, 2, 128], BF16) for i in range(2)]
        rstg2 = [sb(f"rstg2_{i}", [128, 128]) for i in range(2)]; ynT = sb("ynT", [128, 16, 128], BF16)
        yaT_ = [sb(f"yaT_{i}", [128, 8, 128], BF16) for i in range(2)]; ybT = sb("ybT", [128, 8, 128], BF16); mixT = sb("mixT", [128, 8, 128], BF16)
        u5m = sb("u5m", [128, 4, 128], BF16)
        bu = sb("bu", [128, 4, 2, 128]); sgm = sb("sgm", [128, 4, 2, 128]); rts = sb("rts", [128, 4, 128])
        bt1 = sb("bt1", [128, 4, 128]); bt2 = sb("bt2", [128, 4, 128])
        s5s = sb("s5s", [128, 4, 2, 128], BF16)
        cin_ = sb("cinn", [128, 4, 2, 16])
        ctm = sb("ctm", [128, 4, 2, 16])
        fin = sb("fin", [128, 32, 2, 16]); s0all = sb("s0all", [128, 32, 2, 16])
        y5t = sb("y5t", [128, 128]); y5u = sb("y5u", [128, 128]); y5v = sb("y5v", [128, 128])
        y5g = sb("y5g", [128, 8, 128], BF16); y5f = sb("y5f", [128, 8, 128], BF16); glus = sb("glus", [128, 128])
        s5io = sb("s5io", [128, 4, 2, 64]); s5tr = sb("s5tr", [128, 16, 8])
        cvo = sb("cvo", [48, 1024]); cst3 = sb("cst3", [128, 48])

        wcnt = [0]

        WSET = [(0, 1, 2)]

        def wload(src_ap):
            i = WSET[0][wcnt[0] % len(WSET[0])]
            wcnt[0] += 1
            kt = src_ap.shape[0] // 128
            nco = src_ap.shape[1]
            DMA(lambda e: e.dma_start(out=wbuf[i][:, 0:kt, 0:nco], in_=src_ap.rearrange("(j p) f -> p j f", p=128)), ["ws"], [f"wbuf{i}"])
            return wbuf[i], f"wbuf{i}"

        pslot = [0]

        def mm_slot():
            s = MMB[0][pslot[0] % 2]
            pslot[0] += 1
            return pb[s][:, 0:128], f"pb{s}"

        def proj_ft(wt, wk, c0, nrows=128):
            ps, pk = mm_slot()
            for j in range(8):
                T(lambda e, j=j: e.matmul(ps[0:nrows, :], lhsT=wt[:, j, c0:c0 + nrows], rhs=hT[:, j, :], start=(j == 0), stop=(j == 7)), [wk, "hT"], [pk])
            return ps, pk

        def dense_T(wname, kt, rhs_fn, rkeys_fn, evac):
            for cb in range(4):
                halves = [wload(WS[wname][128 * k0:128 * (k0 + 8), 256 * cb:256 * cb + 256]) for k0 in range(0, kt, 8)]
                for s in range(2):
                    m = 2 * cb + s
                    ps, pk = mm_slot()
                    for k in range(kt):
                        wt, wk = halves[k // 8]
                        T(lambda e, wt=wt, k=k, s=s, ps=ps: e.matmul(ps, lhsT=wt[:, k % 8, 128 * s:128 * s + 128], rhs=rhs_fn(k), start=(k == 0), stop=(k == kt - 1)), [wk] + rkeys_fn(k), [pk])
                    evac(m, ps, pk)

        def cmul(out_re, out_im, ar, ai, br, bi, keys_r, keys_w, t1_, t2_, tk):
            V(lambda e: e.tensor_tensor(out=t1_, in0=ar, in1=br, op=ALU.mult), keys_r, [tk[0]])
            V(lambda e: e.tensor_tensor(out=t2_, in0=ai, in1=bi, op=ALU.mult), keys_r, [tk[1]])
            V(lambda e: e.tensor_tensor(out=out_re, in0=t1_, in1=t2_, op=ALU.subtract), tk, keys_w[0:1])
            V(lambda e: e.tensor_tensor(out=t1_, in0=ar, in1=bi, op=ALU.mult), keys_r + keys_w[0:1], [tk[0]])
            V(lambda e: e.tensor_tensor(out=t2_, in0=ai, in1=br, op=ALU.mult), keys_r + keys_w[0:1], [tk[1]])
            V(lambda e: e.tensor_tensor(out=out_im, in0=t1_, in1=t2_, op=ALU.add), tk, keys_w[1:2])

        def recA(ti, par):
            u5 = u5_[par]; z5s = z5s_[par]; gbs = gbs_[par]; yaT = yaT_[par]
            smp = (ti == 16)
            lastp = (ti == 15)
            xsrc = I["xs"] if smp else I["xp"][128 * ti:128 * ti + 128, :]
            ydst = O["y_s"] if smp else O["y_p"][128 * ti:128 * ti + 128, :]
            negm = negm_s if smp else negm_p
            rmk = rm_s if smp else rm_p
            nseq, L = (16, 8) if smp else (1, 128)
            DMA(lambda e: e.dma_start(out=xt[:], in_=xsrc), w=["xt"])
            G(lambda e: e.memset(ss[:], 0.0), w=["ss"])
            A(lambda e: e.activation(out=xsn[:], in_=xt[:], func=AF.Square, accum_out=ss[:]), ["xt", "ss"], ["xsn", "ss"])
            A(lambda e: e.activation(out=rstd[:], in_=ss[:], func=AF.Sqrt, scale=1.0 / D, bias=epscol[:]), ["ss", "epscol"], ["rstd"])
            V(lambda e: e.reciprocal(out=rstd[:], in_=rstd[:]), ["rstd"], ["rstd"])
            V(lambda e: e.tensor_scalar(out=xsn[:], in0=xt[:], scalar1=rstd[:, 0:1], scalar2=None, op0=ALU.mult), ["xt", "rstd"], ["xsn"])
            for j in range(8):
                T(lambda e, j=j: e.transpose(out=pbb[5][:, 128 * j:128 * j + 128], in_=xsn[:, 128 * j:128 * j + 128], identity=identb[:]), ["xsn", "identb"], ["pb5"])
            for j in range(8):
                V(lambda e, j=j: e.tensor_scalar(out=hT[:, j, :], in0=pbb[5][:, 128 * j:128 * j + 128], scalar1=normw[:, j:j + 1], scalar2=None, op0=ALU.mult), ["pb5", "normw"], ["hT"])
            blocks = [("xbc", c0, 256) for c0 in range(XBC0, DT0, 256)] + [("dt", DT0, 32)]
            blocks += [("z", c0, 256) for c0 in range(Z0, XBC0, 256)]
            for nm, b0 in (("u5", U0), ("z5", Z50), ("ga", GA0), ("gb", GB0)):
                blocks += [(nm, c0, 256) for c0 in range(b0, b0 + 1024, 256)]
            base = {"xbc": XBC0, "z": Z0, "u5": U0, "z5": Z50, "ga": GA0, "gb": GB0}
            deferred = []
            for nm, c0, nco in blocks:
                wt, wk = wload(WS["w_in"][:, c0:c0 + nco])
                if nm == "dt":
                    while deferred:
                        deferred.pop(0)()
                    ps, pk = proj_ft(wt, wk, 0, 32)
                    A(lambda e, ps=ps: e.activation(out=dts[:], in_=ps[0:32, :], func=AF.Exp, bias=dtb[:]), [pk, "dtb"], ["dts"])
                    A(lambda e: e.activation(out=dts[:], in_=dts[:], func=AF.Ln, bias=onecol[0:32, :]), ["dts", "onecol"], ["dts"])
                    continue
                for s in range(2):
                    ft = (c0 - base[nm]) // 128 + s
                    ps, pk = proj_ft(wt, wk, 128 * s)
                    if nm == "z":
                        A(lambda e, ps=ps, ft=ft: e.activation(out=zs[:, ft, :], in_=ps, func=AF.Silu), [pk], [f"zs{ft}"])
                    elif nm == "u5":
                        A(lambda e, ps=ps, ft=ft: e.activation(out=u5[:, ft, :], in_=ps, func=AF.Copy), [pk], [f"u5_{par}_{ft}"])
                    elif nm == "z5":
                        A(lambda e, ps=ps, ft=ft: e.activation(out=z5s[:, ft, :], in_=ps, func=AF.Silu), [pk], [f"z5_{par}_{ft}"])
                    elif nm == "ga":
                        A(lambda e, ps=ps, ft=ft: e.activation(out=gas[:, ft, :], in_=ps, func=AF.Sigmoid), [pk], [f"ga{ft}"])
                    elif nm == "gb":
                        A(lambda e, ps=ps, ft=ft: e.activation(out=gbs[:, ft, :], in_=ps, func=AF.Sigmoid), [pk], [f"gb_{par}_{ft}"])
                    else:
                        r_ = raw[ft % 2]; rk = f"raw{ft % 2}"; ac = acc[ft % 2]; ak = f"acc{ft % 2}"
                        EV = V
                        if smp:
                            rv = r_[:, 0:176].rearrange("p (s r) -> p s r", r=11)
                            if ft % 8 == 0:
                                DMA(lambda e, ft=ft: e.dma_start(out=cvo[:], in_=I["conv0"].rearrange("s r c -> (s r) c")[:, 128 * ft:128 * ft + 1024]), w=["cvo"])
                            T(lambda e, ft=ft: e.transpose(out=pb[6][:, 0:48], in_=cvo[:, 128 * (ft % 8):128 * (ft % 8) + 128], identity=identf[0:48, 0:48]), ["cvo", "identf"], ["pb6"])
                            V(lambda e, rv=rv: e.tensor_copy(out=rv[:, :, 0:3], in_=pb[6][:, 0:48].rearrange("p (s r) -> p s r", r=3)), ["pb6"], [rk])
                        else:
                            rv = r_[:, 0:131].unsqueeze(1)
                            A(lambda e, rv=rv, ft=ft: e.activation(out=rv[:, 0, 0:3], in_=halo[:, ft, :], func=AF.Copy), ["halo"], [rk])
                        A(lambda e, rv=rv, ps=ps: e.activation(out=rv[:, :, 3:3 + L], in_=ps.rearrange("p (s l) -> p s l", l=L), func=AF.Copy), [pk], [rk])
                        while deferred:
                            deferred.pop(0)()
                        av = ac[:].rearrange("p (s l) -> p s l", l=L)
                        EV(lambda e, rv=rv, av=av, ft=ft: e.tensor_scalar(out=av, in0=rv[:, :, 0:L], scalar1=convw[:, ft, 0:1], scalar2=convb[:, ft:ft + 1], op0=ALU.mult, op1=ALU.add), [rk, "convw", "convb"], [ak])
                        for k in range(1, 4):
                            EV(lambda e, rv=rv, av=av, ft=ft, k=k: e.scalar_tensor_tensor(out=av, in0=rv[:, :, k:k + L], scalar=convw[:, ft, k:k + 1], in1=av, op0=ALU.mult, op1=ALU.add), [rk, ak, "convw"], [ak])
                        def late(ac=ac, ft=ft, rv=rv, ak=ak, rk=rk):
                            A(lambda e: e.activation(out=xc[:, ft, :], in_=ac[:], func=AF.Silu), [ak], [f"xc{ft}"])
                            if not smp:
                                A(lambda e: e.activation(out=halo[:, ft, :], in_=rv[:, 0, 128:131], func=AF.Copy), [rk], ["halo"])
                        deferred.append(late)
                        if smp or lastp:
                            nr = 3 * nseq
                            EV(lambda e, rv=rv, nr=nr: e.tensor_copy(out=cst3[:, 0:nr].rearrange("p (s r) -> p s r", r=3), in_=rv[:, :, L:L + 3]), [rk], ["cst3"])
                            T(lambda e, nr=nr: e.transpose(out=pb[6][0:nr, 128:256], in_=cst3[:, 0:nr], identity=identf[:]), ["cst3", "identf"], ["pb6"])
                            V(lambda e, ft=ft, nr=nr: e.tensor_copy(out=cvo[0:nr, 128 * (ft % 8):128 * (ft % 8) + 128], in_=pb[6][0:nr, 128:256]), ["pb6"], ["cvo"])
                            if ft % 8 == 7:
                                cdst = (O["conv_s"].rearrange("s r c -> (s r) c") if smp else O["conv_p"])[:, 128 * (ft - 7):128 * (ft - 7) + 1024]
                                DMAO(lambda e, cdst=cdst, nr=nr: e.dma_start(out=cdst, in_=cvo[0:nr, :]), ["cvo"], [])
            V(lambda e: e.tensor_scalar(out=dAs[:], in0=dts[:], scalar1=aneg[:, 0:1], scalar2=None, op0=ALU.mult), ["dts", "aneg"], ["dAs"])
            V(lambda e: e.tensor_tensor_scan(out=acs[:], data0=rmk[:], data1=dAs[:], initial=0.0, op0=ALU.mult, op1=ALU.add), [kn(rmk), "dAs"], ["acs"])
            a3 = acs[:].rearrange("h (s l) -> h s l", l=L)
            V(lambda e: e.tensor_tensor(out=dec[:].rearrange("h (s l) -> h s l", l=L), in0=a3[:, :, L - 1:L].broadcast_to([32, nseq, L]), in1=a3, op=ALU.subtract), ["acs"], ["dec"])
            A(lambda e: e.activation(out=dec[:], in_=dec[:], func=AF.Exp), ["dec"], ["dec"])
            V(lambda e: e.tensor_tensor(out=wdt[:], in0=dts[:], in1=dec[:], op=ALU.mult), ["dts", "dec"], ["wdt"])
            for i_, src in enumerate((acs, dts, wdt)):
                T(lambda e, i_=i_, src=src: e.transpose(out=pb[6][:, 256 + 32 * i_:256 + 32 * i_ + 32], in_=src[:], identity=identf[0:32, 0:32]), [kn(src), "identf"], ["pb6"])
            V(lambda e: e.tensor_copy(out=tokm[:].rearrange("p a b -> p (a b)"), in_=pb[6][:, 256:352]), ["pb6"], ["tokm"])
            for half in range(2):
                bk = f"pb{2 + half}"
                for j in range(8):
                    ft = 8 * half + j
                    T(lambda e, ft=ft, j=j, half=half: e.transpose(out=pbb[2 + half][:, 128 * j:128 * j + 128], in_=xc[:, ft, :], identity=identb[:]), [f"xc{ft}", "identb"], [bk])
                pv = pbb[2 + half][:, :].rearrange("p (a h2 q) -> p a h2 q", h2=2, q=64)
                prs = slice(8 * half, 8 * half + 8)
                dtT = tokm[:, 1, 16 * half:16 * half + 16].rearrange("p (a h2) -> p a h2", h2=2)
                wT = tokm[:, 2, 16 * half:16 * half + 16]
                V(lambda e, pv=pv, prs=prs, dtT=dtT: e.tensor_tensor(out=XA[:, prs, 0:64], in0=pv[:, :, 0, :], in1=dtT[:, :, 0:1].broadcast_to([128, 8, 64]), op=ALU.mult), [bk, "tokm"], ["XA"])
                V(lambda e, pv=pv, prs=prs, dtT=dtT: e.tensor_tensor(out=XB[:, prs, 64:128], in0=pv[:, :, 1, :], in1=dtT[:, :, 1:2].broadcast_to([128, 8, 64]), op=ALU.mult), [bk, "tokm"], ["XB"])
                V(lambda e, half=half, wT=wT: e.tensor_tensor(out=Xd[:, 16 * half:16 * half + 16, :], in0=pbb[2 + half][:, :].rearrange("p (h q) -> p h q", q=64), in1=wT.unsqueeze(2).broadcast_to([128, 16, 64]), op=ALU.mult), [bk, "tokm"], ["Xd"])
            for j in range(8):
                T(lambda e, j=j: e.transpose(out=pbb[6][:, 128 * j:128 * j + 128], in_=xc[:, 16 + j, :], identity=identb[:]), [f"xc{16 + j}", "identb"], ["pb6"])
            A(lambda e: e.activation(out=Btok[:].rearrange("p a b -> p (a b)"), in_=pbb[6][:, :], func=AF.Copy), ["pb6"], ["Btok"])
            def yevac(gq):
                b2 = gq % 2
                yv = yv2[b2]; ysq = ysq2[b2]; rstg = rstg2[b2]
                for j in range(2):
                    pr = 2 * gq + j
                    V(lambda e, pr=pr, j=j, yv=yv: e.scalar_tensor_tensor(out=yv[:, j, :], in0=xc[:, pr, :], scalar=dexp[:, pr:pr + 1], in1=yps(pr), op0=ALU.mult, op1=ALU.add), [f"xc{pr}", "dexp", ypk(pr)], [f"yv{b2}"])
                    V(lambda e, pr=pr, j=j, yv=yv: e.tensor_tensor(out=yv[:, j, :], in0=yv[:, j, :], in1=zs[:, pr, :], op=ALU.mult), [f"yv{b2}", f"zs{pr}"], [f"yv{b2}"])
                    A(lambda e, j=j, yv=yv, ysq=ysq: e.activation(out=ysq[:, j, :], in_=yv[:, j, :], func=AF.Square), [f"yv{b2}"], [f"ysq{b2}"])
                ps = pb[6 + b2][:, 0:128]
                pk = f"pb{6 + b2}"
                for j in range(2):
                    T(lambda e, ps=ps, j=j, ysq=ysq: e.matmul(ps, lhsT=onesb[:], rhs=ysq[:, j, :], start=(j == 0), stop=(j == 1)), ["onesb", f"ysq{b2}"], [pk])
                A(lambda e, ps=ps, rstg=rstg: e.activation(out=rstg[:], in_=ps, func=AF.Sqrt, scale=1.0 / 256, bias=epscol[:]), [pk, "epscol"], [f"rstg{b2}"])
                V(lambda e, rstg=rstg: e.reciprocal(out=rstg[:], in_=rstg[:]), [f"rstg{b2}"], [f"rstg{b2}"])
                for j in range(2):
                    pr = 2 * gq + j
                    V(lambda e, pr=pr, j=j, yv=yv, rstg=rstg: e.scalar_tensor_tensor(out=ynT[:, pr, :], in0=yv[:, j, :], scalar=ssdnw[:, pr:pr + 1], in1=rstg[:], op0=ALU.mult, op1=ALU.mult), [f"yv{b2}", "ssdnw", f"rstg{b2}"], [f"ynT{pr}"])
            if smp:
                ypk = lambda pr: f"pb{2 + pr // 4}"
                yps = lambda pr: pb[2 + pr // 4][:, 128 * (pr % 4):128 * (pr % 4) + 128]
                yfirst = lambda pr: pr % 4 == 0
            else:
                ypk = lambda pr: f"pb{2 + (pr // 2) % 2}"
                yps = lambda pr: pb[2 + (pr // 2) % 2][:, 128 * (pr % 2):128 * (pr % 2) + 128]
                yfirst = lambda pr: pr % 2 == 0
            for g in range(8):
                V(lambda e, g=g: e.tensor_tensor(out=rhsR[:], in0=acs[:].unsqueeze(1).broadcast_to([32, 4, 128]), in1=identf[0:32, 4 * g:4 * g + 4].unsqueeze(2).broadcast_to([32, 4, 128]), op=ALU.mult), ["acs", "identf"], ["rhsR"])
                T(lambda e: e.matmul(pb[6][:, :], lhsT=ones32[:], rhs=rhsR[:].rearrange("p a b -> p (a b)"), start=True, stop=True), ["ones32", "rhsR"], ["pb6"])
                A(lambda e: e.activation(out=Eg[:].rearrange("p a b -> p (a b)"), in_=pb[6][:, :], func=AF.Exp), ["pb6"], ["Eg"])
                G(lambda e, g=g: e.tensor_tensor(out=CEall[:, 4 * g:4 * g + 4, :], in0=Eg[:], in1=xc[:, 24 + g, :].unsqueeze(1).broadcast_to([128, 4, 128]), op=ALU.mult), ["Eg", f"xc{24 + g}"], ["CEall"])
                for h4 in range(4):
                    V(lambda e, g=g, h4=h4: e.scalar_tensor_tensor(out=Dm[:, h4, :], in0=pb[6][:, 128 * h4:128 * h4 + 128], scalar=tokm[:, 0, 4 * g + h4:4 * g + h4 + 1], in1=negm[:], op0=ALU.subtract, op1=ALU.min), ["pb6", "tokm", kn(negm)], ["Dm"])
                A(lambda e: e.activation(out=Lg[:].rearrange("p a b -> p (a b)"), in_=Dm[:].rearrange("p a b -> p (a b)"), func=AF.Exp), ["Dm"], ["Lg"])
                T(lambda e, g=g: e.matmul(pb[7][:, 0:128], lhsT=xc[:, 16 + g, :], rhs=xc[:, 24 + g, :], start=True, stop=True), [f"xc{16 + g}", f"xc{24 + g}"], ["pb7"])
                V(lambda e: e.tensor_tensor(out=Mg[:], in0=Lg[:], in1=pb[7][:, 0:128].unsqueeze(1).broadcast_to([128, 4, 128]), op=ALU.mult), ["Lg", "pb7"], ["Mg"])
                V(lambda e, g=g: e.tensor_copy(out=dch[:, 4 * g:4 * g + 4, 0:nseq], in_=Eg[:].rearrange("p a (s l) -> p a s l", l=L)[:, :, :, L - 1]), ["Eg"], ["dch"])
                for j in range(2):
                    pr = 2 * g + j
                    T(lambda e, pr=pr, j=j: e.matmul(yps(pr), lhsT=XA[:, pr, :], rhs=Mg[:, 2 * j, :], start=yfirst(pr), stop=False, skip_group_check=True), ["XA", "Mg"], [ypk(pr)])
                    T(lambda e, pr=pr, j=j: e.matmul(yps(pr), lhsT=XB[:, pr, :], rhs=Mg[:, 2 * j + 1, :], start=False, stop=False, skip_group_check=True), ["XB", "Mg"], [ypk(pr)])
                if not smp:
                    sak_ = [f"SA{i}" for i in range(4)] + [f"SB{i}" for i in range(4)]
                    for j in range(2):
                        pr = 2 * g + j
                        T(lambda e, pr=pr: e.matmul(yps(pr), lhsT=SA[:, pr, :], rhs=CEall[:, 2 * pr, :], start=False, stop=False, skip_group_check=True), sak_ + ["CEall"], [ypk(pr)])
                        T(lambda e, pr=pr, j=j: e.matmul(yps(pr), lhsT=SB[:, pr, :], rhs=CEall[:, 2 * pr + 1, :], start=False, stop=(j == 1), skip_group_check=True), sak_ + ["CEall"], [ypk(pr)])
                    yevac(g)
            it = 0
            for s in range(nseq if smp else 0):
                cs = slice(L * s, L * s + L)
                last = (s == nseq - 1)
                if smp:
                    Bm = Bms[0]; bmk = "Bm0"
                    G(lambda e, s=s, Bm=Bm: e.tensor_scalar(out=Bm[:].rearrange("p a b -> p (a b)"), in0=Btok[:].rearrange("p a b -> p (a b)"), scalar1=seqmask[:, s:s + 1], scalar2=None, op0=ALU.mult), ["Btok", "seqmask"], [bmk])
                for q4 in range(4 if smp else 1):
                    prs_ = range(4 * q4, 4 * q4 + 4) if smp else range(16)
                    sak = [f"SA{q4}", f"SB{q4}"] if smp else [f"SA{i}" for i in range(4)] + [f"SB{i}" for i in range(4)]
                    if smp:
                        S0q = S0qs[it % 2]; s0k = f"S0q{it % 2}"; Snq = Snqs[it % 2]; snk = f"Snq{it % 2}"
                        tb_ = 7 if it % 2 == 0 else 0
                        nb_ = 6 if it % 2 == 0 else 1
                        it += 1
                        sview = I["ssm0"][s].rearrange("(pr h2) p n -> (h2 p) pr n", h2=2)[:, 4 * q4:4 * q4 + 4, :]
                        DMA(lambda e, sview=sview, S0q=S0q: e.dma_start(out=S0q[:], in_=sview), w=[s0k])
                        for j in range(4):
                            T(lambda e, j=j, S0q=S0q, tb_=tb_: e.transpose(out=pb[tb_][:, 128 * j:128 * j + 128], in_=S0q[:, j, :], identity=identf[:]), [s0k, "identf"], [f"pb{tb_}"])
                        pv = pb[tb_][:, :].rearrange("p (a h2 q) -> p a h2 q", h2=2, q=64)
                        A(lambda e, q4=q4, pv=pv: e.activation(out=SA[:, 4 * q4:4 * q4 + 4, 0:64], in_=pv[:, :, 0, :], func=AF.Copy), [f"pb{tb_}"], [f"SA{q4}"])
                        A(lambda e, q4=q4, pv=pv: e.activation(out=SB[:, 4 * q4:4 * q4 + 4, 64:128], in_=pv[:, :, 1, :], func=AF.Copy), [f"pb{tb_}"], [f"SB{q4}"])
                    for pr in prs_:
                        T(lambda e, pr=pr, cs=cs: e.matmul(yps(pr)[:, cs], lhsT=SA[:, pr, :], rhs=CEall[:, 2 * pr, cs], start=False, stop=False, skip_group_check=True), sak + ["CEall"], [ypk(pr)])
                        T(lambda e, pr=pr, cs=cs, last=last: e.matmul(yps(pr)[:, cs], lhsT=SB[:, pr, :], rhs=CEall[:, 2 * pr + 1, cs], start=False, stop=last, skip_group_check=True), sak + ["CEall"], [ypk(pr)])
                    if smp:
                        for j in range(4):
                            pr = 4 * q4 + j
                            T(lambda e, pr=pr, j=j, Bm=Bm, nb_=nb_: e.matmul(pb[nb_][:, 128 * j:128 * j + 128], lhsT=Xd[:, 2 * pr:2 * pr + 2, :].rearrange("p a b -> p (a b)"), rhs=Bm[:, pr // 2, :], start=True, stop=True), ["Xd", bmk], [f"pb{nb_}"])
                        for j in range(4):
                            pr = 4 * q4 + j
                            for h2 in range(2):
                                hs = slice(64 * h2, 64 * h2 + 64)
                                V(lambda e, pr=pr, j=j, hs=hs, s=s, h2=h2, S0q=S0q, Snq=Snq, nb_=nb_: e.scalar_tensor_tensor(out=Snq[hs, j, :], in0=S0q[hs, j, :], scalar=dch[hs, 2 * pr + h2, s:s + 1], in1=pb[nb_][hs, 128 * j:128 * j + 128], op0=ALU.mult, op1=ALU.add), [s0k, "dch", f"pb{nb_}"], [snk])
                        oview = O["ssm_s"][s].rearrange("(pr h2) p n -> (h2 p) pr n", h2=2)[:, 4 * q4:4 * q4 + 4, :]
                        DMAO(lambda e, oview=oview, Snq=Snq: e.dma_start(out=oview, in_=Snq[:]), [snk], [])
            if smp:
                for gq in range(8):
                    yevac(gq)
            if not smp:
                V(lambda e: e.tensor_tensor(out=STt[:], in0=STt[:], in1=dch[:, :, 0:1].broadcast_to([128, 32, 64]), op=ALU.mult), ["STt", "dch"], ["STt"])
                for rnd in range(2):
                    for g in range(4 * rnd, 4 * rnd + 4):
                        bk_ = 2 + (g % 4) // 2
                        T(lambda e, g=g, bk_=bk_: e.matmul(pb[bk_][:, 256 * (g % 2):256 * (g % 2) + 256], lhsT=Btok[:, g, :], rhs=Xd[:, 4 * g:4 * g + 4, :].rearrange("p a b -> p (a b)"), start=True, stop=True), ["Btok", "Xd"], [f"pb{bk_}"])
                    for b2_ in range(2):
                        h0 = 16 * rnd + 8 * b2_
                        V(lambda e, h0=h0, b2_=b2_: e.tensor_tensor(out=STt[:, h0:h0 + 8, :], in0=STt[:, h0:h0 + 8, :], in1=pb[2 + b2_][:, :].rearrange("p (h q) -> p h q", q=64), op=ALU.add), ["STt", f"pb{2 + b2_}"], ["STt"])
                sv = STt[:].rearrange("p (pr h2) q -> p pr h2 q", h2=2)
                A(lambda e, sv=sv: e.activation(out=SA[:, :, 0:64], in_=sv[:, :, 0, :], func=AF.Copy), ["STt"], [f"SA{i}" for i in range(4)])
                A(lambda e, sv=sv: e.activation(out=SB[:, :, 64:128], in_=sv[:, :, 1, :], func=AF.Copy), ["STt"], [f"SB{i}" for i in range(4)])
                if lastp:
                    for q4 in range(4):
                        for j in range(4):
                            pr = 4 * q4 + j
                            T(lambda e, pr=pr, j=j: e.transpose(out=pb[6][:, 128 * j:128 * j + 128], in_=STt[:, 2 * pr:2 * pr + 2, :].rearrange("p a b -> p (a b)"), identity=identf[:]), ["STt", "identf"], ["pb6"])
                        Snq = Snqs[q4 % 2]; snk = f"Snq{q4 % 2}"
                        V(lambda e, Snq=Snq: e.tensor_copy(out=Snq[:].rearrange("p a b -> p (a b)"), in_=pb[6][:, :]), ["pb6"], [snk])
                        oview = O["ssm_p"].rearrange("(pr h2) p n -> (h2 p) pr n", h2=2)[:, 4 * q4:4 * q4 + 4, :]
                        DMAO(lambda e, oview=oview, Snq=Snq: e.dma_start(out=oview, in_=Snq[:]), [snk], [])
            dense_T("w_down_ssd", 16, lambda k: ynT[:, k, :], lambda k: [f"ynT{k}"],
                    lambda m, ps, pk: V(lambda e: e.tensor_tensor(out=yaT[:, m, :], in0=ps, in1=gas[:, m, :], op=ALU.mult), [pk, f"ga{m}"], [f"yaT_{par}_{m}"]))


        def recB(ti, par):
            u5 = u5_[par]; z5s = z5s_[par]; gbs = gbs_[par]; yaT = yaT_[par]
            smp = (ti == 16)
            lastp = (ti == 15)
            xsrc = I["xs"] if smp else I["xp"][128 * ti:128 * ti + 128, :]
            ydst = O["y_s"] if smp else O["y_p"][128 * ti:128 * ti + 128, :]
            if smp:
                for ri, nm in enumerate(["re0", "im0"]):
                    sv_ = I[nm].rearrange("s (gh gq g2) p -> s gq gh (g2 p)", gq=8, g2=2)
                    for s in range(16):
                        DMA(lambda e, s=s, sv_=sv_: e.dma_start(out=s5io[8 * s:8 * s + 8, :, :, :].rearrange("p a b c -> p a (b c)"), in_=sv_[s]), w=["s5io"])
                    for gh in range(4):
                        T(lambda e, gh=gh: e.transpose(out=pb[5][:, 0:128], in_=s5io[:, gh, :, :].rearrange("p a b -> p (a b)"), identity=identf[:]), ["s5io", "identf"], ["pb5"])
                        V(lambda e, gh=gh, ri=ri: e.tensor_copy(out=s0all[:, 8 * gh:8 * gh + 8, ri, :], in_=pb[5][:, 0:128].rearrange("p (s g) -> p g s", g=8)), ["pb5"], ["s0all"])
            Ls, nsg = (8, 16) if smp else (64, 2)
            for ft in range(8):
                gsl = slice(4 * ft, 4 * ft + 4)
                ecv = ecos[:, gsl, 0:Ls].unsqueeze(2).broadcast_to([128, 4, nsg, Ls])
                esv = esin[:, gsl, 0:Ls].unsqueeze(2).broadcast_to([128, 4, nsg, Ls])
                v4 = lambda ap: ap.rearrange("p q (s l) -> p q s l", l=Ls)
                for q in range(4):
                    A(lambda e, ft=ft, q=q: e.activation(out=u5m[:, q, :], in_=u5[:, ft, :], func=AF.Copy, scale=pairmask[:, q:q + 1]), [f"u5_{par}_{ft}", "pairmask"], ["u5m"])
                for ri in range(2):
                    pbk = f"pb{4 + ri}"
                    for q in range(4):
                        T(lambda e, ft=ft, q=q, ri=ri: e.matmul(pb[4 + ri][:, 128 * q:128 * q + 128], lhsT=wb5[:, ft, ri, :], rhs=u5m[:, q, :], start=True, stop=True), ["wb5", "u5m"], [pbk])
                    A(lambda e, ri=ri: e.activation(out=bu[:, :, ri, :], in_=pb[4 + ri][:, :].rearrange("p (q t) -> p q t", t=128), func=AF.Copy), [pbk], ["bu"])
                bre_, bim_ = v4(bu[:, :, 0, :]), v4(bu[:, :, 1, :])
                V(lambda e, ecv=ecv: e.tensor_tensor(out=v4(bt1[:]), in0=ecv, in1=bre_, op=ALU.mult), ["ecos", "bu"], ["bt1"])
                V(lambda e, esv=esv: e.tensor_tensor(out=v4(bt2[:]), in0=esv, in1=bim_, op=ALU.mult), ["esin", "bu"], ["bt2"])
                V(lambda e: e.tensor_tensor(out=sgm[:, :, 0, :], in0=bt1[:], in1=bt2[:], op=ALU.add), ["bt1", "bt2"], ["sgm"])
                V(lambda e, ecv=ecv: e.tensor_tensor(out=v4(bt1[:]), in0=ecv, in1=bim_, op=ALU.mult), ["ecos", "bu", "sgm"], ["bt1"])
                V(lambda e, esv=esv: e.tensor_tensor(out=v4(bt2[:]), in0=esv, in1=bre_, op=ALU.mult), ["esin", "bu", "sgm"], ["bt2"])
                V(lambda e: e.tensor_tensor(out=sgm[:, :, 1, :], in0=bt1[:], in1=bt2[:], op=ALU.subtract), ["bt1", "bt2"], ["sgm"])
                if smp:
                    V(lambda e, gsl=gsl: e.tensor_tensor(out=rts[:], in0=rdec[:, gsl].unsqueeze(2).broadcast_to([128, 4, 128]), in1=rm128[:].unsqueeze(1).broadcast_to([128, 4, 128]), op=ALU.mult), ["rdec", "rm128"], ["rts"])
                else:
                    V(lambda e, gsl=gsl: e.tensor_copy(out=rts[:], in_=rdec[:, gsl].unsqueeze(2).broadcast_to([128, 4, 128])), ["rdec"], ["rts"])
                if smp:
                    n_ = 16
                    ab = lambda t: t[:, gsl].unsqueeze(2).broadcast_to([128, 4, n_])
                    cmul(cin_[:, :, 0, :], cin_[:, :, 1, :], ab(are), ab(aim), s0all[:, gsl, 0, :], s0all[:, gsl, 1, :],
                         ["are", "aim", "s0all"], ["cinr", "cini"], ctm[:, :, 0, :], ctm[:, :, 1, :], ["ctm0", "ctm1"])
                    for ri in range(2):
                        tgt = v4(sgm[:, :, ri, :])[:, :, :, 0]
                        V(lambda e, tgt=tgt, ri=ri: e.tensor_tensor(out=tgt, in0=tgt, in1=cin_[:, :, ri, :], op=ALU.add), ["sgm", "cinr", "cini"], ["sgm"])
                    for q in range(4):
                        for ri in range(2):
                            EV = V
                            EV(lambda e, q=q, ri=ri: e.tensor_tensor_scan(out=bu[:, q, ri, :], data0=rts[:, q, :], data1=sgm[:, q, ri, :], initial=0.0, op0=ALU.mult, op1=ALU.add), ["rts", "sgm"], ["bu"])
                else:
                    for sg in range(2):
                        c0 = 64 * sg
                        src_re = sgc[:, gsl, 0:1] if sg == 0 else bu[:, :, 0, 63:64]
                        src_im = sgc[:, gsl, 1:2] if sg == 0 else bu[:, :, 1, 63:64]
                        ab = lambda t: t[:, gsl].unsqueeze(2)
                        cmul(cin_[:, :, 0, 0:1], cin_[:, :, 1, 0:1], ab(aere), ab(aeim), src_re, src_im,
                             ["aere", "aeim", "sgc", "bu"], ["cinr", "cini"], ctm[:, :, 0, 0:1], ctm[:, :, 1, 0:1], ["ctm0", "ctm1"])
                        for ri in range(2):
                            tgt = sgm[:, :, ri, c0:c0 + 1]
                            V(lambda e, tgt=tgt, ri=ri: e.tensor_tensor(out=tgt, in0=tgt, in1=cin_[:, :, ri, 0:1], op=ALU.add), ["sgm", "cinr", "cini"], ["sgm"])
                        for q in range(4):
                            for ri in range(2):
                                EV = V
                                EV(lambda e, q=q, ri=ri, c0=c0: e.tensor_tensor_scan(out=bu[:, q, ri, c0:c0 + 64], data0=rts[:, q, c0:c0 + 64], data1=sgm[:, q, ri, c0:c0 + 64], initial=0.0, op0=ALU.mult, op1=ALU.add), ["rts", "sgm"], ["bu"])
                    V(lambda e, gsl=gsl: e.tensor_copy(out=sgc[:, gsl, :], in_=bu[:, :, :, 127]), ["bu"], ["sgc"])
                sre_, sim_ = v4(bu[:, :, 0, :]), v4(bu[:, :, 1, :])
                V(lambda e, ecv=ecv: e.tensor_tensor(out=v4(bt1[:]), in0=ecv, in1=sre_, op=ALU.mult), ["ecos", "bu"], ["bt1"])
                V(lambda e, esv=esv: e.tensor_tensor(out=v4(bt2[:]), in0=esv, in1=sim_, op=ALU.mult), ["esin", "bu"], ["bt2"])
                V(lambda e: e.tensor_tensor(out=s5s[:, :, 0, :], in0=bt1[:], in1=bt2[:], op=ALU.subtract), ["bt1", "bt2"], ["s5s"])
                if smp or lastp:
                    fsrc = lambda t: (v4(t[:])[:, :, :, Ls - 1] if smp else t[:, :, 127:128])
                    nf = 16 if smp else 1
                    V(lambda e, gsl=gsl, fsrc=fsrc, nf=nf: e.tensor_tensor(out=fin[:, gsl, 0, 0:nf], in0=fsrc(bt1), in1=fsrc(bt2), op=ALU.subtract), ["bt1", "bt2"], ["fin"])
                V(lambda e, ecv=ecv: e.tensor_tensor(out=v4(bt1[:]), in0=ecv, in1=sim_, op=ALU.mult), ["ecos", "bu", "s5s", "fin"], ["bt1"])
                V(lambda e, esv=esv: e.tensor_tensor(out=v4(bt2[:]), in0=esv, in1=sre_, op=ALU.mult), ["esin", "bu", "s5s", "fin"], ["bt2"])
                V(lambda e: e.tensor_tensor(out=s5s[:, :, 1, :], in0=bt1[:], in1=bt2[:], op=ALU.add), ["bt1", "bt2"], ["s5s"])
                if smp or lastp:
                    V(lambda e, gsl=gsl, fsrc=fsrc, nf=nf: e.tensor_tensor(out=fin[:, gsl, 1, 0:nf], in0=fsrc(bt1), in1=fsrc(bt2), op=ALU.add), ["bt1", "bt2"], ["fin"])
                ps4 = [pb[4][:, 128 * q:128 * q + 128] for q in range(4)]
                for q in range(4):
                    for ri in range(2):
                        T(lambda e, ft=ft, q=q, ri=ri: e.matmul(ps4[q], lhsT=wd5[:, ft, ri, :], rhs=s5s[:, q, ri, :], start=(ri == 0), stop=(ri == 1)), ["wd5", "s5s"], ["pb4"])
                for q in range(4):
                    rs_ = slice(32 * q, 32 * q + 32)
                    V(lambda e, ft=ft, q=q, rs_=rs_: e.scalar_tensor_tensor(out=y5t[rs_, :], in0=u5[rs_, ft, :], scalar=ds5[rs_, ft:ft + 1], in1=ps4[q][rs_, :], op0=ALU.mult, op1=ALU.add), [f"u5_{par}_{ft}", "ds5", "pb4"], ["y5t"])
                V(lambda e: e.tensor_tensor(out=y5u[:], in0=y5t[:], in1=y5t[:], op=ALU.mult), ["y5t"], ["y5u"])
                V(lambda e: e.tensor_scalar(out=y5u[:], in0=y5u[:], scalar1=0.044715, scalar2=1.0, op0=ALU.mult, op1=ALU.add), ["y5u"], ["y5u"])
                V(lambda e: e.tensor_tensor(out=y5u[:], in0=y5u[:], in1=y5t[:], op=ALU.mult), ["y5u", "y5t"], ["y5u"])
                A(lambda e: e.activation(out=y5v[:], in_=y5u[:], func=AF.Sigmoid, scale=1.5957691216057308), ["y5u"], ["y5v"])
                V(lambda e, ft=ft: e.tensor_tensor(out=y5g[:, ft, :], in0=y5v[:], in1=y5t[:], op=ALU.mult), ["y5v", "y5t"], [f"y5g{ft}"])
            if lastp:
                for ri, nm in enumerate(["re_p", "im_p"]):
                    DMAO(lambda e, ri=ri, nm=nm: e.dma_start(out=O[nm].rearrange("(gp g2) p -> (g2 p) gp", g2=2), in_=fin[:, :, ri, 0], allow_slow_non_contiguous=True), ["fin"], [])
            if smp:
                for ri, nm in enumerate(["re_s", "im_s"]):
                    for gh in range(4):
                        V(lambda e, gh=gh, ri=ri: e.tensor_copy(out=s5tr[:], in_=fin[:, 8 * gh:8 * gh + 8, ri, :].rearrange("p g s -> p s g")), ["fin"], ["s5tr"])
                        T(lambda e: e.transpose(out=pb[5][:, 0:128], in_=s5tr[:].rearrange("p a b -> p (a b)"), identity=identf[:]), ["s5tr", "identf"], ["pb5"])
                        V(lambda e, gh=gh: e.tensor_copy(out=s5io[:, gh, :, :].rearrange("p a b -> p (a b)"), in_=pb[5][:, 0:128]), ["pb5"], ["s5io"])
                    ov = O[nm].rearrange("s (gh gq g2) p -> s gq gh (g2 p)", gq=8, g2=2)
                    for s in range(16):
                        DMAO(lambda e, s=s, ov=ov: e.dma_start(out=ov[s], in_=s5io[8 * s:8 * s + 8, :, :, :].rearrange("p a b c -> p a (b c)")), ["s5io"], [])

            def glu_evac(m, ps, pk):
                A(lambda e: e.activation(out=glus[:], in_=ps, func=AF.Sigmoid, bias=bglu[:, m:m + 1]), [pk, "bglu"], ["glus"])
                V(lambda e: e.tensor_tensor(out=glus[:], in0=glus[:], in1=y5g[:, m, :], op=ALU.mult), ["glus", f"y5g{m}"], ["glus"])
                V(lambda e: e.tensor_tensor(out=y5f[:, m, :], in0=glus[:], in1=z5s[:, m, :], op=ALU.mult), ["glus", f"z5_{par}_{m}"], [f"y5f{m}"])
            dense_T("w_glu", 8, lambda k: y5g[:, k, :], lambda k: [f"y5g{k}"], glu_evac)
            dense_T("w_down_s5", 8, lambda k: y5f[:, k, :], lambda k: [f"y5f{k}"],
                    lambda m, ps, pk: V(lambda e: e.tensor_tensor(out=ybT[:, m, :], in0=ps, in1=gbs[:, m, :], op=ALU.mult), [pk, f"gb_{par}_{m}"], [f"ybT{m}"]))
            for m in range(8):
                V(lambda e, m=m: e.tensor_tensor(out=mixT[:, m, :], in0=yaT[:, m, :], in1=ybT[:, m, :], op=ALU.add), [f"yaT_{par}_{m}", f"ybT{m}"], [f"mixT{m}"])
            DMA(lambda e: e.dma_start(out=ot[:], in_=xsrc), w=["ot"])
            for cb in range(4):
                wt, wk = wload(WS["w_out"][:, 256 * cb:256 * cb + 256])
                hb = cb % 2
                ps = pb[4 + hb][:, 0:256]
                pks = [f"pb{4 + hb}"]
                for k in range(8):
                    T(lambda e, wt=wt, k=k, ps=ps: e.matmul(ps, lhsT=mixT[:, k, :], rhs=wt[:, k, 0:256], start=(k == 0), stop=(k == 7)), [wk, f"mixT{k}"], pks)
                V(lambda e, cb=cb, ps=ps: e.tensor_tensor(out=ot[:, 256 * cb:256 * cb + 256], in0=ps, in1=ot[:, 256 * cb:256 * cb + 256], op=ALU.add), pks + ["ot"], ["ot"])
            G(lambda e: e.memset(ssB[:], 0.0), w=["ssB"])
            A(lambda e: e.activation(out=y5f[:].rearrange("p a b -> p (a b)"), in_=ot[:], func=AF.Square, accum_out=ssB[:]), ["ot", "ssB"], [f"y5f{m}" for m in range(8)] + ["ssB"])
            A(lambda e: e.activation(out=rstdB[:], in_=ssB[:], func=AF.Sqrt, scale=1.0 / D, bias=epscol[:]), ["ssB", "epscol"], ["rstdB"])
            V(lambda e: e.reciprocal(out=rstdB[:], in_=rstdB[:]), ["rstdB"], ["rstdB"])
            V(lambda e: e.scalar_tensor_tensor(out=ot[:], in0=ot[:], scalar=rstdB[:, 0:1], in1=fnw[:], op0=ALU.mult, op1=ALU.mult), ["ot", "rstdB", "fnw"], ["ot"])
            DMAO(lambda e: e.dma_start(out=ydst, in_=ot[:]), ["ot"], [])


        def flush(lst):
            for (eng, fn, r, w, dma) in lst:
                P.op(eng, fn, r, w, dma=dma)

        def record(fn_, ti, par, banks, wset):
            CUR[0] = []
            MMB[0] = banks
            WSET[0] = wset
            fn_(ti, par)
            lst = CUR[0]
            CUR[0] = None
            return lst

        def merge(la, lb):
            out = []
            ia = ib = 0
            na, nb = len(la), len(lb)
            while ia < na or ib < nb:
                if ib >= nb or (ia < na and ia * nb <= ib * na):
                    out.append(la[ia]); ia += 1
                else:
                    out.append(lb[ib]); ib += 1
            return out

        tl = list(tiles) if tiles is not None else list(range(17))
        if tl:
            flush(record(recA, tl[0], 0, (0, 1), (0, 1, 2)))
            for n_, ti in enumerate(tl):
                lb = record(recB, ti, n_ % 2, (4, 5), (3,))
                if n_ + 1 < len(tl):
                    la = record(recA, tl[n_ + 1], (n_ + 1) % 2, (0, 1), (0, 1, 2))
                    if tl[n_ + 1] == 16 or not INTERLEAVE[0]:
                        flush(lb); flush(la)
                    else:
                        flush(merge(la, lb))
                else:
                    flush(lb)
        P.emit(nc)
    return nc


_NC = {}


def _shard(inputs, c):
    d = {}
    d["xp"] = np.ascontiguousarray(inputs["x_prompt"][c])
    d["xs"] = np.ascontiguousarray(inputs["x_sample"][16 * c:16 * c + 16].reshape(128, 1024))
    d["conv0"] = np.ascontiguousarray(inputs["state_conv"][0, 16 * c:16 * c + 16])
    d["ssm0"] = np.ascontiguousarray(inputs["state_ssm"][0, 16 * c:16 * c + 16])
    d["re0"] = np.ascontiguousarray(inputs["state_s5_re"][0, 16 * c:16 * c + 16])
    d["im0"] = np.ascontiguousarray(inputs["state_s5_im"][0, 16 * c:16 * c + 16])
    for k in ("norm_w", "w_in", "conv_w", "conv_b", "dt_bias", "A_log", "D_ssd", "ssd_norm_w", "w_down_ssd", "lam_re",
              "lam_im", "log_dt", "B_re", "B_im", "C_re", "C_im", "D_s5", "w_glu", "b_glu", "w_down_s5", "w_out"):
        d[k] = np.ascontiguousarray(inputs[k][0])
    d["final_norm_w"] = np.ascontiguousarray(inputs["final_norm_w"])
    return d


def kernel(**inputs):
    inputs = {k: np.asarray(v, dtype=np.float32) for k, v in inputs.items()}
    if "nc" not in _NC:
        _NC["nc"] = build()
    nc = _NC["nc"]
    consts = host_consts()
    in_maps = []
    for c in range(NCORES):
        d = _shard(inputs, c)
        d.update(consts)
        in_maps.append(d)
    res = run_bass_kernel_spmd(nc, in_maps, core_ids=list(range(NCORES))).results
    cat = lambda n: np.concatenate([r[n] for r in res], axis=0)
    y_p = np.stack([r["y_p"] for r in res], 0)
    y_s = cat("y_s").reshape(128, 8, 1024)
    conv_p = np.stack([r["conv_p"] for r in res], 0)[None]
    ssm_p = np.stack([r["ssm_p"] for r in res], 0)[None]
    re_p = np.stack([r["re_p"] for r in res], 0)[None]
    im_p = np.stack([r["im_p"] for r in res], 0)[None]
    conv_s = cat("conv_s")[None]
    ssm_s = cat("ssm_s")[None]
    re_s = cat("re_s")[None]
    im_s = cat("im_s")[None]
    return tuple(np.ascontiguousarray(a, dtype=np.float32) for a in
                 (y_p, y_s, conv_p, ssm_p, re_p, im_p, conv_s, ssm_s, re_s, im_s))
```

```python
import contextlib
import math
import numpy as np
import concourse.bass as bass
import concourse.mybir as mybir
from concourse.bass_utils import run_bass_kernel_spmd

F32 = mybir.dt.float32
BF16 = mybir.dt.bfloat16
I32 = mybir.dt.int32
ALU = mybir.AluOpType
AF = mybir.ActivationFunctionType

ENGS = ("pe", "act", "dve", "pool", "sp")
N_DMA_SEM = 24
NCORES = 8
D = 1024
PROJ = 10272
Z0, XBC0, DT0, U0, Z50, GA0, GB0 = 0, 2048, 6144, 6176, 7200, 8224, 9248
EPS = 1e-6
SIN_SCALE = 6.28318


class Prog:
    def __init__(self):
        self.ops = []
        self.last_write = {}
        self.readers = {}
        self.dma_count = {e: 0 for e in ENGS}
        self.slot_last = {}
        self.last_op = {}
        self.pending = {}

    def barrier(self):
        deps = set(self.last_op.values()) | set(self.slot_last.values())
        for e in ENGS:
            self.pending[e] = set(deps) | self.pending.get(e, set())

    def op(self, eng, fn, reads=(), writes=(), dma=False):
        writes = list(writes) + [k for k in reads if k.startswith("pb")]
        reads = [k for k in reads if not k.startswith("pb")]
        oid = len(self.ops)
        deps = set()
        for k in reads:
            if k in self.last_write:
                deps.add(self.last_write[k])
        for k in writes:
            if k in self.last_write:
                deps.add(self.last_write[k])
            deps.update(self.readers.get(k, ()))
        if eng in self.pending:
            deps |= self.pending.pop(eng)
        rec = dict(eng=eng, fn=fn, deps=deps, dma=dma, sig=False, seq=None)
        if dma:
            i = self.dma_count[eng]
            self.dma_count[eng] += 1
            slot = i % N_DMA_SEM
            rec["slot"] = slot
            rec["val"] = 16 * (i // N_DMA_SEM + 1)
            prev = self.slot_last.get((eng, slot))
            if prev is not None:
                deps.add(prev)
            self.slot_last[(eng, slot)] = oid
        else:
            self.last_op[eng] = oid
        deps.discard(oid)
        self.ops.append(rec)
        for k in writes:
            self.last_write[k] = oid
            self.readers[k] = []
        for k in reads:
            self.readers.setdefault(k, []).append(oid)
        return oid

    def emit(self, nc):
        ops = self.ops

        def skip(dd, o):
            return (not dd["dma"]) and dd["eng"] == o["eng"] == "pe" and not o["dma"]
        for o in ops:
            for d in o["deps"]:
                dd = ops[d]
                if dd["dma"] or skip(dd, o):
                    continue
                dd["sig"] = True
        cnt = {e: 0 for e in ENGS}
        for o in ops:
            if o["sig"]:
                cnt[o["eng"]] += 1
                o["seq"] = cnt[o["eng"]]
        with contextlib.ExitStack() as st:
            csem = {e: st.enter_context(nc.semaphore("c_" + e)) for e in ENGS}
            dsem = {(e, s): st.enter_context(nc.semaphore(f"d_{e}{s}"))
                    for e in ENGS if self.dma_count[e] > 0
                    for s in range(min(N_DMA_SEM, self.dma_count[e]))}
            block = st.enter_context(nc.Block())

            def run(eng_name, eng):
                waited = {}
                for o in ops:
                    if o["eng"] != eng_name:
                        continue
                    for d in sorted(o["deps"]):
                        dd = ops[d]
                        if dd["dma"]:
                            key = ("d", dd["eng"], dd["slot"])
                            sem = dsem[(dd["eng"], dd["slot"])]
                            val = dd["val"]
                        else:
                            if skip(dd, o):
                                continue
                            key = ("c", dd["eng"])
                            sem = csem[dd["eng"]]
                            val = dd["seq"]
                        if waited.get(key, 0) >= val:
                            continue
                        waited[key] = val
                        eng.wait_ge(sem, val)
                    ins = o["fn"](eng)
                    if o["dma"]:
                        ins.then_inc(dsem[(eng_name, o["slot"])], 16)
                    elif o["sig"]:
                        ins.then_inc(csem[eng_name], 1)
                if eng_name == "sp":
                    for (e, s), oid in self.slot_last.items():
                        eng.wait_ge(dsem[(e, s)], ops[oid]["val"])

            block.tensor(lambda e: run("pe", e))
            block.scalar(lambda e: run("act", e))
            block.vector(lambda e: run("dve", e))
            block.gpsimd(lambda e: run("pool", e))
            block.sync(lambda e: run("sp", e))


def host_consts():
    c = {}
    c["ident"] = np.eye(128, dtype=np.float32)
    s = np.arange(128)
    c["negm_p"] = np.where(s[None, :] >= s[:, None], 0.0, -30000.0).astype(np.float32)
    same = (s[None, :] // 8) == (s[:, None] // 8)
    c["negm_s"] = np.where((s[None, :] >= s[:, None]) & same, 0.0, -30000.0).astype(np.float32)
    rp = np.ones((32, 128), np.float32); rp[:, 0] = 0
    rs = np.ones((32, 128), np.float32); rs[:, ::8] = 0
    c["rm_p"] = rp
    c["rm_s"] = rs
    r128 = np.ones((128, 128), np.float32); r128[:, ::8] = 0
    c["rm128"] = r128
    c["seqmask"] = (s[:, None] // 8 == np.arange(16)[None, :]).astype(np.float32)
    c["pairmask"] = (s[:, None] // 32 == np.arange(4)[None, :]).astype(np.float32)
    c["iota"] = np.broadcast_to(np.arange(128, dtype=np.float32)[None, :], (128, 128)).copy()
    return c


IN_SPECS = [
    ("xp", (2048, 1024)), ("xs", (128, 1024)), ("conv0", (16, 3, 4096)), ("ssm0", (16, 32, 64, 128)),
    ("re0", (16, 64, 64)), ("im0", (16, 64, 64)), ("norm_w", (1024,)), ("w_in", (1024, PROJ)),
    ("conv_w", (4, 4096)), ("conv_b", (4096,)), ("dt_bias", (32,)), ("A_log", (32,)), ("D_ssd", (32,)),
    ("ssd_norm_w", (2048,)), ("w_down_ssd", (2048, 1024)), ("lam_re", (64, 64)), ("lam_im", (64, 64)),
    ("log_dt", (64,)), ("B_re", (64, 64, 16)), ("B_im", (64, 64, 16)), ("C_re", (64, 16, 64)),
    ("C_im", (64, 16, 64)), ("D_s5", (1024,)), ("w_glu", (1024, 1024)), ("b_glu", (1024,)),
    ("w_down_s5", (1024, 1024)), ("w_out", (1024, 1024)), ("final_norm_w", (1024,)),
    ("ident", (128, 128)), ("negm_p", (128, 128)), ("negm_s", (128, 128)), ("rm_p", (32, 128)),
    ("rm_s", (32, 128)), ("rm128", (128, 128)), ("seqmask", (128, 16)), ("pairmask", (128, 4)),
    ("iota", (128, 128)),
]
OUT_SPECS = [
    ("y_p", (2048, 1024)), ("y_s", (128, 1024)), ("conv_p", (3, 4096)), ("ssm_p", (32, 64, 128)),
    ("re_p", (64, 64)), ("im_p", (64, 64)), ("conv_s", (16, 3, 4096)), ("ssm_s", (16, 32, 64, 128)),
    ("re_s", (16, 64, 64)), ("im_s", (16, 64, 64)),
]


STOP = [None]
INTERLEAVE = [True]
BLK = [None]


class _Stop(Exception):
    pass


def ck(n):
    if STOP[0] is not None and n >= STOP[0]:
        raise _Stop()


def build(tiles=None):
    nc = bass.Bass("TRN2", target_bir_lowering=False)
    I = {n: nc.dram_tensor(n, list(s), F32, kind="ExternalInput").ap() for n, s in IN_SPECS}
    O = {n: nc.dram_tensor(n, list(s), F32, kind="ExternalOutput").ap() for n, s in OUT_SPECS}
    WS = {
        "w_in": nc.dram_tensor("ws_w_in", [1024, PROJ], BF16).ap(),
        "w_down_ssd": nc.dram_tensor("ws_wds", [2048, 1024], BF16).ap(),
        "w_glu": nc.dram_tensor("ws_wglu", [1024, 1024], BF16).ap(),
        "w_down_s5": nc.dram_tensor("ws_wd5", [1024, 1024], BF16).ap(),
        "w_out": nc.dram_tensor("ws_wout", [1024, 1024], BF16).ap(),
    }
    P = Prog()

    CUR = [None]
    MMB = [(0, 1)]

    def _rec(eng, fn, r, w, dma=False):
        if CUR[0] is None:
            P.op(eng, fn, r, w, dma=dma)
        else:
            CUR[0].append((eng, fn, tuple(r), tuple(w), dma))

    def V(fn, r=(), w=()):
        _rec("dve", fn, r, w)

    def G(fn, r=(), w=()):
        _rec("pool", fn, r, w)

    def A(fn, r=(), w=()):
        _rec("act", fn, r, w)

    def T(fn, r=(), w=()):
        _rec("pe", fn, r, w)

    def DMA(fn, r=(), w=()):
        _rec("sp", fn, r, w, dma=True)

    def DMAO(fn, r=(), w=()):
        _rec("pool", fn, r, w, dma=True)

    def DMAH(fn, r=(), w=()):
        _rec("act", fn, r, w, dma=True)

    st = contextlib.ExitStack()
    with st:
        def sb(name, shape, dt=F32, stack=None):
            return (stack or st).enter_context(nc.sbuf_tensor("s_" + name, list(shape), dt))

        def kn(t):
            n_ = getattr(t, "name")
            return n_[2:] if n_.startswith("s_") else n_
        pb = [st.enter_context(nc.psum_tensor(f"pb{i}", [128, 512], F32)) for i in range(8)]
        pbb = [p.bitcast(BF16) for p in pb]

        identf = sb("identf", [128, 128]); identb = sb("identb", [128, 128], BF16)
        negm_p = sb("negm_p", [128, 128]); negm_s = sb("negm_s", [128, 128])
        rm_p = sb("rm_p", [32, 128]); rm_s = sb("rm_s", [32, 128]); rm128 = sb("rm128", [128, 128])
        seqmask = sb("seqmask", [128, 16]); pairmask = sb("pairmask", [128, 4])
        ones32 = sb("ones32", [32, 128]); onesb = sb("onesb", [128, 128], BF16)
        onecol = sb("onecol", [128, 1]); epscol = sb("epscol", [128, 1])
        for nm, t in [("ident", identf), ("negm_p", negm_p), ("negm_s", negm_s), ("rm_p", rm_p), ("rm_s", rm_s),
                      ("rm128", rm128), ("seqmask", seqmask), ("pairmask", pairmask)]:
            DMA(lambda e, nm=nm, t=t: e.dma_start(out=t[:], in_=I[nm]), w=[kn(t)])
        V(lambda e: e.tensor_copy(out=identb[:], in_=identf[:]), ["identf"], ["identb"])
        G(lambda e: e.memset(ones32[:], 1.0), w=["ones32"])
        G(lambda e: e.memset(onesb[:], 1.0), w=["onesb"])
        G(lambda e: e.memset(onecol[:], 1.0), w=["onecol"])
        G(lambda e: e.memset(epscol[:], EPS), w=["epscol"])

        def colvec(name, src, nt):
            t = sb(name, [128, nt])
            DMA(lambda e: e.dma_start(out=t[:], in_=src.rearrange("(j p) -> p j", p=128), allow_slow_non_contiguous=True), w=[name])
            return t
        normw = colvec("normw", I["norm_w"], 8)
        convb = colvec("convb", I["conv_b"], 32)
        ssdnw = colvec("ssdnw", I["ssd_norm_w"], 16)
        ds5 = colvec("ds5", I["D_s5"], 8)
        bglu = colvec("bglu", I["b_glu"], 8)
        convw = sb("convw", [128, 32, 4])
        for k in range(4):
            DMA(lambda e, k=k: e.dma_start(out=convw[:, :, k], in_=I["conv_w"][k].rearrange("(j p) -> p j", p=128), allow_slow_non_contiguous=True), w=["convw"])
        fnw = sb("fnw", [128, 1024])
        DMA(lambda e: e.dma_start(out=fnw[:], in_=I["final_norm_w"].partition_broadcast(128)), w=["fnw"])
        dtb = sb("dtb", [32, 1]); aneg = sb("aneg", [32, 1])
        DMA(lambda e: e.dma_start(out=dtb[:], in_=I["dt_bias"].rearrange("(h o) -> h o", o=1)), w=["dtb"])
        DMA(lambda e: e.dma_start(out=aneg[:], in_=I["A_log"].rearrange("(h o) -> h o", o=1)), w=["aneg"])
        A(lambda e: e.activation(out=aneg[:], in_=aneg[:], func=AF.Exp), ["aneg"], ["aneg"])
        V(lambda e: e.tensor_scalar(out=aneg[:], in0=aneg[:], scalar1=-1.0, scalar2=None, op0=ALU.mult), ["aneg"], ["aneg"])
        dexp = sb("dexp", [128, 16])
        dv = I["D_ssd"].rearrange("(pr h2) -> h2 pr", h2=2)
        for h2 in range(2):
            DMA(lambda e, h2=h2: e.dma_start(out=dexp[64 * h2:64 * h2 + 64, :], in_=dv[h2].partition_broadcast(64), allow_slow_non_contiguous=True), w=["dexp"])
        are = sb("are", [128, 32]); aim = sb("aim", [128, 32]); rdec = sb("rdec", [128, 32])
        aere = sb("aere", [128, 32]); aeim = sb("aeim", [128, 32])
        ecos = sb("ecos", [128, 32, 64]); esin = sb("esin", [128, 32, 64])
        wb5 = sb("wb5", [128, 8, 2, 128], BF16); wd5 = sb("wd5", [128, 8, 2, 128], BF16)
        halo = sb("halo", [128, 32, 3])
        STt = sb("STt", [128, 32, 64])
        SA = sb("SA", [128, 16, 128], BF16); SB = sb("SB", [128, 16, 128], BF16)
        XA = sb("XA", [128, 16, 128], BF16); XB = sb("XB", [128, 16, 128], BF16)
        sgc = sb("sgc", [128, 32, 2])
        for t_ in (halo, STt, SA, SB, XA, XB, sgc, wd5):
            G(lambda e, t_=t_: e.memset(t_[:], 0.0), w=([kn(t_)] if kn(t_) not in ("SA", "SB") else [f"{kn(t_)}{i}" for i in range(4)]))

        pst = contextlib.ExitStack()
        with pst:
            def psb(name, shape, dt=F32):
                return sb(name, shape, dt, stack=pst)
            iota = psb("iota", [128, 128])
            DMA(lambda e: e.dma_start(out=iota[:], in_=I["iota"]), w=["iota"])
            lre = psb("lre", [128, 32]); lim = psb("lim", [128, 32]); stp = psb("stp", [128, 32]); th = psb("th", [128, 32])
            DMA(lambda e: e.dma_start(out=lre[:], in_=I["lam_re"].rearrange("(gp g2) p -> (g2 p) gp", g2=2), allow_slow_non_contiguous=True), w=["lre"])
            DMA(lambda e: e.dma_start(out=lim[:], in_=I["lam_im"].rearrange("(gp g2) p -> (g2 p) gp", g2=2), allow_slow_non_contiguous=True), w=["lim"])
            ldv = I["log_dt"].rearrange("(gp g2) -> g2 gp", g2=2)
            for g2 in range(2):
                DMA(lambda e, g2=g2: e.dma_start(out=stp[64 * g2:64 * g2 + 64, :], in_=ldv[g2].partition_broadcast(64), allow_slow_non_contiguous=True), w=["stp"])
            A(lambda e: e.activation(out=stp[:], in_=stp[:], func=AF.Exp), ["stp"], ["stp"])
            V(lambda e: e.tensor_tensor(out=rdec[:], in0=lre[:], in1=stp[:], op=ALU.mult), ["lre", "stp"], ["rdec"])
            A(lambda e: e.activation(out=rdec[:], in_=rdec[:], func=AF.Exp), ["rdec"], ["rdec"])
            V(lambda e: e.tensor_tensor(out=th[:], in0=lim[:], in1=stp[:], op=ALU.mult), ["lim", "stp"], ["th"])
            V(lambda e: e.tensor_scalar(out=th[:], in0=th[:], scalar1=1.0 / (2.0 * math.pi), scalar2=None, op0=ALU.mult), ["th"], ["th"])
            tmpx = psb("tmpx", [128, 2048]); tmpf = psb("tmpf", [128, 2048]); tmpi = psb("tmpi", [128, 2048], I32)

            def sin_turns(out_ap, mk_x, n, off, rkeys, wkeys):
                V(lambda e: mk_x(e, tmpx[:, 0:n]), rkeys, ["tmpx"])
                if off:
                    V(lambda e: e.tensor_scalar(out=tmpx[:, 0:n], in0=tmpx[:, 0:n], scalar1=float(off), scalar2=None, op0=ALU.add), ["tmpx"], ["tmpx"])
                V(lambda e: e.tensor_copy(out=tmpi[:, 0:n], in_=tmpx[:, 0:n]), ["tmpx"], ["tmpi"])
                V(lambda e: e.tensor_copy(out=tmpf[:, 0:n], in_=tmpi[:, 0:n]), ["tmpi"], ["tmpf"])
                V(lambda e: e.tensor_tensor(out=tmpx[:, 0:n], in0=tmpx[:, 0:n], in1=tmpf[:, 0:n], op=ALU.subtract), ["tmpx", "tmpf"], ["tmpx"])
                A(lambda e: e.activation(out=out_ap, in_=tmpx[:, 0:n], func=AF.Sin, scale=SIN_SCALE), ["tmpx"], wkeys)
            cs1 = psb("cs1", [128, 32]); sn1 = psb("sn1", [128, 32])
            x1 = lambda e, dst: e.tensor_copy(out=dst, in_=th[:])
            sin_turns(sn1[:], x1, 32, 0.0, ["th"], ["sn1"])
            sin_turns(cs1[:], x1, 32, 0.25, ["th"], ["cs1"])
            V(lambda e: e.tensor_tensor(out=are[:], in0=rdec[:], in1=cs1[:], op=ALU.mult), ["rdec", "cs1"], ["are"])
            V(lambda e: e.tensor_tensor(out=aim[:], in0=rdec[:], in1=sn1[:], op=ALU.mult), ["rdec", "sn1"], ["aim"])
            x64 = lambda e, dst: e.tensor_scalar(out=dst, in0=th[:], scalar1=64.0, scalar2=None, op0=ALU.mult)
            sin_turns(sn1[:], x64, 32, 0.0, ["th"], ["sn1"])
            sin_turns(cs1[:], x64, 32, 0.25, ["th"], ["cs1"])
            V(lambda e: e.tensor_tensor(out=aere[:], in0=rdec[:], in1=cs1[:], op=ALU.mult), ["rdec", "cs1"], ["aere"])
            V(lambda e: e.tensor_tensor(out=aeim[:], in0=rdec[:], in1=sn1[:], op=ALU.mult), ["rdec", "sn1"], ["aeim"])
            xt_ = lambda e, dst: e.tensor_tensor(out=dst.rearrange("p (a b) -> p a b", b=64), in0=th[:].unsqueeze(2).broadcast_to([128, 32, 64]),
                                                 in1=iota[:, 0:64].unsqueeze(1).broadcast_to([128, 32, 64]), op=ALU.mult)
            sin_turns(esin[:].rearrange("p a b -> p (a b)"), xt_, 2048, 0.0, ["th", "iota"], ["esin"])
            sin_turns(ecos[:].rearrange("p a b -> p (a b)"), xt_, 2048, 0.25, ["th", "iota"], ["ecos"])
            gre = psb("gre", [128, 32]); gim = psb("gim", [128, 32]); den = psb("den", [128, 32])
            t1 = psb("t1s", [128, 32]); t2 = psb("t2s", [128, 32]); am1 = psb("am1", [128, 32])
            V(lambda e: e.tensor_scalar(out=am1[:], in0=are[:], scalar1=-1.0, scalar2=None, op0=ALU.add), ["are"], ["am1"])
            V(lambda e: e.tensor_tensor(out=den[:], in0=lre[:], in1=lre[:], op=ALU.mult), ["lre"], ["den"])
            V(lambda e: e.tensor_tensor(out=t1[:], in0=lim[:], in1=lim[:], op=ALU.mult), ["lim"], ["t1s"])
            V(lambda e: e.tensor_tensor(out=den[:], in0=den[:], in1=t1[:], op=ALU.add), ["den", "t1s"], ["den"])
            V(lambda e: e.reciprocal(out=den[:], in_=den[:]), ["den"], ["den"])
            V(lambda e: e.tensor_tensor(out=t1[:], in0=am1[:], in1=lre[:], op=ALU.mult), ["am1", "lre"], ["t1s"])
            V(lambda e: e.tensor_tensor(out=t2[:], in0=aim[:], in1=lim[:], op=ALU.mult), ["aim", "lim"], ["t2s"])
            V(lambda e: e.tensor_tensor(out=gre[:], in0=t1[:], in1=t2[:], op=ALU.add), ["t1s", "t2s"], ["gre"])
            V(lambda e: e.tensor_tensor(out=gre[:], in0=gre[:], in1=den[:], op=ALU.mult), ["gre", "den"], ["gre"])
            V(lambda e: e.tensor_tensor(out=t1[:], in0=aim[:], in1=lre[:], op=ALU.mult), ["aim", "lre"], ["t1s"])
            V(lambda e: e.tensor_tensor(out=t2[:], in0=am1[:], in1=lim[:], op=ALU.mult), ["am1", "lim"], ["t2s"])
            V(lambda e: e.tensor_tensor(out=gim[:], in0=t1[:], in1=t2[:], op=ALU.subtract), ["t1s", "t2s"], ["gim"])
            V(lambda e: e.tensor_tensor(out=gim[:], in0=gim[:], in1=den[:], op=ALU.mult), ["gim", "den"], ["gim"])
            bre = psb("bre", [128, 32, 16]); bim = psb("bim", [128, 32, 16]); bbr = psb("bbr", [128, 32, 16]); bbi = psb("bbi", [128, 32, 16]); tb = psb("tbs", [128, 32, 16])
            DMA(lambda e: e.dma_start(out=bre[:], in_=I["B_re"].rearrange("(gp g2) p k -> (g2 p) gp k", g2=2)), w=["bre"])
            DMA(lambda e: e.dma_start(out=bim[:], in_=I["B_im"].rearrange("(gp g2) p k -> (g2 p) gp k", g2=2)), w=["bim"])
            gb_ = lambda t: t[:].unsqueeze(2).broadcast_to([128, 32, 16])
            V(lambda e: e.tensor_tensor(out=bbr[:], in0=bre[:], in1=gb_(gre), op=ALU.mult), ["bre", "gre"], ["bbr"])
            V(lambda e: e.tensor_tensor(out=tb[:], in0=bim[:], in1=gb_(gim), op=ALU.mult), ["bim", "gim"], ["tbs"])
            V(lambda e: e.tensor_tensor(out=bbr[:], in0=bbr[:], in1=tb[:], op=ALU.subtract), ["bbr", "tbs"], ["bbr"])
            V(lambda e: e.tensor_tensor(out=bbi[:], in0=bim[:], in1=gb_(gre), op=ALU.mult), ["bim", "gre"], ["bbi"])
            V(lambda e: e.tensor_tensor(out=tb[:], in0=bre[:], in1=gb_(gim), op=ALU.mult), ["bre", "gim"], ["tbs"])
            V(lambda e: e.tensor_tensor(out=bbi[:], in0=bbi[:], in1=tb[:], op=ALU.add), ["bbi", "tbs"], ["bbi"])
            cin = psb("cin", [128, 4, 2, 64])
            cst = [psb("cstr", [128, 32, 16]), psb("csti", [128, 32, 16])]
            for ri, nm in enumerate(["C_re", "C_im"]):
                cv = I[nm].rearrange("(gh gq g2) k p -> gq k gh g2 p", gq=8, g2=2)
                for gq in range(8):
                    for gh in range(4):
                        DMA(lambda e, gq=gq, gh=gh, cv=cv: e.dma_start(out=cin[16 * gq:16 * gq + 16, gh, :, :], in_=cv[gq][:, gh, :, :]), w=["cin"])
                for gh in range(4):
                    T(lambda e, gh=gh: e.transpose(out=pb[7][:, 0:128], in_=cin[:, gh, :, :].rearrange("p a b -> p (a b)"), identity=identf[:]), ["cin", "identf"], ["pb7"])
                    V(lambda e, gh=gh, ri=ri: e.tensor_copy(out=cst[ri][:, 8 * gh:8 * gh + 8, :], in_=pb[7][:, 0:128].rearrange("p (a b) -> p a b", b=16)), ["pb7"], [kn(cst[ri])])
            zst = psb("zst", [128, 4, 2, 16])
            G(lambda e: e.memset(zst[:], 0.0), w=["zst"])
            for ft in range(8):
                for ri in range(2):
                    src = [bbr, bbi][ri]
                    for h2 in range(2):
                        V(lambda e, ft=ft, h2=h2, src=src: e.tensor_copy(out=zst[64 * h2:64 * h2 + 64, :, h2, :], in_=src[64 * h2:64 * h2 + 64, 4 * ft:4 * ft + 4, :]), [kn(src)], ["zst"])
                    T(lambda e: e.transpose(out=pb[6][:, 0:128], in_=zst[:].rearrange("p a b c -> p (a b c)"), identity=identf[:]), ["zst", "identf"], ["pb6"])
                    V(lambda e, ft=ft, ri=ri: e.tensor_copy(out=wb5[:, ft, ri, :], in_=pb[6][:, 0:128]), ["pb6"], ["wb5"])
                    for h2 in range(2):
                        V(lambda e, ft=ft, ri=ri, h2=h2: e.tensor_scalar(
                            out=wd5[64 * h2:64 * h2 + 64, ft, ri, :].rearrange("p (q g k) -> p q g k", g=2, k=16)[:, :, h2, :],
                            in0=cst[ri][64 * h2:64 * h2 + 64, 4 * ft:4 * ft + 4, :], scalar1=(1.0 if ri == 0 else -1.0), scalar2=None, op0=ALU.mult),
                          [kn(cst[ri])], ["wd5"])
            NST = 6
            stg = [psb(f"stg{i}", [128, 2048]) for i in range(NST)]
            stgb = [psb(f"stgb{i}", [128, 2048], BF16) for i in range(NST)]
            cnt = [0]

            def cast_rows(src, dst, ncols):
                for c0 in range(0, ncols, 2048):
                    w_ = min(2048, ncols - c0)
                    i = cnt[0] % NST
                    cnt[0] += 1
                    DMA(lambda e, i=i, c0=c0, w_=w_: e.dma_start(out=stg[i][:, 0:w_], in_=src[:, c0:c0 + w_]), w=[f"stg{i}"])
                    sel = (1, 2, 2, 2, 1, 0)[cnt[0] % 6]
                    if sel == 0:
                        G(lambda e, i=i, w_=w_: e.tensor_copy(out=stgb[i][:, 0:w_], in_=stg[i][:, 0:w_]), [f"stg{i}"], [f"stgb{i}"])
                    elif sel == 1:
                        A(lambda e, i=i, w_=w_: e.activation(out=stgb[i][:, 0:w_], in_=stg[i][:, 0:w_], func=AF.Copy), [f"stg{i}"], [f"stgb{i}"])
                    else:
                        V(lambda e, i=i, w_=w_: e.tensor_copy(out=stgb[i][:, 0:w_], in_=stg[i][:, 0:w_]), [f"stg{i}"], [f"stgb{i}"])
                    DMAH(lambda e, i=i, c0=c0, w_=w_: e.dma_start(out=dst[:, c0:c0 + w_], in_=stgb[i][:, 0:w_]), [f"stgb{i}"], ["ws"])
            for nm, rows in [("w_in", 1024), ("w_down_ssd", 2048), ("w_glu", 1024), ("w_down_s5", 1024), ("w_out", 1024)]:
                nco = PROJ if nm == "w_in" else 1024
                for r0 in range(0, rows, 128):
                    cast_rows(I[nm][r0:r0 + 128, :], WS[nm][r0:r0 + 128, :], nco)

        P.barrier()

        xt = sb("xt", [128, 1024]); ot = sb("ot", [128, 1024]); ss = sb("ss", [128, 1]); rstd = sb("rstd", [128, 1])
        ssB = sb("ssB", [128, 1]); rstdB = sb("rstdB", [128, 1])
        xsn = sb("xsn", [128, 1024], BF16); hT = sb("hT", [128, 8, 128], BF16)
        wbuf = [sb(f"wbuf{i}", [128, 8, 256], BF16) for i in range(4)]
        zs = sb("zs", [128, 16, 128], BF16)
        xc = sb("xc", [128, 32, 128], BF16)
        raw = [sb(f"raw{i}", [128, 176]) for i in range(2)]
        acc = [sb(f"acc{i}", [128, 128]) for i in range(2)]
        dts = sb("dts", [32, 128]); dAs = sb("dAs", [32, 128]); acs = sb("acs", [32, 128]); dec = sb("dec", [32, 128]); wdt = sb("wdt", [32, 128])
        tokm = sb("tokm", [128, 3, 32])
        u5_ = [sb(f"u5_{i}", [128, 8, 128], BF16) for i in range(2)]; z5s_ = [sb(f"z5s_{i}", [128, 8, 128], BF16) for i in range(2)]
        gas = sb("gas", [128, 8, 128], BF16); gbs_ = [sb(f"gbs_{i}", [128, 8, 128], BF16) for i in range(2)]
        Xd = sb("Xd", [128, 32, 64], BF16); Btok = sb("Btok", [128, 8, 128], BF16)
        rhsR = sb("rhsR", [32, 4, 128]); Eg = sb("Eg", [128, 4, 128]); Dm = sb("Dm", [128, 4, 128])
        Lg = sb("Lg", [128, 4, 128], BF16); Mg = sb("Mg", [128, 4, 128], BF16)
        CEall = sb("CEall", [128, 32, 128], BF16)
        dch = sb("dch", [128, 32, 16])
        S0qs = [sb(f"S0q{i}", [128, 4, 128]) for i in range(2)]; Snqs = [sb(f"Snq{i}", [128, 4, 128]) for i in range(2)]
        Bms = [sb("Bm0", [128, 8, 128], BF16)] * 2
        yv2 = [sb(f"yv2_{i}", [128, 2, 128]) for i in range(2)]; ysq2 = [sb(f"ysq2_{i}", [128, 2, 128], BF16) for i in range(2)]
        rstg2 = [sb(f"rstg2_{i}", [128, 128]) for i in range(2)]; ynT = sb("ynT", [128, 16, 128], BF16)
        yaT_ = [sb(f"yaT_{i}", [128, 8, 128], BF16) for i in range(2)]; ybT = sb("ybT", [128, 8, 128], BF16); mixT = sb("mixT", [128, 8, 128], BF16)
        u5m = sb("u5m", [128, 4, 128], BF16)
        bu = sb("bu", [128, 4, 2, 128]); sgm = sb("sgm", [128, 4, 2, 128]); rts = sb("rts", [128, 4, 128])
        bt1 = sb("bt1", [128, 4, 128]); bt2 = sb("bt2", [128, 4, 128])
        s5s = sb("s5s", [128, 4, 2, 128], BF16)
        cin_ = sb("cinn", [128, 4, 2, 16])
        ctm = sb("ctm", [128, 4, 2, 16])
        fin = sb("fin", [128, 32, 2, 16]); s0all = sb("s0all", [128, 32, 2, 16])
        y5t = sb("y5t", [128, 128]); y5u = sb("y5u", [128, 128]); y5v = sb("y5v", [128, 128])
        y5g = sb("y5g", [128, 8, 128], BF16); y5f = sb("y5f", [128, 8, 128], BF16); glus = sb("glus", [128, 128])
        s5io = sb("s5io", [128, 4, 2, 64]); s5tr = sb("s5tr", [128, 16, 8])
        cvo = sb("cvo", [48, 1024]); cst3 = sb("cst3", [128, 48])

        wcnt = [0]

        WSET = [(0, 1, 2)]

        def wload(src_ap):
            i = WSET[0][wcnt[0] % len(WSET[0])]
            wcnt[0] += 1
            kt = src_ap.shape[0] // 128
            nco = src_ap.shape[1]
            DMA(lambda e: e.dma_start(out=wbuf[i][:, 0:kt, 0:nco], in_=src_ap.rearrange("(j p) f -> p j f", p=128)), ["ws"], [f"wbuf{i}"])
            return wbuf[i], f"wbuf{i}"

        pslot = [0]

        def mm_slot():
            s = MMB[0][pslot[0] % 2]
            pslot[0] += 1
            return pb[s][:, 0:128], f"pb{s}"

        def proj_ft(wt, wk, c0, nrows=128):
            ps, pk = mm_slot()
            for j in range(8):
                T(lambda e, j=j: e.matmul(ps[0:nrows, :], lhsT=wt[:, j, c0:c0 + nrows], rhs=hT[:, j, :], start=(j == 0), stop=(j == 7)), [wk, "hT"], [pk])
            return ps, pk

        def dense_T(wname, kt, rhs_fn, rkeys_fn, evac):
            for cb in range(4):
                halves = [wload(WS[wname][128 * k0:128 * (k0 + 8), 256 * cb:256 * cb + 256]) for k0 in range(0, kt, 8)]
                for s in range(2):
                    m = 2 * cb + s
                    ps, pk = mm_slot()
                    for k in range(kt):
                        wt, wk = halves[k // 8]
                        T(lambda e, wt=wt, k=k, s=s, ps=ps: e.matmul(ps, lhsT=wt[:, k % 8, 128 * s:128 * s + 128], rhs=rhs_fn(k), start=(k == 0), stop=(k == kt - 1)), [wk] + rkeys_fn(k), [pk])
                    evac(m, ps, pk)

        def cmul(out_re, out_im, ar, ai, br, bi, keys_r, keys_w, t1_, t2_, tk):
            V(lambda e: e.tensor_tensor(out=t1_, in0=ar, in1=br, op=ALU.mult), keys_r, [tk[0]])
            V(lambda e: e.tensor_tensor(out=t2_, in0=ai, in1=bi, op=ALU.mult), keys_r, [tk[1]])
            V(lambda e: e.tensor_tensor(out=out_re, in0=t1_, in1=t2_, op=ALU.subtract), tk, keys_w[0:1])
            V(lambda e: e.tensor_tensor(out=t1_, in0=ar, in1=bi, op=ALU.mult), keys_r + keys_w[0:1], [tk[0]])
            V(lambda e: e.tensor_tensor(out=t2_, in0=ai, in1=br, op=ALU.mult), keys_r + keys_w[0:1], [tk[1]])
            V(lambda e: e.tensor_tensor(out=out_im, in0=t1_, in1=t2_, op=ALU.add), tk, keys_w[1:2])

        def recA(ti, par):
            u5 = u5_[par]; z5s = z5s_[par]; gbs = gbs_[par]; yaT = yaT_[par]
            smp = (ti == 16)
            lastp = (ti == 15)
            xsrc = I["xs"] if smp else I["xp"][128 * ti:128 * ti + 128, :]
            ydst = O["y_s"] if smp else O["y_p"][128 * ti:128 * ti + 128, :]
            negm = negm_s if smp else negm_p
            rmk = rm_s if smp else rm_p
            nseq, L = (16, 8) if smp else (1, 128)
            DMA(lambda e: e.dma_start(out=xt[:], in_=xsrc), w=["xt"])
            G(lambda e: e.memset(ss[:], 0.0), w=["ss"])
            A(lambda e: e.activation(out=xsn[:], in_=xt[:], func=AF.Square, accum_out=ss[:]), ["xt", "ss"], ["xsn", "ss"])
            A(lambda e: e.activation(out=rstd[:], in_=ss[:], func=AF.Sqrt, scale=1.0 / D, bias=epscol[:]), ["ss", "epscol"], ["rstd"])
            V(lambda e: e.reciprocal(out=rstd[:], in_=rstd[:]), ["rstd"], ["rstd"])
            V(lambda e: e.tensor_scalar(out=xsn[:], in0=xt[:], scalar1=rstd[:, 0:1], scalar2=None, op0=ALU.mult), ["xt", "rstd"], ["xsn"])
            for j in range(8):
                T(lambda e, j=j: e.transpose(out=pbb[5][:, 128 * j:128 * j + 128], in_=xsn[:, 128 * j:128 * j + 128], identity=identb[:]), ["xsn", "identb"], ["pb5"])
            for j in range(8):
                V(lambda e, j=j: e.tensor_scalar(out=hT[:, j, :], in0=pbb[5][:, 128 * j:128 * j + 128], scalar1=normw[:, j:j + 1], scalar2=None, op0=ALU.mult), ["pb5", "normw"], ["hT"])
            blocks = [("xbc", c0, 256) for c0 in range(XBC0, DT0, 256)] + [("dt", DT0, 32)]
            blocks += [("z", c0, 256) for c0 in range(Z0, XBC0, 256)]
            for nm, b0 in (("u5", U0), ("z5", Z50), ("ga", GA0), ("gb", GB0)):
                blocks += [(nm, c0, 256) for c0 in range(b0, b0 + 1024, 256)]
            base = {"xbc": XBC0, "z": Z0, "u5": U0, "z5": Z50, "ga": GA0, "gb": GB0}
            deferred = []
            for nm, c0, nco in blocks:
                wt, wk = wload(WS["w_in"][:, c0:c0 + nco])
                if nm == "dt":
                    while deferred:
                        deferred.pop(0)()
                    ps, pk = proj_ft(wt, wk, 0, 32)
                    A(lambda e, ps=ps: e.activation(out=dts[:], in_=ps[0:32, :], func=AF.Exp, bias=dtb[:]), [pk, "dtb"], ["dts"])
                    A(lambda e: e.activation(out=dts[:], in_=dts[:], func=AF.Ln, bias=onecol[0:32, :]), ["dts", "onecol"], ["dts"])
                    continue
                for s in range(2):
                    ft = (c0 - base[nm]) // 128 + s
                    ps, pk = proj_ft(wt, wk, 128 * s)
                    if nm == "z":
                        A(lambda e, ps=ps, ft=ft: e.activation(out=zs[:, ft, :], in_=ps, func=AF.Silu), [pk], [f"zs{ft}"])
                    elif nm == "u5":
                        A(lambda e, ps=ps, ft=ft: e.activation(out=u5[:, ft, :], in_=ps, func=AF.Copy), [pk], [f"u5_{par}_{ft}"])
                    elif nm == "z5":
                        A(lambda e, ps=ps, ft=ft: e.activation(out=z5s[:, ft, :], in_=ps, func=AF.Silu), [pk], [f"z5_{par}_{ft}"])
                    elif nm == "ga":
                        A(lambda e, ps=ps, ft=ft: e.activation(out=gas[:, ft, :], in_=ps, func=AF.Sigmoid), [pk], [f"ga{ft}"])
                    elif nm == "gb":
                        A(lambda e, ps=ps, ft=ft: e.activation(out=gbs[:, ft, :], in_=ps, func=AF.Sigmoid), [pk], [f"gb_{par}_{ft}"])
                    else:
                        r_ = raw[ft % 2]; rk = f"raw{ft % 2}"; ac = acc[ft % 2]; ak = f"acc{ft % 2}"
                        EV = V
                        if smp:
                            rv = r_[:, 0:176].rearrange("p (s r) -> p s r", r=11)
                            if ft % 8 == 0:
                                DMA(lambda e, ft=ft: e.dma_start(out=cvo[:], in_=I["conv0"].rearrange("s r c -> (s r) c")[:, 128 * ft:128 * ft + 1024]), w=["cvo"])
                            T(lambda e, ft=ft: e.transpose(out=pb[6][:, 0:48], in_=cvo[:, 128 * (ft % 8):128 * (ft % 8) + 128], identity=identf[0:48, 0:48]), ["cvo", "identf"], ["pb6"])
                            V(lambda e, rv=rv: e.tensor_copy(out=rv[:, :, 0:3], in_=pb[6][:, 0:48].rearrange("p (s r) -> p s r", r=3)), ["pb6"], [rk])
                        else:
                            rv = r_[:, 0:131].unsqueeze(1)
                            A(lambda e, rv=rv, ft=ft: e.activation(out=rv[:, 0, 0:3], in_=halo[:, ft, :], func=AF.Copy), ["halo"], [rk])
                        A(lambda e, rv=rv, ps=ps: e.activation(out=rv[:, :, 3:3 + L], in_=ps.rearrange("p (s l) -> p s l", l=L), func=AF.Copy), [pk], [rk])
                        while deferred:
                            deferred.pop(0)()
                        av = ac[:].rearrange("p (s l) -> p s l", l=L)
                        EV(lambda e, rv=rv, av=av, ft=ft: e.tensor_scalar(out=av, in0=rv[:, :, 0:L], scalar1=convw[:, ft, 0:1], scalar2=convb[:, ft:ft + 1], op0=ALU.mult, op1=ALU.add), [rk, "convw", "convb"], [ak])
                        for k in range(1, 4):
                            EV(lambda e, rv=rv, av=av, ft=ft, k=k: e.scalar_tensor_tensor(out=av, in0=rv[:, :, k:k + L], scalar=convw[:, ft, k:k + 1], in1=av, op0=ALU.mult, op1=ALU.add), [rk, ak, "convw"], [ak])
                        def late(ac=ac, ft=ft, rv=rv, ak=ak, rk=rk):
                            A(lambda e: e.activation(out=xc[:, ft, :], in_=ac[:], func=AF.Silu), [ak], [f"xc{ft}"])
                            if not smp:
                                A(lambda e: e.activation(out=halo[:, ft, :], in_=rv[:, 0, 128:131], func=AF.Copy), [rk], ["halo"])
                        deferred.append(late)
                        if smp or lastp:
                            nr = 3 * nseq
                            EV(lambda e, rv=rv, nr=nr: e.tensor_copy(out=cst3[:, 0:nr].rearrange("p (s r) -> p s r", r=3), in_=rv[:, :, L:L + 3]), [rk], ["cst3"])
                            T(lambda e, nr=nr: e.transpose(out=pb[6][0:nr, 128:256], in_=cst3[:, 0:nr], identity=identf[:]), ["cst3", "identf"], ["pb6"])
                            V(lambda e, ft=ft, nr=nr: e.tensor_copy(out=cvo[0:nr, 128 * (ft % 8):128 * (ft % 8) + 128], in_=pb[6][0:nr, 128:256]), ["pb6"], ["cvo"])
                            if ft % 8 == 7:
                                cdst = (O["conv_s"].rearrange("s r c -> (s r) c") if smp else O["conv_p"])[:, 128 * (ft - 7):128 * (ft - 7) + 1024]
                                DMAO(lambda e, cdst=cdst, nr=nr: e.dma_start(out=cdst, in_=cvo[0:nr, :]), ["cvo"], [])
            V(lambda e: e.tensor_scalar(out=dAs[:], in0=dts[:], scalar1=aneg[:, 0:1], scalar2=None, op0=ALU.mult), ["dts", "aneg"], ["dAs"])
            V(lambda e: e.tensor_tensor_scan(out=acs[:], data0=rmk[:], data1=dAs[:], initial=0.0, op0=ALU.mult, op1=ALU.add), [kn(rmk), "dAs"], ["acs"])
            a3 = acs[:].rearrange("h (s l) -> h s l", l=L)
            V(lambda e: e.tensor_tensor(out=dec[:].rearrange("h (s l) -> h s l", l=L), in0=a3[:, :, L - 1:L].broadcast_to([32, nseq, L]), in1=a3, op=ALU.subtract), ["acs"], ["dec"])
            A(lambda e: e.activation(out=dec[:], in_=dec[:], func=AF.Exp), ["dec"], ["dec"])
            V(lambda e: e.tensor_tensor(out=wdt[:], in0=dts[:], in1=dec[:], op=ALU.mult), ["dts", "dec"], ["wdt"])
            for i_, src in enumerate((acs, dts, wdt)):
                T(lambda e, i_=i_, src=src: e.transpose(out=pb[6][:, 256 + 32 * i_:256 + 32 * i_ + 32], in_=src[:], identity=identf[0:32, 0:32]), [kn(src), "identf"], ["pb6"])
            V(lambda e: e.tensor_copy(out=tokm[:].rearrange("p a b -> p (a b)"), in_=pb[6][:, 256:352]), ["pb6"], ["tokm"])
            for half in range(2):
                bk = f"pb{2 + half}"
                for j in range(8):
                    ft = 8 * half + j
                    T(lambda e, ft=ft, j=j, half=half: e.transpose(out=pbb[2 + half][:, 128 * j:128 * j + 128], in_=xc[:, ft, :], identity=identb[:]), [f"xc{ft}", "identb"], [bk])
                pv = pbb[2 + half][:, :].rearrange("p (a h2 q) -> p a h2 q", h2=2, q=64)
                prs = slice(8 * half, 8 * half + 8)
                dtT = tokm[:, 1, 16 * half:16 * half + 16].rearrange("p (a h2) -> p a h2", h2=2)
                wT = tokm[:, 2, 16 * half:16 * half + 16]
                V(lambda e, pv=pv, prs=prs, dtT=dtT: e.tensor_tensor(out=XA[:, prs, 0:64], in0=pv[:, :, 0, :], in1=dtT[:, :, 0:1].broadcast_to([128, 8, 64]), op=ALU.mult), [bk, "tokm"], ["XA"])
                V(lambda e, pv=pv, prs=prs, dtT=dtT: e.tensor_tensor(out=XB[:, prs, 64:128], in0=pv[:, :, 1, :], in1=dtT[:, :, 1:2].broadcast_to([128, 8, 64]), op=ALU.mult), [bk, "tokm"], ["XB"])
                V(lambda e, half=half, wT=wT: e.tensor_tensor(out=Xd[:, 16 * half:16 * half + 16, :], in0=pbb[2 + half][:, :].rearrange("p (h q) -> p h q", q=64), in1=wT.unsqueeze(2).broadcast_to([128, 16, 64]), op=ALU.mult), [bk, "tokm"], ["Xd"])
            for j in range(8):
                T(lambda e, j=j: e.transpose(out=pbb[6][:, 128 * j:128 * j + 128], in_=xc[:, 16 + j, :], identity=identb[:]), [f"xc{16 + j}", "identb"], ["pb6"])
            A(lambda e: e.activation(out=Btok[:].rearrange("p a b -> p (a b)"), in_=pbb[6][:, :], func=AF.Copy), ["pb6"], ["Btok"])
            def yevac(gq):
                b2 = gq % 2
                yv = yv2[b2]; ysq = ysq2[b2]; rstg = rstg2[b2]
                for j in range(2):
                    pr = 2 * gq + j
                    V(lambda e, pr=pr, j=j, yv=yv: e.scalar_tensor_tensor(out=yv[:, j, :], in0=xc[:, pr, :], scalar=dexp[:, pr:pr + 1], in1=yps(pr), op0=ALU.mult, op1=ALU.add), [f"xc{pr}", "dexp", ypk(pr)], [f"yv{b2}"])
                    V(lambda e, pr=pr, j=j, yv=yv: e.tensor_tensor(out=yv[:, j, :], in0=yv[:, j, :], in1=zs[:, pr, :], op=ALU.mult), [f"yv{b2}", f"zs{pr}"], [f"yv{b2}"])
                    A(lambda e, j=j, yv=yv, ysq=ysq: e.activation(out=ysq[:, j, :], in_=yv[:, j, :], func=AF.Square), [f"yv{b2}"], [f"ysq{b2}"])
                ps = pb[6 + b2][:, 0:128]
                pk = f"pb{6 + b2}"
                for j in range(2):
                    T(lambda e, ps=ps, j=j, ysq=ysq: e.matmul(ps, lhsT=onesb[:], rhs=ysq[:, j, :], start=(j == 0), stop=(j == 1)), ["onesb", f"ysq{b2}"], [pk])
                A(lambda e, ps=ps, rstg=rstg: e.activation(out=rstg[:], in_=ps, func=AF.Sqrt, scale=1.0 / 256, bias=epscol[:]), [pk, "epscol"], [f"rstg{b2}"])
                V(lambda e, rstg=rstg: e.reciprocal(out=rstg[:], in_=rstg[:]), [f"rstg{b2}"], [f"rstg{b2}"])
                for j in range(2):
                    pr = 2 * gq + j
                    V(lambda e, pr=pr, j=j, yv=yv, rstg=rstg: e.scalar_tensor_tensor(out=ynT[:, pr, :], in0=yv[:, j, :], scalar=ssdnw[:, pr:pr + 1], in1=rstg[:], op0=ALU.mult, op1=ALU.mult), [f"yv{b2}", "ssdnw", f"rstg{b2}"], [f"ynT{pr}"])
            if smp:
                ypk = lambda pr: f"pb{2 + pr // 4}"
                yps = lambda pr: pb[2 + pr // 4][:, 128 * (pr % 4):128 * (pr % 4) + 128]
                yfirst = lambda pr: pr % 4 == 0
            else:
                ypk = lambda pr: f"pb{2 + (pr // 2) % 2}"
                yps = lambda pr: pb[2 + (pr // 2) % 2][:, 128 * (pr % 2):128 * (pr % 2) + 128]
                yfirst = lambda pr: pr % 2 == 0
            for g in range(8):
                V(lambda e, g=g: e.tensor_tensor(out=rhsR[:], in0=acs[:].unsqueeze(1).broadcast_to([32, 4, 128]), in1=identf[0:32, 4 * g:4 * g + 4].unsqueeze(2).broadcast_to([32, 4, 128]), op=ALU.mult), ["acs", "identf"], ["rhsR"])
                T(lambda e: e.matmul(pb[6][:, :], lhsT=ones32[:], rhs=rhsR[:].rearrange("p a b -> p (a b)"), start=True, stop=True), ["ones32", "rhsR"], ["pb6"])
                A(lambda e: e.activation(out=Eg[:].rearrange("p a b -> p (a b)"), in_=pb[6][:, :], func=AF.Exp), ["pb6"], ["Eg"])
                G(lambda e, g=g: e.tensor_tensor(out=CEall[:, 4 * g:4 * g + 4, :], in0=Eg[:], in1=xc[:, 24 + g, :].unsqueeze(1).broadcast_to([128, 4, 128]), op=ALU.mult), ["Eg", f"xc{24 + g}"], ["CEall"])
                for h4 in range(4):
                    V(lambda e, g=g, h4=h4: e.scalar_tensor_tensor(out=Dm[:, h4, :], in0=pb[6][:, 128 * h4:128 * h4 + 128], scalar=tokm[:, 0, 4 * g + h4:4 * g + h4 + 1], in1=negm[:], op0=ALU.subtract, op1=ALU.min), ["pb6", "tokm", kn(negm)], ["Dm"])
                A(lambda e: e.activation(out=Lg[:].rearrange("p a b -> p (a b)"), in_=Dm[:].rearrange("p a b -> p (a b)"), func=AF.Exp), ["Dm"], ["Lg"])
                T(lambda e, g=g: e.matmul(pb[7][:, 0:128], lhsT=xc[:, 16 + g, :], rhs=xc[:, 24 + g, :], start=True, stop=True), [f"xc{16 + g}", f"xc{24 + g}"], ["pb7"])
                V(lambda e: e.tensor_tensor(out=Mg[:], in0=Lg[:], in1=pb[7][:, 0:128].unsqueeze(1).broadcast_to([128, 4, 128]), op=ALU.mult), ["Lg", "pb7"], ["Mg"])
                V(lambda e, g=g: e.tensor_copy(out=dch[:, 4 * g:4 * g + 4, 0:nseq], in_=Eg[:].rearrange("p a (s l) -> p a s l", l=L)[:, :, :, L - 1]), ["Eg"], ["dch"])
                for j in range(2):
                    pr = 2 * g + j
                    T(lambda e, pr=pr, j=j: e.matmul(yps(pr), lhsT=XA[:, pr, :], rhs=Mg[:, 2 * j, :], start=yfirst(pr), stop=False, skip_group_check=True), ["XA", "Mg"], [ypk(pr)])
                    T(lambda e, pr=pr, j=j: e.matmul(yps(pr), lhsT=XB[:, pr, :], rhs=Mg[:, 2 * j + 1, :], start=False, stop=False, skip_group_check=True), ["XB", "Mg"], [ypk(pr)])
                if not smp:
                    sak_ = [f"SA{i}" for i in range(4)] + [f"SB{i}" for i in range(4)]
                    for j in range(2):
                        pr = 2 * g + j
                        T(lambda e, pr=pr: e.matmul(yps(pr), lhsT=SA[:, pr, :], rhs=CEall[:, 2 * pr, :], start=False, stop=False, skip_group_check=True), sak_ + ["CEall"], [ypk(pr)])
                        T(lambda e, pr=pr, j=j: e.matmul(yps(pr), lhsT=SB[:, pr, :], rhs=CEall[:, 2 * pr + 1, :], start=False, stop=(j == 1), skip_group_check=True), sak_ + ["CEall"], [ypk(pr)])
                    yevac(g)
            it = 0
            for s in range(nseq if smp else 0):
                cs = slice(L * s, L * s + L)
                last = (s == nseq - 1)
                if smp:
                    Bm = Bms[0]; bmk = "Bm0"
                    G(lambda e, s=s, Bm=Bm: e.tensor_scalar(out=Bm[:].rearrange("p a b -> p (a b)"), in0=Btok[:].rearrange("p a b -> p (a b)"), scalar1=seqmask[:, s:s + 1], scalar2=None, op0=ALU.mult), ["Btok", "seqmask"], [bmk])
                for q4 in range(4 if smp else 1):
                    prs_ = range(4 * q4, 4 * q4 + 4) if smp else range(16)
                    sak = [f"SA{q4}", f"SB{q4}"] if smp else [f"SA{i}" for i in range(4)] + [f"SB{i}" for i in range(4)]
                    if smp:
                        S0q = S0qs[it % 2]; s0k = f"S0q{it % 2}"; Snq = Snqs[it % 2]; snk = f"Snq{it % 2}"
                        tb_ = 7 if it % 2 == 0 else 0
                        nb_ = 6 if it % 2 == 0 else 1
                        it += 1
                        sview = I["ssm0"][s].rearrange("(pr h2) p n -> (h2 p) pr n", h2=2)[:, 4 * q4:4 * q4 + 4, :]
                        DMA(lambda e, sview=sview, S0q=S0q: e.dma_start(out=S0q[:], in_=sview), w=[s0k])
                        for j in range(4):
                            T(lambda e, j=j, S0q=S0q, tb_=tb_: e.transpose(out=pb[tb_][:, 128 * j:128 * j + 128], in_=S0q[:, j, :], identity=identf[:]), [s0k, "identf"], [f"pb{tb_}"])
                        pv = pb[tb_][:, :].rearrange("p (a h2 q) -> p a h2 q", h2=2, q=64)
                        A(lambda e, q4=q4, pv=pv: e.activation(out=SA[:, 4 * q4:4 * q4 + 4, 0:64], in_=pv[:, :, 0, :], func=AF.Copy), [f"pb{tb_}"], [f"SA{q4}"])
                        A(lambda e, q4=q4, pv=pv: e.activation(out=SB[:, 4 * q4:4 * q4 + 4, 64:128], in_=pv[:, :, 1, :], func=AF.Copy), [f"pb{tb_}"], [f"SB{q4}"])
                    for pr in prs_:
                        T(lambda e, pr=pr, cs=cs: e.matmul(yps(pr)[:, cs], lhsT=SA[:, pr, :], rhs=CEall[:, 2 * pr, cs], start=False, stop=False, skip_group_check=True), sak + ["CEall"], [ypk(pr)])
                        T(lambda e, pr=pr, cs=cs, last=last: e.matmul(yps(pr)[:, cs], lhsT=SB[:, pr, :], rhs=CEall[:, 2 * pr + 1, cs], start=False, stop=last, skip_group_check=True), sak + ["CEall"], [ypk(pr)])
                    if smp:
                        for j in range(4):
                            pr = 4 * q4 + j
                            T(lambda e, pr=pr, j=j, Bm=Bm, nb_=nb_: e.matmul(pb[nb_][:, 128 * j:128 * j + 128], lhsT=Xd[:, 2 * pr:2 * pr + 2, :].rearrange("p a b -> p (a b)"), rhs=Bm[:, pr // 2, :], start=True, stop=True), ["Xd", bmk], [f"pb{nb_}"])
                        for j in range(4):
                            pr = 4 * q4 + j
                            for h2 in range(2):
                                hs = slice(64 * h2, 64 * h2 + 64)
                                V(lambda e, pr=pr, j=j, hs=hs, s=s, h2=h2, S0q=S0q, Snq=Snq, nb_=nb_: e.scalar_tensor_tensor(out=Snq[hs, j, :], in0=S0q[hs, j, :], scalar=dch[hs, 2 * pr + h2, s:s + 1], in1=pb[nb_][hs, 128 * j:128 * j + 128], op0=ALU.mult, op1=ALU.add), [s0k, "dch", f"pb{nb_}"], [snk])
                        oview = O["ssm_s"][s].rearrange("(pr h2) p n -> (h2 p) pr n", h2=2)[:, 4 * q4:4 * q4 + 4, :]
                        DMAO(lambda e, oview=oview, Snq=Snq: e.dma_start(out=oview, in_=Snq[:]), [snk], [])
            if smp:
                for gq in range(8):
                    yevac(gq)
            if not smp:
                V(lambda e: e.tensor_tensor(out=STt[:], in0=STt[:], in1=dch[:, :, 0:1].broadcast_to([128, 32, 64]), op=ALU.mult), ["STt", "dch"], ["STt"])
                for rnd in range(2):
                    for g in range(4 * rnd, 4 * rnd + 4):
                        bk_ = 2 + (g % 4) // 2
                        T(lambda e, g=g, bk_=bk_: e.matmul(pb[bk_][:, 256 * (g % 2):256 * (g % 2) + 256], lhsT=Btok[:, g, :], rhs=Xd[:, 4 * g:4 * g + 4, :].rearrange("p a b -> p (a b)"), start=True, stop=True), ["Btok", "Xd"], [f"pb{bk_}"])
                    for b2_ in range(2):
                        h0 = 16 * rnd + 8 * b2_
                        V(lambda e, h0=h0, b2_=b2_: e.tensor_tensor(out=STt[:, h0:h0 + 8, :], in0=STt[:, h0:h0 + 8, :], in1=pb[2 + b2_][:, :].rearrange("p (h q) -> p h q", q=64), op=ALU.add), ["STt", f"pb{2 + b2_}"], ["STt"])
                sv = STt[:].rearrange("p (pr h2) q -> p pr h2 q", h2=2)
                A(lambda e, sv=sv: e.activation(out=SA[:, :, 0:64], in_=sv[:, :, 0, :], func=AF.Copy), ["STt"], [f"SA{i}" for i in range(4)])
                A(lambda e, sv=sv: e.activation(out=SB[:, :, 64:128], in_=sv[:, :, 1, :], func=AF.Copy), ["STt"], [f"SB{i}" for i in range(4)])
                if lastp:
                    for q4 in range(4):
                        for j in range(4):
                            pr = 4 * q4 + j
                            T(lambda e, pr=pr, j=j: e.transpose(out=pb[6][:, 128 * j:128 * j + 128], in_=STt[:, 2 * pr:2 * pr + 2, :].rearrange("p a b -> p (a b)"), identity=identf[:]), ["STt", "identf"], ["pb6"])
                        Snq = Snqs[q4 % 2]; snk = f"Snq{q4 % 2}"
                        V(lambda e, Snq=Snq: e.tensor_copy(out=Snq[:].rearrange("p a b -> p (a b)"), in_=pb[6][:, :]), ["pb6"], [snk])
                        oview = O["ssm_p"].rearrange("(pr h2) p n -> (h2 p) pr n", h2=2)[:, 4 * q4:4 * q4 + 4, :]
                        DMAO(lambda e, oview=oview, Snq=Snq: e.dma_start(out=oview, in_=Snq[:]), [snk], [])
            dense_T("w_down_ssd", 16, lambda k: ynT[:, k, :], lambda k: [f"ynT{k}"],
                    lambda m, ps, pk: V(lambda e: e.tensor_tensor(out=yaT[:, m, :], in0=ps, in1=gas[:, m, :], op=ALU.mult), [pk, f"ga{m}"], [f"yaT_{par}_{m}"]))


        def recB(ti, par):
            u5 = u5_[par]; z5s = z5s_[par]; gbs = gbs_[par]; yaT = yaT_[par]
            smp = (ti == 16)
            lastp = (ti == 15)
            xsrc = I["xs"] if smp else I["xp"][128 * ti:128 * ti + 128, :]
            ydst = O["y_s"] if smp else O["y_p"][128 * ti:128 * ti + 128, :]
            if smp:
                for ri, nm in enumerate(["re0", "im0"]):
                    sv_ = I[nm].rearrange("s (gh gq g2) p -> s gq gh (g2 p)", gq=8, g2=2)
                    for s in range(16):
                        DMA(lambda e, s=s, sv_=sv_: e.dma_start(out=s5io[8 * s:8 * s + 8, :, :, :].rearrange("p a b c -> p a (b c)"), in_=sv_[s]), w=["s5io"])
                    for gh in range(4):
                        T(lambda e, gh=gh: e.transpose(out=pb[5][:, 0:128], in_=s5io[:, gh, :, :].rearrange("p a b -> p (a b)"), identity=identf[:]), ["s5io", "identf"], ["pb5"])
                        V(lambda e, gh=gh, ri=ri: e.tensor_copy(out=s0all[:, 8 * gh:8 * gh + 8, ri, :], in_=pb[5][:, 0:128].rearrange("p (s g) -> p g s", g=8)), ["pb5"], ["s0all"])
            Ls, nsg = (8, 16) if smp else (64, 2)
            for ft in range(8):
                gsl = slice(4 * ft, 4 * ft + 4)
                ecv = ecos[:, gsl, 0:Ls].unsqueeze(2).broadcast_to([128, 4, nsg, Ls])
                esv = esin[:, gsl, 0:Ls].unsqueeze(2).broadcast_to([128, 4, nsg, Ls])
                v4 = lambda ap: ap.rearrange("p q (s l) -> p q s l", l=Ls)
                for q in range(4):
                    A(lambda e, ft=ft, q=q: e.activation(out=u5m[:, q, :], in_=u5[:, ft, :], func=AF.Copy, scale=pairmask[:, q:q + 1]), [f"u5_{par}_{ft}", "pairmask"], ["u5m"])
                for ri in range(2):
                    pbk = f"pb{4 + ri}"
                    for q in range(4):
                        T(lambda e, ft=ft, q=q, ri=ri: e.matmul(pb[4 + ri][:, 128 * q:128 * q + 128], lhsT=wb5[:, ft, ri, :], rhs=u5m[:, q, :], start=True, stop=True), ["wb5", "u5m"], [pbk])
                    A(lambda e, ri=ri: e.activation(out=bu[:, :, ri, :], in_=pb[4 + ri][:, :].rearrange("p (q t) -> p q t", t=128), func=AF.Copy), [pbk], ["bu"])
                bre_, bim_ = v4(bu[:, :, 0, :]), v4(bu[:, :, 1, :])
                V(lambda e, ecv=ecv: e.tensor_tensor(out=v4(bt1[:]), in0=ecv, in1=bre_, op=ALU.mult), ["ecos", "bu"], ["bt1"])
                V(lambda e, esv=esv: e.tensor_tensor(out=v4(bt2[:]), in0=esv, in1=bim_, op=ALU.mult), ["esin", "bu"], ["bt2"])
                V(lambda e: e.tensor_tensor(out=sgm[:, :, 0, :], in0=bt1[:], in1=bt2[:], op=ALU.add), ["bt1", "bt2"], ["sgm"])
                V(lambda e, ecv=ecv: e.tensor_tensor(out=v4(bt1[:]), in0=ecv, in1=bim_, op=ALU.mult), ["ecos", "bu", "sgm"], ["bt1"])
                V(lambda e, esv=esv: e.tensor_tensor(out=v4(bt2[:]), in0=esv, in1=bre_, op=ALU.mult), ["esin", "bu", "sgm"], ["bt2"])
                V(lambda e: e.tensor_tensor(out=sgm[:, :, 1, :], in0=bt1[:], in1=bt2[:], op=ALU.subtract), ["bt1", "bt2"], ["sgm"])
                if smp:
                    V(lambda e, gsl=gsl: e.tensor_tensor(out=rts[:], in0=rdec[:, gsl].unsqueeze(2).broadcast_to([128, 4, 128]), in1=rm128[:].unsqueeze(1).broadcast_to([128, 4, 128]), op=ALU.mult), ["rdec", "rm128"], ["rts"])
                else:
                    V(lambda e, gsl=gsl: e.tensor_copy(out=rts[:], in_=rdec[:, gsl].unsqueeze(2).broadcast_to([128, 4, 128])), ["rdec"], ["rts"])
                if smp:
                    n_ = 16
                    ab = lambda t: t[:, gsl].unsqueeze(2).broadcast_to([128, 4, n_])
                    cmul(cin_[:, :, 0, :], cin_[:, :, 1, :], ab(are), ab(aim), s0all[:, gsl, 0, :], s0all[:, gsl, 1, :],
                         ["are", "aim", "s0all"], ["cinr", "cini"], ctm[:, :, 0, :], ctm[:, :, 1, :], ["ctm0", "ctm1"])
                    for ri in range(2):
                        tgt = v4(sgm[:, :, ri, :])[:, :, :, 0]
                        V(lambda e, tgt=tgt, ri=ri: e.tensor_tensor(out=tgt, in0=tgt, in1=cin_[:, :, ri, :], op=ALU.add), ["sgm", "cinr", "cini"], ["sgm"])
                    for q in range(4):
                        for ri in range(2):
                            EV = V
                            EV(lambda e, q=q, ri=ri: e.tensor_tensor_scan(out=bu[:, q, ri, :], data0=rts[:, q, :], data1=sgm[:, q, ri, :], initial=0.0, op0=ALU.mult, op1=ALU.add), ["rts", "sgm"], ["bu"])
                else:
                    for sg in range(2):
                        c0 = 64 * sg
                        src_re = sgc[:, gsl, 0:1] if sg == 0 else bu[:, :, 0, 63:64]
                        src_im = sgc[:, gsl, 1:2] if sg == 0 else bu[:, :, 1, 63:64]
                        ab = lambda t: t[:, gsl].unsqueeze(2)
                        cmul(cin_[:, :, 0, 0:1], cin_[:, :, 1, 0:1], ab(aere), ab(aeim), src_re, src_im,
                             ["aere", "aeim", "sgc", "bu"], ["cinr", "cini"], ctm[:, :, 0, 0:1], ctm[:, :, 1, 0:1], ["ctm0", "ctm1"])
                        for ri in range(2):
                            tgt = sgm[:, :, ri, c0:c0 + 1]
                            V(lambda e, tgt=tgt, ri=ri: e.tensor_tensor(out=tgt, in0=tgt, in1=cin_[:, :, ri, 0:1], op=ALU.add), ["sgm", "cinr", "cini"], ["sgm"])
                        for q in range(4):
                            for ri in range(2):
                                EV = V
                                EV(lambda e, q=q, ri=ri, c0=c0: e.tensor_tensor_scan(out=bu[:, q, ri, c0:c0 + 64], data0=rts[:, q, c0:c0 + 64], data1=sgm[:, q, ri, c0:c0 + 64], initial=0.0, op0=ALU.mult, op1=ALU.add), ["rts", "sgm"], ["bu"])
                    V(lambda e, gsl=gsl: e.tensor_copy(out=sgc[:, gsl, :], in_=bu[:, :, :, 127]), ["bu"], ["sgc"])
                sre_, sim_ = v4(bu[:, :, 0, :]), v4(bu[:, :, 1, :])
                V(lambda e, ecv=ecv: e.tensor_tensor(out=v4(bt1[:]), in0=ecv, in1=sre_, op=ALU.mult), ["ecos", "bu"], ["bt1"])
                V(lambda e, esv=esv: e.tensor_tensor(out=v4(bt2[:]), in0=esv, in1=sim_, op=ALU.mult), ["esin", "bu"], ["bt2"])
                V(lambda e: e.tensor_tensor(out=s5s[:, :, 0, :], in0=bt1[:], in1=bt2[:], op=ALU.subtract), ["bt1", "bt2"], ["s5s"])
                if smp or lastp:
                    fsrc = lambda t: (v4(t[:])[:, :, :, Ls - 1] if smp else t[:, :, 127:128])
                    nf = 16 if smp else 1
                    V(lambda e, gsl=gsl, fsrc=fsrc, nf=nf: e.tensor_tensor(out=fin[:, gsl, 0, 0:nf], in0=fsrc(bt1), in1=fsrc(bt2), op=ALU.subtract), ["bt1", "bt2"], ["fin"])
                V(lambda e, ecv=ecv: e.tensor_tensor(out=v4(bt1[:]), in0=ecv, in1=sim_, op=ALU.mult), ["ecos", "bu", "s5s", "fin"], ["bt1"])
                V(lambda e, esv=esv: e.tensor_tensor(out=v4(bt2[:]), in0=esv, in1=sre_, op=ALU.mult), ["esin", "bu", "s5s", "fin"], ["bt2"])
                V(lambda e: e.tensor_tensor(out=s5s[:, :, 1, :], in0=bt1[:], in1=bt2[:], op=ALU.add), ["bt1", "bt2"], ["s5s"])
                if smp or lastp:
                    V(lambda e, gsl=gsl, fsrc=fsrc, nf=nf: e.tensor_tensor(out=fin[:, gsl, 1, 0:nf], in0=fsrc(bt1), in1=fsrc(bt2), op=ALU.add), ["bt1", "bt2"], ["fin"])
                ps4 = [pb[4][:, 128 * q:128 * q + 128] for q in range(4)]
                for q in range(4):
                    for ri in range(2):
                        T(lambda e, ft=ft, q=q, ri=ri: e.matmul(ps4[q], lhsT=wd5[:, ft, ri, :], rhs=s5s[:, q, ri, :], start=(ri == 0), stop=(ri == 1)), ["wd5", "s5s"], ["pb4"])
                for q in range(4):
                    rs_ = slice(32 * q, 32 * q + 32)
                    V(lambda e, ft=ft, q=q, rs_=rs_: e.scalar_tensor_tensor(out=y5t[rs_, :], in0=u5[rs_, ft, :], scalar=ds5[rs_, ft:ft + 1], in1=ps4[q][rs_, :], op0=ALU.mult, op1=ALU.add), [f"u5_{par}_{ft}", "ds5", "pb4"], ["y5t"])
                V(lambda e: e.tensor_tensor(out=y5u[:], in0=y5t[:], in1=y5t[:], op=ALU.mult), ["y5t"], ["y5u"])
                V(lambda e: e.tensor_scalar(out=y5u[:], in0=y5u[:], scalar1=0.044715, scalar2=1.0, op0=ALU.mult, op1=ALU.add), ["y5u"], ["y5u"])
                V(lambda e: e.tensor_tensor(out=y5u[:], in0=y5u[:], in1=y5t[:], op=ALU.mult), ["y5u", "y5t"], ["y5u"])
                A(lambda e: e.activation(out=y5v[:], in_=y5u[:], func=AF.Sigmoid, scale=1.5957691216057308), ["y5u"], ["y5v"])
                V(lambda e, ft=ft: e.tensor_tensor(out=y5g[:, ft, :], in0=y5v[:], in1=y5t[:], op=ALU.mult), ["y5v", "y5t"], [f"y5g{ft}"])
            if lastp:
                for ri, nm in enumerate(["re_p", "im_p"]):
                    DMAO(lambda e, ri=ri, nm=nm: e.dma_start(out=O[nm].rearrange("(gp g2) p -> (g2 p) gp", g2=2), in_=fin[:, :, ri, 0], allow_slow_non_contiguous=True), ["fin"], [])
            if smp:
                for ri, nm in enumerate(["re_s", "im_s"]):
                    for gh in range(4):
                        V(lambda e, gh=gh, ri=ri: e.tensor_copy(out=s5tr[:], in_=fin[:, 8 * gh:8 * gh + 8, ri, :].rearrange("p g s -> p s g")), ["fin"], ["s5tr"])
                        T(lambda e: e.transpose(out=pb[5][:, 0:128], in_=s5tr[:].rearrange("p a b -> p (a b)"), identity=identf[:]), ["s5tr", "identf"], ["pb5"])
                        V(lambda e, gh=gh: e.tensor_copy(out=s5io[:, gh, :, :].rearrange("p a b -> p (a b)"), in_=pb[5][:, 0:128]), ["pb5"], ["s5io"])
                    ov = O[nm].rearrange("s (gh gq g2) p -> s gq gh (g2 p)", gq=8, g2=2)
                    for s in range(16):
                        DMAO(lambda e, s=s, ov=ov: e.dma_start(out=ov[s], in_=s5io[8 * s:8 * s + 8, :, :, :].rearrange("p a b c -> p a (b c)")), ["s5io"], [])

            def glu_evac(m, ps, pk):
                A(lambda e: e.activation(out=glus[:], in_=ps, func=AF.Sigmoid, bias=bglu[:, m:m + 1]), [pk, "bglu"], ["glus"])
                V(lambda e: e.tensor_tensor(out=glus[:], in0=glus[:], in1=y5g[:, m, :], op=ALU.mult), ["glus", f"y5g{m}"], ["glus"])
                V(lambda e: e.tensor_tensor(out=y5f[:, m, :], in0=glus[:], in1=z5s[:, m, :], op=ALU.mult), ["glus", f"z5_{par}_{m}"], [f"y5f{m}"])
            dense_T("w_glu", 8, lambda k: y5g[:, k, :], lambda k: [f"y5g{k}"], glu_evac)
            dense_T("w_down_s5", 8, lambda k: y5f[:, k, :], lambda k: [f"y5f{k}"],
                    lambda m, ps, pk: V(lambda e: e.tensor_tensor(out=ybT[:, m, :], in0=ps, in1=gbs[:, m, :], op=ALU.mult), [pk, f"gb_{par}_{m}"], [f"ybT{m}"]))
            for m in range(8):
                V(lambda e, m=m: e.tensor_tensor(out=mixT[:, m, :], in0=yaT[:, m, :], in1=ybT[:, m, :], op=ALU.add), [f"yaT_{par}_{m}", f"ybT{m}"], [f"mixT{m}"])
            DMA(lambda e: e.dma_start(out=ot[:], in_=xsrc), w=["ot"])
            for cb in range(4):
                wt, wk = wload(WS["w_out"][:, 256 * cb:256 * cb + 256])
                hb = cb % 2
                ps = pb[4 + hb][:, 0:256]
                pks = [f"pb{4 + hb}"]
                for k in range(8):
                    T(lambda e, wt=wt, k=k, ps=ps: e.matmul(ps, lhsT=mixT[:, k, :], rhs=wt[:, k, 0:256], start=(k == 0), stop=(k == 7)), [wk, f"mixT{k}"], pks)
                V(lambda e, cb=cb, ps=ps: e.tensor_tensor(out=ot[:, 256 * cb:256 * cb + 256], in0=ps, in1=ot[:, 256 * cb:256 * cb + 256], op=ALU.add), pks + ["ot"], ["ot"])
            G(lambda e: e.memset(ssB[:], 0.0), w=["ssB"])
            A(lambda e: e.activation(out=y5f[:].rearrange("p a b -> p (a b)"), in_=ot[:], func=AF.Square, accum_out=ssB[:]), ["ot", "ssB"], [f"y5f{m}" for m in range(8)] + ["ssB"])
            A(lambda e: e.activation(out=rstdB[:], in_=ssB[:], func=AF.Sqrt, scale=1.0 / D, bias=epscol[:]), ["ssB", "epscol"], ["rstdB"])
            V(lambda e: e.reciprocal(out=rstdB[:], in_=rstdB[:]), ["rstdB"], ["rstdB"])
            V(lambda e: e.scalar_tensor_tensor(out=ot[:], in0=ot[:], scalar=rstdB[:, 0:1], in1=fnw[:], op0=ALU.mult, op1=ALU.mult), ["ot", "rstdB", "fnw"], ["ot"])
            DMAO(lambda e: e.dma_start(out=ydst, in_=ot[:]), ["ot"], [])


        def flush(lst):
            for (eng, fn, r, w, dma) in lst:
                P.op(eng, fn, r, w, dma=dma)

        def record(fn_, ti, par, banks, wset):
            CUR[0] = []
            MMB[0] = banks
            WSET[0] = wset
            fn_(ti, par)
            lst = CUR[0]
            CUR[0] = None
            return lst

        def merge(la, lb):
            out = []
            ia = ib = 0
            na, nb = len(la), len(lb)
            while ia < na or ib < nb:
                if ib >= nb or (ia < na and ia * nb <= ib * na):
                    out.append(la[ia]); ia += 1
                else:
                    out.append(lb[ib]); ib += 1
            return out

        tl = list(tiles) if tiles is not None else list(range(17))
        if tl:
            flush(record(recA, tl[0], 0, (0, 1), (0, 1)))
            for n_, ti in enumerate(tl):
                lb = record(recB, ti, n_ % 2, (4, 5), (2, 3))
                if n_ + 1 < len(tl):
                    la = record(recA, tl[n_ + 1], (n_ + 1) % 2, (0, 1), (0, 1))
                    if tl[n_ + 1] == 16 or not INTERLEAVE[0]:
                        flush(lb); flush(la)
                    else:
                        flush(merge(la, lb))
                else:
                    flush(lb)
        P.emit(nc)
    return nc


_NC = {}


def _shard(inputs, c):
    d = {}
    d["xp"] = np.ascontiguousarray(inputs["x_prompt"][c])
    d["xs"] = np.ascontiguousarray(inputs["x_sample"][16 * c:16 * c + 16].reshape(128, 1024))
    d["conv0"] = np.ascontiguousarray(inputs["state_conv"][0, 16 * c:16 * c + 16])
    d["ssm0"] = np.ascontiguousarray(inputs["state_ssm"][0, 16 * c:16 * c + 16])
    d["re0"] = np.ascontiguousarray(inputs["state_s5_re"][0, 16 * c:16 * c + 16])
    d["im0"] = np.ascontiguousarray(inputs["state_s5_im"][0, 16 * c:16 * c + 16])
    for k in ("norm_w", "w_in", "conv_w", "conv_b", "dt_bias", "A_log", "D_ssd", "ssd_norm_w", "w_down_ssd", "lam_re",
              "lam_im", "log_dt", "B_re", "B_im", "C_re", "C_im", "D_s5", "w_glu", "b_glu", "w_down_s5", "w_out"):
        d[k] = np.ascontiguousarray(inputs[k][0])
    d["final_norm_w"] = np.ascontiguousarray(inputs["final_norm_w"])
    return d


def kernel(**inputs):
    inputs = {k: np.asarray(v, dtype=np.float32) for k, v in inputs.items()}
    if "nc" not in _NC:
        _NC["nc"] = build()
    nc = _NC["nc"]
    consts = host_consts()
    in_maps = []
    for c in range(NCORES):
        d = _shard(inputs, c)
        d.update(consts)
        in_maps.append(d)
    res = run_bass_kernel_spmd(nc, in_maps, core_ids=list(range(NCORES))).results
    cat = lambda n: np.concatenate([r[n] for r in res], axis=0)
    y_p = np.stack([r["y_p"] for r in res], 0)
    y_s = cat("y_s").reshape(128, 8, 1024)
    conv_p = np.stack([r["conv_p"] for r in res], 0)[None]
    ssm_p = np.stack([r["ssm_p"] for r in res], 0)[None]
    re_p = np.stack([r["re_p"] for r in res], 0)[None]
    im_p = np.stack([r["im_p"] for r in res], 0)[None]
    conv_s = cat("conv_s")[None]
    ssm_s = cat("ssm_s")[None]
    re_s = cat("re_s")[None]
    im_s = cat("im_s")[None]
    return tuple(np.ascontiguousarray(a, dtype=np.float32) for a in
                 (y_p, y_s, conv_p, ssm_p, re_p, im_p, conv_s, ssm_s, re_s, im_s))
```

```python
import contextlib
import math
import numpy as np
import concourse.bass as bass
import concourse.mybir as mybir
from concourse.bass_utils import run_bass_kernel_spmd

F32 = mybir.dt.float32
BF16 = mybir.dt.bfloat16
I32 = mybir.dt.int32
ALU = mybir.AluOpType
AF = mybir.ActivationFunctionType

ENGS = ("pe", "act", "dve", "pool", "sp")
N_DMA_SEM = 24
NCORES = 8
D = 1024
PROJ = 10272
Z0, XBC0, DT0, U0, Z50, GA0, GB0 = 0, 2048, 6144, 6176, 7200, 8224, 9248
EPS = 1e-6
SIN_SCALE = 6.28318


class Prog:
    def __init__(self):
        self.ops = []
        self.last_write = {}
        self.readers = {}
        self.dma_count = {e: 0 for e in ENGS}
        self.slot_last = {}
        self.last_op = {}
        self.pending = {}

    def barrier(self):
        deps = set(self.last_op.values()) | set(self.slot_last.values())
        for e in ENGS:
            self.pending[e] = set(deps) | self.pending.get(e, set())

    def op(self, eng, fn, reads=(), writes=(), dma=False):
        writes = list(writes) + [k for k in reads if k.startswith("pb")]
        reads = [k for k in reads if not k.startswith("pb")]
        oid = len(self.ops)
        deps = set()
        for k in reads:
            if k in self.last_write:
                deps.add(self.last_write[k])
        for k in writes:
            if k in self.last_write:
                deps.add(self.last_write[k])
            deps.update(self.readers.get(k, ()))
        if eng in self.pending:
            deps |= self.pending.pop(eng)
        rec = dict(eng=eng, fn=fn, deps=deps, dma=dma, sig=False, seq=None)
        if dma:
            i = self.dma_count[eng]
            self.dma_count[eng] += 1
            slot = i % N_DMA_SEM
            rec["slot"] = slot
            rec["val"] = 16 * (i // N_DMA_SEM + 1)
            prev = self.slot_last.get((eng, slot))
            if prev is not None:
                deps.add(prev)
            self.slot_last[(eng, slot)] = oid
        else:
            self.last_op[eng] = oid
        deps.discard(oid)
        self.ops.append(rec)
        for k in writes:
            self.last_write[k] = oid
            self.readers[k] = []
        for k in reads:
            self.readers.setdefault(k, []).append(oid)
        return oid

    def emit(self, nc):
        ops = self.ops

        def skip(dd, o):
            return (not dd["dma"]) and dd["eng"] == o["eng"] == "pe" and not o["dma"]
        for o in ops:
            for d in o["deps"]:
                dd = ops[d]
                if dd["dma"] or skip(dd, o):
                    continue
                dd["sig"] = True
        cnt = {e: 0 for e in ENGS}
        for o in ops:
            if o["sig"]:
                cnt[o["eng"]] += 1
                o["seq"] = cnt[o["eng"]]
        with contextlib.ExitStack() as st:
            csem = {e: st.enter_context(nc.semaphore("c_" + e)) for e in ENGS}
            dsem = {(e, s): st.enter_context(nc.semaphore(f"d_{e}{s}"))
                    for e in ENGS if self.dma_count[e] > 0
                    for s in range(min(N_DMA_SEM, self.dma_count[e]))}
            block = st.enter_context(nc.Block())

            def run(eng_name, eng):
                waited = {}
                for o in ops:
                    if o["eng"] != eng_name:
                        continue
                    for d in sorted(o["deps"]):
                        dd = ops[d]
                        if dd["dma"]:
                            key = ("d", dd["eng"], dd["slot"])
                            sem = dsem[(dd["eng"], dd["slot"])]
                            val = dd["val"]
                        else:
                            if skip(dd, o):
                                continue
                            key = ("c", dd["eng"])
                            sem = csem[dd["eng"]]
                            val = dd["seq"]
                        if waited.get(key, 0) >= val:
                            continue
                        waited[key] = val
                        eng.wait_ge(sem, val)
                    ins = o["fn"](eng)
                    if o["dma"]:
                        ins.then_inc(dsem[(eng_name, o["slot"])], 16)
                    elif o["sig"]:
                        ins.then_inc(csem[eng_name], 1)
                if eng_name == "sp":
                    for (e, s), oid in self.slot_last.items():
                        eng.wait_ge(dsem[(e, s)], ops[oid]["val"])

            block.tensor(lambda e: run("pe", e))
            block.scalar(lambda e: run("act", e))
            block.vector(lambda e: run("dve", e))
            block.gpsimd(lambda e: run("pool", e))
            block.sync(lambda e: run("sp", e))


def host_consts():
    c = {}
    c["ident"] = np.eye(128, dtype=np.float32)
    s = np.arange(128)
    c["negm_p"] = np.where(s[None, :] >= s[:, None], 0.0, -30000.0).astype(np.float32)
    same = (s[None, :] // 8) == (s[:, None] // 8)
    c["negm_s"] = np.where((s[None, :] >= s[:, None]) & same, 0.0, -30000.0).astype(np.float32)
    rp = np.ones((32, 128), np.float32); rp[:, 0] = 0
    rs = np.ones((32, 128), np.float32); rs[:, ::8] = 0
    c["rm_p"] = rp
    c["rm_s"] = rs
    r128 = np.ones((128, 128), np.float32); r128[:, ::8] = 0
    c["rm128"] = r128
    c["seqmask"] = (s[:, None] // 8 == np.arange(16)[None, :]).astype(np.float32)
    c["pairmask"] = (s[:, None] // 32 == np.arange(4)[None, :]).astype(np.float32)
    c["iota"] = np.broadcast_to(np.arange(128, dtype=np.float32)[None, :], (128, 128)).copy()
    return c


IN_SPECS = [
    ("xp", (2048, 1024)), ("xs", (128, 1024)), ("conv0", (16, 3, 4096)), ("ssm0", (16, 32, 64, 128)),
    ("re0", (16, 64, 64)), ("im0", (16, 64, 64)), ("norm_w", (1024,)), ("w_in", (1024, PROJ)),
    ("conv_w", (4, 4096)), ("conv_b", (4096,)), ("dt_bias", (32,)), ("A_log", (32,)), ("D_ssd", (32,)),
    ("ssd_norm_w", (2048,)), ("w_down_ssd", (2048, 1024)), ("lam_re", (64, 64)), ("lam_im", (64, 64)),
    ("log_dt", (64,)), ("B_re", (64, 64, 16)), ("B_im", (64, 64, 16)), ("C_re", (64, 16, 64)),
    ("C_im", (64, 16, 64)), ("D_s5", (1024,)), ("w_glu", (1024, 1024)), ("b_glu", (1024,)),
    ("w_down_s5", (1024, 1024)), ("w_out", (1024, 1024)), ("final_norm_w", (1024,)),
    ("ident", (128, 128)), ("negm_p", (128, 128)), ("negm_s", (128, 128)), ("rm_p", (32, 128)),
    ("rm_s", (32, 128)), ("rm128", (128, 128)), ("seqmask", (128, 16)), ("pairmask", (128, 4)),
    ("iota", (128, 128)),
]
OUT_SPECS = [
    ("y_p", (2048, 1024)), ("y_s", (128, 1024)), ("conv_p", (3, 4096)), ("ssm_p", (32, 64, 128)),
    ("re_p", (64, 64)), ("im_p", (64, 64)), ("conv_s", (16, 3, 4096)), ("ssm_s", (16, 32, 64, 128)),
    ("re_s", (16, 64, 64)), ("im_s", (16, 64, 64)),
]


STOP = [None]
INTERLEAVE = [True]
BLK = [None]


class _Stop(Exception):
    pass


def ck(n):
    if STOP[0] is not None and n >= STOP[0]:
        raise _Stop()


def build(tiles=None):
    nc = bass.Bass("TRN2", target_bir_lowering=False)
    I = {n: nc.dram_tensor(n, list(s), F32, kind="ExternalInput").ap() for n, s in IN_SPECS}
    O = {n: nc.dram_tensor(n, list(s), F32, kind="ExternalOutput").ap() for n, s in OUT_SPECS}
    WS = {
        "w_in": nc.dram_tensor("ws_w_in", [1024, PROJ], BF16).ap(),
        "w_down_ssd": nc.dram_tensor("ws_wds", [2048, 1024], BF16).ap(),
        "w_glu": nc.dram_tensor("ws_wglu", [1024, 1024], BF16).ap(),
        "w_down_s5": nc.dram_tensor("ws_wd5", [1024, 1024], BF16).ap(),
        "w_out": nc.dram_tensor("ws_wout", [1024, 1024], BF16).ap(),
    }
    P = Prog()

    CUR = [None]
    MMB = [(0, 1)]

    def _rec(eng, fn, r, w, dma=False):
        if CUR[0] is None:
            P.op(eng, fn, r, w, dma=dma)
        else:
            CUR[0].append((eng, fn, tuple(r), tuple(w), dma))

    def V(fn, r=(), w=()):
        _rec("dve", fn, r, w)

    def G(fn, r=(), w=()):
        _rec("pool", fn, r, w)

    def A(fn, r=(), w=()):
        _rec("act", fn, r, w)

    def T(fn, r=(), w=()):
        _rec("pe", fn, r, w)

    def DMA(fn, r=(), w=()):
        _rec("sp", fn, r, w, dma=True)

    def DMAO(fn, r=(), w=()):
        _rec("pool", fn, r, w, dma=True)

    def DMAH(fn, r=(), w=()):
        _rec("act", fn, r, w, dma=True)

    st = contextlib.ExitStack()
    with st:
        def sb(name, shape, dt=F32, stack=None):
            return (stack or st).enter_context(nc.sbuf_tensor("s_" + name, list(shape), dt))

        def kn(t):
            n_ = getattr(t, "name")
            return n_[2:] if n_.startswith("s_") else n_
        pb = [st.enter_context(nc.psum_tensor(f"pb{i}", [128, 512], F32)) for i in range(8)]
        pbb = [p.bitcast(BF16) for p in pb]

        identf = sb("identf", [128, 128]); identb = sb("identb", [128, 128], BF16)
        negm_p = sb("negm_p", [128, 128]); negm_s = sb("negm_s", [128, 128])
        rm_p = sb("rm_p", [32, 128]); rm_s = sb("rm_s", [32, 128]); rm128 = sb("rm128", [128, 128])
        seqmask = sb("seqmask", [128, 16]); pairmask = sb("pairmask", [128, 4])
        ones32 = sb("ones32", [32, 128]); onesb = sb("onesb", [128, 128], BF16)
        onecol = sb("onecol", [128, 1]); epscol = sb("epscol", [128, 1])
        for nm, t in [("ident", identf), ("negm_p", negm_p), ("negm_s", negm_s), ("rm_p", rm_p), ("rm_s", rm_s),
                      ("rm128", rm128), ("seqmask", seqmask), ("pairmask", pairmask)]:
            DMA(lambda e, nm=nm, t=t: e.dma_start(out=t[:], in_=I[nm]), w=[kn(t)])
        V(lambda e: e.tensor_copy(out=identb[:], in_=identf[:]), ["identf"], ["identb"])
        G(lambda e: e.memset(ones32[:], 1.0), w=["ones32"])
        G(lambda e: e.memset(onesb[:], 1.0), w=["onesb"])
        G(lambda e: e.memset(onecol[:], 1.0), w=["onecol"])
        G(lambda e: e.memset(epscol[:], EPS), w=["epscol"])

        def colvec(name, src, nt):
            t = sb(name, [128, nt])
            DMA(lambda e: e.dma_start(out=t[:], in_=src.rearrange("(j p) -> p j", p=128), allow_slow_non_contiguous=True), w=[name])
            return t
        normw = colvec("normw", I["norm_w"], 8)
        convb = colvec("convb", I["conv_b"], 32)
        ssdnw = colvec("ssdnw", I["ssd_norm_w"], 16)
        ds5 = colvec("ds5", I["D_s5"], 8)
        bglu = colvec("bglu", I["b_glu"], 8)
        convw = sb("convw", [128, 32, 4])
        for k in range(4):
            DMA(lambda e, k=k: e.dma_start(out=convw[:, :, k], in_=I["conv_w"][k].rearrange("(j p) -> p j", p=128), allow_slow_non_contiguous=True), w=["convw"])
        fnw = sb("fnw", [128, 1024])
        DMA(lambda e: e.dma_start(out=fnw[:], in_=I["final_norm_w"].partition_broadcast(128)), w=["fnw"])
        dtb = sb("dtb", [32, 1]); aneg = sb("aneg", [32, 1])
        DMA(lambda e: e.dma_start(out=dtb[:], in_=I["dt_bias"].rearrange("(h o) -> h o", o=1)), w=["dtb"])
        DMA(lambda e: e.dma_start(out=aneg[:], in_=I["A_log"].rearrange("(h o) -> h o", o=1)), w=["aneg"])
        A(lambda e: e.activation(out=aneg[:], in_=aneg[:], func=AF.Exp), ["aneg"], ["aneg"])
        V(lambda e: e.tensor_scalar(out=aneg[:], in0=aneg[:], scalar1=-1.0, scalar2=None, op0=ALU.mult), ["aneg"], ["aneg"])
        dexp = sb("dexp", [128, 16])
        dv = I["D_ssd"].rearrange("(pr h2) -> h2 pr", h2=2)
        for h2 in range(2):
            DMA(lambda e, h2=h2: e.dma_start(out=dexp[64 * h2:64 * h2 + 64, :], in_=dv[h2].partition_broadcast(64), allow_slow_non_contiguous=True), w=["dexp"])
        are = sb("are", [128, 32]); aim = sb("aim", [128, 32]); rdec = sb("rdec", [128, 32])
        aere = sb("aere", [128, 32]); aeim = sb("aeim", [128, 32])
        ecos = sb("ecos", [128, 32, 64]); esin = sb("esin", [128, 32, 64])
        wb5 = sb("wb5", [128, 8, 2, 128], BF16); wd5 = sb("wd5", [128, 8, 2, 128], BF16)
        halo = sb("halo", [128, 32, 3])
        STt = sb("STt", [128, 32, 64])
        SA = sb("SA", [128, 16, 128], BF16); SB = sb("SB", [128, 16, 128], BF16)
        XA = sb("XA", [128, 16, 128], BF16); XB = sb("XB", [128, 16, 128], BF16)
        sgc = sb("sgc", [128, 32, 2])
        for t_ in (halo, STt, SA, SB, XA, XB, sgc, wd5):
            G(lambda e, t_=t_: e.memset(t_[:], 0.0), w=([kn(t_)] if kn(t_) not in ("SA", "SB") else [f"{kn(t_)}{i}" for i in range(4)]))

        pst = contextlib.ExitStack()
        with pst:
            def psb(name, shape, dt=F32):
                return sb(name, shape, dt, stack=pst)
            iota = psb("iota", [128, 128])
            DMA(lambda e: e.dma_start(out=iota[:], in_=I["iota"]), w=["iota"])
            lre = psb("lre", [128, 32]); lim = psb("lim", [128, 32]); stp = psb("stp", [128, 32]); th = psb("th", [128, 32])
            DMA(lambda e: e.dma_start(out=lre[:], in_=I["lam_re"].rearrange("(gp g2) p -> (g2 p) gp", g2=2), allow_slow_non_contiguous=True), w=["lre"])
            DMA(lambda e: e.dma_start(out=lim[:], in_=I["lam_im"].rearrange("(gp g2) p -> (g2 p) gp", g2=2), allow_slow_non_contiguous=True), w=["lim"])
            ldv = I["log_dt"].rearrange("(gp g2) -> g2 gp", g2=2)
            for g2 in range(2):
                DMA(lambda e, g2=g2: e.dma_start(out=stp[64 * g2:64 * g2 + 64, :], in_=ldv[g2].partition_broadcast(64), allow_slow_non_contiguous=True), w=["stp"])
            A(lambda e: e.activation(out=stp[:], in_=stp[:], func=AF.Exp), ["stp"], ["stp"])
            V(lambda e: e.tensor_tensor(out=rdec[:], in0=lre[:], in1=stp[:], op=ALU.mult), ["lre", "stp"], ["rdec"])
            A(lambda e: e.activation(out=rdec[:], in_=rdec[:], func=AF.Exp), ["rdec"], ["rdec"])
            V(lambda e: e.tensor_tensor(out=th[:], in0=lim[:], in1=stp[:], op=ALU.mult), ["lim", "stp"], ["th"])
            V(lambda e: e.tensor_scalar(out=th[:], in0=th[:], scalar1=1.0 / (2.0 * math.pi), scalar2=None, op0=ALU.mult), ["th"], ["th"])
            tmpx = psb("tmpx", [128, 2048]); tmpf = psb("tmpf", [128, 2048]); tmpi = psb("tmpi", [128, 2048], I32)

            def sin_turns(out_ap, mk_x, n, off, rkeys, wkeys):
                V(lambda e: mk_x(e, tmpx[:, 0:n]), rkeys, ["tmpx"])
                if off:
                    V(lambda e: e.tensor_scalar(out=tmpx[:, 0:n], in0=tmpx[:, 0:n], scalar1=float(off), scalar2=None, op0=ALU.add), ["tmpx"], ["tmpx"])
                V(lambda e: e.tensor_copy(out=tmpi[:, 0:n], in_=tmpx[:, 0:n]), ["tmpx"], ["tmpi"])
                V(lambda e: e.tensor_copy(out=tmpf[:, 0:n], in_=tmpi[:, 0:n]), ["tmpi"], ["tmpf"])
                V(lambda e: e.tensor_tensor(out=tmpx[:, 0:n], in0=tmpx[:, 0:n], in1=tmpf[:, 0:n], op=ALU.subtract), ["tmpx", "tmpf"], ["tmpx"])
                A(lambda e: e.activation(out=out_ap, in_=tmpx[:, 0:n], func=AF.Sin, scale=SIN_SCALE), ["tmpx"], wkeys)
            cs1 = psb("cs1", [128, 32]); sn1 = psb("sn1", [128, 32])
            x1 = lambda e, dst: e.tensor_copy(out=dst, in_=th[:])
            sin_turns(sn1[:], x1, 32, 0.0, ["th"], ["sn1"])
            sin_turns(cs1[:], x1, 32, 0.25, ["th"], ["cs1"])
            V(lambda e: e.tensor_tensor(out=are[:], in0=rdec[:], in1=cs1[:], op=ALU.mult), ["rdec", "cs1"], ["are"])
            V(lambda e: e.tensor_tensor(out=aim[:], in0=rdec[:], in1=sn1[:], op=ALU.mult), ["rdec", "sn1"], ["aim"])
            x64 = lambda e, dst: e.tensor_scalar(out=dst, in0=th[:], scalar1=64.0, scalar2=None, op0=ALU.mult)
            sin_turns(sn1[:], x64, 32, 0.0, ["th"], ["sn1"])
            sin_turns(cs1[:], x64, 32, 0.25, ["th"], ["cs1"])
            V(lambda e: e.tensor_tensor(out=aere[:], in0=rdec[:], in1=cs1[:], op=ALU.mult), ["rdec", "cs1"], ["aere"])
            V(lambda e: e.tensor_tensor(out=aeim[:], in0=rdec[:], in1=sn1[:], op=ALU.mult), ["rdec", "sn1"], ["aeim"])
            xt_ = lambda e, dst: e.tensor_tensor(out=dst.rearrange("p (a b) -> p a b", b=64), in0=th[:].unsqueeze(2).broadcast_to([128, 32, 64]),
                                                 in1=iota[:, 0:64].unsqueeze(1).broadcast_to([128, 32, 64]), op=ALU.mult)
            sin_turns(esin[:].rearrange("p a b -> p (a b)"), xt_, 2048, 0.0, ["th", "iota"], ["esin"])
            sin_turns(ecos[:].rearrange("p a b -> p (a b)"), xt_, 2048, 0.25, ["th", "iota"], ["ecos"])
            gre = psb("gre", [128, 32]); gim = psb("gim", [128, 32]); den = psb("den", [128, 32])
            t1 = psb("t1s", [128, 32]); t2 = psb("t2s", [128, 32]); am1 = psb("am1", [128, 32])
            V(lambda e: e.tensor_scalar(out=am1[:], in0=are[:], scalar1=-1.0, scalar2=None, op0=ALU.add), ["are"], ["am1"])
            V(lambda e: e.tensor_tensor(out=den[:], in0=lre[:], in1=lre[:], op=ALU.mult), ["lre"], ["den"])
            V(lambda e: e.tensor_tensor(out=t1[:], in0=lim[:], in1=lim[:], op=ALU.mult), ["lim"], ["t1s"])
            V(lambda e: e.tensor_tensor(out=den[:], in0=den[:], in1=t1[:], op=ALU.add), ["den", "t1s"], ["den"])
            V(lambda e: e.reciprocal(out=den[:], in_=den[:]), ["den"], ["den"])
            V(lambda e: e.tensor_tensor(out=t1[:], in0=am1[:], in1=lre[:], op=ALU.mult), ["am1", "lre"], ["t1s"])
            V(lambda e: e.tensor_tensor(out=t2[:], in0=aim[:], in1=lim[:], op=ALU.mult), ["aim", "lim"], ["t2s"])
            V(lambda e: e.tensor_tensor(out=gre[:], in0=t1[:], in1=t2[:], op=ALU.add), ["t1s", "t2s"], ["gre"])
            V(lambda e: e.tensor_tensor(out=gre[:], in0=gre[:], in1=den[:], op=ALU.mult), ["gre", "den"], ["gre"])
            V(lambda e: e.tensor_tensor(out=t1[:], in0=aim[:], in1=lre[:], op=ALU.mult), ["aim", "lre"], ["t1s"])
            V(lambda e: e.tensor_tensor(out=t2[:], in0=am1[:], in1=lim[:], op=ALU.mult), ["am1", "lim"], ["t2s"])
            V(lambda e: e.tensor_tensor(out=gim[:], in0=t1[:], in1=t2[:], op=ALU.subtract), ["t1s", "t2s"], ["gim"])
            V(lambda e: e.tensor_tensor(out=gim[:], in0=gim[:], in1=den[:], op=ALU.mult), ["gim", "den"], ["gim"])
            bre = psb("bre", [128, 32, 16]); bim = psb("bim", [128, 32, 16]); bbr = psb("bbr", [128, 32, 16]); bbi = psb("bbi", [128, 32, 16]); tb = psb("tbs", [128, 32, 16])
            DMA(lambda e: e.dma_start(out=bre[:], in_=I["B_re"].rearrange("(gp g2) p k -> (g2 p) gp k", g2=2)), w=["bre"])
            DMA(lambda e: e.dma_start(out=bim[:], in_=I["B_im"].rearrange("(gp g2) p k -> (g2 p) gp k", g2=2)), w=["bim"])
            gb_ = lambda t: t[:].unsqueeze(2).broadcast_to([128, 32, 16])
            V(lambda e: e.tensor_tensor(out=bbr[:], in0=bre[:], in1=gb_(gre), op=ALU.mult), ["bre", "gre"], ["bbr"])
            V(lambda e: e.tensor_tensor(out=tb[:], in0=bim[:], in1=gb_(gim), op=ALU.mult), ["bim", "gim"], ["tbs"])
            V(lambda e: e.tensor_tensor(out=bbr[:], in0=bbr[:], in1=tb[:], op=ALU.subtract), ["bbr", "tbs"], ["bbr"])
            V(lambda e: e.tensor_tensor(out=bbi[:], in0=bim[:], in1=gb_(gre), op=ALU.mult), ["bim", "gre"], ["bbi"])
            V(lambda e: e.tensor_tensor(out=tb[:], in0=bre[:], in1=gb_(gim), op=ALU.mult), ["bre", "gim"], ["tbs"])
            V(lambda e: e.tensor_tensor(out=bbi[:], in0=bbi[:], in1=tb[:], op=ALU.add), ["bbi", "tbs"], ["bbi"])
            cin = psb("cin", [128, 4, 2, 64])
            cst = [psb("cstr", [128, 32, 16]), psb("csti", [128, 32, 16])]
            for ri, nm in enumerate(["C_re", "C_im"]):
                cv = I[nm].rearrange("(gh gq g2) k p -> gq k gh g2 p", gq=8, g2=2)
                for gq in range(8):
                    for gh in range(4):
                        DMA(lambda e, gq=gq, gh=gh, cv=cv: e.dma_start(out=cin[16 * gq:16 * gq + 16, gh, :, :], in_=cv[gq][:, gh, :, :]), w=["cin"])
                for gh in range(4):
                    T(lambda e, gh=gh: e.transpose(out=pb[7][:, 0:128], in_=cin[:, gh, :, :].rearrange("p a b -> p (a b)"), identity=identf[:]), ["cin", "identf"], ["pb7"])
                    V(lambda e, gh=gh, ri=ri: e.tensor_copy(out=cst[ri][:, 8 * gh:8 * gh + 8, :], in_=pb[7][:, 0:128].rearrange("p (a b) -> p a b", b=16)), ["pb7"], [kn(cst[ri])])
            zst = psb("zst", [128, 4, 2, 16])
            G(lambda e: e.memset(zst[:], 0.0), w=["zst"])
            for ft in range(8):
                for ri in range(2):
                    src = [bbr, bbi][ri]
                    for h2 in range(2):
                        V(lambda e, ft=ft, h2=h2, src=src: e.tensor_copy(out=zst[64 * h2:64 * h2 + 64, :, h2, :], in_=src[64 * h2:64 * h2 + 64, 4 * ft:4 * ft + 4, :]), [kn(src)], ["zst"])
                    T(lambda e: e.transpose(out=pb[6][:, 0:128], in_=zst[:].rearrange("p a b c -> p (a b c)"), identity=identf[:]), ["zst", "identf"], ["pb6"])
                    V(lambda e, ft=ft, ri=ri: e.tensor_copy(out=wb5[:, ft, ri, :], in_=pb[6][:, 0:128]), ["pb6"], ["wb5"])
                    for h2 in range(2):
                        V(lambda e, ft=ft, ri=ri, h2=h2: e.tensor_scalar(
                            out=wd5[64 * h2:64 * h2 + 64, ft, ri, :].rearrange("p (q g k) -> p q g k", g=2, k=16)[:, :, h2, :],
                            in0=cst[ri][64 * h2:64 * h2 + 64, 4 * ft:4 * ft + 4, :], scalar1=(1.0 if ri == 0 else -1.0), scalar2=None, op0=ALU.mult),
                          [kn(cst[ri])], ["wd5"])
            NST = 6
            stg = [psb(f"stg{i}", [128, 2048]) for i in range(NST)]
            stgb = [psb(f"stgb{i}", [128, 2048], BF16) for i in range(NST)]
            cnt = [0]

            def cast_rows(src, dst, ncols):
                for c0 in range(0, ncols, 2048):
                    w_ = min(2048, ncols - c0)
                    i = cnt[0] % NST
                    cnt[0] += 1
                    DMA(lambda e, i=i, c0=c0, w_=w_: e.dma_start(out=stg[i][:, 0:w_], in_=src[:, c0:c0 + w_]), w=[f"stg{i}"])
                    sel = (1, 2, 2, 2, 1, 0)[cnt[0] % 6]
                    if sel == 0:
                        G(lambda e, i=i, w_=w_: e.tensor_copy(out=stgb[i][:, 0:w_], in_=stg[i][:, 0:w_]), [f"stg{i}"], [f"stgb{i}"])
                    elif sel == 1:
                        A(lambda e, i=i, w_=w_: e.activation(out=stgb[i][:, 0:w_], in_=stg[i][:, 0:w_], func=AF.Copy), [f"stg{i}"], [f"stgb{i}"])
                    else:
                        V(lambda e, i=i, w_=w_: e.tensor_copy(out=stgb[i][:, 0:w_], in_=stg[i][:, 0:w_]), [f"stg{i}"], [f"stgb{i}"])
                    DMAH(lambda e, i=i, c0=c0, w_=w_: e.dma_start(out=dst[:, c0:c0 + w_], in_=stgb[i][:, 0:w_]), [f"stgb{i}"], ["ws"])
            for nm, rows in [("w_in", 1024), ("w_down_ssd", 2048), ("w_glu", 1024), ("w_down_s5", 1024), ("w_out", 1024)]:
                nco = PROJ if nm == "w_in" else 1024
                for r0 in range(0, rows, 128):
                    cast_rows(I[nm][r0:r0 + 128, :], WS[nm][r0:r0 + 128, :], nco)

        P.barrier()

        xt = sb("xt", [128, 1024]); ot = sb("ot", [128, 1024]); ss = sb("ss", [128, 1]); rstd = sb("rstd", [128, 1])
        ssB = sb("ssB", [128, 1]); rstdB = sb("rstdB", [128, 1])
        xsn = sb("xsn", [128, 1024], BF16); hT = sb("hT", [128, 8, 128], BF16)
        wbuf = [sb(f"wbuf{i}", [128, 8, 256], BF16) for i in range(4)]
        zs = sb("zs", [128, 16, 128], BF16)
        xc = sb("xc", [128, 32, 128], BF16)
        raw = [sb(f"raw{i}", [128, 176]) for i in range(2)]
        acc = [sb(f"acc{i}", [128, 128]) for i in range(2)]
        dts = sb("dts", [32, 128]); dAs = sb("dAs", [32, 128]); acs = sb("acs", [32, 128]); dec = sb("dec", [32, 128]); wdt = sb("wdt", [32, 128])
        tokm = sb("tokm", [128, 3, 32])
        u5_ = [sb(f"u5_{i}", [128, 8, 128], BF16) for i in range(2)]; z5s_ = [sb(f"z5s_{i}", [128, 8, 128], BF16) for i in range(2)]
        gas = sb("gas", [128, 8, 128], BF16); gbs_ = [sb(f"gbs_{i}", [128, 8, 128], BF16) for i in range(2)]
        Xd = sb("Xd", [128, 32, 64], BF16); Btok = sb("Btok", [128, 8, 128], BF16)
        rhsR = sb("rhsR", [32, 4, 128]); Eg = sb("Eg", [128, 4, 128]); Dm = sb("Dm", [128, 4, 128])
        Lg = sb("Lg", [128, 4, 128], BF16); Mg = sb("Mg", [128, 4, 128], BF16)
        CEall = sb("CEall", [128, 32, 128], BF16)
        dch = sb("dch", [128, 32, 16])
        S0qs = [sb(f"S0q{i}", [128, 4, 128]) for i in range(2)]; Snqs = [sb(f"Snq{i}", [128, 4, 128]) for i in range(2)]
        Bms = [sb("Bm0", [128, 8, 128], BF16)] * 2
        yv2 = [sb(f"yv2_{i}", [128, 2, 128]) for i in range(2)]; ysq2 = [sb(f"ysq2_{i}", [128, 2, 128], BF16) for i in range(2)]
        rstg2 = [sb(f"rstg2_{i}", [128, 128]) for i in range(2)]; ynT = sb("ynT", [128, 16, 128], BF16)
        yaT_ = [sb(f"yaT_{i}", [128, 8, 128], BF16) for i in range(2)]; ybT = sb("ybT", [128, 8, 128], BF16); mixT = sb("mixT", [128, 8, 128], BF16)
        u5m = sb("u5m", [128, 4, 128], BF16)
        bu = sb("bu", [128, 4, 2, 128]); sgm = sb("sgm", [128, 4, 2, 128]); rts = sb("rts", [128, 4, 128])
        bt1 = sb("bt1", [128, 4, 128]); bt2 = sb("bt2", [128, 4, 128])
        s5s = sb("s5s", [128, 4, 2, 128], BF16)
        cin_ = sb("cinn", [128, 4, 2, 16])
        ctm = sb("ctm", [128, 4, 2, 16])
        fin = sb("fin", [128, 32, 2, 16]); s0all = sb("s0all", [128, 32, 2, 16])
        y5t = sb("y5t", [128, 128]); y5u = sb("y5u", [128, 128]); y5v = sb("y5v", [128, 128])
        y5g = sb("y5g", [128, 8, 128], BF16); y5f = sb("y5f", [128, 8, 128], BF16); glus = sb("glus", [128, 128])
        s5io = sb("s5io", [128, 4, 2, 64]); s5tr = sb("s5tr", [128, 16, 8])
        cvo = sb("cvo", [48, 1024]); cst3 = sb("cst3", [128, 48])

        wcnt = [0]

        WSET = [(0, 1, 2)]

        def wload(src_ap):
            i = WSET[0][wcnt[0] % len(WSET[0])]
            wcnt[0] += 1
            kt = src_ap.shape[0] // 128
            nco = src_ap.shape[1]
            DMA(lambda e: e.dma_start(out=wbuf[i][:, 0:kt, 0:nco], in_=src_ap.rearrange("(j p) f -> p j f", p=128)), ["ws"], [f"wbuf{i}"])
            return wbuf[i], f"wbuf{i}"

        pslot = [0]

        def mm_slot():
            s = MMB[0][pslot[0] % 2]
            pslot[0] += 1
            return pb[s][:, 0:128], f"pb{s}"

        def proj_ft(wt, wk, c0, nrows=128):
            ps, pk = mm_slot()
            for j in range(8):
                T(lambda e, j=j: e.matmul(ps[0:nrows, :], lhsT=wt[:, j, c0:c0 + nrows], rhs=hT[:, j, :], start=(j == 0), stop=(j == 7)), [wk, "hT"], [pk])
            return ps, pk

        def dense_T(wname, kt, rhs_fn, rkeys_fn, evac):
            for cb in range(4):
                halves = [wload(WS[wname][128 * k0:128 * (k0 + 8), 256 * cb:256 * cb + 256]) for k0 in range(0, kt, 8)]
                for s in range(2):
                    m = 2 * cb + s
                    ps, pk = mm_slot()
                    for k in range(kt):
                        wt, wk = halves[k // 8]
                        T(lambda e, wt=wt, k=k, s=s, ps=ps: e.matmul(ps, lhsT=wt[:, k % 8, 128 * s:128 * s + 128], rhs=rhs_fn(k), start=(k == 0), stop=(k == kt - 1)), [wk] + rkeys_fn(k), [pk])
                    evac(m, ps, pk)

        def cmul(out_re, out_im, ar, ai, br, bi, keys_r, keys_w, t1_, t2_, tk):
            V(lambda e: e.tensor_tensor(out=t1_, in0=ar, in1=br, op=ALU.mult), keys_r, [tk[0]])
            V(lambda e: e.tensor_tensor(out=t2_, in0=ai, in1=bi, op=ALU.mult), keys_r, [tk[1]])
            V(lambda e: e.tensor_tensor(out=out_re, in0=t1_, in1=t2_, op=ALU.subtract), tk, keys_w[0:1])
            V(lambda e: e.tensor_tensor(out=t1_, in0=ar, in1=bi, op=ALU.mult), keys_r + keys_w[0:1], [tk[0]])
            V(lambda e: e.tensor_tensor(out=t2_, in0=ai, in1=br, op=ALU.mult), keys_r + keys_w[0:1], [tk[1]])
            V(lambda e: e.tensor_tensor(out=out_im, in0=t1_, in1=t2_, op=ALU.add), tk, keys_w[1:2])

        def recA(ti, par):
            u5 = u5_[par]; z5s = z5s_[par]; gbs = gbs_[par]; yaT = yaT_[par]
            smp = (ti == 16)
            lastp = (ti == 15)
            xsrc = I["xs"] if smp else I["xp"][128 * ti:128 * ti + 128, :]
            ydst = O["y_s"] if smp else O["y_p"][128 * ti:128 * ti + 128, :]
            negm = negm_s if smp else negm_p
            rmk = rm_s if smp else rm_p
            nseq, L = (16, 8) if smp else (1, 128)
            DMA(lambda e: e.dma_start(out=xt[:], in_=xsrc), w=["xt"])
            G(lambda e: e.memset(ss[:], 0.0), w=["ss"])
            A(lambda e: e.activation(out=xsn[:], in_=xt[:], func=AF.Square, accum_out=ss[:]), ["xt", "ss"], ["xsn", "ss"])
            A(lambda e: e.activation(out=rstd[:], in_=ss[:], func=AF.Sqrt, scale=1.0 / D, bias=epscol[:]), ["ss", "epscol"], ["rstd"])
            V(lambda e: e.reciprocal(out=rstd[:], in_=rstd[:]), ["rstd"], ["rstd"])
            V(lambda e: e.tensor_scalar(out=xsn[:], in0=xt[:], scalar1=rstd[:, 0:1], scalar2=None, op0=ALU.mult), ["xt", "rstd"], ["xsn"])
            for j in range(8):
                T(lambda e, j=j: e.transpose(out=pbb[5][:, 128 * j:128 * j + 128], in_=xsn[:, 128 * j:128 * j + 128], identity=identb[:]), ["xsn", "identb"], ["pb5"])
            for j in range(8):
                V(lambda e, j=j: e.tensor_scalar(out=hT[:, j, :], in0=pbb[5][:, 128 * j:128 * j + 128], scalar1=normw[:, j:j + 1], scalar2=None, op0=ALU.mult), ["pb5", "normw"], ["hT"])
            blocks = [("xbc", c0, 256) for c0 in range(XBC0, DT0, 256)] + [("dt", DT0, 32)]
            blocks += [("z", c0, 256) for c0 in range(Z0, XBC0, 256)]
            for nm, b0 in (("u5", U0), ("z5", Z50), ("ga", GA0), ("gb", GB0)):
                blocks += [(nm, c0, 256) for c0 in range(b0, b0 + 1024, 256)]
            base = {"xbc": XBC0, "z": Z0, "u5": U0, "z5": Z50, "ga": GA0, "gb": GB0}
            deferred = []
            for nm, c0, nco in blocks:
                wt, wk = wload(WS["w_in"][:, c0:c0 + nco])
                if nm == "dt":
                    while deferred:
                        deferred.pop(0)()
                    ps, pk = proj_ft(wt, wk, 0, 32)
                    A(lambda e, ps=ps: e.activation(out=dts[:], in_=ps[0:32, :], func=AF.Exp, bias=dtb[:]), [pk, "dtb"], ["dts"])
                    A(lambda e: e.activation(out=dts[:], in_=dts[:], func=AF.Ln, bias=onecol[0:32, :]), ["dts", "onecol"], ["dts"])
                    continue
                for s in range(2):
                    ft = (c0 - base[nm]) // 128 + s
                    ps, pk = proj_ft(wt, wk, 128 * s)
                    if nm == "z":
                        A(lambda e, ps=ps, ft=ft: e.activation(out=zs[:, ft, :], in_=ps, func=AF.Silu), [pk], [f"zs{ft}"])
                    elif nm == "u5":
                        A(lambda e, ps=ps, ft=ft: e.activation(out=u5[:, ft, :], in_=ps, func=AF.Copy), [pk], [f"u5_{par}_{ft}"])
                    elif nm == "z5":
                        A(lambda e, ps=ps, ft=ft: e.activation(out=z5s[:, ft, :], in_=ps, func=AF.Silu), [pk], [f"z5_{par}_{ft}"])
                    elif nm == "ga":
                        A(lambda e, ps=ps, ft=ft: e.activation(out=gas[:, ft, :], in_=ps, func=AF.Sigmoid), [pk], [f"ga{ft}"])
                    elif nm == "gb":
                        A(lambda e, ps=ps, ft=ft: e.activation(out=gbs[:, ft, :], in_=ps, func=AF.Sigmoid), [pk], [f"gb_{par}_{ft}"])
                    else:
                        r_ = raw[ft % 2]; rk = f"raw{ft % 2}"; ac = acc[ft % 2]; ak = f"acc{ft % 2}"
                        EV = V
                        if smp:
                            rv = r_[:, 0:176].rearrange("p (s r) -> p s r", r=11)
                            if ft % 8 == 0:
                                DMA(lambda e, ft=ft: e.dma_start(out=cvo[:], in_=I["conv0"].rearrange("s r c -> (s r) c")[:, 128 * ft:128 * ft + 1024]), w=["cvo"])
                            T(lambda e, ft=ft: e.transpose(out=pb[6][:, 0:48], in_=cvo[:, 128 * (ft % 8):128 * (ft % 8) + 128], identity=identf[0:48, 0:48]), ["cvo", "identf"], ["pb6"])
                            V(lambda e, rv=rv: e.tensor_copy(out=rv[:, :, 0:3], in_=pb[6][:, 0:48].rearrange("p (s r) -> p s r", r=3)), ["pb6"], [rk])
                        else:
                            rv = r_[:, 0:131].unsqueeze(1)
                            A(lambda e, rv=rv, ft=ft: e.activation(out=rv[:, 0, 0:3], in_=halo[:, ft, :], func=AF.Copy), ["halo"], [rk])
                        A(lambda e, rv=rv, ps=ps: e.activation(out=rv[:, :, 3:3 + L], in_=ps.rearrange("p (s l) -> p s l", l=L), func=AF.Copy), [pk], [rk])
                        while deferred:
                            deferred.pop(0)()
                        av = ac[:].rearrange("p (s l) -> p s l", l=L)
                        EV(lambda e, rv=rv, av=av, ft=ft: e.tensor_scalar(out=av, in0=rv[:, :, 0:L], scalar1=convw[:, ft, 0:1], scalar2=convb[:, ft:ft + 1], op0=ALU.mult, op1=ALU.add), [rk, "convw", "convb"], [ak])
                        for k in range(1, 4):
                            EV(lambda e, rv=rv, av=av, ft=ft, k=k: e.scalar_tensor_tensor(out=av, in0=rv[:, :, k:k + L], scalar=convw[:, ft, k:k + 1], in1=av, op0=ALU.mult, op1=ALU.add), [rk, ak, "convw"], [ak])
                        def late(ac=ac, ft=ft, rv=rv, ak=ak, rk=rk):
                            A(lambda e: e.activation(out=xc[:, ft, :], in_=ac[:], func=AF.Silu), [ak], [f"xc{ft}"])
                            if not smp:
                                A(lambda e: e.activation(out=halo[:, ft, :], in_=rv[:, 0, 128:131], func=AF.Copy), [rk], ["halo"])
                        deferred.append(late)
                        if smp or lastp:
                            nr = 3 * nseq
                            EV(lambda e, rv=rv, nr=nr: e.tensor_copy(out=cst3[:, 0:nr].rearrange("p (s r) -> p s r", r=3), in_=rv[:, :, L:L + 3]), [rk], ["cst3"])
                            T(lambda e, nr=nr: e.transpose(out=pb[6][0:nr, 128:256], in_=cst3[:, 0:nr], identity=identf[:]), ["cst3", "identf"], ["pb6"])
                            V(lambda e, ft=ft, nr=nr: e.tensor_copy(out=cvo[0:nr, 128 * (ft % 8):128 * (ft % 8) + 128], in_=pb[6][0:nr, 128:256]), ["pb6"], ["cvo"])
                            if ft % 8 == 7:
                                cdst = (O["conv_s"].rearrange("s r c -> (s r) c") if smp else O["conv_p"])[:, 128 * (ft - 7):128 * (ft - 7) + 1024]
                                DMAO(lambda e, cdst=cdst, nr=nr: e.dma_start(out=cdst, in_=cvo[0:nr, :]), ["cvo"], [])
            V(lambda e: e.tensor_scalar(out=dAs[:], in0=dts[:], scalar1=aneg[:, 0:1], scalar2=None, op0=ALU.mult), ["dts", "aneg"], ["dAs"])
            V(lambda e: e.tensor_tensor_scan(out=acs[:], data0=rmk[:], data1=dAs[:], initial=0.0, op0=ALU.mult, op1=ALU.add), [kn(rmk), "dAs"], ["acs"])
            a3 = acs[:].rearrange("h (s l) -> h s l", l=L)
            V(lambda e: e.tensor_tensor(out=dec[:].rearrange("h (s l) -> h s l", l=L), in0=a3[:, :, L - 1:L].broadcast_to([32, nseq, L]), in1=a3, op=ALU.subtract), ["acs"], ["dec"])
            A(lambda e: e.activation(out=dec[:], in_=dec[:], func=AF.Exp), ["dec"], ["dec"])
            V(lambda e: e.tensor_tensor(out=wdt[:], in0=dts[:], in1=dec[:], op=ALU.mult), ["dts", "dec"], ["wdt"])
            for i_, src in enumerate((acs, dts, wdt)):
                T(lambda e, i_=i_, src=src: e.transpose(out=pb[6][:, 256 + 32 * i_:256 + 32 * i_ + 32], in_=src[:], identity=identf[0:32, 0:32]), [kn(src), "identf"], ["pb6"])
            V(lambda e: e.tensor_copy(out=tokm[:].rearrange("p a b -> p (a b)"), in_=pb[6][:, 256:352]), ["pb6"], ["tokm"])
            for half in range(2):
                bk = f"pb{2 + half}"
                for j in range(8):
                    ft = 8 * half + j
                    T(lambda e, ft=ft, j=j, half=half: e.transpose(out=pbb[2 + half][:, 128 * j:128 * j + 128], in_=xc[:, ft, :], identity=identb[:]), [f"xc{ft}", "identb"], [bk])
                pv = pbb[2 + half][:, :].rearrange("p (a h2 q) -> p a h2 q", h2=2, q=64)
                prs = slice(8 * half, 8 * half + 8)
                dtT = tokm[:, 1, 16 * half:16 * half + 16].rearrange("p (a h2) -> p a h2", h2=2)
                wT = tokm[:, 2, 16 * half:16 * half + 16]
                V(lambda e, pv=pv, prs=prs, dtT=dtT: e.tensor_tensor(out=XA[:, prs, 0:64], in0=pv[:, :, 0, :], in1=dtT[:, :, 0:1].broadcast_to([128, 8, 64]), op=ALU.mult), [bk, "tokm"], ["XA"])
                V(lambda e, pv=pv, prs=prs, dtT=dtT: e.tensor_tensor(out=XB[:, prs, 64:128], in0=pv[:, :, 1, :], in1=dtT[:, :, 1:2].broadcast_to([128, 8, 64]), op=ALU.mult), [bk, "tokm"], ["XB"])
                V(lambda e, half=half, wT=wT: e.tensor_tensor(out=Xd[:, 16 * half:16 * half + 16, :], in0=pbb[2 + half][:, :].rearrange("p (h q) -> p h q", q=64), in1=wT.unsqueeze(2).broadcast_to([128, 16, 64]), op=ALU.mult), [bk, "tokm"], ["Xd"])
            for j in range(8):
                T(lambda e, j=j: e.transpose(out=pbb[6][:, 128 * j:128 * j + 128], in_=xc[:, 16 + j, :], identity=identb[:]), [f"xc{16 + j}", "identb"], ["pb6"])
            A(lambda e: e.activation(out=Btok[:].rearrange("p a b -> p (a b)"), in_=pbb[6][:, :], func=AF.Copy), ["pb6"], ["Btok"])
            def yevac(gq):
                b2 = gq % 2
                yv = yv2[b2]; ysq = ysq2[b2]; rstg = rstg2[b2]
                for j in range(2):
                    pr = 2 * gq + j
                    V(lambda e, pr=pr, j=j, yv=yv: e.scalar_tensor_tensor(out=yv[:, j, :], in0=xc[:, pr, :], scalar=dexp[:, pr:pr + 1], in1=yps(pr), op0=ALU.mult, op1=ALU.add), [f"xc{pr}", "dexp", ypk(pr)], [f"yv{b2}"])
                    V(lambda e, pr=pr, j=j, yv=yv: e.tensor_tensor(out=yv[:, j, :], in0=yv[:, j, :], in1=zs[:, pr, :], op=ALU.mult), [f"yv{b2}", f"zs{pr}"], [f"yv{b2}"])
                    A(lambda e, j=j, yv=yv, ysq=ysq: e.activation(out=ysq[:, j, :], in_=yv[:, j, :], func=AF.Square), [f"yv{b2}"], [f"ysq{b2}"])
                ps = pb[6 + b2][:, 0:128]
                pk = f"pb{6 + b2}"
                for j in range(2):
                    T(lambda e, ps=ps, j=j, ysq=ysq: e.matmul(ps, lhsT=onesb[:], rhs=ysq[:, j, :], start=(j == 0), stop=(j == 1)), ["onesb", f"ysq{b2}"], [pk])
                A(lambda e, ps=ps, rstg=rstg: e.activation(out=rstg[:], in_=ps, func=AF.Sqrt, scale=1.0 / 256, bias=epscol[:]), [pk, "epscol"], [f"rstg{b2}"])
                V(lambda e, rstg=rstg: e.reciprocal(out=rstg[:], in_=rstg[:]), [f"rstg{b2}"], [f"rstg{b2}"])
                for j in range(2):
                    pr = 2 * gq + j
                    V(lambda e, pr=pr, j=j, yv=yv, rstg=rstg: e.scalar_tensor_tensor(out=ynT[:, pr, :], in0=yv[:, j, :], scalar=ssdnw[:, pr:pr + 1], in1=rstg[:], op0=ALU.mult, op1=ALU.mult), [f"yv{b2}", "ssdnw", f"rstg{b2}"], [f"ynT{pr}"])
            it = 0
            for hf in ((0, 1) if smp else (None,)):
                grp_range = range(8) if hf is None else range(4 * hf, 4 * hf + 4)
                q4_range = range(1) if hf is None else range(2 * hf, 2 * hf + 2)
                if smp:
                    ypk = lambda pr: f"pb{2 + (pr % 8) // 4}"
                    yps = lambda pr: pb[2 + (pr % 8) // 4][:, 128 * (pr % 4):128 * (pr % 4) + 128]
                    yfirst = lambda pr: pr % 4 == 0
                else:
                    ypk = lambda pr: f"pb{2 + (pr // 2) % 2}"
                    yps = lambda pr: pb[2 + (pr // 2) % 2][:, 128 * (pr % 2):128 * (pr % 2) + 128]
                    yfirst = lambda pr: pr % 2 == 0
                for g in grp_range:
                    V(lambda e, g=g: e.tensor_tensor(out=rhsR[:], in0=acs[:].unsqueeze(1).broadcast_to([32, 4, 128]), in1=identf[0:32, 4 * g:4 * g + 4].unsqueeze(2).broadcast_to([32, 4, 128]), op=ALU.mult), ["acs", "identf"], ["rhsR"])
                    T(lambda e: e.matmul(pb[6][:, :], lhsT=ones32[:], rhs=rhsR[:].rearrange("p a b -> p (a b)"), start=True, stop=True), ["ones32", "rhsR"], ["pb6"])
                    A(lambda e: e.activation(out=Eg[:].rearrange("p a b -> p (a b)"), in_=pb[6][:, :], func=AF.Exp), ["pb6"], ["Eg"])
                    G(lambda e, g=g: e.tensor_tensor(out=CEall[:, 4 * g:4 * g + 4, :], in0=Eg[:], in1=xc[:, 24 + g, :].unsqueeze(1).broadcast_to([128, 4, 128]), op=ALU.mult), ["Eg", f"xc{24 + g}"], ["CEall"])
                    for h4 in range(4):
                        V(lambda e, g=g, h4=h4: e.scalar_tensor_tensor(out=Dm[:, h4, :], in0=pb[6][:, 128 * h4:128 * h4 + 128], scalar=tokm[:, 0, 4 * g + h4:4 * g + h4 + 1], in1=negm[:], op0=ALU.subtract, op1=ALU.min), ["pb6", "tokm", kn(negm)], ["Dm"])
                    A(lambda e: e.activation(out=Lg[:].rearrange("p a b -> p (a b)"), in_=Dm[:].rearrange("p a b -> p (a b)"), func=AF.Exp), ["Dm"], ["Lg"])
                    T(lambda e, g=g: e.matmul(pb[7][:, 0:128], lhsT=xc[:, 16 + g, :], rhs=xc[:, 24 + g, :], start=True, stop=True), [f"xc{16 + g}", f"xc{24 + g}"], ["pb7"])
                    V(lambda e: e.tensor_tensor(out=Mg[:], in0=Lg[:], in1=pb[7][:, 0:128].unsqueeze(1).broadcast_to([128, 4, 128]), op=ALU.mult), ["Lg", "pb7"], ["Mg"])
                    V(lambda e, g=g: e.tensor_copy(out=dch[:, 4 * g:4 * g + 4, 0:nseq], in_=Eg[:].rearrange("p a (s l) -> p a s l", l=L)[:, :, :, L - 1]), ["Eg"], ["dch"])
                    for j in range(2):
                        pr = 2 * g + j
                        T(lambda e, pr=pr, j=j: e.matmul(yps(pr), lhsT=XA[:, pr, :], rhs=Mg[:, 2 * j, :], start=yfirst(pr), stop=False, skip_group_check=True), ["XA", "Mg"], [ypk(pr)])
                        T(lambda e, pr=pr, j=j: e.matmul(yps(pr), lhsT=XB[:, pr, :], rhs=Mg[:, 2 * j + 1, :], start=False, stop=False, skip_group_check=True), ["XB", "Mg"], [ypk(pr)])
                    if not smp:
                        sak_ = [f"SA{i}" for i in range(4)] + [f"SB{i}" for i in range(4)]
                        for j in range(2):
                            pr = 2 * g + j
                            T(lambda e, pr=pr: e.matmul(yps(pr), lhsT=SA[:, pr, :], rhs=CEall[:, 2 * pr, :], start=False, stop=False, skip_group_check=True), sak_ + ["CEall"], [ypk(pr)])
                            T(lambda e, pr=pr, j=j: e.matmul(yps(pr), lhsT=SB[:, pr, :], rhs=CEall[:, 2 * pr + 1, :], start=False, stop=(j == 1), skip_group_check=True), sak_ + ["CEall"], [ypk(pr)])
                        yevac(g)
                for s in range(nseq if smp else 0):
                    cs = slice(L * s, L * s + L)
                    last = (s == nseq - 1)
                    if smp:
                        Bm = Bms[0]; bmk = "Bm0"
                        G(lambda e, s=s, Bm=Bm: e.tensor_scalar(out=Bm[:].rearrange("p a b -> p (a b)"), in0=Btok[:].rearrange("p a b -> p (a b)"), scalar1=seqmask[:, s:s + 1], scalar2=None, op0=ALU.mult), ["Btok", "seqmask"], [bmk])
                    for q4 in q4_range:
                        prs_ = range(4 * q4, 4 * q4 + 4) if smp else range(16)
                        sak = [f"SA{q4}", f"SB{q4}"] if smp else [f"SA{i}" for i in range(4)] + [f"SB{i}" for i in range(4)]
                        if smp:
                            S0q = S0qs[it % 2]; s0k = f"S0q{it % 2}"; Snq = Snqs[it % 2]; snk = f"Snq{it % 2}"
                            tb_ = 7 if it % 2 == 0 else 0
                            nb_ = 6 if it % 2 == 0 else 1
                            it += 1
                            sview = I["ssm0"][s].rearrange("(pr h2) p n -> (h2 p) pr n", h2=2)[:, 4 * q4:4 * q4 + 4, :]
                            DMA(lambda e, sview=sview, S0q=S0q: e.dma_start(out=S0q[:], in_=sview), w=[s0k])
                            for j in range(4):
                                T(lambda e, j=j, S0q=S0q, tb_=tb_: e.transpose(out=pb[tb_][:, 128 * j:128 * j + 128], in_=S0q[:, j, :], identity=identf[:]), [s0k, "identf"], [f"pb{tb_}"])
                            pv = pb[tb_][:, :].rearrange("p (a h2 q) -> p a h2 q", h2=2, q=64)
                            A(lambda e, q4=q4, pv=pv: e.activation(out=SA[:, 4 * q4:4 * q4 + 4, 0:64], in_=pv[:, :, 0, :], func=AF.Copy), [f"pb{tb_}"], [f"SA{q4}"])
                            A(lambda e, q4=q4, pv=pv: e.activation(out=SB[:, 4 * q4:4 * q4 + 4, 64:128], in_=pv[:, :, 1, :], func=AF.Copy), [f"pb{tb_}"], [f"SB{q4}"])
                        for pr in prs_:
                            T(lambda e, pr=pr, cs=cs: e.matmul(yps(pr)[:, cs], lhsT=SA[:, pr, :], rhs=CEall[:, 2 * pr, cs], start=False, stop=False, skip_group_check=True), sak + ["CEall"], [ypk(pr)])
                            T(lambda e, pr=pr, cs=cs, last=last: e.matmul(yps(pr)[:, cs], lhsT=SB[:, pr, :], rhs=CEall[:, 2 * pr + 1, cs], start=False, stop=last, skip_group_check=True), sak + ["CEall"], [ypk(pr)])
                        if smp:
                            for j in range(4):
                                pr = 4 * q4 + j
                                T(lambda e, pr=pr, j=j, Bm=Bm, nb_=nb_: e.matmul(pb[nb_][:, 128 * j:128 * j + 128], lhsT=Xd[:, 2 * pr:2 * pr + 2, :].rearrange("p a b -> p (a b)"), rhs=Bm[:, pr // 2, :], start=True, stop=True), ["Xd", bmk], [f"pb{nb_}"])
                            for j in range(4):
                                pr = 4 * q4 + j
                                for h2 in range(2):
                                    hs = slice(64 * h2, 64 * h2 + 64)
                                    V(lambda e, pr=pr, j=j, hs=hs, s=s, h2=h2, S0q=S0q, Snq=Snq, nb_=nb_: e.scalar_tensor_tensor(out=Snq[hs, j, :], in0=S0q[hs, j, :], scalar=dch[hs, 2 * pr + h2, s:s + 1], in1=pb[nb_][hs, 128 * j:128 * j + 128], op0=ALU.mult, op1=ALU.add), [s0k, "dch", f"pb{nb_}"], [snk])
                            oview = O["ssm_s"][s].rearrange("(pr h2) p n -> (h2 p) pr n", h2=2)[:, 4 * q4:4 * q4 + 4, :]
                            DMAO(lambda e, oview=oview, Snq=Snq: e.dma_start(out=oview, in_=Snq[:]), [snk], [])
                if smp:
                    for gq in grp_range:
                        yevac(gq)
            if smp:
                sv = STt[:].rearrange("p (pr h2) q -> p pr h2 q", h2=2)
                A(lambda e, sv=sv: e.activation(out=SA[:, :, 0:64], in_=sv[:, :, 0, :], func=AF.Copy), ["STt"], [f"SA{i}" for i in range(4)])
                A(lambda e, sv=sv: e.activation(out=SB[:, :, 64:128], in_=sv[:, :, 1, :], func=AF.Copy), ["STt"], [f"SB{i}" for i in range(4)])
            if not smp:
                V(lambda e: e.tensor_tensor(out=STt[:], in0=STt[:], in1=dch[:, :, 0:1].broadcast_to([128, 32, 64]), op=ALU.mult), ["STt", "dch"], ["STt"])
                for rnd in range(2):
                    for g in range(4 * rnd, 4 * rnd + 4):
                        bk_ = 2 + (g % 4) // 2
                        T(lambda e, g=g, bk_=bk_: e.matmul(pb[bk_][:, 256 * (g % 2):256 * (g % 2) + 256], lhsT=Btok[:, g, :], rhs=Xd[:, 4 * g:4 * g + 4, :].rearrange("p a b -> p (a b)"), start=True, stop=True), ["Btok", "Xd"], [f"pb{bk_}"])
                    for b2_ in range(2):
                        h0 = 16 * rnd + 8 * b2_
                        V(lambda e, h0=h0, b2_=b2_: e.tensor_tensor(out=STt[:, h0:h0 + 8, :], in0=STt[:, h0:h0 + 8, :], in1=pb[2 + b2_][:, :].rearrange("p (h q) -> p h q", q=64), op=ALU.add), ["STt", f"pb{2 + b2_}"], ["STt"])
                sv = STt[:].rearrange("p (pr h2) q -> p pr h2 q", h2=2)
                A(lambda e, sv=sv: e.activation(out=SA[:, :, 0:64], in_=sv[:, :, 0, :], func=AF.Copy), ["STt"], [f"SA{i}" for i in range(4)])
                A(lambda e, sv=sv: e.activation(out=SB[:, :, 64:128], in_=sv[:, :, 1, :], func=AF.Copy), ["STt"], [f"SB{i}" for i in range(4)])
                if lastp:
                    for q4 in range(4):
                        for j in range(4):
                            pr = 4 * q4 + j
                            T(lambda e, pr=pr, j=j: e.transpose(out=pb[6][:, 128 * j:128 * j + 128], in_=STt[:, 2 * pr:2 * pr + 2, :].rearrange("p a b -> p (a b)"), identity=identf[:]), ["STt", "identf"], ["pb6"])
                        Snq = Snqs[q4 % 2]; snk = f"Snq{q4 % 2}"
                        V(lambda e, Snq=Snq: e.tensor_copy(out=Snq[:].rearrange("p a b -> p (a b)"), in_=pb[6][:, :]), ["pb6"], [snk])
                        oview = O["ssm_p"].rearrange("(pr h2) p n -> (h2 p) pr n", h2=2)[:, 4 * q4:4 * q4 + 4, :]
                        DMAO(lambda e, oview=oview, Snq=Snq: e.dma_start(out=oview, in_=Snq[:]), [snk], [])
            dense_T("w_down_ssd", 16, lambda k: ynT[:, k, :], lambda k: [f"ynT{k}"],
                    lambda m, ps, pk: V(lambda e: e.tensor_tensor(out=yaT[:, m, :], in0=ps, in1=gas[:, m, :], op=ALU.mult), [pk, f"ga{m}"], [f"yaT_{par}_{m}"]))


        def recB(ti, par):
            u5 = u5_[par]; z5s = z5s_[par]; gbs = gbs_[par]; yaT = yaT_[par]
            smp = (ti == 16)
            lastp = (ti == 15)
            xsrc = I["xs"] if smp else I["xp"][128 * ti:128 * ti + 128, :]
            ydst = O["y_s"] if smp else O["y_p"][128 * ti:128 * ti + 128, :]
            if smp:
                for ri, nm in enumerate(["re0", "im0"]):
                    sv_ = I[nm].rearrange("s (gh gq g2) p -> s gq gh (g2 p)", gq=8, g2=2)
                    for s in range(16):
                        DMA(lambda e, s=s, sv_=sv_: e.dma_start(out=s5io[8 * s:8 * s + 8, :, :, :].rearrange("p a b c -> p a (b c)"), in_=sv_[s]), w=["s5io"])
                    for gh in range(4):
                        T(lambda e, gh=gh: e.transpose(out=pb[5][:, 0:128], in_=s5io[:, gh, :, :].rearrange("p a b -> p (a b)"), identity=identf[:]), ["s5io", "identf"], ["pb5"])
                        V(lambda e, gh=gh, ri=ri: e.tensor_copy(out=s0all[:, 8 * gh:8 * gh + 8, ri, :], in_=pb[5][:, 0:128].rearrange("p (s g) -> p g s", g=8)), ["pb5"], ["s0all"])
            Ls, nsg = (8, 16) if smp else (64, 2)
            for ft in range(8):
                gsl = slice(4 * ft, 4 * ft + 4)
                ecv = ecos[:, gsl, 0:Ls].unsqueeze(2).broadcast_to([128, 4, nsg, Ls])
                esv = esin[:, gsl, 0:Ls].unsqueeze(2).broadcast_to([128, 4, nsg, Ls])
                v4 = lambda ap: ap.rearrange("p q (s l) -> p q s l", l=Ls)
                for q in range(4):
                    A(lambda e, ft=ft, q=q: e.activation(out=u5m[:, q, :], in_=u5[:, ft, :], func=AF.Copy, scale=pairmask[:, q:q + 1]), [f"u5_{par}_{ft}", "pairmask"], ["u5m"])
                for ri in range(2):
                    pbk = f"pb{4 + ri}"
                    for q in range(4):
                        T(lambda e, ft=ft, q=q, ri=ri: e.matmul(pb[4 + ri][:, 128 * q:128 * q + 128], lhsT=wb5[:, ft, ri, :], rhs=u5m[:, q, :], start=True, stop=True), ["wb5", "u5m"], [pbk])
                    A(lambda e, ri=ri: e.activation(out=bu[:, :, ri, :], in_=pb[4 + ri][:, :].rearrange("p (q t) -> p q t", t=128), func=AF.Copy), [pbk], ["bu"])
                bre_, bim_ = v4(bu[:, :, 0, :]), v4(bu[:, :, 1, :])
                V(lambda e, ecv=ecv: e.tensor_tensor(out=v4(bt1[:]), in0=ecv, in1=bre_, op=ALU.mult), ["ecos", "bu"], ["bt1"])
                V(lambda e, esv=esv: e.tensor_tensor(out=v4(bt2[:]), in0=esv, in1=bim_, op=ALU.mult), ["esin", "bu"], ["bt2"])
                V(lambda e: e.tensor_tensor(out=sgm[:, :, 0, :], in0=bt1[:], in1=bt2[:], op=ALU.add), ["bt1", "bt2"], ["sgm"])
                V(lambda e, ecv=ecv: e.tensor_tensor(out=v4(bt1[:]), in0=ecv, in1=bim_, op=ALU.mult), ["ecos", "bu", "sgm"], ["bt1"])
                V(lambda e, esv=esv: e.tensor_tensor(out=v4(bt2[:]), in0=esv, in1=bre_, op=ALU.mult), ["esin", "bu", "sgm"], ["bt2"])
                V(lambda e: e.tensor_tensor(out=sgm[:, :, 1, :], in0=bt1[:], in1=bt2[:], op=ALU.subtract), ["bt1", "bt2"], ["sgm"])
                if smp:
                    V(lambda e, gsl=gsl: e.tensor_tensor(out=rts[:], in0=rdec[:, gsl].unsqueeze(2).broadcast_to([128, 4, 128]), in1=rm128[:].unsqueeze(1).broadcast_to([128, 4, 128]), op=ALU.mult), ["rdec", "rm128"], ["rts"])
                else:
                    V(lambda e, gsl=gsl: e.tensor_copy(out=rts[:], in_=rdec[:, gsl].unsqueeze(2).broadcast_to([128, 4, 128])), ["rdec"], ["rts"])
                if smp:
                    n_ = 16
                    ab = lambda t: t[:, gsl].unsqueeze(2).broadcast_to([128, 4, n_])
                    cmul(cin_[:, :, 0, :], cin_[:, :, 1, :], ab(are), ab(aim), s0all[:, gsl, 0, :], s0all[:, gsl, 1, :],
                         ["are", "aim", "s0all"], ["cinr", "cini"], ctm[:, :, 0, :], ctm[:, :, 1, :], ["ctm0", "ctm1"])
                    for ri in range(2):
                        tgt = v4(sgm[:, :, ri, :])[:, :, :, 0]
                        V(lambda e, tgt=tgt, ri=ri: e.tensor_tensor(out=tgt, in0=tgt, in1=cin_[:, :, ri, :], op=ALU.add), ["sgm", "cinr", "cini"], ["sgm"])
                    for q in range(4):
                        for ri in range(2):
                            EV = V
                            EV(lambda e, q=q, ri=ri: e.tensor_tensor_scan(out=bu[:, q, ri, :], data0=rts[:, q, :], data1=sgm[:, q, ri, :], initial=0.0, op0=ALU.mult, op1=ALU.add), ["rts", "sgm"], ["bu"])
                else:
                    for sg in range(2):
                        c0 = 64 * sg
                        src_re = sgc[:, gsl, 0:1] if sg == 0 else bu[:, :, 0, 63:64]
                        src_im = sgc[:, gsl, 1:2] if sg == 0 else bu[:, :, 1, 63:64]
                        ab = lambda t: t[:, gsl].unsqueeze(2)
                        cmul(cin_[:, :, 0, 0:1], cin_[:, :, 1, 0:1], ab(aere), ab(aeim), src_re, src_im,
                             ["aere", "aeim", "sgc", "bu"], ["cinr", "cini"], ctm[:, :, 0, 0:1], ctm[:, :, 1, 0:1], ["ctm0", "ctm1"])
                        for ri in range(2):
                            tgt = sgm[:, :, ri, c0:c0 + 1]
                            V(lambda e, tgt=tgt, ri=ri: e.tensor_tensor(out=tgt, in0=tgt, in1=cin_[:, :, ri, 0:1], op=ALU.add), ["sgm", "cinr", "cini"], ["sgm"])
                        for q in range(4):
                            for ri in range(2):
                                EV = V
                                EV(lambda e, q=q, ri=ri, c0=c0: e.tensor_tensor_scan(out=bu[:, q, ri, c0:c0 + 64], data0=rts[:, q, c0:c0 + 64], data1=sgm[:, q, ri, c0:c0 + 64], initial=0.0, op0=ALU.mult, op1=ALU.add), ["rts", "sgm"], ["bu"])
                    V(lambda e, gsl=gsl: e.tensor_copy(out=sgc[:, gsl, :], in_=bu[:, :, :, 127]), ["bu"], ["sgc"])
                sre_, sim_ = v4(bu[:, :, 0, :]), v4(bu[:, :, 1, :])
                V(lambda e, ecv=ecv: e.tensor_tensor(out=v4(bt1[:]), in0=ecv, in1=sre_, op=ALU.mult), ["ecos", "bu"], ["bt1"])
                V(lambda e, esv=esv: e.tensor_tensor(out=v4(bt2[:]), in0=esv, in1=sim_, op=ALU.mult), ["esin", "bu"], ["bt2"])
                V(lambda e: e.tensor_tensor(out=s5s[:, :, 0, :], in0=bt1[:], in1=bt2[:], op=ALU.subtract), ["bt1", "bt2"], ["s5s"])
                if smp or lastp:
                    fsrc = lambda t: (v4(t[:])[:, :, :, Ls - 1] if smp else t[:, :, 127:128])
                    nf = 16 if smp else 1
                    V(lambda e, gsl=gsl, fsrc=fsrc, nf=nf: e.tensor_tensor(out=fin[:, gsl, 0, 0:nf], in0=fsrc(bt1), in1=fsrc(bt2), op=ALU.subtract), ["bt1", "bt2"], ["fin"])
                V(lambda e, ecv=ecv: e.tensor_tensor(out=v4(bt1[:]), in0=ecv, in1=sim_, op=ALU.mult), ["ecos", "bu", "s5s", "fin"], ["bt1"])
                V(lambda e, esv=esv: e.tensor_tensor(out=v4(bt2[:]), in0=esv, in1=sre_, op=ALU.mult), ["esin", "bu", "s5s", "fin"], ["bt2"])
                V(lambda e: e.tensor_tensor(out=s5s[:, :, 1, :], in0=bt1[:], in1=bt2[:], op=ALU.add), ["bt1", "bt2"], ["s5s"])
                if smp or lastp:
                    V(lambda e, gsl=gsl, fsrc=fsrc, nf=nf: e.tensor_tensor(out=fin[:, gsl, 1, 0:nf], in0=fsrc(bt1), in1=fsrc(bt2), op=ALU.add), ["bt1", "bt2"], ["fin"])
                ps4 = [pb[4][:, 128 * q:128 * q + 128] for q in range(4)]
                for q in range(4):
                    for ri in range(2):
                        T(lambda e, ft=ft, q=q, ri=ri: e.matmul(ps4[q], lhsT=wd5[:, ft, ri, :], rhs=s5s[:, q, ri, :], start=(ri == 0), stop=(ri == 1)), ["wd5", "s5s"], ["pb4"])
                for q in range(4):
                    rs_ = slice(32 * q, 32 * q + 32)
                    V(lambda e, ft=ft, q=q, rs_=rs_: e.scalar_tensor_tensor(out=y5t[rs_, :], in0=u5[rs_, ft, :], scalar=ds5[rs_, ft:ft + 1], in1=ps4[q][rs_, :], op0=ALU.mult, op1=ALU.add), [f"u5_{par}_{ft}", "ds5", "pb4"], ["y5t"])
                V(lambda e: e.tensor_tensor(out=y5u[:], in0=y5t[:], in1=y5t[:], op=ALU.mult), ["y5t"], ["y5u"])
                V(lambda e: e.tensor_scalar(out=y5u[:], in0=y5u[:], scalar1=0.044715, scalar2=1.0, op0=ALU.mult, op1=ALU.add), ["y5u"], ["y5u"])
                V(lambda e: e.tensor_tensor(out=y5u[:], in0=y5u[:], in1=y5t[:], op=ALU.mult), ["y5u", "y5t"], ["y5u"])
                A(lambda e: e.activation(out=y5v[:], in_=y5u[:], func=AF.Sigmoid, scale=1.5957691216057308), ["y5u"], ["y5v"])
                V(lambda e, ft=ft: e.tensor_tensor(out=y5g[:, ft, :], in0=y5v[:], in1=y5t[:], op=ALU.mult), ["y5v", "y5t"], [f"y5g{ft}"])
            if lastp:
                for ri, nm in enumerate(["re_p", "im_p"]):
                    DMAO(lambda e, ri=ri, nm=nm: e.dma_start(out=O[nm].rearrange("(gp g2) p -> (g2 p) gp", g2=2), in_=fin[:, :, ri, 0], allow_slow_non_contiguous=True), ["fin"], [])
            if smp:
                for ri, nm in enumerate(["re_s", "im_s"]):
                    for gh in range(4):
                        V(lambda e, gh=gh, ri=ri: e.tensor_copy(out=s5tr[:], in_=fin[:, 8 * gh:8 * gh + 8, ri, :].rearrange("p g s -> p s g")), ["fin"], ["s5tr"])
                        T(lambda e: e.transpose(out=pb[5][:, 0:128], in_=s5tr[:].rearrange("p a b -> p (a b)"), identity=identf[:]), ["s5tr", "identf"], ["pb5"])
                        V(lambda e, gh=gh: e.tensor_copy(out=s5io[:, gh, :, :].rearrange("p a b -> p (a b)"), in_=pb[5][:, 0:128]), ["pb5"], ["s5io"])
                    ov = O[nm].rearrange("s (gh gq g2) p -> s gq gh (g2 p)", gq=8, g2=2)
                    for s in range(16):
                        DMAO(lambda e, s=s, ov=ov: e.dma_start(out=ov[s], in_=s5io[8 * s:8 * s + 8, :, :, :].rearrange("p a b c -> p a (b c)")), ["s5io"], [])

            def glu_evac(m, ps, pk):
                A(lambda e: e.activation(out=glus[:], in_=ps, func=AF.Sigmoid, bias=bglu[:, m:m + 1]), [pk, "bglu"], ["glus"])
                V(lambda e: e.tensor_tensor(out=glus[:], in0=glus[:], in1=y5g[:, m, :], op=ALU.mult), ["glus", f"y5g{m}"], ["glus"])
                V(lambda e: e.tensor_tensor(out=y5f[:, m, :], in0=glus[:], in1=z5s[:, m, :], op=ALU.mult), ["glus", f"z5_{par}_{m}"], [f"y5f{m}"])
            dense_T("w_glu", 8, lambda k: y5g[:, k, :], lambda k: [f"y5g{k}"], glu_evac)
            dense_T("w_down_s5", 8, lambda k: y5f[:, k, :], lambda k: [f"y5f{k}"],
                    lambda m, ps, pk: V(lambda e: e.tensor_tensor(out=ybT[:, m, :], in0=ps, in1=gbs[:, m, :], op=ALU.mult), [pk, f"gb_{par}_{m}"], [f"ybT{m}"]))
            for m in range(8):
                V(lambda e, m=m: e.tensor_tensor(out=mixT[:, m, :], in0=yaT[:, m, :], in1=ybT[:, m, :], op=ALU.add), [f"yaT_{par}_{m}", f"ybT{m}"], [f"mixT{m}"])
            DMA(lambda e: e.dma_start(out=ot[:], in_=xsrc), w=["ot"])
            for cb in range(4):
                wt, wk = wload(WS["w_out"][:, 256 * cb:256 * cb + 256])
                hb = cb % 2
                ps = pb[4 + hb][:, 0:256]
                pks = [f"pb{4 + hb}"]
                for k in range(8):
                    T(lambda e, wt=wt, k=k, ps=ps: e.matmul(ps, lhsT=mixT[:, k, :], rhs=wt[:, k, 0:256], start=(k == 0), stop=(k == 7)), [wk, f"mixT{k}"], pks)
                V(lambda e, cb=cb, ps=ps: e.tensor_tensor(out=ot[:, 256 * cb:256 * cb + 256], in0=ps, in1=ot[:, 256 * cb:256 * cb + 256], op=ALU.add), pks + ["ot"], ["ot"])
            G(lambda e: e.memset(ssB[:], 0.0), w=["ssB"])
            A(lambda e: e.activation(out=y5f[:].rearrange("p a b -> p (a b)"), in_=ot[:], func=AF.Square, accum_out=ssB[:]), ["ot", "ssB"], [f"y5f{m}" for m in range(8)] + ["ssB"])
            A(lambda e: e.activation(out=rstdB[:], in_=ssB[:], func=AF.Sqrt, scale=1.0 / D, bias=epscol[:]), ["ssB", "epscol"], ["rstdB"])
            V(lambda e: e.reciprocal(out=rstdB[:], in_=rstdB[:]), ["rstdB"], ["rstdB"])
            V(lambda e: e.scalar_tensor_tensor(out=ot[:], in0=ot[:], scalar=rstdB[:, 0:1], in1=fnw[:], op0=ALU.mult, op1=ALU.mult), ["ot", "rstdB", "fnw"], ["ot"])
            DMAO(lambda e: e.dma_start(out=ydst, in_=ot[:]), ["ot"], [])


        def flush(lst):
            for (eng, fn, r, w, dma) in lst:
                P.op(eng, fn, r, w, dma=dma)

        def record(fn_, ti, par, banks, wset):
            CUR[0] = []
            MMB[0] = banks
            WSET[0] = wset
            fn_(ti, par)
            lst = CUR[0]
            CUR[0] = None
            return lst

        def merge(la, lb):
            out = []
            ia = ib = 0
            na, nb = len(la), len(lb)
            while ia < na or ib < nb:
                if ib >= nb or (ia < na and ia * nb <= ib * na):
                    out.append(la[ia]); ia += 1
                else:
                    out.append(lb[ib]); ib += 1
            return out

        tl = list(tiles) if tiles is not None else (list(range(8)) + [16] + list(range(8, 16)))
        if tl:
            flush(record(recA, tl[0], 0, (0, 1), (0, 1)))
            for n_, ti in enumerate(tl):
                lb = record(recB, ti, n_ % 2, (4, 5), (2, 3))
                if n_ + 1 < len(tl):
                    la = record(recA, tl[n_ + 1], (n_ + 1) % 2, (0, 1), (0, 1))
                    if not INTERLEAVE[0]:
                        flush(lb); flush(la)
                    else:
                        flush(merge(la, lb))
                else:
                    flush(lb)
        P.emit(nc)
    return nc


_NC = {}


def _shard(inputs, c):
    d = {}
    d["xp"] = np.ascontiguousarray(inputs["x_prompt"][c])
    d["xs"] = np.ascontiguousarray(inputs["x_sample"][16 * c:16 * c + 16].reshape(128, 1024))
    d["conv0"] = np.ascontiguousarray(inputs["state_conv"][0, 16 * c:16 * c + 16])
    d["ssm0"] = np.ascontiguousarray(inputs["state_ssm"][0, 16 * c:16 * c + 16])
    d["re0"] = np.ascontiguousarray(inputs["state_s5_re"][0, 16 * c:16 * c + 16])
    d["im0"] = np.ascontiguousarray(inputs["state_s5_im"][0, 16 * c:16 * c + 16])
    for k in ("norm_w", "w_in", "conv_w", "conv_b", "dt_bias", "A_log", "D_ssd", "ssd_norm_w", "w_down_ssd", "lam_re",
              "lam_im", "log_dt", "B_re", "B_im", "C_re", "C_im", "D_s5", "w_glu", "b_glu", "w_down_s5", "w_out"):
        d[k] = np.ascontiguousarray(inputs[k][0])
    d["final_norm_w"] = np.ascontiguousarray(inputs["final_norm_w"])
    return d


def kernel(**inputs):
    inputs = {k: np.asarray(v, dtype=np.float32) for k, v in inputs.items()}
    if "nc" not in _NC:
        _NC["nc"] = build()
    nc = _NC["nc"]
    consts = host_consts()
    in_maps = []
    for c in range(NCORES):
        d = _shard(inputs, c)
        d.update(consts)
        in_maps.append(d)
    res = run_bass_kernel_spmd(nc, in_maps, core_ids=list(range(NCORES))).results
    cat = lambda n: np.concatenate([r[n] for r in res], axis=0)
    y_p = np.stack([r["y_p"] for r in res], 0)
    y_s = cat("y_s").reshape(128, 8, 1024)
    conv_p = np.stack([r["conv_p"] for r in res], 0)[None]
    ssm_p = np.stack([r["ssm_p"] for r in res], 0)[None]
    re_p = np.stack([r["re_p"] for r in res], 0)[None]
    im_p = np.stack([r["im_p"] for r in res], 0)[None]
    conv_s = cat("conv_s")[None]
    ssm_s = cat("ssm_s")[None]
    re_s = cat("re_s")[None]
    im_s = cat("im_s")[None]
    return tuple(np.ascontiguousarray(a, dtype=np.float32) for a in
                 (y_p, y_s, conv_p, ssm_p, re_p, im_p, conv_s, ssm_s, re_s, im_s))
```

```python
import contextlib
import math
import numpy as np
import concourse.bass as bass
import concourse.mybir as mybir
from concourse.bass_utils import run_bass_kernel_spmd

F32 = mybir.dt.float32
BF16 = mybir.dt.bfloat16
I32 = mybir.dt.int32
ALU = mybir.AluOpType
AF = mybir.ActivationFunctionType

ENGS = ("pe", "act", "dve", "pool", "sp")
N_DMA_SEM = 24
NCORES = 8
D = 1024
PROJ = 10272
Z0, XBC0, DT0, U0, Z50, GA0, GB0 = 0, 2048, 6144, 6176, 7200, 8224, 9248
EPS = 1e-6
SIN_SCALE = 6.28318


class Prog:
    def __init__(self):
        self.ops = []
        self.last_write = {}
        self.readers = {}
        self.dma_count = {e: 0 for e in ENGS}
        self.slot_last = {}
        self.last_op = {}
        self.pending = {}

    def barrier(self):
        deps = set(self.last_op.values()) | set(self.slot_last.values())
        for e in ENGS:
            self.pending[e] = set(deps) | self.pending.get(e, set())

    def op(self, eng, fn, reads=(), writes=(), dma=False):
        writes = list(writes) + [k for k in reads if k.startswith("pb")]
        reads = [k for k in reads if not k.startswith("pb")]
        oid = len(self.ops)
        deps = set()
        for k in reads:
            if k in self.last_write:
                deps.add(self.last_write[k])
        for k in writes:
            if k in self.last_write:
                deps.add(self.last_write[k])
            deps.update(self.readers.get(k, ()))
        if eng in self.pending:
            deps |= self.pending.pop(eng)
        rec = dict(eng=eng, fn=fn, deps=deps, dma=dma, sig=False, seq=None)
        if dma:
            i = self.dma_count[eng]
            self.dma_count[eng] += 1
            slot = i % N_DMA_SEM
            rec["slot"] = slot
            rec["val"] = 16 * (i // N_DMA_SEM + 1)
            prev = self.slot_last.get((eng, slot))
            if prev is not None:
                deps.add(prev)
            self.slot_last[(eng, slot)] = oid
        else:
            self.last_op[eng] = oid
        deps.discard(oid)
        self.ops.append(rec)
        for k in writes:
            self.last_write[k] = oid
            self.readers[k] = []
        for k in reads:
            self.readers.setdefault(k, []).append(oid)
        return oid

    def emit(self, nc):
        ops = self.ops

        def skip(dd, o):
            return (not dd["dma"]) and dd["eng"] == o["eng"] == "pe" and not o["dma"]
        for o in ops:
            for d in o["deps"]:
                dd = ops[d]
                if dd["dma"] or skip(dd, o):
                    continue
                dd["sig"] = True
        cnt = {e: 0 for e in ENGS}
        for o in ops:
            if o["sig"]:
                cnt[o["eng"]] += 1
                o["seq"] = cnt[o["eng"]]
        with contextlib.ExitStack() as st:
            csem = {e: st.enter_context(nc.semaphore("c_" + e)) for e in ENGS}
            dsem = {(e, s): st.enter_context(nc.semaphore(f"d_{e}{s}"))
                    for e in ENGS if self.dma_count[e] > 0
                    for s in range(min(N_DMA_SEM, self.dma_count[e]))}
            block = st.enter_context(nc.Block())

            def run(eng_name, eng):
                waited = {}
                for o in ops:
                    if o["eng"] != eng_name:
                        continue
                    for d in sorted(o["deps"]):
                        dd = ops[d]
                        if dd["dma"]:
                            key = ("d", dd["eng"], dd["slot"])
                            sem = dsem[(dd["eng"], dd["slot"])]
                            val = dd["val"]
                        else:
                            if skip(dd, o):
                                continue
                            key = ("c", dd["eng"])
                            sem = csem[dd["eng"]]
                            val = dd["seq"]
                        if waited.get(key, 0) >= val:
                            continue
                        waited[key] = val
                        eng.wait_ge(sem, val)
                    ins = o["fn"](eng)
                    if o["dma"]:
                        ins.then_inc(dsem[(eng_name, o["slot"])], 16)
                    elif o["sig"]:
                        ins.then_inc(csem[eng_name], 1)
                if eng_name == "sp":
                    for (e, s), oid in self.slot_last.items():
                        eng.wait_ge(dsem[(e, s)], ops[oid]["val"])

            block.tensor(lambda e: run("pe", e))
            block.scalar(lambda e: run("act", e))
            block.vector(lambda e: run("dve", e))
            block.gpsimd(lambda e: run("pool", e))
            block.sync(lambda e: run("sp", e))


def host_consts():
    c = {}
    c["ident"] = np.eye(128, dtype=np.float32)
    s = np.arange(128)
    c["negm_p"] = np.where(s[None, :] >= s[:, None], 0.0, -30000.0).astype(np.float32)
    same = (s[None, :] // 8) == (s[:, None] // 8)
    c["negm_s"] = np.where((s[None, :] >= s[:, None]) & same, 0.0, -30000.0).astype(np.float32)
    rp = np.ones((32, 128), np.float32); rp[:, 0] = 0
    rs = np.ones((32, 128), np.float32); rs[:, ::8] = 0
    c["rm_p"] = rp
    c["rm_s"] = rs
    r128 = np.ones((128, 128), np.float32); r128[:, ::8] = 0
    c["rm128"] = r128
    rq = np.ones((128, 256), np.float32); rq[:, ::64] = 0
    c["rmq"] = rq
    c["seqmask"] = (s[:, None] // 8 == np.arange(16)[None, :]).astype(np.float32)
    c["pairmask"] = (s[:, None] // 32 == np.arange(4)[None, :]).astype(np.float32)
    c["iota"] = np.broadcast_to(np.arange(128, dtype=np.float32)[None, :], (128, 128)).copy()
    return c


IN_SPECS = [
    ("xp", (2048, 1024)), ("xs", (128, 1024)), ("conv0", (16, 3, 4096)), ("ssm0", (16, 32, 64, 128)),
    ("re0", (16, 64, 64)), ("im0", (16, 64, 64)), ("norm_w", (1024,)), ("w_in", (1024, PROJ)),
    ("conv_w", (4, 4096)), ("conv_b", (4096,)), ("dt_bias", (32,)), ("A_log", (32,)), ("D_ssd", (32,)),
    ("ssd_norm_w", (2048,)), ("w_down_ssd", (2048, 1024)), ("lam_re", (64, 64)), ("lam_im", (64, 64)),
    ("log_dt", (64,)), ("B_re", (64, 64, 16)), ("B_im", (64, 64, 16)), ("C_re", (64, 16, 64)),
    ("C_im", (64, 16, 64)), ("D_s5", (1024,)), ("w_glu", (1024, 1024)), ("b_glu", (1024,)),
    ("w_down_s5", (1024, 1024)), ("w_out", (1024, 1024)), ("final_norm_w", (1024,)),
    ("ident", (128, 128)), ("negm_p", (128, 128)), ("negm_s", (128, 128)), ("rm_p", (32, 128)),
    ("rm_s", (32, 128)), ("rm128", (128, 128)), ("rmq", (128, 256)), ("seqmask", (128, 16)), ("pairmask", (128, 4)),
    ("iota", (128, 128)),
]
OUT_SPECS = [
    ("y_p", (2048, 1024)), ("y_s", (128, 1024)), ("conv_p", (3, 4096)), ("ssm_p", (32, 64, 128)),
    ("re_p", (64, 64)), ("im_p", (64, 64)), ("conv_s", (16, 3, 4096)), ("ssm_s", (16, 32, 64, 128)),
    ("re_s", (16, 64, 64)), ("im_s", (16, 64, 64)),
]


STOP = [None]
INTERLEAVE = [True]
BLK = [None]


class _Stop(Exception):
    pass


def ck(n):
    if STOP[0] is not None and n >= STOP[0]:
        raise _Stop()


def build(tiles=None):
    nc = bass.Bass("TRN2", target_bir_lowering=False)
    I = {n: nc.dram_tensor(n, list(s), F32, kind="ExternalInput").ap() for n, s in IN_SPECS}
    O = {n: nc.dram_tensor(n, list(s), F32, kind="ExternalOutput").ap() for n, s in OUT_SPECS}
    WS = {
        "w_in": nc.dram_tensor("ws_w_in", [1024, PROJ], BF16).ap(),
        "w_down_ssd": nc.dram_tensor("ws_wds", [2048, 1024], BF16).ap(),
        "w_glu": nc.dram_tensor("ws_wglu", [1024, 1024], BF16).ap(),
        "w_down_s5": nc.dram_tensor("ws_wd5", [1024, 1024], BF16).ap(),
        "w_out": nc.dram_tensor("ws_wout", [1024, 1024], BF16).ap(),
    }
    P = Prog()

    CUR = [None]
    MMB = [(0, 1)]

    TAG = ["pro"]

    def _rec(eng, fn, r, w, dma=False):
        if CUR[0] is None:
            P.op(eng, fn, r, w, dma=dma)
            P.ops[-1]["tag"] = TAG[0]
        else:
            CUR[0].append((eng, fn, tuple(r), tuple(w), dma, TAG[0]))

    def V(fn, r=(), w=()):
        _rec("dve", fn, r, w)

    def G(fn, r=(), w=()):
        _rec("pool", fn, r, w)

    def A(fn, r=(), w=()):
        _rec("act", fn, r, w)

    def T(fn, r=(), w=()):
        _rec("pe", fn, r, w)

    def DMA(fn, r=(), w=()):
        _rec("sp", fn, r, w, dma=True)

    def DMAO(fn, r=(), w=()):
        _rec("pool", fn, r, w, dma=True)

    def DMAH(fn, r=(), w=()):
        _rec("act", fn, r, w, dma=True)

    st = contextlib.ExitStack()
    with st:
        def sb(name, shape, dt=F32, stack=None):
            return (stack or st).enter_context(nc.sbuf_tensor("s_" + name, list(shape), dt))

        def kn(t):
            n_ = getattr(t, "name")
            return n_[2:] if n_.startswith("s_") else n_
        pb = [st.enter_context(nc.psum_tensor(f"pb{i}", [128, 512], F32)) for i in range(8)]
        pbb = [p.bitcast(BF16) for p in pb]

        identf = sb("identf", [128, 128]); identb = sb("identb", [128, 128], BF16)
        negm_p = sb("negm_p", [128, 128]); negm_s = sb("negm_s", [128, 128])
        rm_p = sb("rm_p", [32, 128]); rm_s = sb("rm_s", [32, 128]); rm128 = sb("rm128", [128, 128]); rmq = sb("rmq", [128, 256])
        seqmask = sb("seqmask", [128, 16]); pairmask = sb("pairmask", [128, 4])
        ones32 = sb("ones32", [32, 128]); onesb = sb("onesb", [128, 128], BF16)
        onecol = sb("onecol", [128, 1]); epscol = sb("epscol", [128, 1])
        for nm, t in [("ident", identf), ("negm_p", negm_p), ("negm_s", negm_s), ("rm_p", rm_p), ("rm_s", rm_s),
                      ("rm128", rm128), ("rmq", rmq), ("seqmask", seqmask), ("pairmask", pairmask)]:
            DMA(lambda e, nm=nm, t=t: e.dma_start(out=t[:], in_=I[nm]), w=[kn(t)])
        V(lambda e: e.tensor_copy(out=identb[:], in_=identf[:]), ["identf"], ["identb"])
        G(lambda e: e.memset(ones32[:], 1.0), w=["ones32"])
        G(lambda e: e.memset(onesb[:], 1.0), w=["onesb"])
        G(lambda e: e.memset(onecol[:], 1.0), w=["onecol"])
        G(lambda e: e.memset(epscol[:], EPS), w=["epscol"])

        def colvec(name, src, nt):
            t = sb(name, [128, nt])
            DMA(lambda e: e.dma_start(out=t[:], in_=src.rearrange("(j p) -> p j", p=128), allow_slow_non_contiguous=True), w=[name])
            return t
        normw = colvec("normw", I["norm_w"], 8)
        convb = colvec("convb", I["conv_b"], 32)
        ssdnw = colvec("ssdnw", I["ssd_norm_w"], 16)
        ds5 = colvec("ds5", I["D_s5"], 8)
        bglu = colvec("bglu", I["b_glu"], 8)
        convw = sb("convw", [128, 32, 4])
        for k in range(4):
            DMA(lambda e, k=k: e.dma_start(out=convw[:, :, k], in_=I["conv_w"][k].rearrange("(j p) -> p j", p=128), allow_slow_non_contiguous=True), w=["convw"])
        fnw = sb("fnw", [128, 1024])
        DMA(lambda e: e.dma_start(out=fnw[:], in_=I["final_norm_w"].partition_broadcast(128)), w=["fnw"])
        dtb = sb("dtb", [32, 1]); aneg = sb("aneg", [32, 1])
        DMA(lambda e: e.dma_start(out=dtb[:], in_=I["dt_bias"].rearrange("(h o) -> h o", o=1)), w=["dtb"])
        DMA(lambda e: e.dma_start(out=aneg[:], in_=I["A_log"].rearrange("(h o) -> h o", o=1)), w=["aneg"])
        A(lambda e: e.activation(out=aneg[:], in_=aneg[:], func=AF.Exp), ["aneg"], ["aneg"])
        V(lambda e: e.tensor_scalar(out=aneg[:], in0=aneg[:], scalar1=-1.0, scalar2=None, op0=ALU.mult), ["aneg"], ["aneg"])
        dexp = sb("dexp", [128, 16])
        dv = I["D_ssd"].rearrange("(pr h2) -> h2 pr", h2=2)
        for h2 in range(2):
            DMA(lambda e, h2=h2: e.dma_start(out=dexp[64 * h2:64 * h2 + 64, :], in_=dv[h2].partition_broadcast(64), allow_slow_non_contiguous=True), w=["dexp"])
        are = sb("are", [128, 32]); aim = sb("aim", [128, 32]); rdec = sb("rdec", [128, 32])
        aere = sb("aere", [128, 32]); aeim = sb("aeim", [128, 32])
        ecos = sb("ecos", [128, 32, 64]); esin = sb("esin", [128, 32, 64])
        wb5 = sb("wb5", [128, 8, 2, 128], BF16); wd5 = sb("wd5", [128, 8, 2, 128], BF16)
        halo = sb("halo", [128, 32, 3])
        STt = sb("STt", [128, 32, 64])
        SA = sb("SA", [128, 16, 128], BF16); SB = sb("SB", [128, 16, 128], BF16)
        XA = sb("XA", [128, 16, 128], BF16); XB = sb("XB", [128, 16, 128], BF16)
        sgc = sb("sgc", [128, 32, 2])
        for t_ in (halo, STt, SA, SB, XA, XB, sgc, wd5):
            G(lambda e, t_=t_: e.memset(t_[:], 0.0), w=([kn(t_)] if kn(t_) not in ("SA", "SB") else [f"{kn(t_)}{i}" for i in range(4)]))

        pst = contextlib.ExitStack()
        with pst:
            def psb(name, shape, dt=F32):
                return sb(name, shape, dt, stack=pst)
            iota = psb("iota", [128, 128])
            DMA(lambda e: e.dma_start(out=iota[:], in_=I["iota"]), w=["iota"])
            lre = psb("lre", [128, 32]); lim = psb("lim", [128, 32]); stp = psb("stp", [128, 32]); th = psb("th", [128, 32])
            DMA(lambda e: e.dma_start(out=lre[:], in_=I["lam_re"].rearrange("(gp g2) p -> (g2 p) gp", g2=2), allow_slow_non_contiguous=True), w=["lre"])
            DMA(lambda e: e.dma_start(out=lim[:], in_=I["lam_im"].rearrange("(gp g2) p -> (g2 p) gp", g2=2), allow_slow_non_contiguous=True), w=["lim"])
            ldv = I["log_dt"].rearrange("(gp g2) -> g2 gp", g2=2)
            for g2 in range(2):
                DMA(lambda e, g2=g2: e.dma_start(out=stp[64 * g2:64 * g2 + 64, :], in_=ldv[g2].partition_broadcast(64), allow_slow_non_contiguous=True), w=["stp"])
            A(lambda e: e.activation(out=stp[:], in_=stp[:], func=AF.Exp), ["stp"], ["stp"])
            V(lambda e: e.tensor_tensor(out=rdec[:], in0=lre[:], in1=stp[:], op=ALU.mult), ["lre", "stp"], ["rdec"])
            A(lambda e: e.activation(out=rdec[:], in_=rdec[:], func=AF.Exp), ["rdec"], ["rdec"])
            V(lambda e: e.tensor_tensor(out=th[:], in0=lim[:], in1=stp[:], op=ALU.mult), ["lim", "stp"], ["th"])
            V(lambda e: e.tensor_scalar(out=th[:], in0=th[:], scalar1=1.0 / (2.0 * math.pi), scalar2=None, op0=ALU.mult), ["th"], ["th"])
            tmpx = psb("tmpx", [128, 2048]); tmpf = psb("tmpf", [128, 2048]); tmpi = psb("tmpi", [128, 2048], I32)

            def sin_turns(out_ap, mk_x, n, off, rkeys, wkeys):
                V(lambda e: mk_x(e, tmpx[:, 0:n]), rkeys, ["tmpx"])
                if off:
                    V(lambda e: e.tensor_scalar(out=tmpx[:, 0:n], in0=tmpx[:, 0:n], scalar1=float(off), scalar2=None, op0=ALU.add), ["tmpx"], ["tmpx"])
                V(lambda e: e.tensor_copy(out=tmpi[:, 0:n], in_=tmpx[:, 0:n]), ["tmpx"], ["tmpi"])
                V(lambda e: e.tensor_copy(out=tmpf[:, 0:n], in_=tmpi[:, 0:n]), ["tmpi"], ["tmpf"])
                V(lambda e: e.tensor_tensor(out=tmpx[:, 0:n], in0=tmpx[:, 0:n], in1=tmpf[:, 0:n], op=ALU.subtract), ["tmpx", "tmpf"], ["tmpx"])
                A(lambda e: e.activation(out=out_ap, in_=tmpx[:, 0:n], func=AF.Sin, scale=SIN_SCALE), ["tmpx"], wkeys)
            cs1 = psb("cs1", [128, 32]); sn1 = psb("sn1", [128, 32])
            x1 = lambda e, dst: e.tensor_copy(out=dst, in_=th[:])
            sin_turns(sn1[:], x1, 32, 0.0, ["th"], ["sn1"])
            sin_turns(cs1[:], x1, 32, 0.25, ["th"], ["cs1"])
            V(lambda e: e.tensor_tensor(out=are[:], in0=rdec[:], in1=cs1[:], op=ALU.mult), ["rdec", "cs1"], ["are"])
            V(lambda e: e.tensor_tensor(out=aim[:], in0=rdec[:], in1=sn1[:], op=ALU.mult), ["rdec", "sn1"], ["aim"])
            x64 = lambda e, dst: e.tensor_scalar(out=dst, in0=th[:], scalar1=64.0, scalar2=None, op0=ALU.mult)
            sin_turns(sn1[:], x64, 32, 0.0, ["th"], ["sn1"])
            sin_turns(cs1[:], x64, 32, 0.25, ["th"], ["cs1"])
            V(lambda e: e.tensor_tensor(out=aere[:], in0=rdec[:], in1=cs1[:], op=ALU.mult), ["rdec", "cs1"], ["aere"])
            V(lambda e: e.tensor_tensor(out=aeim[:], in0=rdec[:], in1=sn1[:], op=ALU.mult), ["rdec", "sn1"], ["aeim"])
            xt_ = lambda e, dst: e.tensor_tensor(out=dst.rearrange("p (a b) -> p a b", b=64), in0=th[:].unsqueeze(2).broadcast_to([128, 32, 64]),
                                                 in1=iota[:, 0:64].unsqueeze(1).broadcast_to([128, 32, 64]), op=ALU.mult)
            sin_turns(esin[:].rearrange("p a b -> p (a b)"), xt_, 2048, 0.0, ["th", "iota"], ["esin"])
            sin_turns(ecos[:].rearrange("p a b -> p (a b)"), xt_, 2048, 0.25, ["th", "iota"], ["ecos"])
            gre = psb("gre", [128, 32]); gim = psb("gim", [128, 32]); den = psb("den", [128, 32])
            t1 = psb("t1s", [128, 32]); t2 = psb("t2s", [128, 32]); am1 = psb("am1", [128, 32])
            V(lambda e: e.tensor_scalar(out=am1[:], in0=are[:], scalar1=-1.0, scalar2=None, op0=ALU.add), ["are"], ["am1"])
            V(lambda e: e.tensor_tensor(out=den[:], in0=lre[:], in1=lre[:], op=ALU.mult), ["lre"], ["den"])
            V(lambda e: e.tensor_tensor(out=t1[:], in0=lim[:], in1=lim[:], op=ALU.mult), ["lim"], ["t1s"])
            V(lambda e: e.tensor_tensor(out=den[:], in0=den[:], in1=t1[:], op=ALU.add), ["den", "t1s"], ["den"])
            V(lambda e: e.reciprocal(out=den[:], in_=den[:]), ["den"], ["den"])
            V(lambda e: e.tensor_tensor(out=t1[:], in0=am1[:], in1=lre[:], op=ALU.mult), ["am1", "lre"], ["t1s"])
            V(lambda e: e.tensor_tensor(out=t2[:], in0=aim[:], in1=lim[:], op=ALU.mult), ["aim", "lim"], ["t2s"])
            V(lambda e: e.tensor_tensor(out=gre[:], in0=t1[:], in1=t2[:], op=ALU.add), ["t1s", "t2s"], ["gre"])
            V(lambda e: e.tensor_tensor(out=gre[:], in0=gre[:], in1=den[:], op=ALU.mult), ["gre", "den"], ["gre"])
            V(lambda e: e.tensor_tensor(out=t1[:], in0=aim[:], in1=lre[:], op=ALU.mult), ["aim", "lre"], ["t1s"])
            V(lambda e: e.tensor_tensor(out=t2[:], in0=am1[:], in1=lim[:], op=ALU.mult), ["am1", "lim"], ["t2s"])
            V(lambda e: e.tensor_tensor(out=gim[:], in0=t1[:], in1=t2[:], op=ALU.subtract), ["t1s", "t2s"], ["gim"])
            V(lambda e: e.tensor_tensor(out=gim[:], in0=gim[:], in1=den[:], op=ALU.mult), ["gim", "den"], ["gim"])
            bre = psb("bre", [128, 32, 16]); bim = psb("bim", [128, 32, 16]); bbr = psb("bbr", [128, 32, 16]); bbi = psb("bbi", [128, 32, 16]); tb = psb("tbs", [128, 32, 16])
            DMA(lambda e: e.dma_start(out=bre[:], in_=I["B_re"].rearrange("(gp g2) p k -> (g2 p) gp k", g2=2)), w=["bre"])
            DMA(lambda e: e.dma_start(out=bim[:], in_=I["B_im"].rearrange("(gp g2) p k -> (g2 p) gp k", g2=2)), w=["bim"])
            gb_ = lambda t: t[:].unsqueeze(2).broadcast_to([128, 32, 16])
            V(lambda e: e.tensor_tensor(out=bbr[:], in0=bre[:], in1=gb_(gre), op=ALU.mult), ["bre", "gre"], ["bbr"])
            V(lambda e: e.tensor_tensor(out=tb[:], in0=bim[:], in1=gb_(gim), op=ALU.mult), ["bim", "gim"], ["tbs"])
            V(lambda e: e.tensor_tensor(out=bbr[:], in0=bbr[:], in1=tb[:], op=ALU.subtract), ["bbr", "tbs"], ["bbr"])
            V(lambda e: e.tensor_tensor(out=bbi[:], in0=bim[:], in1=gb_(gre), op=ALU.mult), ["bim", "gre"], ["bbi"])
            V(lambda e: e.tensor_tensor(out=tb[:], in0=bre[:], in1=gb_(gim), op=ALU.mult), ["bre", "gim"], ["tbs"])
            V(lambda e: e.tensor_tensor(out=bbi[:], in0=bbi[:], in1=tb[:], op=ALU.add), ["bbi", "tbs"], ["bbi"])
            cin = psb("cin", [128, 4, 2, 64])
            cst = [psb("cstr", [128, 32, 16]), psb("csti", [128, 32, 16])]
            for ri, nm in enumerate(["C_re", "C_im"]):
                cv = I[nm].rearrange("(gh gq g2) k p -> gq k gh g2 p", gq=8, g2=2)
                for gq in range(8):
                    for gh in range(4):
                        DMA(lambda e, gq=gq, gh=gh, cv=cv: e.dma_start(out=cin[16 * gq:16 * gq + 16, gh, :, :], in_=cv[gq][:, gh, :, :]), w=["cin"])
                for gh in range(4):
                    T(lambda e, gh=gh: e.transpose(out=pb[7][:, 0:128], in_=cin[:, gh, :, :].rearrange("p a b -> p (a b)"), identity=identf[:]), ["cin", "identf"], ["pb7"])
                    V(lambda e, gh=gh, ri=ri: e.tensor_copy(out=cst[ri][:, 8 * gh:8 * gh + 8, :], in_=pb[7][:, 0:128].rearrange("p (a b) -> p a b", b=16)), ["pb7"], [kn(cst[ri])])
            zst = psb("zst", [128, 4, 2, 16])
            G(lambda e: e.memset(zst[:], 0.0), w=["zst"])
            for ft in range(8):
                for ri in range(2):
                    src = [bbr, bbi][ri]
                    for h2 in range(2):
                        V(lambda e, ft=ft, h2=h2, src=src: e.tensor_copy(out=zst[64 * h2:64 * h2 + 64, :, h2, :], in_=src[64 * h2:64 * h2 + 64, 4 * ft:4 * ft + 4, :]), [kn(src)], ["zst"])
                    T(lambda e: e.transpose(out=pb[6][:, 0:128], in_=zst[:].rearrange("p a b c -> p (a b c)"), identity=identf[:]), ["zst", "identf"], ["pb6"])
                    V(lambda e, ft=ft, ri=ri: e.tensor_copy(out=wb5[:, ft, ri, :], in_=pb[6][:, 0:128]), ["pb6"], ["wb5"])
                    for h2 in range(2):
                        V(lambda e, ft=ft, ri=ri, h2=h2: e.tensor_scalar(
                            out=wd5[64 * h2:64 * h2 + 64, ft, ri, :].rearrange("p (q g k) -> p q g k", g=2, k=16)[:, :, h2, :],
                            in0=cst[ri][64 * h2:64 * h2 + 64, 4 * ft:4 * ft + 4, :], scalar1=(1.0 if ri == 0 else -1.0), scalar2=None, op0=ALU.mult),
                          [kn(cst[ri])], ["wd5"])
            NST = 6
            stg = [psb(f"stg{i}", [128, 2048]) for i in range(NST)]
            stgb = [psb(f"stgb{i}", [128, 2048], BF16) for i in range(NST)]
            cnt = [0]

            def cast_rows(src, dst, ncols):
                for c0 in range(0, ncols, 2048):
                    w_ = min(2048, ncols - c0)
                    i = cnt[0] % NST
                    cnt[0] += 1
                    DMA(lambda e, i=i, c0=c0, w_=w_: e.dma_start(out=stg[i][:, 0:w_], in_=src[:, c0:c0 + w_]), w=[f"stg{i}"])
                    sel = (1, 2, 2, 2, 1, 0)[cnt[0] % 6]
                    if sel == 0:
                        G(lambda e, i=i, w_=w_: e.tensor_copy(out=stgb[i][:, 0:w_], in_=stg[i][:, 0:w_]), [f"stg{i}"], [f"stgb{i}"])
                    elif sel == 1:
                        A(lambda e, i=i, w_=w_: e.activation(out=stgb[i][:, 0:w_], in_=stg[i][:, 0:w_], func=AF.Copy), [f"stg{i}"], [f"stgb{i}"])
                    else:
                        V(lambda e, i=i, w_=w_: e.tensor_copy(out=stgb[i][:, 0:w_], in_=stg[i][:, 0:w_]), [f"stg{i}"], [f"stgb{i}"])
                    DMAH(lambda e, i=i, c0=c0, w_=w_: e.dma_start(out=dst[:, c0:c0 + w_], in_=stgb[i][:, 0:w_]), [f"stgb{i}"], ["ws"])
            for nm, rows in [("w_in", 1024), ("w_down_ssd", 2048), ("w_glu", 1024), ("w_down_s5", 1024), ("w_out", 1024)]:
                nco = PROJ if nm == "w_in" else 1024
                for r0 in range(0, rows, 128):
                    cast_rows(I[nm][r0:r0 + 128, :], WS[nm][r0:r0 + 128, :], nco)

        P.barrier()

        xt = sb("xt", [128, 1024]); ot = sb("ot", [128, 1024]); ss = sb("ss", [128, 1]); rstd = sb("rstd", [128, 1])
        ssB = sb("ssB", [128, 1]); rstdB = sb("rstdB", [128, 1])
        xsn = sb("xsn", [128, 1024], BF16); hT = sb("hT", [128, 8, 128], BF16)
        wbuf = [sb(f"wbuf{i}", [128, 8, 256], BF16) for i in range(4)]
        zs = sb("zs", [128, 16, 128], BF16)
        xc = sb("xc", [128, 32, 128], BF16)
        raw = [sb(f"raw{i}", [128, 176]) for i in range(2)]
        acc = [sb(f"acc{i}", [128, 128]) for i in range(2)]
        dts = sb("dts", [32, 128]); dAs = sb("dAs", [32, 128]); acs = sb("acs", [32, 128]); dec = sb("dec", [32, 128]); wdt = sb("wdt", [32, 128])
        tokm = sb("tokm", [128, 3, 32])
        u5_ = [sb(f"u5_{i}", [128, 8, 128], BF16) for i in range(2)]; z5s_ = [sb(f"z5s_{i}", [128, 8, 128], BF16) for i in range(2)]
        gas = sb("gas", [128, 8, 128], BF16); gbs_ = [sb(f"gbs_{i}", [128, 8, 128], BF16) for i in range(2)]
        Xd = sb("Xd", [128, 32, 64], BF16); Btok = sb("Btok", [128, 8, 128], BF16)
        rhsR = sb("rhsR", [32, 4, 128]); Eg = sb("Eg", [128, 4, 128]); Dm = sb("Dm", [128, 4, 128])
        Lg = sb("Lg", [128, 4, 128], BF16); Mg = sb("Mg", [128, 4, 128], BF16)
        CEall = sb("CEall", [128, 32, 128], BF16)
        dch = sb("dch", [128, 32, 16])
        S0qs = [sb(f"S0q{i}", [128, 4, 128]) for i in range(2)]; Snqs = [sb(f"Snq{i}", [128, 4, 128]) for i in range(2)]
        Bms = [sb("Bm0", [128, 8, 128], BF16)] * 2
        yv2 = [sb(f"yv2_{i}", [128, 2, 128]) for i in range(2)]; ysq2 = [sb(f"ysq2_{i}", [128, 2, 128], BF16) for i in range(2)]
        rstg2 = [sb(f"rstg2_{i}", [128, 128]) for i in range(2)]; ynT = sb("ynT", [128, 16, 128], BF16)
        yaT_ = [sb(f"yaT_{i}", [128, 8, 128], BF16) for i in range(2)]; ybT = sb("ybT", [128, 8, 128], BF16); mixT = sb("mixT", [128, 8, 128], BF16)
        u5m = sb("u5m", [128, 4, 128], BF16)
        bu = sb("bu", [128, 4, 2, 128]); sgm = sb("sgm", [128, 4, 2, 128]); rts = sb("rts", [128, 4, 128])
        bt1 = sb("bt1", [128, 4, 128]); bt2 = sb("bt2", [128, 4, 128])
        s5s = sb("s5s", [128, 4, 2, 128], BF16)
        cin_ = sb("cinn", [128, 4, 2, 16])
        ctm = sb("ctm", [128, 4, 2, 16])
        fin = sb("fin", [128, 32, 2, 16]); s0all = sb("s0all", [128, 32, 2, 16])
        y5t = sb("y5t", [128, 128]); y5u = sb("y5u", [128, 128]); y5v = sb("y5v", [128, 128])
        y5g = sb("y5g", [128, 8, 128], BF16); y5f = sb("y5f", [128, 8, 128], BF16); glus = sb("glus", [128, 128])
        s5io = sb("s5io", [128, 4, 2, 64]); s5tr = sb("s5tr", [128, 16, 8])
        cvo = sb("cvo", [48, 1024]); cst3 = sb("cst3", [128, 48])

        wcnt = [0]

        WSET = [(0, 1, 2)]

        def wload(src_ap):
            i = WSET[0][wcnt[0] % len(WSET[0])]
            wcnt[0] += 1
            kt = src_ap.shape[0] // 128
            nco = src_ap.shape[1]
            DMA(lambda e: e.dma_start(out=wbuf[i][:, 0:kt, 0:nco], in_=src_ap.rearrange("(j p) f -> p j f", p=128)), ["ws"], [f"wbuf{i}"])
            return wbuf[i], f"wbuf{i}"

        pslot = [0]

        def mm_slot():
            s = MMB[0][pslot[0] % 2]
            pslot[0] += 1
            return pb[s][:, 0:128], f"pb{s}"

        def proj_ft(wt, wk, c0, nrows=128):
            ps, pk = mm_slot()
            for j in range(8):
                T(lambda e, j=j: e.matmul(ps[0:nrows, :], lhsT=wt[:, j, c0:c0 + nrows], rhs=hT[:, j, :], start=(j == 0), stop=(j == 7)), [wk, "hT"], [pk])
            return ps, pk

        def dense_T(wname, kt, rhs_fn, rkeys_fn, evac):
            for cb in range(4):
                halves = [wload(WS[wname][128 * k0:128 * (k0 + 8), 256 * cb:256 * cb + 256]) for k0 in range(0, kt, 8)]
                for s in range(2):
                    m = 2 * cb + s
                    ps, pk = mm_slot()
                    for k in range(kt):
                        wt, wk = halves[k // 8]
                        T(lambda e, wt=wt, k=k, s=s, ps=ps: e.matmul(ps, lhsT=wt[:, k % 8, 128 * s:128 * s + 128], rhs=rhs_fn(k), start=(k == 0), stop=(k == kt - 1)), [wk] + rkeys_fn(k), [pk])
                    evac(m, ps, pk)

        def cmul(out_re, out_im, ar, ai, br, bi, keys_r, keys_w, t1_, t2_, tk):
            V(lambda e: e.tensor_tensor(out=t1_, in0=ar, in1=br, op=ALU.mult), keys_r, [tk[0]])
            V(lambda e: e.tensor_tensor(out=t2_, in0=ai, in1=bi, op=ALU.mult), keys_r, [tk[1]])
            V(lambda e: e.tensor_tensor(out=out_re, in0=t1_, in1=t2_, op=ALU.subtract), tk, keys_w[0:1])
            V(lambda e: e.tensor_tensor(out=t1_, in0=ar, in1=bi, op=ALU.mult), keys_r + keys_w[0:1], [tk[0]])
            V(lambda e: e.tensor_tensor(out=t2_, in0=ai, in1=br, op=ALU.mult), keys_r + keys_w[0:1], [tk[1]])
            V(lambda e: e.tensor_tensor(out=out_im, in0=t1_, in1=t2_, op=ALU.add), tk, keys_w[1:2])

        def recA(ti, par):
            u5 = u5_[par]; z5s = z5s_[par]; gbs = gbs_[par]; yaT = yaT_[par]
            smp = (ti == 16)
            lastp = (ti == 15)
            xsrc = I["xs"] if smp else I["xp"][128 * ti:128 * ti + 128, :]
            ydst = O["y_s"] if smp else O["y_p"][128 * ti:128 * ti + 128, :]
            negm = negm_s if smp else negm_p
            rmk = rm_s if smp else rm_p
            nseq, L = (16, 8) if smp else (1, 128)
            TAG[0] = "A.T1"
            DMA(lambda e: e.dma_start(out=xt[:], in_=xsrc), w=["xt"])
            G(lambda e: e.memset(ss[:], 0.0), w=["ss"])
            A(lambda e: e.activation(out=xsn[:], in_=xt[:], func=AF.Square, accum_out=ss[:]), ["xt", "ss"], ["xsn", "ss"])
            A(lambda e: e.activation(out=rstd[:], in_=ss[:], func=AF.Sqrt, scale=1.0 / D, bias=epscol[:]), ["ss", "epscol"], ["rstd"])
            V(lambda e: e.reciprocal(out=rstd[:], in_=rstd[:]), ["rstd"], ["rstd"])
            V(lambda e: e.tensor_scalar(out=xsn[:], in0=xt[:], scalar1=rstd[:, 0:1], scalar2=None, op0=ALU.mult), ["xt", "rstd"], ["xsn"])
            for j in range(8):
                T(lambda e, j=j: e.transpose(out=pbb[5][:, 128 * j:128 * j + 128], in_=xsn[:, 128 * j:128 * j + 128], identity=identb[:]), ["xsn", "identb"], ["pb5"])
            for j in range(8):
                V(lambda e, j=j: e.tensor_scalar(out=hT[:, j, :], in0=pbb[5][:, 128 * j:128 * j + 128], scalar1=normw[:, j:j + 1], scalar2=None, op0=ALU.mult), ["pb5", "normw"], ["hT"])
            TAG[0] = "A.proj"
            blocks = [("xbc", c0, 256) for c0 in range(XBC0, DT0, 256)] + [("dt", DT0, 32)]
            blocks += [("z", c0, 256) for c0 in range(Z0, XBC0, 256)]
            for nm, b0 in (("u5", U0), ("z5", Z50), ("ga", GA0), ("gb", GB0)):
                blocks += [(nm, c0, 256) for c0 in range(b0, b0 + 1024, 256)]
            base = {"xbc": XBC0, "z": Z0, "u5": U0, "z5": Z50, "ga": GA0, "gb": GB0}
            deferred = []
            for nm, c0, nco in blocks:
                wt, wk = wload(WS["w_in"][:, c0:c0 + nco])
                if nm == "dt":
                    while deferred:
                        deferred.pop(0)()
                    ps, pk = proj_ft(wt, wk, 0, 32)
                    A(lambda e, ps=ps: e.activation(out=dts[:], in_=ps[0:32, :], func=AF.Exp, bias=dtb[:]), [pk, "dtb"], ["dts"])
                    A(lambda e: e.activation(out=dts[:], in_=dts[:], func=AF.Ln, bias=onecol[0:32, :]), ["dts", "onecol"], ["dts"])
                    continue
                for s in range(2):
                    ft = (c0 - base[nm]) // 128 + s
                    ps, pk = proj_ft(wt, wk, 128 * s)
                    if nm == "z":
                        A(lambda e, ps=ps, ft=ft: e.activation(out=zs[:, ft, :], in_=ps, func=AF.Silu), [pk], [f"zs{ft}"])
                    elif nm == "u5":
                        A(lambda e, ps=ps, ft=ft: e.activation(out=u5[:, ft, :], in_=ps, func=AF.Copy), [pk], [f"u5_{par}_{ft}"])
                    elif nm == "z5":
                        A(lambda e, ps=ps, ft=ft: e.activation(out=z5s[:, ft, :], in_=ps, func=AF.Silu), [pk], [f"z5_{par}_{ft}"])
                    elif nm == "ga":
                        A(lambda e, ps=ps, ft=ft: e.activation(out=gas[:, ft, :], in_=ps, func=AF.Sigmoid), [pk], [f"ga{ft}"])
                    elif nm == "gb":
                        A(lambda e, ps=ps, ft=ft: e.activation(out=gbs[:, ft, :], in_=ps, func=AF.Sigmoid), [pk], [f"gb_{par}_{ft}"])
                    else:
                        r_ = raw[ft % 2]; rk = f"raw{ft % 2}"; ac = acc[ft % 2]; ak = f"acc{ft % 2}"
                        EV = V
                        if smp:
                            rv = r_[:, 0:176].rearrange("p (s r) -> p s r", r=11)
                            if ft % 8 == 0:
                                DMA(lambda e, ft=ft: e.dma_start(out=cvo[:], in_=I["conv0"].rearrange("s r c -> (s r) c")[:, 128 * ft:128 * ft + 1024]), w=["cvo"])
                            T(lambda e, ft=ft: e.transpose(out=pb[6][:, 0:48], in_=cvo[:, 128 * (ft % 8):128 * (ft % 8) + 128], identity=identf[0:48, 0:48]), ["cvo", "identf"], ["pb6"])
                            V(lambda e, rv=rv: e.tensor_copy(out=rv[:, :, 0:3], in_=pb[6][:, 0:48].rearrange("p (s r) -> p s r", r=3)), ["pb6"], [rk])
                        else:
                            rv = r_[:, 0:131].unsqueeze(1)
                            A(lambda e, rv=rv, ft=ft: e.activation(out=rv[:, 0, 0:3], in_=halo[:, ft, :], func=AF.Copy), ["halo"], [rk])
                        A(lambda e, rv=rv, ps=ps: e.activation(out=rv[:, :, 3:3 + L], in_=ps.rearrange("p (s l) -> p s l", l=L), func=AF.Copy), [pk], [rk])
                        while deferred:
                            deferred.pop(0)()
                        av = ac[:].rearrange("p (s l) -> p s l", l=L)
                        EV(lambda e, rv=rv, av=av, ft=ft: e.tensor_scalar(out=av, in0=rv[:, :, 0:L], scalar1=convw[:, ft, 0:1], scalar2=convb[:, ft:ft + 1], op0=ALU.mult, op1=ALU.add), [rk, "convw", "convb"], [ak])
                        for k in range(1, 4):
                            EV(lambda e, rv=rv, av=av, ft=ft, k=k: e.scalar_tensor_tensor(out=av, in0=rv[:, :, k:k + L], scalar=convw[:, ft, k:k + 1], in1=av, op0=ALU.mult, op1=ALU.add), [rk, ak, "convw"], [ak])
                        def late(ac=ac, ft=ft, rv=rv, ak=ak, rk=rk):
                            A(lambda e: e.activation(out=xc[:, ft, :], in_=ac[:], func=AF.Silu), [ak], [f"xc{ft}"])
                            if not smp:
                                A(lambda e: e.activation(out=halo[:, ft, :], in_=rv[:, 0, 128:131], func=AF.Copy), [rk], ["halo"])
                        deferred.append(late)
                        if smp or lastp:
                            nr = 3 * nseq
                            EV(lambda e, rv=rv, nr=nr: e.tensor_copy(out=cst3[:, 0:nr].rearrange("p (s r) -> p s r", r=3), in_=rv[:, :, L:L + 3]), [rk], ["cst3"])
                            T(lambda e, nr=nr: e.transpose(out=pb[6][0:nr, 128:256], in_=cst3[:, 0:nr], identity=identf[:]), ["cst3", "identf"], ["pb6"])
                            V(lambda e, ft=ft, nr=nr: e.tensor_copy(out=cvo[0:nr, 128 * (ft % 8):128 * (ft % 8) + 128], in_=pb[6][0:nr, 128:256]), ["pb6"], ["cvo"])
                            if ft % 8 == 7:
                                cdst = (O["conv_s"].rearrange("s r c -> (s r) c") if smp else O["conv_p"])[:, 128 * (ft - 7):128 * (ft - 7) + 1024]
                                DMAO(lambda e, cdst=cdst, nr=nr: e.dma_start(out=cdst, in_=cvo[0:nr, :]), ["cvo"], [])
            TAG[0] = "A.dt"
            V(lambda e: e.tensor_scalar(out=dAs[:], in0=dts[:], scalar1=aneg[:, 0:1], scalar2=None, op0=ALU.mult), ["dts", "aneg"], ["dAs"])
            V(lambda e: e.tensor_tensor_scan(out=acs[:], data0=rmk[:], data1=dAs[:], initial=0.0, op0=ALU.mult, op1=ALU.add), [kn(rmk), "dAs"], ["acs"])
            a3 = acs[:].rearrange("h (s l) -> h s l", l=L)
            V(lambda e: e.tensor_tensor(out=dec[:].rearrange("h (s l) -> h s l", l=L), in0=a3[:, :, L - 1:L].broadcast_to([32, nseq, L]), in1=a3, op=ALU.subtract), ["acs"], ["dec"])
            A(lambda e: e.activation(out=dec[:], in_=dec[:], func=AF.Exp), ["dec"], ["dec"])
            V(lambda e: e.tensor_tensor(out=wdt[:], in0=dts[:], in1=dec[:], op=ALU.mult), ["dts", "dec"], ["wdt"])
            for i_, src in enumerate((acs, dts, wdt)):
                T(lambda e, i_=i_, src=src: e.transpose(out=pb[6][:, 256 + 32 * i_:256 + 32 * i_ + 32], in_=src[:], identity=identf[0:32, 0:32]), [kn(src), "identf"], ["pb6"])
            V(lambda e: e.tensor_copy(out=tokm[:].rearrange("p a b -> p (a b)"), in_=pb[6][:, 256:352]), ["pb6"], ["tokm"])
            TAG[0] = "A.X"
            for half in range(2):
                bk = f"pb{2 + half}"
                for j in range(8):
                    ft = 8 * half + j
                    T(lambda e, ft=ft, j=j, half=half: e.transpose(out=pbb[2 + half][:, 128 * j:128 * j + 128], in_=xc[:, ft, :], identity=identb[:]), [f"xc{ft}", "identb"], [bk])
                pv = pbb[2 + half][:, :].rearrange("p (a h2 q) -> p a h2 q", h2=2, q=64)
                prs = slice(8 * half, 8 * half + 8)
                dtT = tokm[:, 1, 16 * half:16 * half + 16].rearrange("p (a h2) -> p a h2", h2=2)
                wT = tokm[:, 2, 16 * half:16 * half + 16]
                V(lambda e, pv=pv, prs=prs, dtT=dtT: e.tensor_tensor(out=XA[:, prs, 0:64], in0=pv[:, :, 0, :], in1=dtT[:, :, 0:1].broadcast_to([128, 8, 64]), op=ALU.mult), [bk, "tokm"], ["XA"])
                V(lambda e, pv=pv, prs=prs, dtT=dtT: e.tensor_tensor(out=XB[:, prs, 64:128], in0=pv[:, :, 1, :], in1=dtT[:, :, 1:2].broadcast_to([128, 8, 64]), op=ALU.mult), [bk, "tokm"], ["XB"])
                V(lambda e, half=half, wT=wT: e.tensor_tensor(out=Xd[:, 16 * half:16 * half + 16, :], in0=pbb[2 + half][:, :].rearrange("p (h q) -> p h q", q=64), in1=wT.unsqueeze(2).broadcast_to([128, 16, 64]), op=ALU.mult), [bk, "tokm"], ["Xd"])
            for j in range(8):
                T(lambda e, j=j: e.transpose(out=pbb[6][:, 128 * j:128 * j + 128], in_=xc[:, 16 + j, :], identity=identb[:]), [f"xc{16 + j}", "identb"], ["pb6"])
            A(lambda e: e.activation(out=Btok[:].rearrange("p a b -> p (a b)"), in_=pbb[6][:, :], func=AF.Copy), ["pb6"], ["Btok"])
            def yevac(gq):
                b2 = gq % 2
                yv = yv2[b2]; ysq = ysq2[b2]; rstg = rstg2[b2]
                for j in range(2):
                    pr = 2 * gq + j
                    V(lambda e, pr=pr, j=j, yv=yv: e.scalar_tensor_tensor(out=yv[:, j, :], in0=xc[:, pr, :], scalar=dexp[:, pr:pr + 1], in1=yps(pr), op0=ALU.mult, op1=ALU.add), [f"xc{pr}", "dexp", ypk(pr)], [f"yv{b2}"])
                    V(lambda e, pr=pr, j=j, yv=yv: e.tensor_tensor(out=yv[:, j, :], in0=yv[:, j, :], in1=zs[:, pr, :], op=ALU.mult), [f"yv{b2}", f"zs{pr}"], [f"yv{b2}"])
                    A(lambda e, j=j, yv=yv, ysq=ysq: e.activation(out=ysq[:, j, :], in_=yv[:, j, :], func=AF.Square), [f"yv{b2}"], [f"ysq{b2}"])
                ps = pb[6 + b2][:, 0:128]
                pk = f"pb{6 + b2}"
                for j in range(2):
                    T(lambda e, ps=ps, j=j, ysq=ysq: e.matmul(ps, lhsT=onesb[:], rhs=ysq[:, j, :], start=(j == 0), stop=(j == 1)), ["onesb", f"ysq{b2}"], [pk])
                A(lambda e, ps=ps, rstg=rstg: e.activation(out=rstg[:], in_=ps, func=AF.Sqrt, scale=1.0 / 256, bias=epscol[:]), [pk, "epscol"], [f"rstg{b2}"])
                V(lambda e, rstg=rstg: e.reciprocal(out=rstg[:], in_=rstg[:]), [f"rstg{b2}"], [f"rstg{b2}"])
                for j in range(2):
                    pr = 2 * gq + j
                    V(lambda e, pr=pr, j=j, yv=yv, rstg=rstg: e.scalar_tensor_tensor(out=ynT[:, pr, :], in0=yv[:, j, :], scalar=ssdnw[:, pr:pr + 1], in1=rstg[:], op0=ALU.mult, op1=ALU.mult), [f"yv{b2}", "ssdnw", f"rstg{b2}"], [f"ynT{pr}"])
            TAG[0] = "A.grp"
            it = 0
            for hf in ((0, 1) if smp else (None,)):
                grp_range = range(8) if hf is None else range(4 * hf, 4 * hf + 4)
                q4_range = range(1) if hf is None else range(2 * hf, 2 * hf + 2)
                if smp:
                    ypk = lambda pr: f"pb{2 + (pr % 8) // 4}"
                    yps = lambda pr: pb[2 + (pr % 8) // 4][:, 128 * (pr % 4):128 * (pr % 4) + 128]
                    yfirst = lambda pr: pr % 4 == 0
                else:
                    ypk = lambda pr: f"pb{2 + (pr // 2) % 2}"
                    yps = lambda pr: pb[2 + (pr // 2) % 2][:, 128 * (pr % 2):128 * (pr % 2) + 128]
                    yfirst = lambda pr: pr % 2 == 0
                for g in grp_range:
                    V(lambda e, g=g: e.tensor_tensor(out=rhsR[:], in0=acs[:].unsqueeze(1).broadcast_to([32, 4, 128]), in1=identf[0:32, 4 * g:4 * g + 4].unsqueeze(2).broadcast_to([32, 4, 128]), op=ALU.mult), ["acs", "identf"], ["rhsR"])
                    T(lambda e: e.matmul(pb[6][:, :], lhsT=ones32[:], rhs=rhsR[:].rearrange("p a b -> p (a b)"), start=True, stop=True), ["ones32", "rhsR"], ["pb6"])
                    A(lambda e: e.activation(out=Eg[:].rearrange("p a b -> p (a b)"), in_=pb[6][:, :], func=AF.Exp), ["pb6"], ["Eg"])
                    G(lambda e, g=g: e.tensor_tensor(out=CEall[:, 4 * g:4 * g + 4, :], in0=Eg[:], in1=xc[:, 24 + g, :].unsqueeze(1).broadcast_to([128, 4, 128]), op=ALU.mult), ["Eg", f"xc{24 + g}"], ["CEall"])
                    for h4 in range(4):
                        V(lambda e, g=g, h4=h4: e.scalar_tensor_tensor(out=Dm[:, h4, :], in0=pb[6][:, 128 * h4:128 * h4 + 128], scalar=tokm[:, 0, 4 * g + h4:4 * g + h4 + 1], in1=negm[:], op0=ALU.subtract, op1=ALU.min), ["pb6", "tokm", kn(negm)], ["Dm"])
                    A(lambda e: e.activation(out=Lg[:].rearrange("p a b -> p (a b)"), in_=Dm[:].rearrange("p a b -> p (a b)"), func=AF.Exp), ["Dm"], ["Lg"])
                    T(lambda e, g=g: e.matmul(pb[7][:, 0:128], lhsT=xc[:, 16 + g, :], rhs=xc[:, 24 + g, :], start=True, stop=True), [f"xc{16 + g}", f"xc{24 + g}"], ["pb7"])
                    V(lambda e: e.tensor_tensor(out=Mg[:], in0=Lg[:], in1=pb[7][:, 0:128].unsqueeze(1).broadcast_to([128, 4, 128]), op=ALU.mult), ["Lg", "pb7"], ["Mg"])
                    V(lambda e, g=g: e.tensor_copy(out=dch[:, 4 * g:4 * g + 4, 0:nseq], in_=Eg[:].rearrange("p a (s l) -> p a s l", l=L)[:, :, :, L - 1]), ["Eg"], ["dch"])
                    for j in range(2):
                        pr = 2 * g + j
                        T(lambda e, pr=pr, j=j: e.matmul(yps(pr), lhsT=XA[:, pr, :], rhs=Mg[:, 2 * j, :], start=yfirst(pr), stop=False, skip_group_check=True), ["XA", "Mg"], [ypk(pr)])
                        T(lambda e, pr=pr, j=j: e.matmul(yps(pr), lhsT=XB[:, pr, :], rhs=Mg[:, 2 * j + 1, :], start=False, stop=False, skip_group_check=True), ["XB", "Mg"], [ypk(pr)])
                    if not smp:
                        sak_ = [f"SA{i}" for i in range(4)] + [f"SB{i}" for i in range(4)]
                        for j in range(2):
                            pr = 2 * g + j
                            T(lambda e, pr=pr: e.matmul(yps(pr), lhsT=SA[:, pr, :], rhs=CEall[:, 2 * pr, :], start=False, stop=False, skip_group_check=True), sak_ + ["CEall"], [ypk(pr)])
                            T(lambda e, pr=pr, j=j: e.matmul(yps(pr), lhsT=SB[:, pr, :], rhs=CEall[:, 2 * pr + 1, :], start=False, stop=(j == 1), skip_group_check=True), sak_ + ["CEall"], [ypk(pr)])
                        yevac(g)
                for s in range(nseq if smp else 0):
                    cs = slice(L * s, L * s + L)
                    last = (s == nseq - 1)
                    if smp:
                        Bm = Bms[0]; bmk = "Bm0"
                        G(lambda e, s=s, Bm=Bm: e.tensor_scalar(out=Bm[:].rearrange("p a b -> p (a b)"), in0=Btok[:].rearrange("p a b -> p (a b)"), scalar1=seqmask[:, s:s + 1], scalar2=None, op0=ALU.mult), ["Btok", "seqmask"], [bmk])
                    for q4 in q4_range:
                        prs_ = range(4 * q4, 4 * q4 + 4) if smp else range(16)
                        sak = [f"SA{q4}", f"SB{q4}"] if smp else [f"SA{i}" for i in range(4)] + [f"SB{i}" for i in range(4)]
                        if smp:
                            S0q = S0qs[it % 2]; s0k = f"S0q{it % 2}"; Snq = Snqs[it % 2]; snk = f"Snq{it % 2}"
                            tb_ = 7 if it % 2 == 0 else 0
                            nb_ = 6 if it % 2 == 0 else 1
                            it += 1
                            sview = I["ssm0"][s].rearrange("(pr h2) p n -> (h2 p) pr n", h2=2)[:, 4 * q4:4 * q4 + 4, :]
                            DMA(lambda e, sview=sview, S0q=S0q: e.dma_start(out=S0q[:], in_=sview), w=[s0k])
                            for j in range(4):
                                T(lambda e, j=j, S0q=S0q, tb_=tb_: e.transpose(out=pb[tb_][:, 128 * j:128 * j + 128], in_=S0q[:, j, :], identity=identf[:]), [s0k, "identf"], [f"pb{tb_}"])
                            pv = pb[tb_][:, :].rearrange("p (a h2 q) -> p a h2 q", h2=2, q=64)
                            A(lambda e, q4=q4, pv=pv: e.activation(out=SA[:, 4 * q4:4 * q4 + 4, 0:64], in_=pv[:, :, 0, :], func=AF.Copy), [f"pb{tb_}"], [f"SA{q4}"])
                            A(lambda e, q4=q4, pv=pv: e.activation(out=SB[:, 4 * q4:4 * q4 + 4, 64:128], in_=pv[:, :, 1, :], func=AF.Copy), [f"pb{tb_}"], [f"SB{q4}"])
                        for pr in prs_:
                            T(lambda e, pr=pr, cs=cs: e.matmul(yps(pr)[:, cs], lhsT=SA[:, pr, :], rhs=CEall[:, 2 * pr, cs], start=False, stop=False, skip_group_check=True), sak + ["CEall"], [ypk(pr)])
                            T(lambda e, pr=pr, cs=cs, last=last: e.matmul(yps(pr)[:, cs], lhsT=SB[:, pr, :], rhs=CEall[:, 2 * pr + 1, cs], start=False, stop=last, skip_group_check=True), sak + ["CEall"], [ypk(pr)])
                        if smp:
                            for j in range(4):
                                pr = 4 * q4 + j
                                T(lambda e, pr=pr, j=j, Bm=Bm, nb_=nb_: e.matmul(pb[nb_][:, 128 * j:128 * j + 128], lhsT=Xd[:, 2 * pr:2 * pr + 2, :].rearrange("p a b -> p (a b)"), rhs=Bm[:, pr // 2, :], start=True, stop=True), ["Xd", bmk], [f"pb{nb_}"])
                            for j in range(4):
                                pr = 4 * q4 + j
                                for h2 in range(2):
                                    hs = slice(64 * h2, 64 * h2 + 64)
                                    V(lambda e, pr=pr, j=j, hs=hs, s=s, h2=h2, S0q=S0q, Snq=Snq, nb_=nb_: e.scalar_tensor_tensor(out=Snq[hs, j, :], in0=S0q[hs, j, :], scalar=dch[hs, 2 * pr + h2, s:s + 1], in1=pb[nb_][hs, 128 * j:128 * j + 128], op0=ALU.mult, op1=ALU.add), [s0k, "dch", f"pb{nb_}"], [snk])
                            oview = O["ssm_s"][s].rearrange("(pr h2) p n -> (h2 p) pr n", h2=2)[:, 4 * q4:4 * q4 + 4, :]
                            DMAO(lambda e, oview=oview, Snq=Snq: e.dma_start(out=oview, in_=Snq[:]), [snk], [])
                if smp:
                    for gq in grp_range:
                        yevac(gq)
            if smp:
                sv = STt[:].rearrange("p (pr h2) q -> p pr h2 q", h2=2)
                A(lambda e, sv=sv: e.activation(out=SA[:, :, 0:64], in_=sv[:, :, 0, :], func=AF.Copy), ["STt"], [f"SA{i}" for i in range(4)])
                A(lambda e, sv=sv: e.activation(out=SB[:, :, 64:128], in_=sv[:, :, 1, :], func=AF.Copy), ["STt"], [f"SB{i}" for i in range(4)])
            TAG[0] = "A.state"
            if not smp:
                V(lambda e: e.tensor_tensor(out=STt[:], in0=STt[:], in1=dch[:, :, 0:1].broadcast_to([128, 32, 64]), op=ALU.mult), ["STt", "dch"], ["STt"])
                for rnd in range(2):
                    for g in range(4 * rnd, 4 * rnd + 4):
                        bk_ = 2 + (g % 4) // 2
                        T(lambda e, g=g, bk_=bk_: e.matmul(pb[bk_][:, 256 * (g % 2):256 * (g % 2) + 256], lhsT=Btok[:, g, :], rhs=Xd[:, 4 * g:4 * g + 4, :].rearrange("p a b -> p (a b)"), start=True, stop=True), ["Btok", "Xd"], [f"pb{bk_}"])
                    for b2_ in range(2):
                        h0 = 16 * rnd + 8 * b2_
                        V(lambda e, h0=h0, b2_=b2_: e.tensor_tensor(out=STt[:, h0:h0 + 8, :], in0=STt[:, h0:h0 + 8, :], in1=pb[2 + b2_][:, :].rearrange("p (h q) -> p h q", q=64), op=ALU.add), ["STt", f"pb{2 + b2_}"], ["STt"])
                sv = STt[:].rearrange("p (pr h2) q -> p pr h2 q", h2=2)
                A(lambda e, sv=sv: e.activation(out=SA[:, :, 0:64], in_=sv[:, :, 0, :], func=AF.Copy), ["STt"], [f"SA{i}" for i in range(4)])
                A(lambda e, sv=sv: e.activation(out=SB[:, :, 64:128], in_=sv[:, :, 1, :], func=AF.Copy), ["STt"], [f"SB{i}" for i in range(4)])
                if lastp:
                    for q4 in range(4):
                        for j in range(4):
                            pr = 4 * q4 + j
                            T(lambda e, pr=pr, j=j: e.transpose(out=pb[6][:, 128 * j:128 * j + 128], in_=STt[:, 2 * pr:2 * pr + 2, :].rearrange("p a b -> p (a b)"), identity=identf[:]), ["STt", "identf"], ["pb6"])
                        Snq = Snqs[q4 % 2]; snk = f"Snq{q4 % 2}"
                        V(lambda e, Snq=Snq: e.tensor_copy(out=Snq[:].rearrange("p a b -> p (a b)"), in_=pb[6][:, :]), ["pb6"], [snk])
                        oview = O["ssm_p"].rearrange("(pr h2) p n -> (h2 p) pr n", h2=2)[:, 4 * q4:4 * q4 + 4, :]
                        DMAO(lambda e, oview=oview, Snq=Snq: e.dma_start(out=oview, in_=Snq[:]), [snk], [])
            TAG[0] = "A.dense"
            dense_T("w_down_ssd", 16, lambda k: ynT[:, k, :], lambda k: [f"ynT{k}"],
                    lambda m, ps, pk: V(lambda e: e.tensor_tensor(out=yaT[:, m, :], in0=ps, in1=gas[:, m, :], op=ALU.mult), [pk, f"ga{m}"], [f"yaT_{par}_{m}"]))


        def recB(ti, par):
            u5 = u5_[par]; z5s = z5s_[par]; gbs = gbs_[par]; yaT = yaT_[par]
            smp = (ti == 16)
            lastp = (ti == 15)
            xsrc = I["xs"] if smp else I["xp"][128 * ti:128 * ti + 128, :]
            ydst = O["y_s"] if smp else O["y_p"][128 * ti:128 * ti + 128, :]
            if smp:
                for ri, nm in enumerate(["re0", "im0"]):
                    sv_ = I[nm].rearrange("s (gh gq g2) p -> s gq gh (g2 p)", gq=8, g2=2)
                    for s in range(16):
                        DMA(lambda e, s=s, sv_=sv_: e.dma_start(out=s5io[8 * s:8 * s + 8, :, :, :].rearrange("p a b c -> p a (b c)"), in_=sv_[s]), w=["s5io"])
                    for gh in range(4):
                        T(lambda e, gh=gh: e.transpose(out=pb[5][:, 0:128], in_=s5io[:, gh, :, :].rearrange("p a b -> p (a b)"), identity=identf[:]), ["s5io", "identf"], ["pb5"])
                        V(lambda e, gh=gh, ri=ri: e.tensor_copy(out=s0all[:, 8 * gh:8 * gh + 8, ri, :], in_=pb[5][:, 0:128].rearrange("p (s g) -> p g s", g=8)), ["pb5"], ["s0all"])
            TAG[0] = "B.s5"
            nsp, Lx = (1, 128) if smp else (2, 64)
            Ls = 8 if smp else 64

            def vw(t_, r=None):
                v = t_[:].rearrange("p a b c -> p (a b c)").rearrange("p (r s q t) -> p r s q t", r=2, s=nsp, q=4)
                return v if r is None else v[:, r]

            def pv_(t_):
                return t_[:].rearrange("p a b -> p (a b)").rearrange("p (s q t) -> p s q t", s=nsp, q=4)

            def sub(ap):
                return ap.rearrange("p s q (n l) -> p s q n l", l=Ls) if smp else ap
            for ft in range(8):
                gsl = slice(4 * ft, 4 * ft + 4)
                if smp:
                    ecv = ecos[:, gsl, 0:Ls].unsqueeze(1).unsqueeze(3).broadcast_to([128, 1, 4, 16, Ls])
                    esv = esin[:, gsl, 0:Ls].unsqueeze(1).unsqueeze(3).broadcast_to([128, 1, 4, 16, Ls])
                else:
                    ecv = ecos[:, gsl, 0:Ls].unsqueeze(1).broadcast_to([128, 2, 4, Ls])
                    esv = esin[:, gsl, 0:Ls].unsqueeze(1).broadcast_to([128, 2, 4, Ls])
                for q in range(4):
                    A(lambda e, ft=ft, q=q: e.activation(out=u5m[:, q, :], in_=u5[:, ft, :], func=AF.Copy, scale=pairmask[:, q:q + 1]), [f"u5_{par}_{ft}", "pairmask"], ["u5m"])
                for ri in range(2):
                    pbk = f"pb{4 + ri}"
                    for q in range(4):
                        T(lambda e, ft=ft, q=q, ri=ri: e.matmul(pb[4 + ri][:, 128 * q:128 * q + 128], lhsT=wb5[:, ft, ri, :], rhs=u5m[:, q, :], start=True, stop=True), ["wb5", "u5m"], [pbk])
                    A(lambda e, ri=ri: e.activation(out=vw(bu, ri), in_=pb[4 + ri][:, :].rearrange("p (q s t) -> p s q t", s=nsp, t=Lx), func=AF.Copy), [pbk], ["bu"])
                bre_, bim_ = sub(vw(bu, 0)), sub(vw(bu, 1))
                V(lambda e, ecv=ecv, bre_=bre_: e.tensor_tensor(out=sub(pv_(bt1)), in0=ecv, in1=bre_, op=ALU.mult), ["ecos", "bu"], ["bt1"])
                V(lambda e, esv=esv, bim_=bim_: e.tensor_tensor(out=sub(pv_(bt2)), in0=esv, in1=bim_, op=ALU.mult), ["esin", "bu"], ["bt2"])
                V(lambda e: e.tensor_tensor(out=vw(sgm, 0), in0=pv_(bt1), in1=pv_(bt2), op=ALU.add), ["bt1", "bt2"], ["sgm"])
                V(lambda e, ecv=ecv, bim_=bim_: e.tensor_tensor(out=sub(pv_(bt1)), in0=ecv, in1=bim_, op=ALU.mult), ["ecos", "bu", "sgm"], ["bt1"])
                V(lambda e, esv=esv, bre_=bre_: e.tensor_tensor(out=sub(pv_(bt2)), in0=esv, in1=bre_, op=ALU.mult), ["esin", "bu", "sgm"], ["bt2"])
                V(lambda e: e.tensor_tensor(out=vw(sgm, 1), in0=pv_(bt1), in1=pv_(bt2), op=ALU.subtract), ["bt1", "bt2"], ["sgm"])
                rtv = rts[:].rearrange("p a b -> p (a b)")[:, 0:4 * Lx]
                if smp:
                    V(lambda e, gsl=gsl: e.tensor_tensor(out=rts[:], in0=rdec[:, gsl].unsqueeze(2).broadcast_to([128, 4, 128]), in1=rm128[:].unsqueeze(1).broadcast_to([128, 4, 128]), op=ALU.mult), ["rdec", "rm128"], ["rts"])
                else:
                    V(lambda e, gsl=gsl, rtv=rtv: e.tensor_tensor(out=rtv.rearrange("p (q t) -> p q t", t=64), in0=rdec[:, gsl].unsqueeze(2).broadcast_to([128, 4, 64]), in1=rmq[:].rearrange("p (q t) -> p q t", t=64), op=ALU.mult), ["rdec", "rmq"], ["rts"])
                flat = lambda ap: ap.rearrange("p q t -> p (q t)")
                if smp:
                    ab = lambda t: t[:, gsl].unsqueeze(2).broadcast_to([128, 4, 16])
                    cmul(cin_[:, :, 0, :], cin_[:, :, 1, :], ab(are), ab(aim), s0all[:, gsl, 0, :], s0all[:, gsl, 1, :],
                         ["are", "aim", "s0all"], ["cinr", "cini"], ctm[:, :, 0, :], ctm[:, :, 1, :], ["ctm0", "ctm1"])
                    for ri in range(2):
                        tgt = sub(vw(sgm, ri))[:, 0, :, :, 0]
                        V(lambda e, tgt=tgt, ri=ri: e.tensor_tensor(out=tgt, in0=tgt, in1=cin_[:, :, ri, :], op=ALU.add), ["sgm", "cinr", "cini"], ["sgm"])
                    for ri in range(2):
                        V(lambda e, ri=ri, rtv=rtv: e.tensor_tensor_scan(out=flat(vw(bu, ri)[:, 0]), data0=rtv, data1=flat(vw(sgm, ri)[:, 0]), initial=0.0, op0=ALU.mult, op1=ALU.add), ["rts", "sgm"], ["bu"])
                else:
                    for sg in range(2):
                        src_re = sgc[:, gsl, 0:1] if sg == 0 else vw(bu, 0)[:, 0, :, 63:64]
                        src_im = sgc[:, gsl, 1:2] if sg == 0 else vw(bu, 1)[:, 0, :, 63:64]
                        ab = lambda t: t[:, gsl].unsqueeze(2)
                        cmul(cin_[:, :, 0, 0:1], cin_[:, :, 1, 0:1], ab(aere), ab(aeim), src_re, src_im,
                             ["aere", "aeim", "sgc", "bu"], ["cinr", "cini"], ctm[:, :, 0, 0:1], ctm[:, :, 1, 0:1], ["ctm0", "ctm1"])
                        for ri in range(2):
                            tgt = vw(sgm, ri)[:, sg, :, 0:1]
                            V(lambda e, tgt=tgt, ri=ri: e.tensor_tensor(out=tgt, in0=tgt, in1=cin_[:, :, ri, 0:1], op=ALU.add), ["sgm", "cinr", "cini"], ["sgm"])
                        for ri in range(2):
                            V(lambda e, ri=ri, sg=sg, rtv=rtv: e.tensor_tensor_scan(out=flat(vw(bu, ri)[:, sg]), data0=rtv, data1=flat(vw(sgm, ri)[:, sg]), initial=0.0, op0=ALU.mult, op1=ALU.add), ["rts", "sgm"], ["bu"])
                    V(lambda e, gsl=gsl: e.tensor_copy(out=sgc[:, gsl, :], in_=vw(bu)[:, :, 1, :, 63].rearrange("p r q -> p q r")), ["bu"], ["sgc"])
                sre_, sim_ = sub(vw(bu, 0)), sub(vw(bu, 1))
                s5v = lambda r: s5s[:, :, r, :].rearrange("p q (s t) -> p s q t", s=nsp)
                V(lambda e, ecv=ecv, sre_=sre_: e.tensor_tensor(out=sub(pv_(bt1)), in0=ecv, in1=sre_, op=ALU.mult), ["ecos", "bu"], ["bt1"])
                V(lambda e, esv=esv, sim_=sim_: e.tensor_tensor(out=sub(pv_(bt2)), in0=esv, in1=sim_, op=ALU.mult), ["esin", "bu"], ["bt2"])
                V(lambda e: e.tensor_tensor(out=s5v(0), in0=pv_(bt1), in1=pv_(bt2), op=ALU.subtract), ["bt1", "bt2"], ["s5s"])
                if smp or lastp:
                    fsrc = lambda t: (sub(pv_(t))[:, 0, :, :, Ls - 1] if smp else pv_(t)[:, 1, :, 63:64])
                    nf = 16 if smp else 1
                    V(lambda e, gsl=gsl, fsrc=fsrc, nf=nf: e.tensor_tensor(out=fin[:, gsl, 0, 0:nf], in0=fsrc(bt1), in1=fsrc(bt2), op=ALU.subtract), ["bt1", "bt2"], ["fin"])
                V(lambda e, ecv=ecv, sim_=sim_: e.tensor_tensor(out=sub(pv_(bt1)), in0=ecv, in1=sim_, op=ALU.mult), ["ecos", "bu", "s5s", "fin"], ["bt1"])
                V(lambda e, esv=esv, sre_=sre_: e.tensor_tensor(out=sub(pv_(bt2)), in0=esv, in1=sre_, op=ALU.mult), ["esin", "bu", "s5s", "fin"], ["bt2"])
                V(lambda e: e.tensor_tensor(out=s5v(1), in0=pv_(bt1), in1=pv_(bt2), op=ALU.add), ["bt1", "bt2"], ["s5s"])
                if smp or lastp:
                    V(lambda e, gsl=gsl, fsrc=fsrc, nf=nf: e.tensor_tensor(out=fin[:, gsl, 1, 0:nf], in0=fsrc(bt1), in1=fsrc(bt2), op=ALU.add), ["bt1", "bt2"], ["fin"])
                ps4 = [pb[4][:, 128 * q:128 * q + 128] for q in range(4)]
                for q in range(4):
                    for ri in range(2):
                        T(lambda e, ft=ft, q=q, ri=ri: e.matmul(ps4[q], lhsT=wd5[:, ft, ri, :], rhs=s5s[:, q, ri, :], start=(ri == 0), stop=(ri == 1)), ["wd5", "s5s"], ["pb4"])
                for q in range(4):
                    rs_ = slice(32 * q, 32 * q + 32)
                    V(lambda e, ft=ft, q=q, rs_=rs_: e.scalar_tensor_tensor(out=y5t[rs_, :], in0=u5[rs_, ft, :], scalar=ds5[rs_, ft:ft + 1], in1=ps4[q][rs_, :], op0=ALU.mult, op1=ALU.add), [f"u5_{par}_{ft}", "ds5", "pb4"], ["y5t"])
                A(lambda e, ft=ft: e.activation(out=y5g[:, ft, :], in_=y5t[:], func=AF.Gelu_apprx_tanh), ["y5t"], [f"y5g{ft}"])
            if lastp:
                for ri, nm in enumerate(["re_p", "im_p"]):
                    DMAO(lambda e, ri=ri, nm=nm: e.dma_start(out=O[nm].rearrange("(gp g2) p -> (g2 p) gp", g2=2), in_=fin[:, :, ri, 0], allow_slow_non_contiguous=True), ["fin"], [])
            if smp:
                for ri, nm in enumerate(["re_s", "im_s"]):
                    for gh in range(4):
                        V(lambda e, gh=gh, ri=ri: e.tensor_copy(out=s5tr[:], in_=fin[:, 8 * gh:8 * gh + 8, ri, :].rearrange("p g s -> p s g")), ["fin"], ["s5tr"])
                        T(lambda e: e.transpose(out=pb[5][:, 0:128], in_=s5tr[:].rearrange("p a b -> p (a b)"), identity=identf[:]), ["s5tr", "identf"], ["pb5"])
                        V(lambda e, gh=gh: e.tensor_copy(out=s5io[:, gh, :, :].rearrange("p a b -> p (a b)"), in_=pb[5][:, 0:128]), ["pb5"], ["s5io"])
                    ov = O[nm].rearrange("s (gh gq g2) p -> s gq gh (g2 p)", gq=8, g2=2)
                    for s in range(16):
                        DMAO(lambda e, s=s, ov=ov: e.dma_start(out=ov[s], in_=s5io[8 * s:8 * s + 8, :, :, :].rearrange("p a b c -> p a (b c)")), ["s5io"], [])

            TAG[0] = "B.glu"
            def glu_evac(m, ps, pk):
                A(lambda e: e.activation(out=glus[:], in_=ps, func=AF.Sigmoid, bias=bglu[:, m:m + 1]), [pk, "bglu"], ["glus"])
                V(lambda e: e.tensor_tensor(out=glus[:], in0=glus[:], in1=y5g[:, m, :], op=ALU.mult), ["glus", f"y5g{m}"], ["glus"])
                V(lambda e: e.tensor_tensor(out=y5f[:, m, :], in0=glus[:], in1=z5s[:, m, :], op=ALU.mult), ["glus", f"z5_{par}_{m}"], [f"y5f{m}"])
            dense_T("w_glu", 8, lambda k: y5g[:, k, :], lambda k: [f"y5g{k}"], glu_evac)
            dense_T("w_down_s5", 8, lambda k: y5f[:, k, :], lambda k: [f"y5f{k}"],
                    lambda m, ps, pk: V(lambda e: e.tensor_tensor(out=ybT[:, m, :], in0=ps, in1=gbs[:, m, :], op=ALU.mult), [pk, f"gb_{par}_{m}"], [f"ybT{m}"]))
            for m in range(8):
                V(lambda e, m=m: e.tensor_tensor(out=mixT[:, m, :], in0=yaT[:, m, :], in1=ybT[:, m, :], op=ALU.add), [f"yaT_{par}_{m}", f"ybT{m}"], [f"mixT{m}"])
            TAG[0] = "B.out"
            DMA(lambda e: e.dma_start(out=ot[:], in_=xsrc), w=["ot"])
            for cb in range(4):
                wt, wk = wload(WS["w_out"][:, 256 * cb:256 * cb + 256])
                hb = cb % 2
                ps = pb[4 + hb][:, 0:256]
                pks = [f"pb{4 + hb}"]
                for k in range(8):
                    T(lambda e, wt=wt, k=k, ps=ps: e.matmul(ps, lhsT=mixT[:, k, :], rhs=wt[:, k, 0:256], start=(k == 0), stop=(k == 7)), [wk, f"mixT{k}"], pks)
                V(lambda e, cb=cb, ps=ps: e.tensor_tensor(out=ot[:, 256 * cb:256 * cb + 256], in0=ps, in1=ot[:, 256 * cb:256 * cb + 256], op=ALU.add), pks + ["ot"], ["ot"])
            G(lambda e: e.memset(ssB[:], 0.0), w=["ssB"])
            A(lambda e: e.activation(out=y5f[:].rearrange("p a b -> p (a b)"), in_=ot[:], func=AF.Square, accum_out=ssB[:]), ["ot", "ssB"], [f"y5f{m}" for m in range(8)] + ["ssB"])
            A(lambda e: e.activation(out=rstdB[:], in_=ssB[:], func=AF.Sqrt, scale=1.0 / D, bias=epscol[:]), ["ssB", "epscol"], ["rstdB"])
            V(lambda e: e.reciprocal(out=rstdB[:], in_=rstdB[:]), ["rstdB"], ["rstdB"])
            V(lambda e: e.scalar_tensor_tensor(out=ot[:], in0=ot[:], scalar=rstdB[:, 0:1], in1=fnw[:], op0=ALU.mult, op1=ALU.mult), ["ot", "rstdB", "fnw"], ["ot"])
            DMAO(lambda e: e.dma_start(out=ydst, in_=ot[:]), ["ot"], [])


        def flush(lst):
            for (eng, fn, r, w, dma, tag) in lst:
                P.op(eng, fn, r, w, dma=dma)
                P.ops[-1]["tag"] = tag

        def record(fn_, ti, par, banks, wset):
            CUR[0] = []
            MMB[0] = banks
            WSET[0] = wset
            fn_(ti, par)
            lst = CUR[0]
            CUR[0] = None
            return lst

        def merge(la, lb):
            out = []
            ia = ib = 0
            na, nb = len(la), len(lb)
            while ia < na or ib < nb:
                if ib >= nb or (ia < na and ia * nb <= ib * na):
                    out.append(la[ia]); ia += 1
                else:
                    out.append(lb[ib]); ib += 1
            return out

        tl = list(tiles) if tiles is not None else (list(range(8)) + [16] + list(range(8, 16)))
        if tl:
            flush(record(recA, tl[0], 0, (0, 1), (0, 1)))
            for n_, ti in enumerate(tl):
                lb = record(recB, ti, n_ % 2, (4, 5), (2, 3))
                if n_ + 1 < len(tl):
                    la = record(recA, tl[n_ + 1], (n_ + 1) % 2, (0, 1), (0, 1))
                    if not INTERLEAVE[0]:
                        flush(lb); flush(la)
                    else:
                        flush(merge(la, lb))
                else:
                    flush(lb)
        P.emit(nc)
    return nc


_NC = {}


def _shard(inputs, c):
    d = {}
    d["xp"] = np.ascontiguousarray(inputs["x_prompt"][c])
    d["xs"] = np.ascontiguousarray(inputs["x_sample"][16 * c:16 * c + 16].reshape(128, 1024))
    d["conv0"] = np.ascontiguousarray(inputs["state_conv"][0, 16 * c:16 * c + 16])
    d["ssm0"] = np.ascontiguousarray(inputs["state_ssm"][0, 16 * c:16 * c + 16])
    d["re0"] = np.ascontiguousarray(inputs["state_s5_re"][0, 16 * c:16 * c + 16])
    d["im0"] = np.ascontiguousarray(inputs["state_s5_im"][0, 16 * c:16 * c + 16])
    for k in ("norm_w", "w_in", "conv_w", "conv_b", "dt_bias", "A_log", "D_ssd", "ssd_norm_w", "w_down_ssd", "lam_re",
              "lam_im", "log_dt", "B_re", "B_im", "C_re", "C_im", "D_s5", "w_glu", "b_glu", "w_down_s5", "w_out"):
        d[k] = np.ascontiguousarray(inputs[k][0])
    d["final_norm_w"] = np.ascontiguousarray(inputs["final_norm_w"])
    return d


def kernel(**inputs):
    inputs = {k: np.asarray(v, dtype=np.float32) for k, v in inputs.items()}
    if "nc" not in _NC:
        _NC["nc"] = build()
    nc = _NC["nc"]
    consts = host_consts()
    in_maps = []
    for c in range(NCORES):
        d = _shard(inputs, c)
        d.update(consts)
        in_maps.append(d)
    res = run_bass_kernel_spmd(nc, in_maps, core_ids=list(range(NCORES))).results
    cat = lambda n: np.concatenate([r[n] for r in res], axis=0)
    y_p = np.stack([r["y_p"] for r in res], 0)
    y_s = cat("y_s").reshape(128, 8, 1024)
    conv_p = np.stack([r["conv_p"] for r in res], 0)[None]
    ssm_p = np.stack([r["ssm_p"] for r in res], 0)[None]
    re_p = np.stack([r["re_p"] for r in res], 0)[None]
    im_p = np.stack([r["im_p"] for r in res], 0)[None]
    conv_s = cat("conv_s")[None]
    ssm_s = cat("ssm_s")[None]
    re_s = cat("re_s")[None]
    im_s = cat("im_s")[None]
    return tuple(np.ascontiguousarray(a, dtype=np.float32) for a in
                 (y_p, y_s, conv_p, ssm_p, re_p, im_p, conv_s, ssm_s, re_s, im_s))
```

```python
import contextlib
import math
import numpy as np
import concourse.bass as bass
import concourse.mybir as mybir
from concourse.bass_utils import run_bass_kernel_spmd

F32 = mybir.dt.float32
BF16 = mybir.dt.bfloat16
I32 = mybir.dt.int32
ALU = mybir.AluOpType
AF = mybir.ActivationFunctionType

ENGS = ("pe", "act", "dve", "pool", "sp")
N_DMA_SEM = 24
NCORES = 8
D = 1024
PROJ = 10272
Z0, XBC0, DT0, U0, Z50, GA0, GB0 = 0, 2048, 6144, 6176, 7200, 8224, 9248
EPS = 1e-6
SIN_SCALE = 6.283185


class Prog:
    def __init__(self):
        self.ops = []
        self.last_write = {}
        self.readers = {}
        self.dma_count = {e: 0 for e in ENGS}
        self.slot_last = {}
        self.last_op = {}
        self.pending = {}

    def barrier(self):
        deps = set(self.last_op.values()) | set(self.slot_last.values())
        for e in ENGS:
            self.pending[e] = set(deps) | self.pending.get(e, set())

    def op(self, eng, fn, reads=(), writes=(), dma=False):
        writes = list(writes) + [k for k in reads if k.startswith("pb")]
        reads = [k for k in reads if not k.startswith("pb")]
        oid = len(self.ops)
        deps = set()
        for k in reads:
            if k in self.last_write:
                deps.add(self.last_write[k])
        for k in writes:
            if k in self.last_write:
                deps.add(self.last_write[k])
            deps.update(self.readers.get(k, ()))
        if eng in self.pending:
            deps |= self.pending.pop(eng)
        rec = dict(eng=eng, fn=fn, deps=deps, dma=dma, sig=False, seq=None)
        if dma:
            i = self.dma_count[eng]
            self.dma_count[eng] += 1
            slot = i % N_DMA_SEM
            rec["slot"] = slot
            rec["val"] = 16 * (i // N_DMA_SEM + 1)
            prev = self.slot_last.get((eng, slot))
            if prev is not None:
                deps.add(prev)
            self.slot_last[(eng, slot)] = oid
        else:
            self.last_op[eng] = oid
        deps.discard(oid)
        self.ops.append(rec)
        for k in writes:
            self.last_write[k] = oid
            self.readers[k] = []
        for k in reads:
            self.readers.setdefault(k, []).append(oid)
        return oid

    def emit(self, nc):
        ops = self.ops

        def skip(dd, o):
            return (not dd["dma"]) and dd["eng"] == o["eng"] == "pe" and not o["dma"]
        for o in ops:
            for d in o["deps"]:
                dd = ops[d]
                if dd["dma"] or skip(dd, o):
                    continue
                dd["sig"] = True
        cnt = {e: 0 for e in ENGS}
        for o in ops:
            if o["sig"]:
                cnt[o["eng"]] += 1
                o["seq"] = cnt[o["eng"]]
        with contextlib.ExitStack() as st:
            csem = {e: st.enter_context(nc.semaphore("c_" + e)) for e in ENGS}
            dsem = {(e, s): st.enter_context(nc.semaphore(f"d_{e}{s}"))
                    for e in ENGS if self.dma_count[e] > 0
                    for s in range(min(N_DMA_SEM, self.dma_count[e]))}
            block = st.enter_context(nc.Block())

            def run(eng_name, eng):
                waited = {}
                for o in ops:
                    if o["eng"] != eng_name:
                        continue
                    for d in sorted(o["deps"]):
                        dd = ops[d]
                        if dd["dma"]:
                            key = ("d", dd["eng"], dd["slot"])
                            sem = dsem[(dd["eng"], dd["slot"])]
                            val = dd["val"]
                        else:
                            if skip(dd, o):
                                continue
                            key = ("c", dd["eng"])
                            sem = csem[dd["eng"]]
                            val = dd["seq"]
                        if waited.get(key, 0) >= val:
                            continue
                        waited[key] = val
                        eng.wait_ge(sem, val)
                    ins = o["fn"](eng)
                    if o["dma"]:
                        ins.then_inc(dsem[(eng_name, o["slot"])], 16)
                    elif o["sig"]:
                        ins.then_inc(csem[eng_name], 1)
                if eng_name == "sp":
                    for (e, s), oid in self.slot_last.items():
                        eng.wait_ge(dsem[(e, s)], ops[oid]["val"])

            block.tensor(lambda e: run("pe", e))
            block.scalar(lambda e: run("act", e))
            block.vector(lambda e: run("dve", e))
            block.gpsimd(lambda e: run("pool", e))
            block.sync(lambda e: run("sp", e))


def host_consts():
    c = {}
    c["ident"] = np.eye(128, dtype=np.float32)
    s = np.arange(128)
    c["negm_p"] = np.where(s[None, :] >= s[:, None], 0.0, -30000.0).astype(np.float32)
    same = (s[None, :] // 8) == (s[:, None] // 8)
    c["negm_s"] = np.where((s[None, :] >= s[:, None]) & same, 0.0, -30000.0).astype(np.float32)
    rp = np.ones((32, 128), np.float32); rp[:, 0] = 0
    rs = np.ones((32, 128), np.float32); rs[:, ::8] = 0
    c["rm_p"] = rp
    c["rm_s"] = rs
    r128 = np.ones((128, 128), np.float32); r128[:, ::8] = 0
    c["rm128"] = r128
    rq = np.ones((128, 256), np.float32); rq[:, ::64] = 0
    c["rmq"] = rq
    c["seqmask"] = (s[:, None] // 8 == np.arange(16)[None, :]).astype(np.float32)
    c["pairmask"] = (s[:, None] // 32 == np.arange(4)[None, :]).astype(np.float32)
    c["iota"] = np.broadcast_to(np.arange(128, dtype=np.float32)[None, :], (128, 128)).copy()
    return c


IN_SPECS = [
    ("xp", (2048, 1024)), ("xs", (128, 1024)), ("conv0", (16, 3, 4096)), ("ssm0", (16, 32, 64, 128)),
    ("re0", (16, 64, 64)), ("im0", (16, 64, 64)), ("norm_w", (1024,)), ("w_in", (1024, PROJ)),
    ("conv_w", (4, 4096)), ("conv_b", (4096,)), ("dt_bias", (32,)), ("A_log", (32,)), ("D_ssd", (32,)),
    ("ssd_norm_w", (2048,)), ("w_down_ssd", (2048, 1024)), ("lam_re", (64, 64)), ("lam_im", (64, 64)),
    ("log_dt", (64,)), ("B_re", (64, 64, 16)), ("B_im", (64, 64, 16)), ("C_re", (64, 16, 64)),
    ("C_im", (64, 16, 64)), ("D_s5", (1024,)), ("w_glu", (1024, 1024)), ("b_glu", (1024,)),
    ("w_down_s5", (1024, 1024)), ("w_out", (1024, 1024)), ("final_norm_w", (1024,)),
    ("ident", (128, 128)), ("negm_p", (128, 128)), ("negm_s", (128, 128)), ("rm_p", (32, 128)),
    ("rm_s", (32, 128)), ("rm128", (128, 128)), ("rmq", (128, 256)), ("seqmask", (128, 16)), ("pairmask", (128, 4)),
    ("iota", (128, 128)),
]
OUT_SPECS = [
    ("y_p", (2048, 1024)), ("y_s", (128, 1024)), ("conv_p", (3, 4096)), ("ssm_p", (32, 64, 128)),
    ("re_p", (64, 64)), ("im_p", (64, 64)), ("conv_s", (16, 3, 4096)), ("ssm_s", (16, 32, 64, 128)),
    ("re_s", (16, 64, 64)), ("im_s", (16, 64, 64)),
]


STOP = [None]
INTERLEAVE = [True]
BLK = [None]


class _Stop(Exception):
    pass


def ck(n):
    if STOP[0] is not None and n >= STOP[0]:
        raise _Stop()


def build(tiles=None):
    nc = bass.Bass("TRN2", target_bir_lowering=False)
    I = {n: nc.dram_tensor(n, list(s), F32, kind="ExternalInput").ap() for n, s in IN_SPECS}
    O = {n: nc.dram_tensor(n, list(s), F32, kind="ExternalOutput").ap() for n, s in OUT_SPECS}
    WS = {
        "w_in": nc.dram_tensor("ws_w_in", [1024, PROJ], BF16).ap(),
        "w_down_ssd": nc.dram_tensor("ws_wds", [2048, 1024], BF16).ap(),
        "w_glu": nc.dram_tensor("ws_wglu", [1024, 1024], BF16).ap(),
        "w_down_s5": nc.dram_tensor("ws_wd5", [1024, 1024], BF16).ap(),
        "w_out": nc.dram_tensor("ws_wout", [1024, 1024], BF16).ap(),
    }
    P = Prog()

    CUR = [None]
    MMB = [(0, 1)]

    TAG = ["pro"]

    def _rec(eng, fn, r, w, dma=False):
        if CUR[0] is None:
            P.op(eng, fn, r, w, dma=dma)
            P.ops[-1]["tag"] = TAG[0]
        else:
            CUR[0].append((eng, fn, tuple(r), tuple(w), dma, TAG[0]))

    def V(fn, r=(), w=()):
        _rec("dve", fn, r, w)

    def G(fn, r=(), w=()):
        _rec("pool", fn, r, w)

    def A(fn, r=(), w=()):
        _rec("act", fn, r, w)

    def T(fn, r=(), w=()):
        _rec("pe", fn, r, w)

    def DMA(fn, r=(), w=()):
        _rec("sp", fn, r, w, dma=True)

    def DMAO(fn, r=(), w=()):
        _rec("pool", fn, r, w, dma=True)

    def DMAH(fn, r=(), w=()):
        _rec("act", fn, r, w, dma=True)

    st = contextlib.ExitStack()
    with st:
        def sb(name, shape, dt=F32, stack=None):
            return (stack or st).enter_context(nc.sbuf_tensor("s_" + name, list(shape), dt))

        def kn(t):
            n_ = getattr(t, "name")
            return n_[2:] if n_.startswith("s_") else n_
        pb = [st.enter_context(nc.psum_tensor(f"pb{i}", [128, 512], F32)) for i in range(8)]
        pbb = [p.bitcast(BF16) for p in pb]

        identf = sb("identf", [128, 128]); identb = sb("identb", [128, 128], BF16)
        negm_p = sb("negm_p", [128, 128]); negm_s = sb("negm_s", [128, 128])
        rm_p = sb("rm_p", [32, 128]); rm_s = sb("rm_s", [32, 128]); rm128 = sb("rm128", [128, 128]); rmq = sb("rmq", [128, 256])
        seqmask = sb("seqmask", [128, 16]); pairmask = sb("pairmask", [128, 4])
        ones32 = sb("ones32", [32, 128]); onesb = sb("onesb", [128, 128], BF16)
        onecol = sb("onecol", [128, 1]); epscol = sb("epscol", [128, 1])
        for nm, t in [("ident", identf), ("negm_p", negm_p), ("negm_s", negm_s), ("rm_p", rm_p), ("rm_s", rm_s),
                      ("rm128", rm128), ("rmq", rmq), ("seqmask", seqmask), ("pairmask", pairmask)]:
            DMA(lambda e, nm=nm, t=t: e.dma_start(out=t[:], in_=I[nm]), w=[kn(t)])
        V(lambda e: e.tensor_copy(out=identb[:], in_=identf[:]), ["identf"], ["identb"])
        G(lambda e: e.memset(ones32[:], 1.0), w=["ones32"])
        G(lambda e: e.memset(onesb[:], 1.0), w=["onesb"])
        G(lambda e: e.memset(onecol[:], 1.0), w=["onecol"])
        G(lambda e: e.memset(epscol[:], EPS), w=["epscol"])

        def colvec(name, src, nt):
            t = sb(name, [128, nt])
            DMA(lambda e: e.dma_start(out=t[:], in_=src.rearrange("(j p) -> p j", p=128), allow_slow_non_contiguous=True), w=[name])
            return t
        normw = colvec("normw", I["norm_w"], 8)
        convb = colvec("convb", I["conv_b"], 32)
        ssdnw = colvec("ssdnw", I["ssd_norm_w"], 16)
        ds5 = colvec("ds5", I["D_s5"], 8)
        bglu = colvec("bglu", I["b_glu"], 8)
        convw = sb("convw", [128, 32, 4])
        for k in range(4):
            DMA(lambda e, k=k: e.dma_start(out=convw[:, :, k], in_=I["conv_w"][k].rearrange("(j p) -> p j", p=128), allow_slow_non_contiguous=True), w=["convw"])
        fnw = sb("fnw", [128, 1024])
        DMA(lambda e: e.dma_start(out=fnw[:], in_=I["final_norm_w"].partition_broadcast(128)), w=["fnw"])
        dtb = sb("dtb", [32, 1]); aneg = sb("aneg", [32, 1])
        DMA(lambda e: e.dma_start(out=dtb[:], in_=I["dt_bias"].rearrange("(h o) -> h o", o=1)), w=["dtb"])
        DMA(lambda e: e.dma_start(out=aneg[:], in_=I["A_log"].rearrange("(h o) -> h o", o=1)), w=["aneg"])
        A(lambda e: e.activation(out=aneg[:], in_=aneg[:], func=AF.Exp), ["aneg"], ["aneg"])
        V(lambda e: e.tensor_scalar(out=aneg[:], in0=aneg[:], scalar1=-1.0, scalar2=None, op0=ALU.mult), ["aneg"], ["aneg"])
        dexp = sb("dexp", [128, 16])
        dv = I["D_ssd"].rearrange("(pr h2) -> h2 pr", h2=2)
        for h2 in range(2):
            DMA(lambda e, h2=h2: e.dma_start(out=dexp[64 * h2:64 * h2 + 64, :], in_=dv[h2].partition_broadcast(64), allow_slow_non_contiguous=True), w=["dexp"])
        are = sb("are", [128, 32]); aim = sb("aim", [128, 32]); rdec = sb("rdec", [128, 32])
        aere = sb("aere", [128, 32]); aeim = sb("aeim", [128, 32])
        ecos = sb("ecos", [128, 32, 64]); esin = sb("esin", [128, 32, 64])
        wb5 = sb("wb5", [128, 8, 2, 128], BF16); wd5 = sb("wd5", [128, 8, 2, 128], BF16)
        halo = sb("halo", [128, 32, 3])
        STt = sb("STt", [128, 32, 64])
        SA = sb("SA", [128, 16, 128], BF16); SB = sb("SB", [128, 16, 128], BF16)
        XA = sb("XA", [128, 16, 128], BF16); XB = sb("XB", [128, 16, 128], BF16)
        sgc = sb("sgc", [128, 32, 2])
        for t_ in (halo, STt, SA, SB, XA, XB, sgc, wd5):
            G(lambda e, t_=t_: e.memset(t_[:], 0.0), w=([kn(t_)] if kn(t_) not in ("SA", "SB") else [f"{kn(t_)}{i}" for i in range(4)]))

        pst = contextlib.ExitStack()
        with pst:
            def psb(name, shape, dt=F32):
                return sb(name, shape, dt, stack=pst)
            iota = psb("iota", [128, 128])
            DMA(lambda e: e.dma_start(out=iota[:], in_=I["iota"]), w=["iota"])
            lre = psb("lre", [128, 32]); lim = psb("lim", [128, 32]); stp = psb("stp", [128, 32]); th = psb("th", [128, 32])
            DMA(lambda e: e.dma_start(out=lre[:], in_=I["lam_re"].rearrange("(gp g2) p -> (g2 p) gp", g2=2), allow_slow_non_contiguous=True), w=["lre"])
            DMA(lambda e: e.dma_start(out=lim[:], in_=I["lam_im"].rearrange("(gp g2) p -> (g2 p) gp", g2=2), allow_slow_non_contiguous=True), w=["lim"])
            ldv = I["log_dt"].rearrange("(gp g2) -> g2 gp", g2=2)
            for g2 in range(2):
                DMA(lambda e, g2=g2: e.dma_start(out=stp[64 * g2:64 * g2 + 64, :], in_=ldv[g2].partition_broadcast(64), allow_slow_non_contiguous=True), w=["stp"])
            A(lambda e: e.activation(out=stp[:], in_=stp[:], func=AF.Exp), ["stp"], ["stp"])
            V(lambda e: e.tensor_tensor(out=rdec[:], in0=lre[:], in1=stp[:], op=ALU.mult), ["lre", "stp"], ["rdec"])
            A(lambda e: e.activation(out=rdec[:], in_=rdec[:], func=AF.Exp), ["rdec"], ["rdec"])
            V(lambda e: e.tensor_tensor(out=th[:], in0=lim[:], in1=stp[:], op=ALU.mult), ["lim", "stp"], ["th"])
            V(lambda e: e.tensor_scalar(out=th[:], in0=th[:], scalar1=1.0 / (2.0 * math.pi), scalar2=None, op0=ALU.mult), ["th"], ["th"])
            tmpx = psb("tmpx", [128, 2048]); tmpf = psb("tmpf", [128, 2048]); tmpi = psb("tmpi", [128, 2048], I32)

            def sin_turns(out_ap, mk_x, n, off, rkeys, wkeys):
                V(lambda e: mk_x(e, tmpx[:, 0:n]), rkeys, ["tmpx"])
                if off:
                    V(lambda e: e.tensor_scalar(out=tmpx[:, 0:n], in0=tmpx[:, 0:n], scalar1=float(off), scalar2=None, op0=ALU.add), ["tmpx"], ["tmpx"])
                V(lambda e: e.tensor_copy(out=tmpi[:, 0:n], in_=tmpx[:, 0:n]), ["tmpx"], ["tmpi"])
                V(lambda e: e.tensor_copy(out=tmpf[:, 0:n], in_=tmpi[:, 0:n]), ["tmpi"], ["tmpf"])
                V(lambda e: e.tensor_tensor(out=tmpx[:, 0:n], in0=tmpx[:, 0:n], in1=tmpf[:, 0:n], op=ALU.subtract), ["tmpx", "tmpf"], ["tmpx"])
                A(lambda e: e.activation(out=out_ap, in_=tmpx[:, 0:n], func=AF.Sin, scale=SIN_SCALE), ["tmpx"], wkeys)
            cs1 = psb("cs1", [128, 32]); sn1 = psb("sn1", [128, 32])
            x1 = lambda e, dst: e.tensor_copy(out=dst, in_=th[:])
            sin_turns(sn1[:], x1, 32, 0.0, ["th"], ["sn1"])
            sin_turns(cs1[:], x1, 32, 0.25, ["th"], ["cs1"])
            V(lambda e: e.tensor_tensor(out=are[:], in0=rdec[:], in1=cs1[:], op=ALU.mult), ["rdec", "cs1"], ["are"])
            V(lambda e: e.tensor_tensor(out=aim[:], in0=rdec[:], in1=sn1[:], op=ALU.mult), ["rdec", "sn1"], ["aim"])
            x64 = lambda e, dst: e.tensor_scalar(out=dst, in0=th[:], scalar1=64.0, scalar2=None, op0=ALU.mult)
            sin_turns(sn1[:], x64, 32, 0.0, ["th"], ["sn1"])
            sin_turns(cs1[:], x64, 32, 0.25, ["th"], ["cs1"])
            V(lambda e: e.tensor_tensor(out=aere[:], in0=rdec[:], in1=cs1[:], op=ALU.mult), ["rdec", "cs1"], ["aere"])
            V(lambda e: e.tensor_tensor(out=aeim[:], in0=rdec[:], in1=sn1[:], op=ALU.mult), ["rdec", "sn1"], ["aeim"])
            xt_ = lambda e, dst: e.tensor_tensor(out=dst.rearrange("p (a b) -> p a b", b=64), in0=th[:].unsqueeze(2).broadcast_to([128, 32, 64]),
                                                 in1=iota[:, 0:64].unsqueeze(1).broadcast_to([128, 32, 64]), op=ALU.mult)
            sin_turns(esin[:].rearrange("p a b -> p (a b)"), xt_, 2048, 0.0, ["th", "iota"], ["esin"])
            sin_turns(ecos[:].rearrange("p a b -> p (a b)"), xt_, 2048, 0.25, ["th", "iota"], ["ecos"])
            gre = psb("gre", [128, 32]); gim = psb("gim", [128, 32]); den = psb("den", [128, 32])
            t1 = psb("t1s", [128, 32]); t2 = psb("t2s", [128, 32]); am1 = psb("am1", [128, 32])
            V(lambda e: e.tensor_scalar(out=am1[:], in0=are[:], scalar1=-1.0, scalar2=None, op0=ALU.add), ["are"], ["am1"])
            V(lambda e: e.tensor_tensor(out=den[:], in0=lre[:], in1=lre[:], op=ALU.mult), ["lre"], ["den"])
            V(lambda e: e.tensor_tensor(out=t1[:], in0=lim[:], in1=lim[:], op=ALU.mult), ["lim"], ["t1s"])
            V(lambda e: e.tensor_tensor(out=den[:], in0=den[:], in1=t1[:], op=ALU.add), ["den", "t1s"], ["den"])
            V(lambda e: e.reciprocal(out=den[:], in_=den[:]), ["den"], ["den"])
            V(lambda e: e.tensor_tensor(out=t1[:], in0=am1[:], in1=lre[:], op=ALU.mult), ["am1", "lre"], ["t1s"])
            V(lambda e: e.tensor_tensor(out=t2[:], in0=aim[:], in1=lim[:], op=ALU.mult), ["aim", "lim"], ["t2s"])
            V(lambda e: e.tensor_tensor(out=gre[:], in0=t1[:], in1=t2[:], op=ALU.add), ["t1s", "t2s"], ["gre"])
            V(lambda e: e.tensor_tensor(out=gre[:], in0=gre[:], in1=den[:], op=ALU.mult), ["gre", "den"], ["gre"])
            V(lambda e: e.tensor_tensor(out=t1[:], in0=aim[:], in1=lre[:], op=ALU.mult), ["aim", "lre"], ["t1s"])
            V(lambda e: e.tensor_tensor(out=t2[:], in0=am1[:], in1=lim[:], op=ALU.mult), ["am1", "lim"], ["t2s"])
            V(lambda e: e.tensor_tensor(out=gim[:], in0=t1[:], in1=t2[:], op=ALU.subtract), ["t1s", "t2s"], ["gim"])
            V(lambda e: e.tensor_tensor(out=gim[:], in0=gim[:], in1=den[:], op=ALU.mult), ["gim", "den"], ["gim"])
            bre = psb("bre", [128, 32, 16]); bim = psb("bim", [128, 32, 16]); bbr = psb("bbr", [128, 32, 16]); bbi = psb("bbi", [128, 32, 16]); tb = psb("tbs", [128, 32, 16])
            DMA(lambda e: e.dma_start(out=bre[:], in_=I["B_re"].rearrange("(gp g2) p k -> (g2 p) gp k", g2=2)), w=["bre"])
            DMA(lambda e: e.dma_start(out=bim[:], in_=I["B_im"].rearrange("(gp g2) p k -> (g2 p) gp k", g2=2)), w=["bim"])
            gb_ = lambda t: t[:].unsqueeze(2).broadcast_to([128, 32, 16])
            V(lambda e: e.tensor_tensor(out=bbr[:], in0=bre[:], in1=gb_(gre), op=ALU.mult), ["bre", "gre"], ["bbr"])
            V(lambda e: e.tensor_tensor(out=tb[:], in0=bim[:], in1=gb_(gim), op=ALU.mult), ["bim", "gim"], ["tbs"])
            V(lambda e: e.tensor_tensor(out=bbr[:], in0=bbr[:], in1=tb[:], op=ALU.subtract), ["bbr", "tbs"], ["bbr"])
            V(lambda e: e.tensor_tensor(out=bbi[:], in0=bim[:], in1=gb_(gre), op=ALU.mult), ["bim", "gre"], ["bbi"])
            V(lambda e: e.tensor_tensor(out=tb[:], in0=bre[:], in1=gb_(gim), op=ALU.mult), ["bre", "gim"], ["tbs"])
            V(lambda e: e.tensor_tensor(out=bbi[:], in0=bbi[:], in1=tb[:], op=ALU.add), ["bbi", "tbs"], ["bbi"])
            cin = psb("cin", [128, 4, 2, 64])
            cst = [psb("cstr", [128, 32, 16]), psb("csti", [128, 32, 16])]
            for ri, nm in enumerate(["C_re", "C_im"]):
                cv = I[nm].rearrange("(gh gq g2) k p -> gq k gh g2 p", gq=8, g2=2)
                for gq in range(8):
                    for gh in range(4):
                        DMA(lambda e, gq=gq, gh=gh, cv=cv: e.dma_start(out=cin[16 * gq:16 * gq + 16, gh, :, :], in_=cv[gq][:, gh, :, :]), w=["cin"])
                for gh in range(4):
                    T(lambda e, gh=gh: e.transpose(out=pb[7][:, 0:128], in_=cin[:, gh, :, :].rearrange("p a b -> p (a b)"), identity=identf[:]), ["cin", "identf"], ["pb7"])
                    V(lambda e, gh=gh, ri=ri: e.tensor_copy(out=cst[ri][:, 8 * gh:8 * gh + 8, :], in_=pb[7][:, 0:128].rearrange("p (a b) -> p a b", b=16)), ["pb7"], [kn(cst[ri])])
            zst = psb("zst", [128, 4, 2, 16])
            G(lambda e: e.memset(zst[:], 0.0), w=["zst"])
            for ft in range(8):
                for ri in range(2):
                    src = [bbr, bbi][ri]
                    for h2 in range(2):
                        V(lambda e, ft=ft, h2=h2, src=src: e.tensor_copy(out=zst[64 * h2:64 * h2 + 64, :, h2, :], in_=src[64 * h2:64 * h2 + 64, 4 * ft:4 * ft + 4, :]), [kn(src)], ["zst"])
                    T(lambda e: e.transpose(out=pb[6][:, 0:128], in_=zst[:].rearrange("p a b c -> p (a b c)"), identity=identf[:]), ["zst", "identf"], ["pb6"])
                    V(lambda e, ft=ft, ri=ri: e.tensor_copy(out=wb5[:, ft, ri, :], in_=pb[6][:, 0:128]), ["pb6"], ["wb5"])
                    for h2 in range(2):
                        V(lambda e, ft=ft, ri=ri, h2=h2: e.tensor_scalar(
                            out=wd5[64 * h2:64 * h2 + 64, ft, ri, :].rearrange("p (q g k) -> p q g k", g=2, k=16)[:, :, h2, :],
                            in0=cst[ri][64 * h2:64 * h2 + 64, 4 * ft:4 * ft + 4, :], scalar1=(1.0 if ri == 0 else -1.0), scalar2=None, op0=ALU.mult),
                          [kn(cst[ri])], ["wd5"])
            NST = 6
            stg = [psb(f"stg{i}", [128, 2048]) for i in range(NST)]
            stgb = [psb(f"stgb{i}", [128, 2048], BF16) for i in range(NST)]
            cnt = [0]

            def cast_rows(src, dst, ncols):
                for c0 in range(0, ncols, 2048):
                    w_ = min(2048, ncols - c0)
                    i = cnt[0] % NST
                    cnt[0] += 1
                    DMA(lambda e, i=i, c0=c0, w_=w_: e.dma_start(out=stg[i][:, 0:w_], in_=src[:, c0:c0 + w_]), w=[f"stg{i}"])
                    sel = (1, 2, 2, 2, 1, 0)[cnt[0] % 6]
                    if sel == 0:
                        G(lambda e, i=i, w_=w_: e.tensor_copy(out=stgb[i][:, 0:w_], in_=stg[i][:, 0:w_]), [f"stg{i}"], [f"stgb{i}"])
                    elif sel == 1:
                        A(lambda e, i=i, w_=w_: e.activation(out=stgb[i][:, 0:w_], in_=stg[i][:, 0:w_], func=AF.Copy), [f"stg{i}"], [f"stgb{i}"])
                    else:
                        V(lambda e, i=i, w_=w_: e.tensor_copy(out=stgb[i][:, 0:w_], in_=stg[i][:, 0:w_]), [f"stg{i}"], [f"stgb{i}"])
                    DMAH(lambda e, i=i, c0=c0, w_=w_: e.dma_start(out=dst[:, c0:c0 + w_], in_=stgb[i][:, 0:w_]), [f"stgb{i}"], ["ws"])
            for nm, rows in [("w_in", 1024), ("w_down_ssd", 2048), ("w_glu", 1024), ("w_down_s5", 1024), ("w_out", 1024)]:
                nco = PROJ if nm == "w_in" else 1024
                for r0 in range(0, rows, 128):
                    cast_rows(I[nm][r0:r0 + 128, :], WS[nm][r0:r0 + 128, :], nco)

        P.barrier()

        xt = sb("xt", [128, 1024]); ot = sb("ot", [128, 1024]); ss = sb("ss", [128, 1]); rstd = sb("rstd", [128, 1])
        ssB = sb("ssB", [128, 1]); rstdB = sb("rstdB", [128, 1])
        xsn = sb("xsn", [128, 1024], BF16); hT = sb("hT", [128, 8, 128], BF16)
        wbuf = [sb(f"wbuf{i}", [128, 8, 256], BF16) for i in range(4)]
        zs = sb("zs", [128, 16, 128], BF16)
        xc = sb("xc", [128, 32, 128], BF16)
        raw = [sb(f"raw{i}", [128, 176]) for i in range(2)]
        acc = [sb(f"acc{i}", [128, 128]) for i in range(2)]
        dts = sb("dts", [32, 128]); dAs = sb("dAs", [32, 128]); acs = sb("acs", [32, 128]); dec = sb("dec", [32, 128]); wdt = sb("wdt", [32, 128])
        tokm = sb("tokm", [128, 3, 32])
        u5_ = [sb(f"u5_{i}", [128, 8, 128], BF16) for i in range(2)]; z5s_ = [sb(f"z5s_{i}", [128, 8, 128], BF16) for i in range(2)]
        gas = sb("gas", [128, 8, 128], BF16); gbs_ = [sb(f"gbs_{i}", [128, 8, 128], BF16) for i in range(2)]
        Xd = sb("Xd", [128, 32, 64], BF16); Btok = sb("Btok", [128, 8, 128], BF16)
        rhsR = sb("rhsR", [32, 4, 128]); Egs = [sb("Eg0", [128, 4, 128])] * 2; Dm = sb("Dm", [128, 4, 128])
        Lgs = [sb(f"Lg{i}", [128, 4, 128], BF16) for i in range(2)]; Mg = sb("Mg", [128, 4, 128], BF16)
        CEall = sb("CEall", [128, 32, 128], BF16)
        dch = sb("dch", [128, 32, 16])
        S0qs = [sb(f"S0q{i}", [128, 4, 128]) for i in range(2)]; Snqs = [sb(f"Snq{i}", [128, 4, 128]) for i in range(2)]
        Bms = [sb("Bm0", [128, 8, 128], BF16)] * 2
        yv2 = [sb(f"yv2_{i}", [128, 2, 128]) for i in range(2)]; ysq2 = [sb(f"ysq2_{i}", [128, 2, 128], BF16) for i in range(2)]
        rstg2 = [sb(f"rstg2_{i}", [128, 128]) for i in range(2)]; ynT = sb("ynT", [128, 16, 128], BF16)
        yaT_ = [sb(f"yaT_{i}", [128, 8, 128], BF16) for i in range(2)]; ybT = sb("ybT", [128, 8, 128], BF16); mixT = sb("mixT", [128, 8, 128], BF16)
        u5m = sb("u5m", [128, 4, 128], BF16)
        bu = sb("bu", [128, 4, 2, 128]); sgm = sb("sgm", [128, 4, 2, 128]); rts = sb("rts", [128, 4, 128])
        bt1 = sb("bt1", [128, 4, 128]); bt2 = sb("bt2", [128, 4, 128])
        s5s = sb("s5s", [128, 4, 2, 128], BF16)
        cin_ = sb("cinn", [128, 4, 2, 16])
        ctm = sb("ctm", [128, 4, 2, 16])
        fin = sb("fin", [128, 32, 2, 16]); s0all = sb("s0all", [128, 32, 2, 16])
        y5t = sb("y5t", [128, 128])
        y5g = sb("y5g", [128, 8, 128], BF16); y5f = sb("y5f", [128, 8, 128], BF16); glus = sb("glus", [128, 128])
        s5io = sb("s5io", [128, 4, 2, 64]); s5tr = sb("s5tr", [128, 16, 8])
        cvo = sb("cvo", [48, 1024]); cst3 = sb("cst3", [128, 48])

        wcnt = [0]

        WSET = [(0, 1, 2)]

        def wload(src_ap):
            i = WSET[0][wcnt[0] % len(WSET[0])]
            wcnt[0] += 1
            kt = src_ap.shape[0] // 128
            nco = src_ap.shape[1]
            DMA(lambda e: e.dma_start(out=wbuf[i][:, 0:kt, 0:nco], in_=src_ap.rearrange("(j p) f -> p j f", p=128)), ["ws"], [f"wbuf{i}"])
            return wbuf[i], f"wbuf{i}"

        pslot = [0]

        def mm_slot():
            s = MMB[0][pslot[0] % 2]
            pslot[0] += 1
            return pb[s][:, 0:128], f"pb{s}"

        def proj_ft(wt, wk, c0, nrows=128):
            ps, pk = mm_slot()
            for j in range(8):
                T(lambda e, j=j: e.matmul(ps[0:nrows, :], lhsT=wt[:, j, c0:c0 + nrows], rhs=hT[:, j, :], start=(j == 0), stop=(j == 7)), [wk, "hT"], [pk])
            return ps, pk

        def dense_T(wname, kt, rhs_fn, rkeys_fn, evac):
            for cb in range(4):
                halves = [wload(WS[wname][128 * k0:128 * (k0 + 8), 256 * cb:256 * cb + 256]) for k0 in range(0, kt, 8)]
                for s in range(2):
                    m = 2 * cb + s
                    ps, pk = mm_slot()
                    for k in range(kt):
                        wt, wk = halves[k // 8]
                        T(lambda e, wt=wt, k=k, s=s, ps=ps: e.matmul(ps, lhsT=wt[:, k % 8, 128 * s:128 * s + 128], rhs=rhs_fn(k), start=(k == 0), stop=(k == kt - 1)), [wk] + rkeys_fn(k), [pk])
                    evac(m, ps, pk)

        def cmul(out_re, out_im, ar, ai, br, bi, keys_r, keys_w, t1_, t2_, tk):
            V(lambda e: e.tensor_tensor(out=t1_, in0=ar, in1=br, op=ALU.mult), keys_r, [tk[0]])
            V(lambda e: e.tensor_tensor(out=t2_, in0=ai, in1=bi, op=ALU.mult), keys_r, [tk[1]])
            V(lambda e: e.tensor_tensor(out=out_re, in0=t1_, in1=t2_, op=ALU.subtract), tk, keys_w[0:1])
            V(lambda e: e.tensor_tensor(out=t1_, in0=ar, in1=bi, op=ALU.mult), keys_r + keys_w[0:1], [tk[0]])
            V(lambda e: e.tensor_tensor(out=t2_, in0=ai, in1=br, op=ALU.mult), keys_r + keys_w[0:1], [tk[1]])
            V(lambda e: e.tensor_tensor(out=out_im, in0=t1_, in1=t2_, op=ALU.add), tk, keys_w[1:2])

        def recA(ti, par):
            u5 = u5_[par]; z5s = z5s_[par]; gbs = gbs_[par]; yaT = yaT_[par]
            smp = (ti == 16)
            lastp = (ti == 15)
            xsrc = I["xs"] if smp else I["xp"][128 * ti:128 * ti + 128, :]
            ydst = O["y_s"] if smp else O["y_p"][128 * ti:128 * ti + 128, :]
            negm = negm_s if smp else negm_p
            rmk = rm_s if smp else rm_p
            nseq, L = (16, 8) if smp else (1, 128)
            TAG[0] = "A.T1"
            DMA(lambda e: e.dma_start(out=xt[:], in_=xsrc), w=["xt"])
            G(lambda e: e.memset(ss[:], 0.0), w=["ss"])
            A(lambda e: e.activation(out=xsn[:], in_=xt[:], func=AF.Square, accum_out=ss[:]), ["xt", "ss"], ["xsn", "ss"])
            A(lambda e: e.activation(out=rstd[:], in_=ss[:], func=AF.Sqrt, scale=1.0 / D, bias=epscol[:]), ["ss", "epscol"], ["rstd"])
            V(lambda e: e.reciprocal(out=rstd[:], in_=rstd[:]), ["rstd"], ["rstd"])
            V(lambda e: e.tensor_scalar(out=xsn[:], in0=xt[:], scalar1=rstd[:, 0:1], scalar2=None, op0=ALU.mult), ["xt", "rstd"], ["xsn"])
            for j in range(8):
                T(lambda e, j=j: e.transpose(out=pbb[5][:, 128 * j:128 * j + 128], in_=xsn[:, 128 * j:128 * j + 128], identity=identb[:]), ["xsn", "identb"], ["pb5"])
            for j in range(8):
                V(lambda e, j=j: e.tensor_scalar(out=hT[:, j, :], in0=pbb[5][:, 128 * j:128 * j + 128], scalar1=normw[:, j:j + 1], scalar2=None, op0=ALU.mult), ["pb5", "normw"], ["hT"])
            TAG[0] = "A.proj"
            blocks = [("xbc", c0, 256) for c0 in range(XBC0, DT0, 256)] + [("dt", DT0, 32)]
            blocks += [("z", c0, 256) for c0 in range(Z0, XBC0, 256)]
            for nm, b0 in (("u5", U0), ("z5", Z50), ("ga", GA0), ("gb", GB0)):
                blocks += [(nm, c0, 256) for c0 in range(b0, b0 + 1024, 256)]
            base = {"xbc": XBC0, "z": Z0, "u5": U0, "z5": Z50, "ga": GA0, "gb": GB0}
            deferred = []
            for nm, c0, nco in blocks:
                wt, wk = wload(WS["w_in"][:, c0:c0 + nco])
                if nm == "dt":
                    while deferred:
                        deferred.pop(0)()
                    ps, pk = proj_ft(wt, wk, 0, 32)
                    A(lambda e, ps=ps: e.activation(out=dts[:], in_=ps[0:32, :], func=AF.Exp, bias=dtb[:]), [pk, "dtb"], ["dts"])
                    A(lambda e: e.activation(out=dts[:], in_=dts[:], func=AF.Ln, bias=onecol[0:32, :]), ["dts", "onecol"], ["dts"])
                    continue
                for s in range(2):
                    ft = (c0 - base[nm]) // 128 + s
                    ps, pk = proj_ft(wt, wk, 128 * s)
                    if nm == "z":
                        A(lambda e, ps=ps, ft=ft: e.activation(out=zs[:, ft, :], in_=ps, func=AF.Silu), [pk], [f"zs{ft}"])
                    elif nm == "u5":
                        A(lambda e, ps=ps, ft=ft: e.activation(out=u5[:, ft, :], in_=ps, func=AF.Copy), [pk], [f"u5_{par}_{ft}"])
                    elif nm == "z5":
                        A(lambda e, ps=ps, ft=ft: e.activation(out=z5s[:, ft, :], in_=ps, func=AF.Silu), [pk], [f"z5_{par}_{ft}"])
                    elif nm == "ga":
                        A(lambda e, ps=ps, ft=ft: e.activation(out=gas[:, ft, :], in_=ps, func=AF.Sigmoid), [pk], [f"ga{ft}"])
                    elif nm == "gb":
                        A(lambda e, ps=ps, ft=ft: e.activation(out=gbs[:, ft, :], in_=ps, func=AF.Sigmoid), [pk], [f"gb_{par}_{ft}"])
                    else:
                        r_ = raw[ft % 2]; rk = f"raw{ft % 2}"; ac = acc[ft % 2]; ak = f"acc{ft % 2}"
                        EV = V
                        if smp:
                            rv = r_[:, 0:176].rearrange("p (s r) -> p s r", r=11)
                            if ft % 8 == 0:
                                DMA(lambda e, ft=ft: e.dma_start(out=cvo[:], in_=I["conv0"].rearrange("s r c -> (s r) c")[:, 128 * ft:128 * ft + 1024]), w=["cvo"])
                            T(lambda e, ft=ft: e.transpose(out=pb[6][:, 0:48], in_=cvo[:, 128 * (ft % 8):128 * (ft % 8) + 128], identity=identf[0:48, 0:48]), ["cvo", "identf"], ["pb6"])
                            V(lambda e, rv=rv: e.tensor_copy(out=rv[:, :, 0:3], in_=pb[6][:, 0:48].rearrange("p (s r) -> p s r", r=3)), ["pb6"], [rk])
                        else:
                            rv = r_[:, 0:131].unsqueeze(1)
                            A(lambda e, rv=rv, ft=ft: e.activation(out=rv[:, 0, 0:3], in_=halo[:, ft, :], func=AF.Copy), ["halo"], [rk])
                        A(lambda e, rv=rv, ps=ps: e.activation(out=rv[:, :, 3:3 + L], in_=ps.rearrange("p (s l) -> p s l", l=L), func=AF.Copy), [pk], [rk])
                        while deferred:
                            deferred.pop(0)()
                        av = ac[:].rearrange("p (s l) -> p s l", l=L)
                        EV(lambda e, rv=rv, av=av, ft=ft: e.tensor_scalar(out=av, in0=rv[:, :, 0:L], scalar1=convw[:, ft, 0:1], scalar2=convb[:, ft:ft + 1], op0=ALU.mult, op1=ALU.add), [rk, "convw", "convb"], [ak])
                        for k in range(1, 4):
                            EV(lambda e, rv=rv, av=av, ft=ft, k=k: e.scalar_tensor_tensor(out=av, in0=rv[:, :, k:k + L], scalar=convw[:, ft, k:k + 1], in1=av, op0=ALU.mult, op1=ALU.add), [rk, ak, "convw"], [ak])
                        def late(ac=ac, ft=ft, rv=rv, ak=ak, rk=rk):
                            A(lambda e: e.activation(out=xc[:, ft, :], in_=ac[:], func=AF.Silu), [ak], [f"xc{ft}"])
                            if not smp:
                                A(lambda e: e.activation(out=halo[:, ft, :], in_=rv[:, 0, 128:131], func=AF.Copy), [rk], ["halo"])
                        deferred.append(late)
                        if smp or lastp:
                            nr = 3 * nseq
                            EV(lambda e, rv=rv, nr=nr: e.tensor_copy(out=cst3[:, 0:nr].rearrange("p (s r) -> p s r", r=3), in_=rv[:, :, L:L + 3]), [rk], ["cst3"])
                            T(lambda e, nr=nr: e.transpose(out=pb[6][0:nr, 128:256], in_=cst3[:, 0:nr], identity=identf[:]), ["cst3", "identf"], ["pb6"])
                            V(lambda e, ft=ft, nr=nr: e.tensor_copy(out=cvo[0:nr, 128 * (ft % 8):128 * (ft % 8) + 128], in_=pb[6][0:nr, 128:256]), ["pb6"], ["cvo"])
                            if ft % 8 == 7:
                                cdst = (O["conv_s"].rearrange("s r c -> (s r) c") if smp else O["conv_p"])[:, 128 * (ft - 7):128 * (ft - 7) + 1024]
                                DMAO(lambda e, cdst=cdst, nr=nr: e.dma_start(out=cdst, in_=cvo[0:nr, :]), ["cvo"], [])
            TAG[0] = "A.dt"
            V(lambda e: e.tensor_scalar(out=dAs[:], in0=dts[:], scalar1=aneg[:, 0:1], scalar2=None, op0=ALU.mult), ["dts", "aneg"], ["dAs"])
            V(lambda e: e.tensor_tensor_scan(out=acs[:], data0=rmk[:], data1=dAs[:], initial=0.0, op0=ALU.mult, op1=ALU.add), [kn(rmk), "dAs"], ["acs"])
            a3 = acs[:].rearrange("h (s l) -> h s l", l=L)
            V(lambda e: e.tensor_tensor(out=dec[:].rearrange("h (s l) -> h s l", l=L), in0=a3[:, :, L - 1:L].broadcast_to([32, nseq, L]), in1=a3, op=ALU.subtract), ["acs"], ["dec"])
            A(lambda e: e.activation(out=dec[:], in_=dec[:], func=AF.Exp), ["dec"], ["dec"])
            V(lambda e: e.tensor_tensor(out=wdt[:], in0=dts[:], in1=dec[:], op=ALU.mult), ["dts", "dec"], ["wdt"])
            for i_, src in enumerate((acs, dts, wdt)):
                T(lambda e, i_=i_, src=src: e.transpose(out=pb[6][:, 256 + 32 * i_:256 + 32 * i_ + 32], in_=src[:], identity=identf[0:32, 0:32]), [kn(src), "identf"], ["pb6"])
            V(lambda e: e.tensor_copy(out=tokm[:].rearrange("p a b -> p (a b)"), in_=pb[6][:, 256:352]), ["pb6"], ["tokm"])
            TAG[0] = "A.X"
            for half in range(2):
                bk = f"pb{2 + half}"
                for j in range(8):
                    ft = 8 * half + j
                    T(lambda e, ft=ft, j=j, half=half: e.transpose(out=pbb[2 + half][:, 128 * j:128 * j + 128], in_=xc[:, ft, :], identity=identb[:]), [f"xc{ft}", "identb"], [bk])
                pv = pbb[2 + half][:, :].rearrange("p (a h2 q) -> p a h2 q", h2=2, q=64)
                prs = slice(8 * half, 8 * half + 8)
                dtT = tokm[:, 1, 16 * half:16 * half + 16].rearrange("p (a h2) -> p a h2", h2=2)
                wT = tokm[:, 2, 16 * half:16 * half + 16]
                V(lambda e, pv=pv, prs=prs, dtT=dtT: e.tensor_tensor(out=XA[:, prs, 0:64], in0=pv[:, :, 0, :], in1=dtT[:, :, 0:1].broadcast_to([128, 8, 64]), op=ALU.mult), [bk, "tokm"], ["XA"])
                V(lambda e, pv=pv, prs=prs, dtT=dtT: e.tensor_tensor(out=XB[:, prs, 64:128], in0=pv[:, :, 1, :], in1=dtT[:, :, 1:2].broadcast_to([128, 8, 64]), op=ALU.mult), [bk, "tokm"], ["XB"])
                V(lambda e, half=half, wT=wT: e.tensor_tensor(out=Xd[:, 16 * half:16 * half + 16, :], in0=pbb[2 + half][:, :].rearrange("p (h q) -> p h q", q=64), in1=wT.unsqueeze(2).broadcast_to([128, 16, 64]), op=ALU.mult), [bk, "tokm"], ["Xd"])
            for j in range(8):
                T(lambda e, j=j: e.transpose(out=pbb[6][:, 128 * j:128 * j + 128], in_=xc[:, 16 + j, :], identity=identb[:]), [f"xc{16 + j}", "identb"], ["pb6"])
            A(lambda e: e.activation(out=Btok[:].rearrange("p a b -> p (a b)"), in_=pbb[6][:, :], func=AF.Copy), ["pb6"], ["Btok"])
            def yevac(gq):
                b2 = gq % 2
                yv = yv2[b2]; ysq = ysq2[b2]; rstg = rstg2[b2]
                for j in range(2):
                    pr = 2 * gq + j
                    V(lambda e, pr=pr, j=j, yv=yv: e.scalar_tensor_tensor(out=yv[:, j, :], in0=xc[:, pr, :], scalar=dexp[:, pr:pr + 1], in1=yps(pr), op0=ALU.mult, op1=ALU.add), [f"xc{pr}", "dexp", ypk(pr)], [f"yv{b2}"])
                    V(lambda e, pr=pr, j=j, yv=yv: e.tensor_tensor(out=yv[:, j, :], in0=yv[:, j, :], in1=zs[:, pr, :], op=ALU.mult), [f"yv{b2}", f"zs{pr}"], [f"yv{b2}"])
                    A(lambda e, j=j, yv=yv, ysq=ysq: e.activation(out=ysq[:, j, :], in_=yv[:, j, :], func=AF.Square), [f"yv{b2}"], [f"ysq{b2}"])
                ps = pb[7][:, 128 * b2:128 * b2 + 128]
                pk = "pb7"
                for j in range(2):
                    T(lambda e, ps=ps, j=j, ysq=ysq: e.matmul(ps, lhsT=onesb[:], rhs=ysq[:, j, :], start=(j == 0), stop=(j == 1)), ["onesb", f"ysq{b2}"], [pk])
                A(lambda e, ps=ps, rstg=rstg: e.activation(out=rstg[:], in_=ps, func=AF.Sqrt, scale=1.0 / 256, bias=epscol[:]), [pk, "epscol"], [f"rstg{b2}"])
                V(lambda e, rstg=rstg: e.reciprocal(out=rstg[:], in_=rstg[:]), [f"rstg{b2}"], [f"rstg{b2}"])
                for j in range(2):
                    pr = 2 * gq + j
                    V(lambda e, pr=pr, j=j, yv=yv, rstg=rstg: e.scalar_tensor_tensor(out=ynT[:, pr, :], in0=yv[:, j, :], scalar=ssdnw[:, pr:pr + 1], in1=rstg[:], op0=ALU.mult, op1=ALU.mult), [f"yv{b2}", "ssdnw", f"rstg{b2}"], [f"ynT{pr}"])
            TAG[0] = "A.grp"
            it = 0
            for hf in ((0, 1) if smp else (None,)):
                grp_range = range(8) if hf is None else range(4 * hf, 4 * hf + 4)
                q4_range = range(1) if hf is None else range(2 * hf, 2 * hf + 2)
                if smp:
                    ypk = lambda pr: f"pb{2 + (pr % 8) // 4}"
                    yps = lambda pr: pb[2 + (pr % 8) // 4][:, 128 * (pr % 4):128 * (pr % 4) + 128]
                    yfirst = lambda pr: pr % 4 == 0
                else:
                    ypk = lambda pr: f"pb{2 + (pr // 2) % 2}"
                    yps = lambda pr: pb[2 + (pr // 2) % 2][:, 128 * (pr % 2):128 * (pr % 2) + 128]
                    yfirst = lambda pr: pr % 2 == 0
                def stage1(g):
                    Eg = Egs[g % 2]; Lg = Lgs[g % 2]; cbk = g % 2
                    V(lambda e, g=g: e.tensor_tensor(out=rhsR[:], in0=acs[:].unsqueeze(1).broadcast_to([32, 4, 128]), in1=identf[0:32, 4 * g:4 * g + 4].unsqueeze(2).broadcast_to([32, 4, 128]), op=ALU.mult), ["acs", "identf"], ["rhsR"])
                    T(lambda e: e.matmul(pb[6][:, :], lhsT=ones32[:], rhs=rhsR[:].rearrange("p a b -> p (a b)"), start=True, stop=True), ["ones32", "rhsR"], ["pb6"])
                    A(lambda e, Eg=Eg: e.activation(out=Eg[:].rearrange("p a b -> p (a b)"), in_=pb[6][:, :], func=AF.Exp), ["pb6"], ["Eg0"])
                    G(lambda e, g=g, Eg=Eg: e.tensor_tensor(out=CEall[:, 4 * g:4 * g + 4, :], in0=Eg[:], in1=xc[:, 24 + g, :].unsqueeze(1).broadcast_to([128, 4, 128]), op=ALU.mult), ["Eg0", f"xc{24 + g}"], ["CEall"])
                    for h4 in range(4):
                        V(lambda e, g=g, h4=h4: e.scalar_tensor_tensor(out=Dm[:, h4, :], in0=pb[6][:, 128 * h4:128 * h4 + 128], scalar=tokm[:, 0, 4 * g + h4:4 * g + h4 + 1], in1=negm[:], op0=ALU.subtract, op1=ALU.min), ["pb6", "tokm", kn(negm)], ["Dm"])
                    A(lambda e, Lg=Lg: e.activation(out=Lg[:].rearrange("p a b -> p (a b)"), in_=Dm[:].rearrange("p a b -> p (a b)"), func=AF.Exp), ["Dm"], [f"Lg{g % 2}"])
                    T(lambda e, g=g: e.matmul(pb[g % 2][:, 0:128], lhsT=xc[:, 16 + g, :], rhs=xc[:, 24 + g, :], start=True, stop=True), [f"xc{16 + g}", f"xc{24 + g}"], [f"pb{g % 2}"])
                    V(lambda e, g=g, Eg=Eg: e.tensor_copy(out=dch[:, 4 * g:4 * g + 4, 0:nseq], in_=Eg[:].rearrange("p a (s l) -> p a s l", l=L)[:, :, :, L - 1]), ["Eg0"], ["dch"])
                def stage2(g):
                    Eg = Egs[g % 2]; Lg = Lgs[g % 2]; cbk = g % 2
                    V(lambda e, Lg=Lg, cbk=cbk: e.tensor_tensor(out=Mg[:], in0=Lg[:], in1=pb[cbk][:, 0:128].unsqueeze(1).broadcast_to([128, 4, 128]), op=ALU.mult), [f"Lg{g % 2}", f"pb{g % 2}"], ["Mg"])
                    for j in range(2):
                        pr = 2 * g + j
                        T(lambda e, pr=pr, j=j: e.matmul(yps(pr), lhsT=XA[:, pr, :], rhs=Mg[:, 2 * j, :], start=yfirst(pr), stop=False, skip_group_check=True), ["XA", "Mg"], [ypk(pr)])
                        T(lambda e, pr=pr, j=j: e.matmul(yps(pr), lhsT=XB[:, pr, :], rhs=Mg[:, 2 * j + 1, :], start=False, stop=False, skip_group_check=True), ["XB", "Mg"], [ypk(pr)])
                    if not smp:
                        sak_ = [f"SA{i}" for i in range(4)] + [f"SB{i}" for i in range(4)]
                        for j in range(2):
                            pr = 2 * g + j
                            T(lambda e, pr=pr: e.matmul(yps(pr), lhsT=SA[:, pr, :], rhs=CEall[:, 2 * pr, :], start=False, stop=False, skip_group_check=True), sak_ + ["CEall"], [ypk(pr)])
                            T(lambda e, pr=pr, j=j: e.matmul(yps(pr), lhsT=SB[:, pr, :], rhs=CEall[:, 2 * pr + 1, :], start=False, stop=(j == 1), skip_group_check=True), sak_ + ["CEall"], [ypk(pr)])
                        yevac(g)

                grps_ = list(grp_range)
                stage1(grps_[0])
                for i_g, g_ in enumerate(grps_):
                    if i_g + 1 < len(grps_):
                        stage1(grps_[i_g + 1])
                    stage2(g_)
                for s in range(nseq if smp else 0):
                    cs = slice(L * s, L * s + L)
                    last = (s == nseq - 1)
                    if smp:
                        Bm = Bms[0]; bmk = "Bm0"
                        G(lambda e, s=s, Bm=Bm: e.tensor_scalar(out=Bm[:].rearrange("p a b -> p (a b)"), in0=Btok[:].rearrange("p a b -> p (a b)"), scalar1=seqmask[:, s:s + 1], scalar2=None, op0=ALU.mult), ["Btok", "seqmask"], [bmk])
                    for q4 in q4_range:
                        prs_ = range(4 * q4, 4 * q4 + 4) if smp else range(16)
                        sak = [f"SA{q4}", f"SB{q4}"] if smp else [f"SA{i}" for i in range(4)] + [f"SB{i}" for i in range(4)]
                        if smp:
                            S0q = S0qs[it % 2]; s0k = f"S0q{it % 2}"; Snq = Snqs[it % 2]; snk = f"Snq{it % 2}"
                            tb_ = 7 if it % 2 == 0 else 0
                            nb_ = 6 if it % 2 == 0 else 1
                            it += 1
                            sview = I["ssm0"][s].rearrange("(pr h2) p n -> (h2 p) pr n", h2=2)[:, 4 * q4:4 * q4 + 4, :]
                            DMA(lambda e, sview=sview, S0q=S0q: e.dma_start(out=S0q[:], in_=sview), w=[s0k])
                            for j in range(4):
                                T(lambda e, j=j, S0q=S0q, tb_=tb_: e.transpose(out=pb[tb_][:, 128 * j:128 * j + 128], in_=S0q[:, j, :], identity=identf[:]), [s0k, "identf"], [f"pb{tb_}"])
                            pv = pb[tb_][:, :].rearrange("p (a h2 q) -> p a h2 q", h2=2, q=64)
                            A(lambda e, q4=q4, pv=pv: e.activation(out=SA[:, 4 * q4:4 * q4 + 4, 0:64], in_=pv[:, :, 0, :], func=AF.Copy), [f"pb{tb_}"], [f"SA{q4}"])
                            A(lambda e, q4=q4, pv=pv: e.activation(out=SB[:, 4 * q4:4 * q4 + 4, 64:128], in_=pv[:, :, 1, :], func=AF.Copy), [f"pb{tb_}"], [f"SB{q4}"])
                        for pr in prs_:
                            T(lambda e, pr=pr, cs=cs: e.matmul(yps(pr)[:, cs], lhsT=SA[:, pr, :], rhs=CEall[:, 2 * pr, cs], start=False, stop=False, skip_group_check=True), sak + ["CEall"], [ypk(pr)])
                            T(lambda e, pr=pr, cs=cs, last=last: e.matmul(yps(pr)[:, cs], lhsT=SB[:, pr, :], rhs=CEall[:, 2 * pr + 1, cs], start=False, stop=last, skip_group_check=True), sak + ["CEall"], [ypk(pr)])
                        if smp:
                            for j in range(4):
                                pr = 4 * q4 + j
                                T(lambda e, pr=pr, j=j, Bm=Bm, nb_=nb_: e.matmul(pb[nb_][:, 128 * j:128 * j + 128], lhsT=Xd[:, 2 * pr:2 * pr + 2, :].rearrange("p a b -> p (a b)"), rhs=Bm[:, pr // 2, :], start=True, stop=True), ["Xd", bmk], [f"pb{nb_}"])
                            for j in range(4):
                                pr = 4 * q4 + j
                                for h2 in range(2):
                                    hs = slice(64 * h2, 64 * h2 + 64)
                                    V(lambda e, pr=pr, j=j, hs=hs, s=s, h2=h2, S0q=S0q, Snq=Snq, nb_=nb_: e.scalar_tensor_tensor(out=Snq[hs, j, :], in0=S0q[hs, j, :], scalar=dch[hs, 2 * pr + h2, s:s + 1], in1=pb[nb_][hs, 128 * j:128 * j + 128], op0=ALU.mult, op1=ALU.add), [s0k, "dch", f"pb{nb_}"], [snk])
                            oview = O["ssm_s"][s].rearrange("(pr h2) p n -> (h2 p) pr n", h2=2)[:, 4 * q4:4 * q4 + 4, :]
                            DMAO(lambda e, oview=oview, Snq=Snq: e.dma_start(out=oview, in_=Snq[:]), [snk], [])
                if smp:
                    for gq in grp_range:
                        yevac(gq)
            if smp:
                sv = STt[:].rearrange("p (pr h2) q -> p pr h2 q", h2=2)
                A(lambda e, sv=sv: e.activation(out=SA[:, :, 0:64], in_=sv[:, :, 0, :], func=AF.Copy), ["STt"], [f"SA{i}" for i in range(4)])
                A(lambda e, sv=sv: e.activation(out=SB[:, :, 64:128], in_=sv[:, :, 1, :], func=AF.Copy), ["STt"], [f"SB{i}" for i in range(4)])
            TAG[0] = "A.state"
            if not smp:
                V(lambda e: e.tensor_tensor(out=STt[:], in0=STt[:], in1=dch[:, :, 0:1].broadcast_to([128, 32, 64]), op=ALU.mult), ["STt", "dch"], ["STt"])
                for rnd in range(2):
                    for g in range(4 * rnd, 4 * rnd + 4):
                        bk_ = 2 + (g % 4) // 2
                        T(lambda e, g=g, bk_=bk_: e.matmul(pb[bk_][:, 256 * (g % 2):256 * (g % 2) + 256], lhsT=Btok[:, g, :], rhs=Xd[:, 4 * g:4 * g + 4, :].rearrange("p a b -> p (a b)"), start=True, stop=True), ["Btok", "Xd"], [f"pb{bk_}"])
                    for b2_ in range(2):
                        h0 = 16 * rnd + 8 * b2_
                        V(lambda e, h0=h0, b2_=b2_: e.tensor_tensor(out=STt[:, h0:h0 + 8, :], in0=STt[:, h0:h0 + 8, :], in1=pb[2 + b2_][:, :].rearrange("p (h q) -> p h q", q=64), op=ALU.add), ["STt", f"pb{2 + b2_}"], ["STt"])
                sv = STt[:].rearrange("p (pr h2) q -> p pr h2 q", h2=2)
                A(lambda e, sv=sv: e.activation(out=SA[:, :, 0:64], in_=sv[:, :, 0, :], func=AF.Copy), ["STt"], [f"SA{i}" for i in range(4)])
                A(lambda e, sv=sv: e.activation(out=SB[:, :, 64:128], in_=sv[:, :, 1, :], func=AF.Copy), ["STt"], [f"SB{i}" for i in range(4)])
                if lastp:
                    for q4 in range(4):
                        for j in range(4):
                            pr = 4 * q4 + j
                            T(lambda e, pr=pr, j=j: e.transpose(out=pb[6][:, 128 * j:128 * j + 128], in_=STt[:, 2 * pr:2 * pr + 2, :].rearrange("p a b -> p (a b)"), identity=identf[:]), ["STt", "identf"], ["pb6"])
                        Snq = Snqs[q4 % 2]; snk = f"Snq{q4 % 2}"
                        V(lambda e, Snq=Snq: e.tensor_copy(out=Snq[:].rearrange("p a b -> p (a b)"), in_=pb[6][:, :]), ["pb6"], [snk])
                        oview = O["ssm_p"].rearrange("(pr h2) p n -> (h2 p) pr n", h2=2)[:, 4 * q4:4 * q4 + 4, :]
                        DMAO(lambda e, oview=oview, Snq=Snq: e.dma_start(out=oview, in_=Snq[:]), [snk], [])
            TAG[0] = "A.dense"
            dense_T("w_down_ssd", 16, lambda k: ynT[:, k, :], lambda k: [f"ynT{k}"],
                    lambda m, ps, pk: V(lambda e: e.tensor_tensor(out=yaT[:, m, :], in0=ps, in1=gas[:, m, :], op=ALU.mult), [pk, f"ga{m}"], [f"yaT_{par}_{m}"]))


        def recB(ti, par):
            u5 = u5_[par]; z5s = z5s_[par]; gbs = gbs_[par]; yaT = yaT_[par]
            smp = (ti == 16)
            lastp = (ti == 15)
            xsrc = I["xs"] if smp else I["xp"][128 * ti:128 * ti + 128, :]
            ydst = O["y_s"] if smp else O["y_p"][128 * ti:128 * ti + 128, :]
            if smp:
                for ri, nm in enumerate(["re0", "im0"]):
                    sv_ = I[nm].rearrange("s (gh gq g2) p -> s gq gh (g2 p)", gq=8, g2=2)
                    for s in range(16):
                        DMA(lambda e, s=s, sv_=sv_: e.dma_start(out=s5io[8 * s:8 * s + 8, :, :, :].rearrange("p a b c -> p a (b c)"), in_=sv_[s]), w=["s5io"])
                    for gh in range(4):
                        T(lambda e, gh=gh: e.transpose(out=pb[5][:, 0:128], in_=s5io[:, gh, :, :].rearrange("p a b -> p (a b)"), identity=identf[:]), ["s5io", "identf"], ["pb5"])
                        V(lambda e, gh=gh, ri=ri: e.tensor_copy(out=s0all[:, 8 * gh:8 * gh + 8, ri, :], in_=pb[5][:, 0:128].rearrange("p (s g) -> p g s", g=8)), ["pb5"], ["s0all"])
            TAG[0] = "B.s5"
            nsp, Lx = (1, 128) if smp else (2, 64)
            Ls = 8 if smp else 64

            def vw(t_, r=None):
                v = t_[:].rearrange("p a b c -> p (a b c)").rearrange("p (r s q t) -> p r s q t", r=2, s=nsp, q=4)
                return v if r is None else v[:, r]

            def pv_(t_):
                return t_[:].rearrange("p a b -> p (a b)").rearrange("p (s q t) -> p s q t", s=nsp, q=4)

            def sub(ap):
                return ap.rearrange("p s q (n l) -> p s q n l", l=Ls) if smp else ap
            for ft in range(8):
                gsl = slice(4 * ft, 4 * ft + 4)
                if smp:
                    ecv = ecos[:, gsl, 0:Ls].unsqueeze(1).unsqueeze(3).broadcast_to([128, 1, 4, 16, Ls])
                    esv = esin[:, gsl, 0:Ls].unsqueeze(1).unsqueeze(3).broadcast_to([128, 1, 4, 16, Ls])
                else:
                    ecv = ecos[:, gsl, 0:Ls].unsqueeze(1).broadcast_to([128, 2, 4, Ls])
                    esv = esin[:, gsl, 0:Ls].unsqueeze(1).broadcast_to([128, 2, 4, Ls])
                for q in range(4):
                    A(lambda e, ft=ft, q=q: e.activation(out=u5m[:, q, :], in_=u5[:, ft, :], func=AF.Copy, scale=pairmask[:, q:q + 1]), [f"u5_{par}_{ft}", "pairmask"], ["u5m"])
                for ri in range(2):
                    pbk = f"pb{4 + ri}"
                    for q in range(4):
                        T(lambda e, ft=ft, q=q, ri=ri: e.matmul(pb[4 + ri][:, 128 * q:128 * q + 128], lhsT=wb5[:, ft, ri, :], rhs=u5m[:, q, :], start=True, stop=True), ["wb5", "u5m"], [pbk])
                    A(lambda e, ri=ri: e.activation(out=vw(bu, ri), in_=pb[4 + ri][:, :].rearrange("p (q s t) -> p s q t", s=nsp, t=Lx), func=AF.Copy), [pbk], ["bu"])
                bre_, bim_ = sub(vw(bu, 0)), sub(vw(bu, 1))
                V(lambda e, ecv=ecv, bre_=bre_: e.tensor_tensor(out=sub(pv_(bt1)), in0=ecv, in1=bre_, op=ALU.mult), ["ecos", "bu"], ["bt1"])
                V(lambda e, esv=esv, bim_=bim_: e.tensor_tensor(out=sub(pv_(bt2)), in0=esv, in1=bim_, op=ALU.mult), ["esin", "bu"], ["bt2"])
                V(lambda e: e.tensor_tensor(out=vw(sgm, 0), in0=pv_(bt1), in1=pv_(bt2), op=ALU.add), ["bt1", "bt2"], ["sgm"])
                V(lambda e, ecv=ecv, bim_=bim_: e.tensor_tensor(out=sub(pv_(bt1)), in0=ecv, in1=bim_, op=ALU.mult), ["ecos", "bu", "sgm"], ["bt1"])
                V(lambda e, esv=esv, bre_=bre_: e.tensor_tensor(out=sub(pv_(bt2)), in0=esv, in1=bre_, op=ALU.mult), ["esin", "bu", "sgm"], ["bt2"])
                V(lambda e: e.tensor_tensor(out=vw(sgm, 1), in0=pv_(bt1), in1=pv_(bt2), op=ALU.subtract), ["bt1", "bt2"], ["sgm"])
                rtv = rts[:].rearrange("p a b -> p (a b)")[:, 0:4 * Lx]
                if smp:
                    V(lambda e, gsl=gsl: e.tensor_tensor(out=rts[:], in0=rdec[:, gsl].unsqueeze(2).broadcast_to([128, 4, 128]), in1=rm128[:].unsqueeze(1).broadcast_to([128, 4, 128]), op=ALU.mult), ["rdec", "rm128"], ["rts"])
                else:
                    V(lambda e, gsl=gsl, rtv=rtv: e.tensor_tensor(out=rtv.rearrange("p (q t) -> p q t", t=64), in0=rdec[:, gsl].unsqueeze(2).broadcast_to([128, 4, 64]), in1=rmq[:].rearrange("p (q t) -> p q t", t=64), op=ALU.mult), ["rdec", "rmq"], ["rts"])
                flat = lambda ap: ap.rearrange("p q t -> p (q t)")
                if smp:
                    ab = lambda t: t[:, gsl].unsqueeze(2).broadcast_to([128, 4, 16])
                    cmul(cin_[:, :, 0, :], cin_[:, :, 1, :], ab(are), ab(aim), s0all[:, gsl, 0, :], s0all[:, gsl, 1, :],
                         ["are", "aim", "s0all"], ["cinr", "cini"], ctm[:, :, 0, :], ctm[:, :, 1, :], ["ctm0", "ctm1"])
                    for ri in range(2):
                        tgt = sub(vw(sgm, ri))[:, 0, :, :, 0]
                        V(lambda e, tgt=tgt, ri=ri: e.tensor_tensor(out=tgt, in0=tgt, in1=cin_[:, :, ri, :], op=ALU.add), ["sgm", "cinr", "cini"], ["sgm"])
                    for ri in range(2):
                        V(lambda e, ri=ri, rtv=rtv: e.tensor_tensor_scan(out=flat(vw(bu, ri)[:, 0]), data0=rtv, data1=flat(vw(sgm, ri)[:, 0]), initial=0.0, op0=ALU.mult, op1=ALU.add), ["rts", "sgm"], ["bu"])
                else:
                    for sg in range(2):
                        src_re = sgc[:, gsl, 0:1] if sg == 0 else vw(bu, 0)[:, 0, :, 63:64]
                        src_im = sgc[:, gsl, 1:2] if sg == 0 else vw(bu, 1)[:, 0, :, 63:64]
                        ab = lambda t: t[:, gsl].unsqueeze(2)
                        cmul(cin_[:, :, 0, 0:1], cin_[:, :, 1, 0:1], ab(aere), ab(aeim), src_re, src_im,
                             ["aere", "aeim", "sgc", "bu"], ["cinr", "cini"], ctm[:, :, 0, 0:1], ctm[:, :, 1, 0:1], ["ctm0", "ctm1"])
                        for ri in range(2):
                            tgt = vw(sgm, ri)[:, sg, :, 0:1]
                            V(lambda e, tgt=tgt, ri=ri: e.tensor_tensor(out=tgt, in0=tgt, in1=cin_[:, :, ri, 0:1], op=ALU.add), ["sgm", "cinr", "cini"], ["sgm"])
                        for ri in range(2):
                            V(lambda e, ri=ri, sg=sg, rtv=rtv: e.tensor_tensor_scan(out=flat(vw(bu, ri)[:, sg]), data0=rtv, data1=flat(vw(sgm, ri)[:, sg]), initial=0.0, op0=ALU.mult, op1=ALU.add), ["rts", "sgm"], ["bu"])
                    V(lambda e, gsl=gsl: e.tensor_copy(out=sgc[:, gsl, :], in_=vw(bu)[:, :, 1, :, 63].rearrange("p r q -> p q r")), ["bu"], ["sgc"])
                sre_, sim_ = sub(vw(bu, 0)), sub(vw(bu, 1))
                s5v = lambda r: s5s[:, :, r, :].rearrange("p q (s t) -> p s q t", s=nsp)
                V(lambda e, ecv=ecv, sre_=sre_: e.tensor_tensor(out=sub(pv_(bt1)), in0=ecv, in1=sre_, op=ALU.mult), ["ecos", "bu"], ["bt1"])
                V(lambda e, esv=esv, sim_=sim_: e.tensor_tensor(out=sub(pv_(bt2)), in0=esv, in1=sim_, op=ALU.mult), ["esin", "bu"], ["bt2"])
                V(lambda e: e.tensor_tensor(out=s5v(0), in0=pv_(bt1), in1=pv_(bt2), op=ALU.subtract), ["bt1", "bt2"], ["s5s"])
                if smp or lastp:
                    fsrc = lambda t: (sub(pv_(t))[:, 0, :, :, Ls - 1] if smp else pv_(t)[:, 1, :, 63:64])
                    nf = 16 if smp else 1
                    V(lambda e, gsl=gsl, fsrc=fsrc, nf=nf: e.tensor_tensor(out=fin[:, gsl, 0, 0:nf], in0=fsrc(bt1), in1=fsrc(bt2), op=ALU.subtract), ["bt1", "bt2"], ["fin"])
                V(lambda e, ecv=ecv, sim_=sim_: e.tensor_tensor(out=sub(pv_(bt1)), in0=ecv, in1=sim_, op=ALU.mult), ["ecos", "bu", "s5s", "fin"], ["bt1"])
                V(lambda e, esv=esv, sre_=sre_: e.tensor_tensor(out=sub(pv_(bt2)), in0=esv, in1=sre_, op=ALU.mult), ["esin", "bu", "s5s", "fin"], ["bt2"])
                V(lambda e: e.tensor_tensor(out=s5v(1), in0=pv_(bt1), in1=pv_(bt2), op=ALU.add), ["bt1", "bt2"], ["s5s"])
                if smp or lastp:
                    V(lambda e, gsl=gsl, fsrc=fsrc, nf=nf: e.tensor_tensor(out=fin[:, gsl, 1, 0:nf], in0=fsrc(bt1), in1=fsrc(bt2), op=ALU.add), ["bt1", "bt2"], ["fin"])
                ps4 = [pb[4][:, 128 * q:128 * q + 128] for q in range(4)]
                for q in range(4):
                    for ri in range(2):
                        T(lambda e, ft=ft, q=q, ri=ri: e.matmul(ps4[q], lhsT=wd5[:, ft, ri, :], rhs=s5s[:, q, ri, :], start=(ri == 0), stop=(ri == 1)), ["wd5", "s5s"], ["pb4"])
                for q in range(4):
                    rs_ = slice(32 * q, 32 * q + 32)
                    V(lambda e, ft=ft, q=q, rs_=rs_: e.scalar_tensor_tensor(out=y5t[rs_, :], in0=u5[rs_, ft, :], scalar=ds5[rs_, ft:ft + 1], in1=ps4[q][rs_, :], op0=ALU.mult, op1=ALU.add), [f"u5_{par}_{ft}", "ds5", "pb4"], ["y5t"])
                A(lambda e, ft=ft: e.activation(out=y5g[:, ft, :], in_=y5t[:], func=AF.Gelu_apprx_tanh), ["y5t"], [f"y5g{ft}"])
            if lastp:
                for ri, nm in enumerate(["re_p", "im_p"]):
                    DMAO(lambda e, ri=ri, nm=nm: e.dma_start(out=O[nm].rearrange("(gp g2) p -> (g2 p) gp", g2=2), in_=fin[:, :, ri, 0], allow_slow_non_contiguous=True), ["fin"], [])
            if smp:
                for ri, nm in enumerate(["re_s", "im_s"]):
                    for gh in range(4):
                        V(lambda e, gh=gh, ri=ri: e.tensor_copy(out=s5tr[:], in_=fin[:, 8 * gh:8 * gh + 8, ri, :].rearrange("p g s -> p s g")), ["fin"], ["s5tr"])
                        T(lambda e: e.transpose(out=pb[5][:, 0:128], in_=s5tr[:].rearrange("p a b -> p (a b)"), identity=identf[:]), ["s5tr", "identf"], ["pb5"])
                        V(lambda e, gh=gh: e.tensor_copy(out=s5io[:, gh, :, :].rearrange("p a b -> p (a b)"), in_=pb[5][:, 0:128]), ["pb5"], ["s5io"])
                    ov = O[nm].rearrange("s (gh gq g2) p -> s gq gh (g2 p)", gq=8, g2=2)
                    for s in range(16):
                        DMAO(lambda e, s=s, ov=ov: e.dma_start(out=ov[s], in_=s5io[8 * s:8 * s + 8, :, :, :].rearrange("p a b c -> p a (b c)")), ["s5io"], [])

            TAG[0] = "B.glu"
            def glu_evac(m, ps, pk):
                A(lambda e: e.activation(out=glus[:], in_=ps, func=AF.Sigmoid, bias=bglu[:, m:m + 1]), [pk, "bglu"], ["glus"])
                V(lambda e: e.tensor_tensor(out=glus[:], in0=glus[:], in1=y5g[:, m, :], op=ALU.mult), ["glus", f"y5g{m}"], ["glus"])
                V(lambda e: e.tensor_tensor(out=y5f[:, m, :], in0=glus[:], in1=z5s[:, m, :], op=ALU.mult), ["glus", f"z5_{par}_{m}"], [f"y5f{m}"])
            dense_T("w_glu", 8, lambda k: y5g[:, k, :], lambda k: [f"y5g{k}"], glu_evac)
            dense_T("w_down_s5", 8, lambda k: y5f[:, k, :], lambda k: [f"y5f{k}"],
                    lambda m, ps, pk: V(lambda e: e.tensor_tensor(out=ybT[:, m, :], in0=ps, in1=gbs[:, m, :], op=ALU.mult), [pk, f"gb_{par}_{m}"], [f"ybT{m}"]))
            for m in range(8):
                V(lambda e, m=m: e.tensor_tensor(out=mixT[:, m, :], in0=yaT[:, m, :], in1=ybT[:, m, :], op=ALU.add), [f"yaT_{par}_{m}", f"ybT{m}"], [f"mixT{m}"])
            TAG[0] = "B.out"
            DMA(lambda e: e.dma_start(out=ot[:], in_=xsrc), w=["ot"])
            for cb in range(4):
                wt, wk = wload(WS["w_out"][:, 256 * cb:256 * cb + 256])
                hb = cb % 2
                ps = pb[4 + hb][:, 0:256]
                pks = [f"pb{4 + hb}"]
                for k in range(8):
                    T(lambda e, wt=wt, k=k, ps=ps: e.matmul(ps, lhsT=mixT[:, k, :], rhs=wt[:, k, 0:256], start=(k == 0), stop=(k == 7)), [wk, f"mixT{k}"], pks)
                V(lambda e, cb=cb, ps=ps: e.tensor_tensor(out=ot[:, 256 * cb:256 * cb + 256], in0=ps, in1=ot[:, 256 * cb:256 * cb + 256], op=ALU.add), pks + ["ot"], ["ot"])
            G(lambda e: e.memset(ssB[:], 0.0), w=["ssB"])
            A(lambda e: e.activation(out=y5f[:].rearrange("p a b -> p (a b)"), in_=ot[:], func=AF.Square, accum_out=ssB[:]), ["ot", "ssB"], [f"y5f{m}" for m in range(8)] + ["ssB"])
            A(lambda e: e.activation(out=rstdB[:], in_=ssB[:], func=AF.Sqrt, scale=1.0 / D, bias=epscol[:]), ["ssB", "epscol"], ["rstdB"])
            V(lambda e: e.reciprocal(out=rstdB[:], in_=rstdB[:]), ["rstdB"], ["rstdB"])
            V(lambda e: e.scalar_tensor_tensor(out=ot[:], in0=ot[:], scalar=rstdB[:, 0:1], in1=fnw[:], op0=ALU.mult, op1=ALU.mult), ["ot", "rstdB", "fnw"], ["ot"])
            DMAO(lambda e: e.dma_start(out=ydst, in_=ot[:]), ["ot"], [])


        def flush(lst):
            for (eng, fn, r, w, dma, tag) in lst:
                P.op(eng, fn, r, w, dma=dma)
                P.ops[-1]["tag"] = tag

        def record(fn_, ti, par, banks, wset):
            CUR[0] = []
            MMB[0] = banks
            WSET[0] = wset
            fn_(ti, par)
            lst = CUR[0]
            CUR[0] = None
            return lst

        def merge(la, lb):
            out = []
            ia = ib = 0
            na, nb = len(la), len(lb)
            while ia < na or ib < nb:
                if ib >= nb or (ia < na and ia * nb <= ib * na):
                    out.append(la[ia]); ia += 1
                else:
                    out.append(lb[ib]); ib += 1
            return out

        tl = list(tiles) if tiles is not None else (list(range(8)) + [16] + list(range(8, 16)))
        if tl:
            flush(record(recA, tl[0], 0, (0, 1), (0, 1)))
            for n_, ti in enumerate(tl):
                lb = record(recB, ti, n_ % 2, (4, 5), (2, 3))
                if n_ + 1 < len(tl):
                    la = record(recA, tl[n_ + 1], (n_ + 1) % 2, (0, 1), (0, 1))
                    if not INTERLEAVE[0]:
                        flush(lb); flush(la)
                    else:
                        flush(merge(la, lb))
                else:
                    flush(lb)
        P.emit(nc)
    return nc


_NC = {}


def _shard(inputs, c):
    d = {}
    d["xp"] = np.ascontiguousarray(inputs["x_prompt"][c])
    d["xs"] = np.ascontiguousarray(inputs["x_sample"][16 * c:16 * c + 16].reshape(128, 1024))
    d["conv0"] = np.ascontiguousarray(inputs["state_conv"][0, 16 * c:16 * c + 16])
    d["ssm0"] = np.ascontiguousarray(inputs["state_ssm"][0, 16 * c:16 * c + 16])
    d["re0"] = np.ascontiguousarray(inputs["state_s5_re"][0, 16 * c:16 * c + 16])
    d["im0"] = np.ascontiguousarray(inputs["state_s5_im"][0, 16 * c:16 * c + 16])
    for k in ("norm_w", "w_in", "conv_w", "conv_b", "dt_bias", "A_log", "D_ssd", "ssd_norm_w", "w_down_ssd", "lam_re",
              "lam_im", "log_dt", "B_re", "B_im", "C_re", "C_im", "D_s5", "w_glu", "b_glu", "w_down_s5", "w_out"):
        d[k] = np.ascontiguousarray(inputs[k][0])
    d["final_norm_w"] = np.ascontiguousarray(inputs["final_norm_w"])
    return d


def kernel(**inputs):
    inputs = {k: np.asarray(v, dtype=np.float32) for k, v in inputs.items()}
    if "nc" not in _NC:
        _NC["nc"] = build()
    nc = _NC["nc"]
    consts = host_consts()
    in_maps = []
    for c in range(NCORES):
        d = _shard(inputs, c)
        d.update(consts)
        in_maps.append(d)
    res = run_bass_kernel_spmd(nc, in_maps, core_ids=list(range(NCORES))).results
    cat = lambda n: np.concatenate([r[n] for r in res], axis=0)
    y_p = np.stack([r["y_p"] for r in res], 0)
    y_s = cat("y_s").reshape(128, 8, 1024)
    conv_p = np.stack([r["conv_p"] for r in res], 0)[None]
    ssm_p = np.stack([r["ssm_p"] for r in res], 0)[None]
    re_p = np.stack([r["re_p"] for r in res], 0)[None]
    im_p = np.stack([r["im_p"] for r in res], 0)[None]
    conv_s = cat("conv_s")[None]
    ssm_s = cat("ssm_s")[None]
    re_s = cat("re_s")[None]
    im_s = cat("im_s")[None]
    return tuple(np.ascontiguousarray(a, dtype=np.float32) for a in
                 (y_p, y_s, conv_p, ssm_p, re_p, im_p, conv_s, ssm_s, re_s, im_s))
```

```python
import contextlib
import math
import numpy as np
import concourse.bass as bass
import concourse.mybir as mybir
from concourse.bass_utils import run_bass_kernel_spmd

F32 = mybir.dt.float32
BF16 = mybir.dt.bfloat16
I32 = mybir.dt.int32
ALU = mybir.AluOpType
AF = mybir.ActivationFunctionType

ENGS = ("pe", "act", "dve", "pool", "sp")
N_DMA_SEM = 24
NCORES = 8
D = 1024
PROJ = 10272
Z0, XBC0, DT0, U0, Z50, GA0, GB0 = 0, 2048, 6144, 6176, 7200, 8224, 9248
EPS = 1e-6
SIN_SCALE = 6.283185


class Prog:
    def __init__(self):
        self.ops = []
        self.last_write = {}
        self.readers = {}
        self.dma_count = {e: 0 for e in ENGS}
        self.slot_last = {}
        self.last_op = {}
        self.pending = {}

    def barrier(self):
        deps = set(self.last_op.values()) | set(self.slot_last.values())
        for e in ENGS:
            self.pending[e] = set(deps) | self.pending.get(e, set())

    def op(self, eng, fn, reads=(), writes=(), dma=False):
        writes = list(writes) + [k for k in reads if k.startswith("pb")]
        reads = [k for k in reads if not k.startswith("pb")]
        oid = len(self.ops)
        deps = set()
        for k in reads:
            if k in self.last_write:
                deps.add(self.last_write[k])
        for k in writes:
            if k in self.last_write:
                deps.add(self.last_write[k])
            deps.update(self.readers.get(k, ()))
        if eng in self.pending:
            deps |= self.pending.pop(eng)
        rec = dict(eng=eng, fn=fn, deps=deps, dma=dma, sig=False, seq=None)
        if dma:
            i = self.dma_count[eng]
            self.dma_count[eng] += 1
            slot = i % N_DMA_SEM
            rec["slot"] = slot
            rec["val"] = 16 * (i // N_DMA_SEM + 1)
            prev = self.slot_last.get((eng, slot))
            if prev is not None:
                deps.add(prev)
            self.slot_last[(eng, slot)] = oid
        else:
            self.last_op[eng] = oid
        deps.discard(oid)
        self.ops.append(rec)
        for k in writes:
            self.last_write[k] = oid
            self.readers[k] = []
        for k in reads:
            self.readers.setdefault(k, []).append(oid)
        return oid

    def emit(self, nc):
        ops = self.ops

        def skip(dd, o):
            return (not dd["dma"]) and dd["eng"] == o["eng"] == "pe" and not o["dma"]
        for o in ops:
            for d in o["deps"]:
                dd = ops[d]
                if dd["dma"] or skip(dd, o):
                    continue
                dd["sig"] = True
        cnt = {e: 0 for e in ENGS}
        for o in ops:
            if o["sig"]:
                cnt[o["eng"]] += 1
                o["seq"] = cnt[o["eng"]]
        with contextlib.ExitStack() as st:
            csem = {e: st.enter_context(nc.semaphore("c_" + e)) for e in ENGS}
            dsem = {(e, s): st.enter_context(nc.semaphore(f"d_{e}{s}"))
                    for e in ENGS if self.dma_count[e] > 0
                    for s in range(min(N_DMA_SEM, self.dma_count[e]))}
            block = st.enter_context(nc.Block())

            def run(eng_name, eng):
                waited = {}
                for o in ops:
                    if o["eng"] != eng_name:
                        continue
                    for d in sorted(o["deps"]):
                        dd = ops[d]
                        if dd["dma"]:
                            key = ("d", dd["eng"], dd["slot"])
                            sem = dsem[(dd["eng"], dd["slot"])]
                            val = dd["val"]
                        else:
                            if skip(dd, o):
                                continue
                            key = ("c", dd["eng"])
                            sem = csem[dd["eng"]]
                            val = dd["seq"]
                        if waited.get(key, 0) >= val:
                            continue
                        waited[key] = val
                        eng.wait_ge(sem, val)
                    ins = o["fn"](eng)
                    if o["dma"]:
                        ins.then_inc(dsem[(eng_name, o["slot"])], 16)
                    elif o["sig"]:
                        ins.then_inc(csem[eng_name], 1)
                if eng_name == "sp":
                    for (e, s), oid in self.slot_last.items():
                        eng.wait_ge(dsem[(e, s)], ops[oid]["val"])

            block.tensor(lambda e: run("pe", e))
            block.scalar(lambda e: run("act", e))
            block.vector(lambda e: run("dve", e))
            block.gpsimd(lambda e: run("pool", e))
            block.sync(lambda e: run("sp", e))


def host_consts():
    c = {}
    c["ident"] = np.eye(128, dtype=np.float32)
    s = np.arange(128)
    c["negm_p"] = np.where(s[None, :] >= s[:, None], 0.0, -30000.0).astype(np.float32)
    same = (s[None, :] // 8) == (s[:, None] // 8)
    c["negm_s"] = np.where((s[None, :] >= s[:, None]) & same, 0.0, -30000.0).astype(np.float32)
    rp = np.ones((32, 128), np.float32); rp[:, 0] = 0
    rs = np.ones((32, 128), np.float32); rs[:, ::8] = 0
    c["rm_p"] = rp
    c["rm_s"] = rs
    r128 = np.ones((128, 128), np.float32); r128[:, ::8] = 0
    c["rm128"] = r128
    rq = np.ones((128, 256), np.float32); rq[:, ::64] = 0
    c["rmq"] = rq
    c["seqmask"] = (s[:, None] // 8 == np.arange(16)[None, :]).astype(np.float32)
    c["pairmask"] = (s[:, None] // 32 == np.arange(4)[None, :]).astype(np.float32)
    c["iota"] = np.broadcast_to(np.arange(128, dtype=np.float32)[None, :], (128, 128)).copy()
    return c


IN_SPECS = [
    ("xp", (2048, 1024)), ("xs", (128, 1024)), ("conv0", (16, 3, 4096)), ("ssm0", (16, 32, 64, 128)),
    ("re0", (16, 64, 64)), ("im0", (16, 64, 64)), ("norm_w", (1024,)), ("w_in", (1024, PROJ)),
    ("conv_w", (4, 4096)), ("conv_b", (4096,)), ("dt_bias", (32,)), ("A_log", (32,)), ("D_ssd", (32,)),
    ("ssd_norm_w", (2048,)), ("w_down_ssd", (2048, 1024)), ("lam_re", (64, 64)), ("lam_im", (64, 64)),
    ("log_dt", (64,)), ("B_re", (64, 64, 16)), ("B_im", (64, 64, 16)), ("C_re", (64, 16, 64)),
    ("C_im", (64, 16, 64)), ("D_s5", (1024,)), ("w_glu", (1024, 1024)), ("b_glu", (1024,)),
    ("w_down_s5", (1024, 1024)), ("w_out", (1024, 1024)), ("final_norm_w", (1024,)),
    ("ident", (128, 128)), ("negm_p", (128, 128)), ("negm_s", (128, 128)), ("rm_p", (32, 128)),
    ("rm_s", (32, 128)), ("rm128", (128, 128)), ("rmq", (128, 256)), ("seqmask", (128, 16)), ("pairmask", (128, 4)),
    ("iota", (128, 128)),
]
OUT_SPECS = [
    ("y_p", (2048, 1024)), ("y_s", (128, 1024)), ("conv_p", (3, 4096)), ("ssm_p", (32, 64, 128)),
    ("re_p", (64, 64)), ("im_p", (64, 64)), ("conv_s", (16, 3, 4096)), ("ssm_s", (16, 32, 64, 128)),
    ("re_s", (16, 64, 64)), ("im_s", (16, 64, 64)),
]


STOP = [None]
INTERLEAVE = [True]
SPLITA = [True]
BLK = [None]


class _Stop(Exception):
    pass


def ck(n):
    if STOP[0] is not None and n >= STOP[0]:
        raise _Stop()


def build(tiles=None):
    nc = bass.Bass("TRN2", target_bir_lowering=False)
    I = {n: nc.dram_tensor(n, list(s), F32, kind="ExternalInput").ap() for n, s in IN_SPECS}
    O = {n: nc.dram_tensor(n, list(s), F32, kind="ExternalOutput").ap() for n, s in OUT_SPECS}
    WS = {
        "w_in": nc.dram_tensor("ws_w_in", [1024, PROJ], BF16).ap(),
        "w_down_ssd": nc.dram_tensor("ws_wds", [2048, 1024], BF16).ap(),
        "w_glu": nc.dram_tensor("ws_wglu", [1024, 1024], BF16).ap(),
        "w_down_s5": nc.dram_tensor("ws_wd5", [1024, 1024], BF16).ap(),
        "w_out": nc.dram_tensor("ws_wout", [1024, 1024], BF16).ap(),
    }
    P = Prog()

    CUR = [None]
    MMB = [(0, 1)]

    TAG = ["pro"]

    def _rec(eng, fn, r, w, dma=False):
        if CUR[0] is None:
            P.op(eng, fn, r, w, dma=dma)
            P.ops[-1]["tag"] = TAG[0]
        else:
            CUR[0].append((eng, fn, tuple(r), tuple(w), dma, TAG[0]))

    def V(fn, r=(), w=()):
        _rec("dve", fn, r, w)

    def G(fn, r=(), w=()):
        _rec("pool", fn, r, w)

    def A(fn, r=(), w=()):
        _rec("act", fn, r, w)

    def T(fn, r=(), w=()):
        _rec("pe", fn, r, w)

    def DMA(fn, r=(), w=()):
        _rec("sp", fn, r, w, dma=True)

    def DMAO(fn, r=(), w=()):
        _rec("pool", fn, r, w, dma=True)

    def DMAH(fn, r=(), w=()):
        _rec("act", fn, r, w, dma=True)

    st = contextlib.ExitStack()
    with st:
        def sb(name, shape, dt=F32, stack=None):
            return (stack or st).enter_context(nc.sbuf_tensor("s_" + name, list(shape), dt))

        def kn(t):
            n_ = getattr(t, "name")
            return n_[2:] if n_.startswith("s_") else n_
        pb = [st.enter_context(nc.psum_tensor(f"pb{i}", [128, 512], F32)) for i in range(8)]
        pbb = [p.bitcast(BF16) for p in pb]

        identf = sb("identf", [128, 128]); identb = sb("identb", [128, 128], BF16)
        negm_p = sb("negm_p", [128, 128]); negm_s = sb("negm_s", [128, 128])
        rm_p = sb("rm_p", [32, 128]); rm_s = sb("rm_s", [32, 128]); rm128 = sb("rm128", [128, 128]); rmq = sb("rmq", [128, 256])
        seqmask = sb("seqmask", [128, 16]); pairmask = sb("pairmask", [128, 4])
        ones32 = sb("ones32", [32, 128]); onesb = sb("onesb", [128, 128], BF16)
        onecol = sb("onecol", [128, 1]); epscol = sb("epscol", [128, 1])
        for nm, t in [("ident", identf), ("negm_p", negm_p), ("negm_s", negm_s), ("rm_p", rm_p), ("rm_s", rm_s),
                      ("rm128", rm128), ("rmq", rmq), ("seqmask", seqmask), ("pairmask", pairmask)]:
            DMA(lambda e, nm=nm, t=t: e.dma_start(out=t[:], in_=I[nm]), w=[kn(t)])
        V(lambda e: e.tensor_copy(out=identb[:], in_=identf[:]), ["identf"], ["identb"])
        G(lambda e: e.memset(ones32[:], 1.0), w=["ones32"])
        G(lambda e: e.memset(onesb[:], 1.0), w=["onesb"])
        G(lambda e: e.memset(onecol[:], 1.0), w=["onecol"])
        G(lambda e: e.memset(epscol[:], EPS), w=["epscol"])

        def colvec(name, src, nt):
            t = sb(name, [128, nt])
            DMA(lambda e: e.dma_start(out=t[:], in_=src.rearrange("(j p) -> p j", p=128), allow_slow_non_contiguous=True), w=[name])
            return t
        normw = colvec("normw", I["norm_w"], 8)
        convb = colvec("convb", I["conv_b"], 32)
        ssdnw = colvec("ssdnw", I["ssd_norm_w"], 16)
        ds5 = colvec("ds5", I["D_s5"], 8)
        bglu = colvec("bglu", I["b_glu"], 8)
        convw = sb("convw", [128, 32, 4])
        for k in range(4):
            DMA(lambda e, k=k: e.dma_start(out=convw[:, :, k], in_=I["conv_w"][k].rearrange("(j p) -> p j", p=128), allow_slow_non_contiguous=True), w=["convw"])
        fnw = sb("fnw", [128, 1024])
        DMA(lambda e: e.dma_start(out=fnw[:], in_=I["final_norm_w"].partition_broadcast(128)), w=["fnw"])
        dtb = sb("dtb", [32, 1]); aneg = sb("aneg", [32, 1])
        DMA(lambda e: e.dma_start(out=dtb[:], in_=I["dt_bias"].rearrange("(h o) -> h o", o=1)), w=["dtb"])
        DMA(lambda e: e.dma_start(out=aneg[:], in_=I["A_log"].rearrange("(h o) -> h o", o=1)), w=["aneg"])
        A(lambda e: e.activation(out=aneg[:], in_=aneg[:], func=AF.Exp), ["aneg"], ["aneg"])
        V(lambda e: e.tensor_scalar(out=aneg[:], in0=aneg[:], scalar1=-1.0, scalar2=None, op0=ALU.mult), ["aneg"], ["aneg"])
        dexp = sb("dexp", [128, 16])
        dv = I["D_ssd"].rearrange("(pr h2) -> h2 pr", h2=2)
        for h2 in range(2):
            DMA(lambda e, h2=h2: e.dma_start(out=dexp[64 * h2:64 * h2 + 64, :], in_=dv[h2].partition_broadcast(64), allow_slow_non_contiguous=True), w=["dexp"])
        are = sb("are", [128, 32]); aim = sb("aim", [128, 32]); rdec = sb("rdec", [128, 32])
        aere = sb("aere", [128, 32]); aeim = sb("aeim", [128, 32])
        ecos = sb("ecos", [128, 32, 64]); esin = sb("esin", [128, 32, 64])
        wb5 = sb("wb5", [128, 8, 2, 128], BF16); wd5 = sb("wd5", [128, 8, 2, 128], BF16)
        halo = sb("halo", [128, 32, 3])
        STt = sb("STt", [128, 32, 64])
        SA = sb("SA", [128, 16, 128], BF16); SB = sb("SB", [128, 16, 128], BF16)
        XA = sb("XA", [128, 16, 128], BF16); XB = sb("XB", [128, 16, 128], BF16)
        sgc = sb("sgc", [128, 32, 2])
        for t_ in (halo, STt, SA, SB, XA, XB, sgc, wd5):
            G(lambda e, t_=t_: e.memset(t_[:], 0.0), w=([kn(t_)] if kn(t_) not in ("SA", "SB") else [f"{kn(t_)}{i}" for i in range(4)]))

        pst = contextlib.ExitStack()
        with pst:
            def psb(name, shape, dt=F32):
                return sb(name, shape, dt, stack=pst)
            iota = psb("iota", [128, 128])
            DMA(lambda e: e.dma_start(out=iota[:], in_=I["iota"]), w=["iota"])
            lre = psb("lre", [128, 32]); lim = psb("lim", [128, 32]); stp = psb("stp", [128, 32]); th = psb("th", [128, 32])
            DMA(lambda e: e.dma_start(out=lre[:], in_=I["lam_re"].rearrange("(gp g2) p -> (g2 p) gp", g2=2), allow_slow_non_contiguous=True), w=["lre"])
            DMA(lambda e: e.dma_start(out=lim[:], in_=I["lam_im"].rearrange("(gp g2) p -> (g2 p) gp", g2=2), allow_slow_non_contiguous=True), w=["lim"])
            ldv = I["log_dt"].rearrange("(gp g2) -> g2 gp", g2=2)
            for g2 in range(2):
                DMA(lambda e, g2=g2: e.dma_start(out=stp[64 * g2:64 * g2 + 64, :], in_=ldv[g2].partition_broadcast(64), allow_slow_non_contiguous=True), w=["stp"])
            A(lambda e: e.activation(out=stp[:], in_=stp[:], func=AF.Exp), ["stp"], ["stp"])
            V(lambda e: e.tensor_tensor(out=rdec[:], in0=lre[:], in1=stp[:], op=ALU.mult), ["lre", "stp"], ["rdec"])
            A(lambda e: e.activation(out=rdec[:], in_=rdec[:], func=AF.Exp), ["rdec"], ["rdec"])
            V(lambda e: e.tensor_tensor(out=th[:], in0=lim[:], in1=stp[:], op=ALU.mult), ["lim", "stp"], ["th"])
            V(lambda e: e.tensor_scalar(out=th[:], in0=th[:], scalar1=1.0 / (2.0 * math.pi), scalar2=None, op0=ALU.mult), ["th"], ["th"])
            tmpx = psb("tmpx", [128, 2048]); tmpf = psb("tmpf", [128, 2048]); tmpi = psb("tmpi", [128, 2048], I32)

            def sin_turns(out_ap, mk_x, n, off, rkeys, wkeys):
                V(lambda e: mk_x(e, tmpx[:, 0:n]), rkeys, ["tmpx"])
                if off:
                    V(lambda e: e.tensor_scalar(out=tmpx[:, 0:n], in0=tmpx[:, 0:n], scalar1=float(off), scalar2=None, op0=ALU.add), ["tmpx"], ["tmpx"])
                V(lambda e: e.tensor_copy(out=tmpi[:, 0:n], in_=tmpx[:, 0:n]), ["tmpx"], ["tmpi"])
                V(lambda e: e.tensor_copy(out=tmpf[:, 0:n], in_=tmpi[:, 0:n]), ["tmpi"], ["tmpf"])
                V(lambda e: e.tensor_tensor(out=tmpx[:, 0:n], in0=tmpx[:, 0:n], in1=tmpf[:, 0:n], op=ALU.subtract), ["tmpx", "tmpf"], ["tmpx"])
                A(lambda e: e.activation(out=out_ap, in_=tmpx[:, 0:n], func=AF.Sin, scale=SIN_SCALE), ["tmpx"], wkeys)
            cs1 = psb("cs1", [128, 32]); sn1 = psb("sn1", [128, 32])
            x1 = lambda e, dst: e.tensor_copy(out=dst, in_=th[:])
            sin_turns(sn1[:], x1, 32, 0.0, ["th"], ["sn1"])
            sin_turns(cs1[:], x1, 32, 0.25, ["th"], ["cs1"])
            V(lambda e: e.tensor_tensor(out=are[:], in0=rdec[:], in1=cs1[:], op=ALU.mult), ["rdec", "cs1"], ["are"])
            V(lambda e: e.tensor_tensor(out=aim[:], in0=rdec[:], in1=sn1[:], op=ALU.mult), ["rdec", "sn1"], ["aim"])
            x64 = lambda e, dst: e.tensor_scalar(out=dst, in0=th[:], scalar1=64.0, scalar2=None, op0=ALU.mult)
            sin_turns(sn1[:], x64, 32, 0.0, ["th"], ["sn1"])
            sin_turns(cs1[:], x64, 32, 0.25, ["th"], ["cs1"])
            V(lambda e: e.tensor_tensor(out=aere[:], in0=rdec[:], in1=cs1[:], op=ALU.mult), ["rdec", "cs1"], ["aere"])
            V(lambda e: e.tensor_tensor(out=aeim[:], in0=rdec[:], in1=sn1[:], op=ALU.mult), ["rdec", "sn1"], ["aeim"])
            xt_ = lambda e, dst: e.tensor_tensor(out=dst.rearrange("p (a b) -> p a b", b=64), in0=th[:].unsqueeze(2).broadcast_to([128, 32, 64]),
                                                 in1=iota[:, 0:64].unsqueeze(1).broadcast_to([128, 32, 64]), op=ALU.mult)
            sin_turns(esin[:].rearrange("p a b -> p (a b)"), xt_, 2048, 0.0, ["th", "iota"], ["esin"])
            sin_turns(ecos[:].rearrange("p a b -> p (a b)"), xt_, 2048, 0.25, ["th", "iota"], ["ecos"])
            gre = psb("gre", [128, 32]); gim = psb("gim", [128, 32]); den = psb("den", [128, 32])
            t1 = psb("t1s", [128, 32]); t2 = psb("t2s", [128, 32]); am1 = psb("am1", [128, 32])
            V(lambda e: e.tensor_scalar(out=am1[:], in0=are[:], scalar1=-1.0, scalar2=None, op0=ALU.add), ["are"], ["am1"])
            V(lambda e: e.tensor_tensor(out=den[:], in0=lre[:], in1=lre[:], op=ALU.mult), ["lre"], ["den"])
            V(lambda e: e.tensor_tensor(out=t1[:], in0=lim[:], in1=lim[:], op=ALU.mult), ["lim"], ["t1s"])
            V(lambda e: e.tensor_tensor(out=den[:], in0=den[:], in1=t1[:], op=ALU.add), ["den", "t1s"], ["den"])
            V(lambda e: e.reciprocal(out=den[:], in_=den[:]), ["den"], ["den"])
            V(lambda e: e.tensor_tensor(out=t1[:], in0=am1[:], in1=lre[:], op=ALU.mult), ["am1", "lre"], ["t1s"])
            V(lambda e: e.tensor_tensor(out=t2[:], in0=aim[:], in1=lim[:], op=ALU.mult), ["aim", "lim"], ["t2s"])
            V(lambda e: e.tensor_tensor(out=gre[:], in0=t1[:], in1=t2[:], op=ALU.add), ["t1s", "t2s"], ["gre"])
            V(lambda e: e.tensor_tensor(out=gre[:], in0=gre[:], in1=den[:], op=ALU.mult), ["gre", "den"], ["gre"])
            V(lambda e: e.tensor_tensor(out=t1[:], in0=aim[:], in1=lre[:], op=ALU.mult), ["aim", "lre"], ["t1s"])
            V(lambda e: e.tensor_tensor(out=t2[:], in0=am1[:], in1=lim[:], op=ALU.mult), ["am1", "lim"], ["t2s"])
            V(lambda e: e.tensor_tensor(out=gim[:], in0=t1[:], in1=t2[:], op=ALU.subtract), ["t1s", "t2s"], ["gim"])
            V(lambda e: e.tensor_tensor(out=gim[:], in0=gim[:], in1=den[:], op=ALU.mult), ["gim", "den"], ["gim"])
            bre = psb("bre", [128, 32, 16]); bim = psb("bim", [128, 32, 16]); bbr = psb("bbr", [128, 32, 16]); bbi = psb("bbi", [128, 32, 16]); tb = psb("tbs", [128, 32, 16])
            DMA(lambda e: e.dma_start(out=bre[:], in_=I["B_re"].rearrange("(gp g2) p k -> (g2 p) gp k", g2=2)), w=["bre"])
            DMA(lambda e: e.dma_start(out=bim[:], in_=I["B_im"].rearrange("(gp g2) p k -> (g2 p) gp k", g2=2)), w=["bim"])
            gb_ = lambda t: t[:].unsqueeze(2).broadcast_to([128, 32, 16])
            V(lambda e: e.tensor_tensor(out=bbr[:], in0=bre[:], in1=gb_(gre), op=ALU.mult), ["bre", "gre"], ["bbr"])
            V(lambda e: e.tensor_tensor(out=tb[:], in0=bim[:], in1=gb_(gim), op=ALU.mult), ["bim", "gim"], ["tbs"])
            V(lambda e: e.tensor_tensor(out=bbr[:], in0=bbr[:], in1=tb[:], op=ALU.subtract), ["bbr", "tbs"], ["bbr"])
            V(lambda e: e.tensor_tensor(out=bbi[:], in0=bim[:], in1=gb_(gre), op=ALU.mult), ["bim", "gre"], ["bbi"])
            V(lambda e: e.tensor_tensor(out=tb[:], in0=bre[:], in1=gb_(gim), op=ALU.mult), ["bre", "gim"], ["tbs"])
            V(lambda e: e.tensor_tensor(out=bbi[:], in0=bbi[:], in1=tb[:], op=ALU.add), ["bbi", "tbs"], ["bbi"])
            cin = psb("cin", [128, 4, 2, 64])
            cst = [psb("cstr", [128, 32, 16]), psb("csti", [128, 32, 16])]
            for ri, nm in enumerate(["C_re", "C_im"]):
                cv = I[nm].rearrange("(gh gq g2) k p -> gq k gh g2 p", gq=8, g2=2)
                for gq in range(8):
                    for gh in range(4):
                        DMA(lambda e, gq=gq, gh=gh, cv=cv: e.dma_start(out=cin[16 * gq:16 * gq + 16, gh, :, :], in_=cv[gq][:, gh, :, :]), w=["cin"])
                for gh in range(4):
                    T(lambda e, gh=gh: e.transpose(out=pb[7][:, 0:128], in_=cin[:, gh, :, :].rearrange("p a b -> p (a b)"), identity=identf[:]), ["cin", "identf"], ["pb7"])
                    V(lambda e, gh=gh, ri=ri: e.tensor_copy(out=cst[ri][:, 8 * gh:8 * gh + 8, :], in_=pb[7][:, 0:128].rearrange("p (a b) -> p a b", b=16)), ["pb7"], [kn(cst[ri])])
            zst = psb("zst", [128, 4, 2, 16])
            G(lambda e: e.memset(zst[:], 0.0), w=["zst"])
            for ft in range(8):
                for ri in range(2):
                    src = [bbr, bbi][ri]
                    for h2 in range(2):
                        V(lambda e, ft=ft, h2=h2, src=src: e.tensor_copy(out=zst[64 * h2:64 * h2 + 64, :, h2, :], in_=src[64 * h2:64 * h2 + 64, 4 * ft:4 * ft + 4, :]), [kn(src)], ["zst"])
                    T(lambda e: e.transpose(out=pb[6][:, 0:128], in_=zst[:].rearrange("p a b c -> p (a b c)"), identity=identf[:]), ["zst", "identf"], ["pb6"])
                    V(lambda e, ft=ft, ri=ri: e.tensor_copy(out=wb5[:, ft, ri, :], in_=pb[6][:, 0:128]), ["pb6"], ["wb5"])
                    for h2 in range(2):
                        V(lambda e, ft=ft, ri=ri, h2=h2: e.tensor_scalar(
                            out=wd5[64 * h2:64 * h2 + 64, ft, ri, :].rearrange("p (q g k) -> p q g k", g=2, k=16)[:, :, h2, :],
                            in0=cst[ri][64 * h2:64 * h2 + 64, 4 * ft:4 * ft + 4, :], scalar1=(1.0 if ri == 0 else -1.0), scalar2=None, op0=ALU.mult),
                          [kn(cst[ri])], ["wd5"])
            NST = 6
            stg = [psb(f"stg{i}", [128, 2048]) for i in range(NST)]
            stgb = [psb(f"stgb{i}", [128, 2048], BF16) for i in range(NST)]
            cnt = [0]

            def cast_rows(src, dst, ncols):
                for c0 in range(0, ncols, 2048):
                    w_ = min(2048, ncols - c0)
                    i = cnt[0] % NST
                    cnt[0] += 1
                    DMA(lambda e, i=i, c0=c0, w_=w_: e.dma_start(out=stg[i][:, 0:w_], in_=src[:, c0:c0 + w_]), w=[f"stg{i}"])
                    sel = (1, 2, 2, 2, 1, 0)[cnt[0] % 6]
                    if sel == 0:
                        G(lambda e, i=i, w_=w_: e.tensor_copy(out=stgb[i][:, 0:w_], in_=stg[i][:, 0:w_]), [f"stg{i}"], [f"stgb{i}"])
                    elif sel == 1:
                        A(lambda e, i=i, w_=w_: e.activation(out=stgb[i][:, 0:w_], in_=stg[i][:, 0:w_], func=AF.Copy), [f"stg{i}"], [f"stgb{i}"])
                    else:
                        V(lambda e, i=i, w_=w_: e.tensor_copy(out=stgb[i][:, 0:w_], in_=stg[i][:, 0:w_]), [f"stg{i}"], [f"stgb{i}"])
                    DMAH(lambda e, i=i, c0=c0, w_=w_: e.dma_start(out=dst[:, c0:c0 + w_], in_=stgb[i][:, 0:w_]), [f"stgb{i}"], ["ws"])
            for nm, rows in [("w_in", 1024), ("w_down_ssd", 2048), ("w_glu", 1024), ("w_down_s5", 1024), ("w_out", 1024)]:
                nco = PROJ if nm == "w_in" else 1024
                for r0 in range(0, rows, 128):
                    cast_rows(I[nm][r0:r0 + 128, :], WS[nm][r0:r0 + 128, :], nco)

        P.barrier()

        xt = sb("xt", [128, 1024]); ot = sb("ot", [128, 1024]); ss = sb("ss", [128, 1]); rstd = sb("rstd", [128, 1])
        ssB = sb("ssB", [128, 1]); rstdB = sb("rstdB", [128, 1])
        xsn = sb("xsn", [128, 1024], BF16); hT = sb("hT", [128, 8, 128], BF16)
        wbuf = [sb(f"wbuf{i}", [128, 8, 256], BF16) for i in range(4)]
        zs = sb("zs", [128, 16, 128], BF16)
        xc = sb("xc", [128, 32, 128], BF16)
        raw = [sb(f"raw{i}", [128, 176]) for i in range(2)]
        acc = [sb(f"acc{i}", [128, 128]) for i in range(2)]
        dts = sb("dts", [32, 128]); dAs = sb("dAs", [32, 128]); acs = sb("acs", [32, 128]); dec = sb("dec", [32, 128]); wdt = sb("wdt", [32, 128])
        tokm = sb("tokm", [128, 3, 32])
        u5_ = [sb(f"u5_{i}", [128, 8, 128], BF16) for i in range(2)]; z5s_ = [sb(f"z5s_{i}", [128, 8, 128], BF16) for i in range(2)]
        gas = sb("gas", [128, 8, 128], BF16); gbs_ = [sb(f"gbs_{i}", [128, 8, 128], BF16) for i in range(2)]
        Xd = sb("Xd", [128, 32, 64], BF16); Btok = sb("Btok", [128, 8, 128], BF16)
        rhsR = sb("rhsR", [32, 4, 128]); Egs = [sb("Eg0", [128, 4, 128])] * 2; Dm = sb("Dm", [128, 4, 128])
        Lgs = [sb(f"Lg{i}", [128, 4, 128], BF16) for i in range(2)]; Mg = sb("Mg", [128, 4, 128], BF16)
        CEall = sb("CEall", [128, 32, 128], BF16)
        dch = sb("dch", [128, 32, 16])
        S0qs = [sb(f"S0q{i}", [128, 4, 128]) for i in range(2)]; Snqs = [sb(f"Snq{i}", [128, 4, 128]) for i in range(2)]
        Bms = [sb("Bm0", [128, 8, 128], BF16)] * 2
        yv2 = [sb(f"yv2_{i}", [128, 2, 128]) for i in range(2)]; ysq2 = [sb(f"ysq2_{i}", [128, 2, 128], BF16) for i in range(2)]
        rstg2 = [sb(f"rstg2_{i}", [128, 128]) for i in range(2)]; ynT = sb("ynT", [128, 16, 128], BF16)
        yaT_ = [sb(f"yaT_{i}", [128, 8, 128], BF16) for i in range(2)]; ybT = sb("ybT", [128, 8, 128], BF16); mixT = sb("mixT", [128, 8, 128], BF16)
        u5m = sb("u5m", [128, 4, 128], BF16)
        bu = sb("bu", [128, 4, 2, 128]); sgm = sb("sgm", [128, 4, 2, 128]); rts = sb("rts", [128, 4, 128])
        bt1 = sb("bt1", [128, 4, 128]); bt2 = sb("bt2", [128, 4, 128])
        s5s = sb("s5s", [128, 4, 2, 128], BF16)
        cin_ = sb("cinn", [128, 4, 2, 16])
        ctm = sb("ctm", [128, 4, 2, 16])
        fin = sb("fin", [128, 32, 2, 16]); s0all = sb("s0all", [128, 32, 2, 16])
        y5t = sb("y5t", [128, 128])
        y5g = sb("y5g", [128, 8, 128], BF16); y5f = sb("y5f", [128, 8, 128], BF16); glus = sb("glus", [128, 128])
        s5io = sb("s5io", [128, 4, 2, 64]); s5tr = sb("s5tr", [128, 16, 8])
        cvo = sb("cvo", [48, 1024]); cst3 = sb("cst3", [128, 48])

        wcnt = [0]

        WSET = [(0, 1, 2)]

        def wload(src_ap):
            i = WSET[0][wcnt[0] % len(WSET[0])]
            wcnt[0] += 1
            kt = src_ap.shape[0] // 128
            nco = src_ap.shape[1]
            DMA(lambda e: e.dma_start(out=wbuf[i][:, 0:kt, 0:nco], in_=src_ap.rearrange("(j p) f -> p j f", p=128)), ["ws"], [f"wbuf{i}"])
            return wbuf[i], f"wbuf{i}"

        pslot = [0]

        def mm_slot():
            s = MMB[0][pslot[0] % 2]
            pslot[0] += 1
            return pb[s][:, 0:128], f"pb{s}"

        def proj_ft(wt, wk, c0, nrows=128):
            ps, pk = mm_slot()
            for j in range(8):
                T(lambda e, j=j: e.matmul(ps[0:nrows, :], lhsT=wt[:, j, c0:c0 + nrows], rhs=hT[:, j, :], start=(j == 0), stop=(j == 7)), [wk, "hT"], [pk])
            return ps, pk

        def dense_T(wname, kt, rhs_fn, rkeys_fn, evac):
            for cb in range(4):
                halves = [wload(WS[wname][128 * k0:128 * (k0 + 8), 256 * cb:256 * cb + 256]) for k0 in range(0, kt, 8)]
                for s in range(2):
                    m = 2 * cb + s
                    ps, pk = mm_slot()
                    for k in range(kt):
                        wt, wk = halves[k // 8]
                        T(lambda e, wt=wt, k=k, s=s, ps=ps: e.matmul(ps, lhsT=wt[:, k % 8, 128 * s:128 * s + 128], rhs=rhs_fn(k), start=(k == 0), stop=(k == kt - 1)), [wk] + rkeys_fn(k), [pk])
                    evac(m, ps, pk)

        def cmul(out_re, out_im, ar, ai, br, bi, keys_r, keys_w, t1_, t2_, tk):
            V(lambda e: e.tensor_tensor(out=t1_, in0=ar, in1=br, op=ALU.mult), keys_r, [tk[0]])
            V(lambda e: e.tensor_tensor(out=t2_, in0=ai, in1=bi, op=ALU.mult), keys_r, [tk[1]])
            V(lambda e: e.tensor_tensor(out=out_re, in0=t1_, in1=t2_, op=ALU.subtract), tk, keys_w[0:1])
            V(lambda e: e.tensor_tensor(out=t1_, in0=ar, in1=bi, op=ALU.mult), keys_r + keys_w[0:1], [tk[0]])
            V(lambda e: e.tensor_tensor(out=t2_, in0=ai, in1=br, op=ALU.mult), keys_r + keys_w[0:1], [tk[1]])
            V(lambda e: e.tensor_tensor(out=out_im, in0=t1_, in1=t2_, op=ALU.add), tk, keys_w[1:2])

        def merge(la, lb):
            out = []
            ia = ib = 0
            na, nb = len(la), len(lb)
            while ia < na or ib < nb:
                if ib >= nb or (ia < na and ia * nb <= ib * na):
                    out.append(la[ia]); ia += 1
                else:
                    out.append(lb[ib]); ib += 1
            return out

        def recA(ti, par):
            u5 = u5_[par]; z5s = z5s_[par]; gbs = gbs_[par]; yaT = yaT_[par]
            smp = (ti == 16)
            lastp = (ti == 15)
            xsrc = I["xs"] if smp else I["xp"][128 * ti:128 * ti + 128, :]
            ydst = O["y_s"] if smp else O["y_p"][128 * ti:128 * ti + 128, :]
            negm = negm_s if smp else negm_p
            rmk = rm_s if smp else rm_p
            nseq, L = (16, 8) if smp else (1, 128)
            TAG[0] = "A.T1"
            DMA(lambda e: e.dma_start(out=xt[:], in_=xsrc), w=["xt"])
            G(lambda e: e.memset(ss[:], 0.0), w=["ss"])
            A(lambda e: e.activation(out=xsn[:], in_=xt[:], func=AF.Square, accum_out=ss[:]), ["xt", "ss"], ["xsn", "ss"])
            A(lambda e: e.activation(out=rstd[:], in_=ss[:], func=AF.Sqrt, scale=1.0 / D, bias=epscol[:]), ["ss", "epscol"], ["rstd"])
            V(lambda e: e.reciprocal(out=rstd[:], in_=rstd[:]), ["rstd"], ["rstd"])
            V(lambda e: e.tensor_scalar(out=xsn[:], in0=xt[:], scalar1=rstd[:, 0:1], scalar2=None, op0=ALU.mult), ["xt", "rstd"], ["xsn"])
            for j in range(8):
                T(lambda e, j=j: e.transpose(out=pbb[5][:, 128 * j:128 * j + 128], in_=xsn[:, 128 * j:128 * j + 128], identity=identb[:]), ["xsn", "identb"], ["pb5"])
            for j in range(8):
                V(lambda e, j=j: e.tensor_scalar(out=hT[:, j, :], in0=pbb[5][:, 128 * j:128 * j + 128], scalar1=normw[:, j:j + 1], scalar2=None, op0=ALU.mult), ["pb5", "normw"], ["hT"])
            TAG[0] = "A.proj"
            blocks = [("xbc", c0, 256) for c0 in range(XBC0, DT0, 256)] + [("dt", DT0, 32)]
            blocks += [("z", c0, 256) for c0 in range(Z0, XBC0, 256)]
            for nm, b0 in (("u5", U0), ("z5", Z50), ("ga", GA0), ("gb", GB0)):
                blocks += [(nm, c0, 256) for c0 in range(b0, b0 + 1024, 256)]
            base = {"xbc": XBC0, "z": Z0, "u5": U0, "z5": Z50, "ga": GA0, "gb": GB0}
            deferred = []
            main_list = CUR[0]
            late_list = []
            for nm, c0, nco in blocks:
                if nm == "u5" and c0 == U0 and SPLITA[0] and not smp:
                    while deferred:
                        deferred.pop(0)()
                    CUR[0] = late_list
                wt, wk = wload(WS["w_in"][:, c0:c0 + nco])
                if nm == "dt":
                    while deferred:
                        deferred.pop(0)()
                    ps, pk = proj_ft(wt, wk, 0, 32)
                    A(lambda e, ps=ps: e.activation(out=dts[:], in_=ps[0:32, :], func=AF.Exp, bias=dtb[:]), [pk, "dtb"], ["dts"])
                    A(lambda e: e.activation(out=dts[:], in_=dts[:], func=AF.Ln, bias=onecol[0:32, :]), ["dts", "onecol"], ["dts"])
                    continue
                for s in range(2):
                    ft = (c0 - base[nm]) // 128 + s
                    ps, pk = proj_ft(wt, wk, 128 * s)
                    if nm == "z":
                        A(lambda e, ps=ps, ft=ft: e.activation(out=zs[:, ft, :], in_=ps, func=AF.Silu), [pk], [f"zs{ft}"])
                    elif nm == "u5":
                        A(lambda e, ps=ps, ft=ft: e.activation(out=u5[:, ft, :], in_=ps, func=AF.Copy), [pk], [f"u5_{par}_{ft}"])
                    elif nm == "z5":
                        A(lambda e, ps=ps, ft=ft: e.activation(out=z5s[:, ft, :], in_=ps, func=AF.Silu), [pk], [f"z5_{par}_{ft}"])
                    elif nm == "ga":
                        A(lambda e, ps=ps, ft=ft: e.activation(out=gas[:, ft, :], in_=ps, func=AF.Sigmoid), [pk], [f"ga{ft}"])
                    elif nm == "gb":
                        A(lambda e, ps=ps, ft=ft: e.activation(out=gbs[:, ft, :], in_=ps, func=AF.Sigmoid), [pk], [f"gb_{par}_{ft}"])
                    else:
                        r_ = raw[ft % 2]; rk = f"raw{ft % 2}"; ac = acc[ft % 2]; ak = f"acc{ft % 2}"
                        EV = V
                        if smp:
                            rv = r_[:, 0:176].rearrange("p (s r) -> p s r", r=11)
                            if ft % 8 == 0:
                                DMA(lambda e, ft=ft: e.dma_start(out=cvo[:], in_=I["conv0"].rearrange("s r c -> (s r) c")[:, 128 * ft:128 * ft + 1024]), w=["cvo"])
                            T(lambda e, ft=ft: e.transpose(out=pb[6][:, 0:48], in_=cvo[:, 128 * (ft % 8):128 * (ft % 8) + 128], identity=identf[0:48, 0:48]), ["cvo", "identf"], ["pb6"])
                            V(lambda e, rv=rv: e.tensor_copy(out=rv[:, :, 0:3], in_=pb[6][:, 0:48].rearrange("p (s r) -> p s r", r=3)), ["pb6"], [rk])
                        else:
                            rv = r_[:, 0:131].unsqueeze(1)
                            A(lambda e, rv=rv, ft=ft: e.activation(out=rv[:, 0, 0:3], in_=halo[:, ft, :], func=AF.Copy), ["halo"], [rk])
                        A(lambda e, rv=rv, ps=ps: e.activation(out=rv[:, :, 3:3 + L], in_=ps.rearrange("p (s l) -> p s l", l=L), func=AF.Copy), [pk], [rk])
                        while deferred:
                            deferred.pop(0)()
                        av = ac[:].rearrange("p (s l) -> p s l", l=L)
                        EV(lambda e, rv=rv, av=av, ft=ft: e.tensor_scalar(out=av, in0=rv[:, :, 0:L], scalar1=convw[:, ft, 0:1], scalar2=convb[:, ft:ft + 1], op0=ALU.mult, op1=ALU.add), [rk, "convw", "convb"], [ak])
                        for k in range(1, 4):
                            EV(lambda e, rv=rv, av=av, ft=ft, k=k: e.scalar_tensor_tensor(out=av, in0=rv[:, :, k:k + L], scalar=convw[:, ft, k:k + 1], in1=av, op0=ALU.mult, op1=ALU.add), [rk, ak, "convw"], [ak])
                        def late(ac=ac, ft=ft, rv=rv, ak=ak, rk=rk):
                            A(lambda e: e.activation(out=xc[:, ft, :], in_=ac[:], func=AF.Silu), [ak], [f"xc{ft}"])
                            if not smp:
                                A(lambda e: e.activation(out=halo[:, ft, :], in_=rv[:, 0, 128:131], func=AF.Copy), [rk], ["halo"])
                        deferred.append(late)
                        if smp or lastp:
                            nr = 3 * nseq
                            EV(lambda e, rv=rv, nr=nr: e.tensor_copy(out=cst3[:, 0:nr].rearrange("p (s r) -> p s r", r=3), in_=rv[:, :, L:L + 3]), [rk], ["cst3"])
                            T(lambda e, nr=nr: e.transpose(out=pb[6][0:nr, 128:256], in_=cst3[:, 0:nr], identity=identf[:]), ["cst3", "identf"], ["pb6"])
                            V(lambda e, ft=ft, nr=nr: e.tensor_copy(out=cvo[0:nr, 128 * (ft % 8):128 * (ft % 8) + 128], in_=pb[6][0:nr, 128:256]), ["pb6"], ["cvo"])
                            if ft % 8 == 7:
                                cdst = (O["conv_s"].rearrange("s r c -> (s r) c") if smp else O["conv_p"])[:, 128 * (ft - 7):128 * (ft - 7) + 1024]
                                DMAO(lambda e, cdst=cdst, nr=nr: e.dma_start(out=cdst, in_=cvo[0:nr, :]), ["cvo"], [])
            ssd_list = []
            if CUR[0] is late_list:
                CUR[0] = ssd_list
            TAG[0] = "A.dt"
            V(lambda e: e.tensor_scalar(out=dAs[:], in0=dts[:], scalar1=aneg[:, 0:1], scalar2=None, op0=ALU.mult), ["dts", "aneg"], ["dAs"])
            V(lambda e: e.tensor_tensor_scan(out=acs[:], data0=rmk[:], data1=dAs[:], initial=0.0, op0=ALU.mult, op1=ALU.add), [kn(rmk), "dAs"], ["acs"])
            a3 = acs[:].rearrange("h (s l) -> h s l", l=L)
            V(lambda e: e.tensor_tensor(out=dec[:].rearrange("h (s l) -> h s l", l=L), in0=a3[:, :, L - 1:L].broadcast_to([32, nseq, L]), in1=a3, op=ALU.subtract), ["acs"], ["dec"])
            A(lambda e: e.activation(out=dec[:], in_=dec[:], func=AF.Exp), ["dec"], ["dec"])
            V(lambda e: e.tensor_tensor(out=wdt[:], in0=dts[:], in1=dec[:], op=ALU.mult), ["dts", "dec"], ["wdt"])
            for i_, src in enumerate((acs, dts, wdt)):
                T(lambda e, i_=i_, src=src: e.transpose(out=pb[6][:, 256 + 32 * i_:256 + 32 * i_ + 32], in_=src[:], identity=identf[0:32, 0:32]), [kn(src), "identf"], ["pb6"])
            V(lambda e: e.tensor_copy(out=tokm[:].rearrange("p a b -> p (a b)"), in_=pb[6][:, 256:352]), ["pb6"], ["tokm"])
            TAG[0] = "A.X"
            for half in range(2):
                bk = f"pb{2 + half}"
                for j in range(8):
                    ft = 8 * half + j
                    T(lambda e, ft=ft, j=j, half=half: e.transpose(out=pbb[2 + half][:, 128 * j:128 * j + 128], in_=xc[:, ft, :], identity=identb[:]), [f"xc{ft}", "identb"], [bk])
                pv = pbb[2 + half][:, :].rearrange("p (a h2 q) -> p a h2 q", h2=2, q=64)
                prs = slice(8 * half, 8 * half + 8)
                dtT = tokm[:, 1, 16 * half:16 * half + 16].rearrange("p (a h2) -> p a h2", h2=2)
                wT = tokm[:, 2, 16 * half:16 * half + 16]
                V(lambda e, pv=pv, prs=prs, dtT=dtT: e.tensor_tensor(out=XA[:, prs, 0:64], in0=pv[:, :, 0, :], in1=dtT[:, :, 0:1].broadcast_to([128, 8, 64]), op=ALU.mult), [bk, "tokm"], ["XA"])
                V(lambda e, pv=pv, prs=prs, dtT=dtT: e.tensor_tensor(out=XB[:, prs, 64:128], in0=pv[:, :, 1, :], in1=dtT[:, :, 1:2].broadcast_to([128, 8, 64]), op=ALU.mult), [bk, "tokm"], ["XB"])
                V(lambda e, half=half, wT=wT: e.tensor_tensor(out=Xd[:, 16 * half:16 * half + 16, :], in0=pbb[2 + half][:, :].rearrange("p (h q) -> p h q", q=64), in1=wT.unsqueeze(2).broadcast_to([128, 16, 64]), op=ALU.mult), [bk, "tokm"], ["Xd"])
            for j in range(8):
                T(lambda e, j=j: e.transpose(out=pbb[6][:, 128 * j:128 * j + 128], in_=xc[:, 16 + j, :], identity=identb[:]), [f"xc{16 + j}", "identb"], ["pb6"])
            A(lambda e: e.activation(out=Btok[:].rearrange("p a b -> p (a b)"), in_=pbb[6][:, :], func=AF.Copy), ["pb6"], ["Btok"])
            def yevac(gq):
                b2 = gq % 2
                yv = yv2[b2]; ysq = ysq2[b2]; rstg = rstg2[b2]
                for j in range(2):
                    pr = 2 * gq + j
                    V(lambda e, pr=pr, j=j, yv=yv: e.scalar_tensor_tensor(out=yv[:, j, :], in0=xc[:, pr, :], scalar=dexp[:, pr:pr + 1], in1=yps(pr), op0=ALU.mult, op1=ALU.add), [f"xc{pr}", "dexp", ypk(pr)], [f"yv{b2}"])
                    V(lambda e, pr=pr, j=j, yv=yv: e.tensor_tensor(out=yv[:, j, :], in0=yv[:, j, :], in1=zs[:, pr, :], op=ALU.mult), [f"yv{b2}", f"zs{pr}"], [f"yv{b2}"])
                    A(lambda e, j=j, yv=yv, ysq=ysq: e.activation(out=ysq[:, j, :], in_=yv[:, j, :], func=AF.Square), [f"yv{b2}"], [f"ysq{b2}"])
                ps = pb[7][:, 128 * b2:128 * b2 + 128]
                pk = "pb7"
                for j in range(2):
                    T(lambda e, ps=ps, j=j, ysq=ysq: e.matmul(ps, lhsT=onesb[:], rhs=ysq[:, j, :], start=(j == 0), stop=(j == 1)), ["onesb", f"ysq{b2}"], [pk])
                A(lambda e, ps=ps, rstg=rstg: e.activation(out=rstg[:], in_=ps, func=AF.Sqrt, scale=1.0 / 256, bias=epscol[:]), [pk, "epscol"], [f"rstg{b2}"])
                V(lambda e, rstg=rstg: e.reciprocal(out=rstg[:], in_=rstg[:]), [f"rstg{b2}"], [f"rstg{b2}"])
                for j in range(2):
                    pr = 2 * gq + j
                    V(lambda e, pr=pr, j=j, yv=yv, rstg=rstg: e.scalar_tensor_tensor(out=ynT[:, pr, :], in0=yv[:, j, :], scalar=ssdnw[:, pr:pr + 1], in1=rstg[:], op0=ALU.mult, op1=ALU.mult), [f"yv{b2}", "ssdnw", f"rstg{b2}"], [f"ynT{pr}"])
            TAG[0] = "A.grp"
            it = 0
            for hf in ((0, 1) if smp else (None,)):
                grp_range = range(8) if hf is None else range(4 * hf, 4 * hf + 4)
                q4_range = range(1) if hf is None else range(2 * hf, 2 * hf + 2)
                if smp:
                    ypk = lambda pr: f"pb{2 + (pr % 8) // 4}"
                    yps = lambda pr: pb[2 + (pr % 8) // 4][:, 128 * (pr % 4):128 * (pr % 4) + 128]
                    yfirst = lambda pr: pr % 4 == 0
                else:
                    ypk = lambda pr: f"pb{2 + (pr // 2) % 2}"
                    yps = lambda pr: pb[2 + (pr // 2) % 2][:, 128 * (pr % 2):128 * (pr % 2) + 128]
                    yfirst = lambda pr: pr % 2 == 0
                def stage1(g):
                    Eg = Egs[g % 2]; Lg = Lgs[g % 2]; cbk = g % 2
                    V(lambda e, g=g: e.tensor_tensor(out=rhsR[:], in0=acs[:].unsqueeze(1).broadcast_to([32, 4, 128]), in1=identf[0:32, 4 * g:4 * g + 4].unsqueeze(2).broadcast_to([32, 4, 128]), op=ALU.mult), ["acs", "identf"], ["rhsR"])
                    T(lambda e: e.matmul(pb[6][:, :], lhsT=ones32[:], rhs=rhsR[:].rearrange("p a b -> p (a b)"), start=True, stop=True), ["ones32", "rhsR"], ["pb6"])
                    A(lambda e, Eg=Eg: e.activation(out=Eg[:].rearrange("p a b -> p (a b)"), in_=pb[6][:, :], func=AF.Exp), ["pb6"], ["Eg0"])
                    G(lambda e, g=g, Eg=Eg: e.tensor_tensor(out=CEall[:, 4 * g:4 * g + 4, :], in0=Eg[:], in1=xc[:, 24 + g, :].unsqueeze(1).broadcast_to([128, 4, 128]), op=ALU.mult), ["Eg0", f"xc{24 + g}"], ["CEall"])
                    for h4 in range(4):
                        V(lambda e, g=g, h4=h4: e.scalar_tensor_tensor(out=Dm[:, h4, :], in0=pb[6][:, 128 * h4:128 * h4 + 128], scalar=tokm[:, 0, 4 * g + h4:4 * g + h4 + 1], in1=negm[:], op0=ALU.subtract, op1=ALU.min), ["pb6", "tokm", kn(negm)], ["Dm"])
                    A(lambda e, Lg=Lg: e.activation(out=Lg[:].rearrange("p a b -> p (a b)"), in_=Dm[:].rearrange("p a b -> p (a b)"), func=AF.Exp), ["Dm"], [f"Lg{g % 2}"])
                    T(lambda e, g=g: e.matmul(pb[7][:, 256 + 128 * (g % 2):256 + 128 * (g % 2) + 128], lhsT=xc[:, 16 + g, :], rhs=xc[:, 24 + g, :], start=True, stop=True), [f"xc{16 + g}", f"xc{24 + g}"], ["pb7"])
                    V(lambda e, g=g, Eg=Eg: e.tensor_copy(out=dch[:, 4 * g:4 * g + 4, 0:nseq], in_=Eg[:].rearrange("p a (s l) -> p a s l", l=L)[:, :, :, L - 1]), ["Eg0"], ["dch"])
                def stage2(g):
                    Eg = Egs[g % 2]; Lg = Lgs[g % 2]; cbk = g % 2
                    V(lambda e, Lg=Lg, cbk=cbk: e.tensor_tensor(out=Mg[:], in0=Lg[:], in1=pb[7][:, 256 + 128 * cbk:256 + 128 * cbk + 128].unsqueeze(1).broadcast_to([128, 4, 128]), op=ALU.mult), [f"Lg{g % 2}", "pb7"], ["Mg"])
                    for j in range(2):
                        pr = 2 * g + j
                        T(lambda e, pr=pr, j=j: e.matmul(yps(pr), lhsT=XA[:, pr, :], rhs=Mg[:, 2 * j, :], start=yfirst(pr), stop=False, skip_group_check=True), ["XA", "Mg"], [ypk(pr)])
                        T(lambda e, pr=pr, j=j: e.matmul(yps(pr), lhsT=XB[:, pr, :], rhs=Mg[:, 2 * j + 1, :], start=False, stop=False, skip_group_check=True), ["XB", "Mg"], [ypk(pr)])
                    if not smp:
                        sak_ = [f"SA{i}" for i in range(4)] + [f"SB{i}" for i in range(4)]
                        for j in range(2):
                            pr = 2 * g + j
                            T(lambda e, pr=pr: e.matmul(yps(pr), lhsT=SA[:, pr, :], rhs=CEall[:, 2 * pr, :], start=False, stop=False, skip_group_check=True), sak_ + ["CEall"], [ypk(pr)])
                            T(lambda e, pr=pr, j=j: e.matmul(yps(pr), lhsT=SB[:, pr, :], rhs=CEall[:, 2 * pr + 1, :], start=False, stop=(j == 1), skip_group_check=True), sak_ + ["CEall"], [ypk(pr)])
                        yevac(g)

                grps_ = list(grp_range)
                stage1(grps_[0])
                for i_g, g_ in enumerate(grps_):
                    if i_g + 1 < len(grps_):
                        stage1(grps_[i_g + 1])
                    stage2(g_)
                if smp:
                    dchP = yv2[0][:].rearrange("p a b -> p (a b)").rearrange("p (pr s) -> p pr s", s=16)
                    dv_ = dch[:, 16 * hf:16 * hf + 16, :].rearrange("p (pr h2) s -> p pr h2 s", h2=2)
                    for h2 in range(2):
                        hs = slice(64 * h2, 64 * h2 + 64)
                        V(lambda e, hs=hs, h2=h2, dv_=dv_, hf=hf, dchP=dchP: e.tensor_copy(out=dchP[hs, 8 * hf:8 * hf + 8, :], in_=dv_[hs, :, h2, :]), ["dch"], ["yv0"])
                for s in range(nseq if smp else 0):
                    cs = slice(L * s, L * s + L)
                    last = (s == nseq - 1)
                    if smp:
                        Bm = Bms[0]; bmk = "Bm0"
                        G(lambda e, s=s, Bm=Bm: e.tensor_scalar(out=Bm[:].rearrange("p a b -> p (a b)"), in0=Btok[:].rearrange("p a b -> p (a b)"), scalar1=seqmask[:, s:s + 1], scalar2=None, op0=ALU.mult), ["Btok", "seqmask"], [bmk])
                    for q4 in q4_range:
                        prs_ = range(4 * q4, 4 * q4 + 4) if smp else range(16)
                        sak = [f"SA{q4}", f"SB{q4}"] if smp else [f"SA{i}" for i in range(4)] + [f"SB{i}" for i in range(4)]
                        if smp:
                            S0q = S0qs[it % 2]; s0k = f"S0q{it % 2}"; Snq = Snqs[it % 2]; snk = f"Snq{it % 2}"
                            tb_ = 7 if it % 2 == 0 else 0
                            nb_ = 6 if it % 2 == 0 else 1
                            it += 1
                            sview = I["ssm0"][s].rearrange("(pr h2) p n -> (h2 p) pr n", h2=2)[:, 4 * q4:4 * q4 + 4, :]
                            DMA(lambda e, sview=sview, S0q=S0q: e.dma_start(out=S0q[:], in_=sview), w=[s0k])
                            for j in range(4):
                                T(lambda e, j=j, S0q=S0q, tb_=tb_: e.transpose(out=pb[tb_][:, 128 * j:128 * j + 128], in_=S0q[:, j, :], identity=identf[:]), [s0k, "identf"], [f"pb{tb_}"])
                            pv = pb[tb_][:, :].rearrange("p (a h2 q) -> p a h2 q", h2=2, q=64)
                            A(lambda e, q4=q4, pv=pv: e.activation(out=SA[:, 4 * q4:4 * q4 + 4, 0:64], in_=pv[:, :, 0, :], func=AF.Copy), [f"pb{tb_}"], [f"SA{q4}"])
                            A(lambda e, q4=q4, pv=pv: e.activation(out=SB[:, 4 * q4:4 * q4 + 4, 64:128], in_=pv[:, :, 1, :], func=AF.Copy), [f"pb{tb_}"], [f"SB{q4}"])
                        for pr in prs_:
                            T(lambda e, pr=pr, cs=cs: e.matmul(yps(pr)[:, cs], lhsT=SA[:, pr, :], rhs=CEall[:, 2 * pr, cs], start=False, stop=False, skip_group_check=True), sak + ["CEall"], [ypk(pr)])
                            T(lambda e, pr=pr, cs=cs, last=last: e.matmul(yps(pr)[:, cs], lhsT=SB[:, pr, :], rhs=CEall[:, 2 * pr + 1, cs], start=False, stop=last, skip_group_check=True), sak + ["CEall"], [ypk(pr)])
                        if smp:
                            for j in range(4):
                                pr = 4 * q4 + j
                                T(lambda e, pr=pr, j=j, Bm=Bm, nb_=nb_: e.matmul(pb[nb_][:, 128 * j:128 * j + 128], lhsT=Xd[:, 2 * pr:2 * pr + 2, :].rearrange("p a b -> p (a b)"), rhs=Bm[:, pr // 2, :], start=True, stop=True), ["Xd", bmk], [f"pb{nb_}"])
                            for j in range(4):
                                pr = 4 * q4 + j
                                V(lambda e, pr=pr, j=j, s=s, S0q=S0q, Snq=Snq, nb_=nb_, dchP=dchP: e.scalar_tensor_tensor(out=Snq[:, j, :], in0=S0q[:, j, :], scalar=dchP[:, pr, s:s + 1], in1=pb[nb_][:, 128 * j:128 * j + 128], op0=ALU.mult, op1=ALU.add), [s0k, "yv0", f"pb{nb_}"], [snk])
                            oview = O["ssm_s"][s].rearrange("(pr h2) p n -> (h2 p) pr n", h2=2)[:, 4 * q4:4 * q4 + 4, :]
                            DMAO(lambda e, oview=oview, Snq=Snq: e.dma_start(out=oview, in_=Snq[:]), [snk], [])
                if smp:
                    for gq in grp_range:
                        yevac(gq)
            if smp:
                sv = STt[:].rearrange("p (pr h2) q -> p pr h2 q", h2=2)
                A(lambda e, sv=sv: e.activation(out=SA[:, :, 0:64], in_=sv[:, :, 0, :], func=AF.Copy), ["STt"], [f"SA{i}" for i in range(4)])
                A(lambda e, sv=sv: e.activation(out=SB[:, :, 64:128], in_=sv[:, :, 1, :], func=AF.Copy), ["STt"], [f"SB{i}" for i in range(4)])
            TAG[0] = "A.state"
            if not smp:
                V(lambda e: e.tensor_tensor(out=STt[:], in0=STt[:], in1=dch[:, :, 0:1].broadcast_to([128, 32, 64]), op=ALU.mult), ["STt", "dch"], ["STt"])
                for rnd in range(2):
                    for g in range(4 * rnd, 4 * rnd + 4):
                        bk_ = 2 + (g % 4) // 2
                        T(lambda e, g=g, bk_=bk_: e.matmul(pb[bk_][:, 256 * (g % 2):256 * (g % 2) + 256], lhsT=Btok[:, g, :], rhs=Xd[:, 4 * g:4 * g + 4, :].rearrange("p a b -> p (a b)"), start=True, stop=True), ["Btok", "Xd"], [f"pb{bk_}"])
                    for b2_ in range(2):
                        h0 = 16 * rnd + 8 * b2_
                        V(lambda e, h0=h0, b2_=b2_: e.tensor_tensor(out=STt[:, h0:h0 + 8, :], in0=STt[:, h0:h0 + 8, :], in1=pb[2 + b2_][:, :].rearrange("p (h q) -> p h q", q=64), op=ALU.add), ["STt", f"pb{2 + b2_}"], ["STt"])
                sv = STt[:].rearrange("p (pr h2) q -> p pr h2 q", h2=2)
                A(lambda e, sv=sv: e.activation(out=SA[:, :, 0:64], in_=sv[:, :, 0, :], func=AF.Copy), ["STt"], [f"SA{i}" for i in range(4)])
                A(lambda e, sv=sv: e.activation(out=SB[:, :, 64:128], in_=sv[:, :, 1, :], func=AF.Copy), ["STt"], [f"SB{i}" for i in range(4)])
                if lastp:
                    for q4 in range(4):
                        for j in range(4):
                            pr = 4 * q4 + j
                            T(lambda e, pr=pr, j=j: e.transpose(out=pb[6][:, 128 * j:128 * j + 128], in_=STt[:, 2 * pr:2 * pr + 2, :].rearrange("p a b -> p (a b)"), identity=identf[:]), ["STt", "identf"], ["pb6"])
                        Snq = Snqs[q4 % 2]; snk = f"Snq{q4 % 2}"
                        V(lambda e, Snq=Snq: e.tensor_copy(out=Snq[:].rearrange("p a b -> p (a b)"), in_=pb[6][:, :]), ["pb6"], [snk])
                        oview = O["ssm_p"].rearrange("(pr h2) p n -> (h2 p) pr n", h2=2)[:, 4 * q4:4 * q4 + 4, :]
                        DMAO(lambda e, oview=oview, Snq=Snq: e.dma_start(out=oview, in_=Snq[:]), [snk], [])
            if CUR[0] is ssd_list:
                main_list.extend(merge(late_list, ssd_list))
                CUR[0] = main_list
            TAG[0] = "A.dense"
            dense_T("w_down_ssd", 16, lambda k: ynT[:, k, :], lambda k: [f"ynT{k}"],
                    lambda m, ps, pk: V(lambda e: e.tensor_tensor(out=yaT[:, m, :], in0=ps, in1=gas[:, m, :], op=ALU.mult), [pk, f"ga{m}"], [f"yaT_{par}_{m}"]))


        def recB(ti, par):
            u5 = u5_[par]; z5s = z5s_[par]; gbs = gbs_[par]; yaT = yaT_[par]
            smp = (ti == 16)
            lastp = (ti == 15)
            xsrc = I["xs"] if smp else I["xp"][128 * ti:128 * ti + 128, :]
            ydst = O["y_s"] if smp else O["y_p"][128 * ti:128 * ti + 128, :]
            if smp:
                for ri, nm in enumerate(["re0", "im0"]):
                    sv_ = I[nm].rearrange("s (gh gq g2) p -> s gq gh (g2 p)", gq=8, g2=2)
                    for s in range(16):
                        DMA(lambda e, s=s, sv_=sv_: e.dma_start(out=s5io[8 * s:8 * s + 8, :, :, :].rearrange("p a b c -> p a (b c)"), in_=sv_[s]), w=["s5io"])
                    for gh in range(4):
                        T(lambda e, gh=gh: e.transpose(out=pb[5][:, 0:128], in_=s5io[:, gh, :, :].rearrange("p a b -> p (a b)"), identity=identf[:]), ["s5io", "identf"], ["pb5"])
                        V(lambda e, gh=gh, ri=ri: e.tensor_copy(out=s0all[:, 8 * gh:8 * gh + 8, ri, :], in_=pb[5][:, 0:128].rearrange("p (s g) -> p g s", g=8)), ["pb5"], ["s0all"])
            TAG[0] = "B.s5"
            nsp, Lx = (1, 128) if smp else (2, 64)
            Ls = 8 if smp else 64

            def vw(t_, r=None):
                v = t_[:].rearrange("p a b c -> p (a b c)").rearrange("p (r s q t) -> p r s q t", r=2, s=nsp, q=4)
                return v if r is None else v[:, r]

            def pv_(t_):
                return t_[:].rearrange("p a b -> p (a b)").rearrange("p (s q t) -> p s q t", s=nsp, q=4)

            def sub(ap):
                return ap.rearrange("p s q (n l) -> p s q n l", l=Ls) if smp else ap
            for ft in range(8):
                gsl = slice(4 * ft, 4 * ft + 4)
                if smp:
                    ecv = ecos[:, gsl, 0:Ls].unsqueeze(1).unsqueeze(3).broadcast_to([128, 1, 4, 16, Ls])
                    esv = esin[:, gsl, 0:Ls].unsqueeze(1).unsqueeze(3).broadcast_to([128, 1, 4, 16, Ls])
                else:
                    ecv = ecos[:, gsl, 0:Ls].unsqueeze(1).broadcast_to([128, 2, 4, Ls])
                    esv = esin[:, gsl, 0:Ls].unsqueeze(1).broadcast_to([128, 2, 4, Ls])
                for q in range(4):
                    A(lambda e, ft=ft, q=q: e.activation(out=u5m[:, q, :], in_=u5[:, ft, :], func=AF.Copy, scale=pairmask[:, q:q + 1]), [f"u5_{par}_{ft}", "pairmask"], ["u5m"])
                for ri in range(2):
                    pbk = f"pb{4 + ri}"
                    for q in range(4):
                        T(lambda e, ft=ft, q=q, ri=ri: e.matmul(pb[4 + ri][:, 128 * q:128 * q + 128], lhsT=wb5[:, ft, ri, :], rhs=u5m[:, q, :], start=True, stop=True), ["wb5", "u5m"], [pbk])
                    A(lambda e, ri=ri: e.activation(out=vw(bu, ri), in_=pb[4 + ri][:, :].rearrange("p (q s t) -> p s q t", s=nsp, t=Lx), func=AF.Copy), [pbk], ["bu"])
                bre_, bim_ = sub(vw(bu, 0)), sub(vw(bu, 1))
                V(lambda e, ecv=ecv, bre_=bre_: e.tensor_tensor(out=sub(pv_(bt1)), in0=ecv, in1=bre_, op=ALU.mult), ["ecos", "bu"], ["bt1"])
                V(lambda e, esv=esv, bim_=bim_: e.tensor_tensor(out=sub(pv_(bt2)), in0=esv, in1=bim_, op=ALU.mult), ["esin", "bu"], ["bt2"])
                V(lambda e: e.tensor_tensor(out=vw(sgm, 0), in0=pv_(bt1), in1=pv_(bt2), op=ALU.add), ["bt1", "bt2"], ["sgm"])
                V(lambda e, ecv=ecv, bim_=bim_: e.tensor_tensor(out=sub(pv_(bt1)), in0=ecv, in1=bim_, op=ALU.mult), ["ecos", "bu", "sgm"], ["bt1"])
                V(lambda e, esv=esv, bre_=bre_: e.tensor_tensor(out=sub(pv_(bt2)), in0=esv, in1=bre_, op=ALU.mult), ["esin", "bu", "sgm"], ["bt2"])
                V(lambda e: e.tensor_tensor(out=vw(sgm, 1), in0=pv_(bt1), in1=pv_(bt2), op=ALU.subtract), ["bt1", "bt2"], ["sgm"])
                rtv = rts[:].rearrange("p a b -> p (a b)")[:, 0:4 * Lx]
                if smp:
                    V(lambda e, gsl=gsl: e.tensor_tensor(out=rts[:], in0=rdec[:, gsl].unsqueeze(2).broadcast_to([128, 4, 128]), in1=rm128[:].unsqueeze(1).broadcast_to([128, 4, 128]), op=ALU.mult), ["rdec", "rm128"], ["rts"])
                else:
                    V(lambda e, gsl=gsl, rtv=rtv: e.tensor_tensor(out=rtv.rearrange("p (q t) -> p q t", t=64), in0=rdec[:, gsl].unsqueeze(2).broadcast_to([128, 4, 64]), in1=rmq[:].rearrange("p (q t) -> p q t", t=64), op=ALU.mult), ["rdec", "rmq"], ["rts"])
                flat = lambda ap: ap.rearrange("p q t -> p (q t)")
                if smp:
                    ab = lambda t: t[:, gsl].unsqueeze(2).broadcast_to([128, 4, 16])
                    cmul(cin_[:, :, 0, :], cin_[:, :, 1, :], ab(are), ab(aim), s0all[:, gsl, 0, :], s0all[:, gsl, 1, :],
                         ["are", "aim", "s0all"], ["cinr", "cini"], ctm[:, :, 0, :], ctm[:, :, 1, :], ["ctm0", "ctm1"])
                    for ri in range(2):
                        tgt = sub(vw(sgm, ri))[:, 0, :, :, 0]
                        V(lambda e, tgt=tgt, ri=ri: e.tensor_tensor(out=tgt, in0=tgt, in1=cin_[:, :, ri, :], op=ALU.add), ["sgm", "cinr", "cini"], ["sgm"])
                    for ri in range(2):
                        V(lambda e, ri=ri, rtv=rtv: e.tensor_tensor_scan(out=flat(vw(bu, ri)[:, 0]), data0=rtv, data1=flat(vw(sgm, ri)[:, 0]), initial=0.0, op0=ALU.mult, op1=ALU.add), ["rts", "sgm"], ["bu"])
                else:
                    for sg in range(2):
                        src_re = sgc[:, gsl, 0:1] if sg == 0 else vw(bu, 0)[:, 0, :, 63:64]
                        src_im = sgc[:, gsl, 1:2] if sg == 0 else vw(bu, 1)[:, 0, :, 63:64]
                        ab = lambda t: t[:, gsl].unsqueeze(2)
                        cmul(cin_[:, :, 0, 0:1], cin_[:, :, 1, 0:1], ab(aere), ab(aeim), src_re, src_im,
                             ["aere", "aeim", "sgc", "bu"], ["cinr", "cini"], ctm[:, :, 0, 0:1], ctm[:, :, 1, 0:1], ["ctm0", "ctm1"])
                        for ri in range(2):
                            tgt = vw(sgm, ri)[:, sg, :, 0:1]
                            V(lambda e, tgt=tgt, ri=ri: e.tensor_tensor(out=tgt, in0=tgt, in1=cin_[:, :, ri, 0:1], op=ALU.add), ["sgm", "cinr", "cini"], ["sgm"])
                        for ri in range(2):
                            V(lambda e, ri=ri, sg=sg, rtv=rtv: e.tensor_tensor_scan(out=flat(vw(bu, ri)[:, sg]), data0=rtv, data1=flat(vw(sgm, ri)[:, sg]), initial=0.0, op0=ALU.mult, op1=ALU.add), ["rts", "sgm"], ["bu"])
                    V(lambda e, gsl=gsl: e.tensor_copy(out=sgc[:, gsl, :], in_=vw(bu)[:, :, 1, :, 63].rearrange("p r q -> p q r")), ["bu"], ["sgc"])
                sre_, sim_ = sub(vw(bu, 0)), sub(vw(bu, 1))
                s5v = lambda r: s5s[:, :, r, :].rearrange("p q (s t) -> p s q t", s=nsp)
                V(lambda e, ecv=ecv, sre_=sre_: e.tensor_tensor(out=sub(pv_(bt1)), in0=ecv, in1=sre_, op=ALU.mult), ["ecos", "bu"], ["bt1"])
                V(lambda e, esv=esv, sim_=sim_: e.tensor_tensor(out=sub(pv_(bt2)), in0=esv, in1=sim_, op=ALU.mult), ["esin", "bu"], ["bt2"])
                V(lambda e: e.tensor_tensor(out=s5v(0), in0=pv_(bt1), in1=pv_(bt2), op=ALU.subtract), ["bt1", "bt2"], ["s5s"])
                if smp or lastp:
                    fsrc = lambda t: (sub(pv_(t))[:, 0, :, :, Ls - 1] if smp else pv_(t)[:, 1, :, 63:64])
                    nf = 16 if smp else 1
                    V(lambda e, gsl=gsl, fsrc=fsrc, nf=nf: e.tensor_tensor(out=fin[:, gsl, 0, 0:nf], in0=fsrc(bt1), in1=fsrc(bt2), op=ALU.subtract), ["bt1", "bt2"], ["fin"])
                V(lambda e, ecv=ecv, sim_=sim_: e.tensor_tensor(out=sub(pv_(bt1)), in0=ecv, in1=sim_, op=ALU.mult), ["ecos", "bu", "s5s", "fin"], ["bt1"])
                V(lambda e, esv=esv, sre_=sre_: e.tensor_tensor(out=sub(pv_(bt2)), in0=esv, in1=sre_, op=ALU.mult), ["esin", "bu", "s5s", "fin"], ["bt2"])
                V(lambda e: e.tensor_tensor(out=s5v(1), in0=pv_(bt1), in1=pv_(bt2), op=ALU.add), ["bt1", "bt2"], ["s5s"])
                if smp or lastp:
                    V(lambda e, gsl=gsl, fsrc=fsrc, nf=nf: e.tensor_tensor(out=fin[:, gsl, 1, 0:nf], in0=fsrc(bt1), in1=fsrc(bt2), op=ALU.add), ["bt1", "bt2"], ["fin"])
                ps4 = [pb[4][:, 128 * q:128 * q + 128] for q in range(4)]
                for q in range(4):
                    for ri in range(2):
                        T(lambda e, ft=ft, q=q, ri=ri: e.matmul(ps4[q], lhsT=wd5[:, ft, ri, :], rhs=s5s[:, q, ri, :], start=(ri == 0), stop=(ri == 1)), ["wd5", "s5s"], ["pb4"])
                for q in range(4):
                    rs_ = slice(32 * q, 32 * q + 32)
                    V(lambda e, ft=ft, q=q, rs_=rs_: e.scalar_tensor_tensor(out=y5t[rs_, :], in0=u5[rs_, ft, :], scalar=ds5[rs_, ft:ft + 1], in1=ps4[q][rs_, :], op0=ALU.mult, op1=ALU.add), [f"u5_{par}_{ft}", "ds5", "pb4"], ["y5t"])
                A(lambda e, ft=ft: e.activation(out=y5g[:, ft, :], in_=y5t[:], func=AF.Gelu_apprx_tanh), ["y5t"], [f"y5g{ft}"])
            if lastp:
                for ri, nm in enumerate(["re_p", "im_p"]):
                    DMAO(lambda e, ri=ri, nm=nm: e.dma_start(out=O[nm].rearrange("(gp g2) p -> (g2 p) gp", g2=2), in_=fin[:, :, ri, 0], allow_slow_non_contiguous=True), ["fin"], [])
            if smp:
                for ri, nm in enumerate(["re_s", "im_s"]):
                    for gh in range(4):
                        V(lambda e, gh=gh, ri=ri: e.tensor_copy(out=s5tr[:], in_=fin[:, 8 * gh:8 * gh + 8, ri, :].rearrange("p g s -> p s g")), ["fin"], ["s5tr"])
                        T(lambda e: e.transpose(out=pb[5][:, 0:128], in_=s5tr[:].rearrange("p a b -> p (a b)"), identity=identf[:]), ["s5tr", "identf"], ["pb5"])
                        V(lambda e, gh=gh: e.tensor_copy(out=s5io[:, gh, :, :].rearrange("p a b -> p (a b)"), in_=pb[5][:, 0:128]), ["pb5"], ["s5io"])
                    ov = O[nm].rearrange("s (gh gq g2) p -> s gq gh (g2 p)", gq=8, g2=2)
                    for s in range(16):
                        DMAO(lambda e, s=s, ov=ov: e.dma_start(out=ov[s], in_=s5io[8 * s:8 * s + 8, :, :, :].rearrange("p a b c -> p a (b c)")), ["s5io"], [])

            TAG[0] = "B.glu"
            def glu_evac(m, ps, pk):
                A(lambda e: e.activation(out=glus[:], in_=ps, func=AF.Sigmoid, bias=bglu[:, m:m + 1]), [pk, "bglu"], ["glus"])
                V(lambda e: e.tensor_tensor(out=glus[:], in0=glus[:], in1=y5g[:, m, :], op=ALU.mult), ["glus", f"y5g{m}"], ["glus"])
                V(lambda e: e.tensor_tensor(out=y5f[:, m, :], in0=glus[:], in1=z5s[:, m, :], op=ALU.mult), ["glus", f"z5_{par}_{m}"], [f"y5f{m}"])
            dense_T("w_glu", 8, lambda k: y5g[:, k, :], lambda k: [f"y5g{k}"], glu_evac)
            dense_T("w_down_s5", 8, lambda k: y5f[:, k, :], lambda k: [f"y5f{k}"],
                    lambda m, ps, pk: V(lambda e: e.tensor_tensor(out=ybT[:, m, :], in0=ps, in1=gbs[:, m, :], op=ALU.mult), [pk, f"gb_{par}_{m}"], [f"ybT{m}"]))
            for m in range(8):
                V(lambda e, m=m: e.tensor_tensor(out=mixT[:, m, :], in0=yaT[:, m, :], in1=ybT[:, m, :], op=ALU.add), [f"yaT_{par}_{m}", f"ybT{m}"], [f"mixT{m}"])
            TAG[0] = "B.out"
            DMA(lambda e: e.dma_start(out=ot[:], in_=xsrc), w=["ot"])
            for cb in range(4):
                wt, wk = wload(WS["w_out"][:, 256 * cb:256 * cb + 256])
                hb = cb % 2
                ps = pb[4 + hb][:, 0:256]
                pks = [f"pb{4 + hb}"]
                for k in range(8):
                    T(lambda e, wt=wt, k=k, ps=ps: e.matmul(ps, lhsT=mixT[:, k, :], rhs=wt[:, k, 0:256], start=(k == 0), stop=(k == 7)), [wk, f"mixT{k}"], pks)
                V(lambda e, cb=cb, ps=ps: e.tensor_tensor(out=ot[:, 256 * cb:256 * cb + 256], in0=ps, in1=ot[:, 256 * cb:256 * cb + 256], op=ALU.add), pks + ["ot"], ["ot"])
            G(lambda e: e.memset(ssB[:], 0.0), w=["ssB"])
            A(lambda e: e.activation(out=y5f[:].rearrange("p a b -> p (a b)"), in_=ot[:], func=AF.Square, accum_out=ssB[:]), ["ot", "ssB"], [f"y5f{m}" for m in range(8)] + ["ssB"])
            A(lambda e: e.activation(out=rstdB[:], in_=ssB[:], func=AF.Sqrt, scale=1.0 / D, bias=epscol[:]), ["ssB", "epscol"], ["rstdB"])
            V(lambda e: e.reciprocal(out=rstdB[:], in_=rstdB[:]), ["rstdB"], ["rstdB"])
            V(lambda e: e.scalar_tensor_tensor(out=ot[:], in0=ot[:], scalar=rstdB[:, 0:1], in1=fnw[:], op0=ALU.mult, op1=ALU.mult), ["ot", "rstdB", "fnw"], ["ot"])
            DMAO(lambda e: e.dma_start(out=ydst, in_=ot[:]), ["ot"], [])


        def flush(lst):
            for (eng, fn, r, w, dma, tag) in lst:
                P.op(eng, fn, r, w, dma=dma)
                P.ops[-1]["tag"] = tag

        def record(fn_, ti, par, banks, wset):
            CUR[0] = []
            MMB[0] = banks
            WSET[0] = wset
            fn_(ti, par)
            lst = CUR[0]
            CUR[0] = None
            return lst

        tl = list(tiles) if tiles is not None else (list(range(8)) + [16] + list(range(8, 16)))
        if tl:
            flush(record(recA, tl[0], 0, (0, 1), (0, 1)))
            for n_, ti in enumerate(tl):
                lb = record(recB, ti, n_ % 2, (4, 5), (2, 3))
                if n_ + 1 < len(tl):
                    la = record(recA, tl[n_ + 1], (n_ + 1) % 2, (0, 1), (0, 1))
                    if not INTERLEAVE[0]:
                        flush(lb); flush(la)
                    else:
                        flush(merge(la, lb))
                else:
                    flush(lb)
        P.emit(nc)
    return nc


_NC = {}


def _shard(inputs, c):
    d = {}
    d["xp"] = np.ascontiguousarray(inputs["x_prompt"][c])
    d["xs"] = np.ascontiguousarray(inputs["x_sample"][16 * c:16 * c + 16].reshape(128, 1024))
    d["conv0"] = np.ascontiguousarray(inputs["state_conv"][0, 16 * c:16 * c + 16])
    d["ssm0"] = np.ascontiguousarray(inputs["state_ssm"][0, 16 * c:16 * c + 16])
    d["re0"] = np.ascontiguousarray(inputs["state_s5_re"][0, 16 * c:16 * c + 16])
    d["im0"] = np.ascontiguousarray(inputs["state_s5_im"][0, 16 * c:16 * c + 16])
    for k in ("norm_w", "w_in", "conv_w", "conv_b", "dt_bias", "A_log", "D_ssd", "ssd_norm_w", "w_down_ssd", "lam_re",
              "lam_im", "log_dt", "B_re", "B_im", "C_re", "C_im", "D_s5", "w_glu", "b_glu", "w_down_s5", "w_out"):
        d[k] = np.ascontiguousarray(inputs[k][0])
    d["final_norm_w"] = np.ascontiguousarray(inputs["final_norm_w"])
    return d


def kernel(**inputs):
    inputs = {k: np.asarray(v, dtype=np.float32) for k, v in inputs.items()}
    if "nc" not in _NC:
        _NC["nc"] = build()
    nc = _NC["nc"]
    consts = host_consts()
    in_maps = []
    for c in range(NCORES):
        d = _shard(inputs, c)
        d.update(consts)
        in_maps.append(d)
    res = run_bass_kernel_spmd(nc, in_maps, core_ids=list(range(NCORES))).results
    cat = lambda n: np.concatenate([r[n] for r in res], axis=0)
    y_p = np.stack([r["y_p"] for r in res], 0)
    y_s = cat("y_s").reshape(128, 8, 1024)
    conv_p = np.stack([r["conv_p"] for r in res], 0)[None]
    ssm_p = np.stack([r["ssm_p"] for r in res], 0)[None]
    re_p = np.stack([r["re_p"] for r in res], 0)[None]
    im_p = np.stack([r["im_p"] for r in res], 0)[None]
    conv_s = cat("conv_s")[None]
    ssm_s = cat("ssm_s")[None]
    re_s = cat("re_s")[None]
    im_s = cat("im_s")[None]
    return tuple(np.ascontiguousarray(a, dtype=np.float32) for a in
                 (y_p, y_s, conv_p, ssm_p, re_p, im_p, conv_s, ssm_s, re_s, im_s))
```

```python
import contextlib
import math
import numpy as np
import concourse.bass as bass
import concourse.mybir as mybir
from concourse.bass_utils import run_bass_kernel_spmd

F32 = mybir.dt.float32
BF16 = mybir.dt.bfloat16
I32 = mybir.dt.int32
ALU = mybir.AluOpType
AF = mybir.ActivationFunctionType

ENGS = ("pe", "act", "dve", "pool", "sp")
N_DMA_SEM = 24
NCORES = 8
D = 1024
PROJ = 10272
Z0, XBC0, DT0, U0, Z50, GA0, GB0 = 0, 2048, 6144, 6176, 7200, 8224, 9248
EPS = 1e-6
SIN_SCALE = 6.283185


class Prog:
    def __init__(self):
        self.ops = []
        self.last_write = {}
        self.readers = {}
        self.dma_count = {e: 0 for e in ENGS}
        self.slot_last = {}
        self.last_op = {}
        self.pending = {}

    def barrier(self):
        deps = set(self.last_op.values()) | set(self.slot_last.values())
        for e in ENGS:
            self.pending[e] = set(deps) | self.pending.get(e, set())

    def op(self, eng, fn, reads=(), writes=(), dma=False):
        writes = list(writes) + [k for k in reads if k.startswith("pb")]
        reads = [k for k in reads if not k.startswith("pb")]
        oid = len(self.ops)
        deps = set()
        for k in reads:
            if k in self.last_write:
                deps.add(self.last_write[k])
        for k in writes:
            if k in self.last_write:
                deps.add(self.last_write[k])
            deps.update(self.readers.get(k, ()))
        if eng in self.pending:
            deps |= self.pending.pop(eng)
        rec = dict(eng=eng, fn=fn, deps=deps, dma=dma, sig=False, seq=None)
        if dma:
            i = self.dma_count[eng]
            self.dma_count[eng] += 1
            slot = i % N_DMA_SEM
            rec["slot"] = slot
            rec["val"] = 16 * (i // N_DMA_SEM + 1)
            prev = self.slot_last.get((eng, slot))
            if prev is not None:
                deps.add(prev)
            self.slot_last[(eng, slot)] = oid
        else:
            self.last_op[eng] = oid
        deps.discard(oid)
        self.ops.append(rec)
        for k in writes:
            self.last_write[k] = oid
            self.readers[k] = []
        for k in reads:
            self.readers.setdefault(k, []).append(oid)
        return oid

    def emit(self, nc):
        ops = self.ops

        def skip(dd, o):
            return (not dd["dma"]) and dd["eng"] == o["eng"] == "pe" and not o["dma"]
        for o in ops:
            for d in o["deps"]:
                dd = ops[d]
                if dd["dma"] or skip(dd, o):
                    continue
                dd["sig"] = True
        cnt = {e: 0 for e in ENGS}
        for o in ops:
            if o["sig"]:
                cnt[o["eng"]] += 1
                o["seq"] = cnt[o["eng"]]
        with contextlib.ExitStack() as st:
            csem = {e: st.enter_context(nc.semaphore("c_" + e)) for e in ENGS}
            dsem = {(e, s): st.enter_context(nc.semaphore(f"d_{e}{s}"))
                    for e in ENGS if self.dma_count[e] > 0
                    for s in range(min(N_DMA_SEM, self.dma_count[e]))}
            block = st.enter_context(nc.Block())

            def run(eng_name, eng):
                waited = {}
                for o in ops:
                    if o["eng"] != eng_name:
                        continue
                    for d in sorted(o["deps"]):
                        dd = ops[d]
                        if dd["dma"]:
                            key = ("d", dd["eng"], dd["slot"])
                            sem = dsem[(dd["eng"], dd["slot"])]
                            val = dd["val"]
                        else:
                            if skip(dd, o):
                                continue
                            key = ("c", dd["eng"])
                            sem = csem[dd["eng"]]
                            val = dd["seq"]
                        if waited.get(key, 0) >= val:
                            continue
                        waited[key] = val
                        eng.wait_ge(sem, val)
                    ins = o["fn"](eng)
                    if o["dma"]:
                        ins.then_inc(dsem[(eng_name, o["slot"])], 16)
                    elif o["sig"]:
                        ins.then_inc(csem[eng_name], 1)
                if eng_name == "sp":
                    for (e, s), oid in self.slot_last.items():
                        eng.wait_ge(dsem[(e, s)], ops[oid]["val"])

            block.tensor(lambda e: run("pe", e))
            block.scalar(lambda e: run("act", e))
            block.vector(lambda e: run("dve", e))
            block.gpsimd(lambda e: run("pool", e))
            block.sync(lambda e: run("sp", e))


def host_consts():
    c = {}
    c["ident"] = np.eye(128, dtype=np.float32)
    s = np.arange(128)
    c["negm_p"] = np.where(s[None, :] >= s[:, None], 0.0, -30000.0).astype(np.float32)
    same = (s[None, :] // 8) == (s[:, None] // 8)
    c["negm_s"] = np.where((s[None, :] >= s[:, None]) & same, 0.0, -30000.0).astype(np.float32)
    rp = np.ones((32, 128), np.float32); rp[:, 0] = 0
    rs = np.ones((32, 128), np.float32); rs[:, ::8] = 0
    c["rm_p"] = rp
    c["rm_s"] = rs
    r128 = np.ones((128, 128), np.float32); r128[:, ::8] = 0
    c["rm128"] = r128
    rq = np.ones((128, 256), np.float32); rq[:, ::64] = 0
    c["rmq"] = rq
    c["seqmask"] = (s[:, None] // 8 == np.arange(16)[None, :]).astype(np.float32)
    c["pairmask"] = (s[:, None] // 32 == np.arange(4)[None, :]).astype(np.float32)
    c["iota"] = np.broadcast_to(np.arange(128, dtype=np.float32)[None, :], (128, 128)).copy()
    return c


IN_SPECS = [
    ("xp", (2048, 1024)), ("xs", (128, 1024)), ("conv0", (16, 3, 4096)), ("ssm0", (16, 32, 64, 128)),
    ("re0", (16, 64, 64)), ("im0", (16, 64, 64)), ("norm_w", (1024,)), ("w_in", (1024, PROJ)),
    ("conv_w", (4, 4096)), ("conv_b", (4096,)), ("dt_bias", (32,)), ("A_log", (32,)), ("D_ssd", (32,)),
    ("ssd_norm_w", (2048,)), ("w_down_ssd", (2048, 1024)), ("lam_re", (64, 64)), ("lam_im", (64, 64)),
    ("log_dt", (64,)), ("B_re", (64, 64, 16)), ("B_im", (64, 64, 16)), ("C_re", (64, 16, 64)),
    ("C_im", (64, 16, 64)), ("D_s5", (1024,)), ("w_glu", (1024, 1024)), ("b_glu", (1024,)),
    ("w_down_s5", (1024, 1024)), ("w_out", (1024, 1024)), ("final_norm_w", (1024,)),
    ("ident", (128, 128)), ("negm_p", (128, 128)), ("negm_s", (128, 128)), ("rm_p", (32, 128)),
    ("rm_s", (32, 128)), ("rm128", (128, 128)), ("rmq", (128, 256)), ("seqmask", (128, 16)), ("pairmask", (128, 4)),
    ("iota", (128, 128)),
]
OUT_SPECS = [
    ("y_p", (2048, 1024)), ("y_s", (128, 1024)), ("conv_p", (3, 4096)), ("ssm_p", (32, 64, 128)),
    ("re_p", (64, 64)), ("im_p", (64, 64)), ("conv_s", (16, 3, 4096)), ("ssm_s", (16, 32, 64, 128)),
    ("re_s", (16, 64, 64)), ("im_s", (16, 64, 64)),
]


STOP = [None]
INTERLEAVE = [True]
SPLITA = [True]
BLK = [None]


class _Stop(Exception):
    pass


def ck(n):
    if STOP[0] is not None and n >= STOP[0]:
        raise _Stop()


def build(tiles=None):
    nc = bass.Bass("TRN2", target_bir_lowering=False)
    I = {n: nc.dram_tensor(n, list(s), F32, kind="ExternalInput").ap() for n, s in IN_SPECS}
    O = {n: nc.dram_tensor(n, list(s), F32, kind="ExternalOutput").ap() for n, s in OUT_SPECS}
    WS = {
        "w_in": nc.dram_tensor("ws_w_in", [1024, PROJ], BF16).ap(),
        "w_down_ssd": nc.dram_tensor("ws_wds", [2048, 1024], BF16).ap(),
        "w_glu": nc.dram_tensor("ws_wglu", [1024, 1024], BF16).ap(),
        "w_down_s5": nc.dram_tensor("ws_wd5", [1024, 1024], BF16).ap(),
        "w_out": nc.dram_tensor("ws_wout", [1024, 1024], BF16).ap(),
    }
    P = Prog()

    CUR = [None]
    MMB = [(0, 1)]

    TAG = ["pro"]

    def _rec(eng, fn, r, w, dma=False):
        if CUR[0] is None:
            P.op(eng, fn, r, w, dma=dma)
            P.ops[-1]["tag"] = TAG[0]
        else:
            CUR[0].append((eng, fn, tuple(r), tuple(w), dma, TAG[0]))

    def V(fn, r=(), w=()):
        _rec("dve", fn, r, w)

    def G(fn, r=(), w=()):
        _rec("pool", fn, r, w)

    def A(fn, r=(), w=()):
        _rec("act", fn, r, w)

    def T(fn, r=(), w=()):
        _rec("pe", fn, r, w)

    def DMA(fn, r=(), w=()):
        _rec("sp", fn, r, w, dma=True)

    def DMAO(fn, r=(), w=()):
        _rec("pool", fn, r, w, dma=True)

    def DMAH(fn, r=(), w=()):
        _rec("act", fn, r, w, dma=True)

    st = contextlib.ExitStack()
    with st:
        def sb(name, shape, dt=F32, stack=None):
            return (stack or st).enter_context(nc.sbuf_tensor("s_" + name, list(shape), dt))

        def kn(t):
            n_ = getattr(t, "name")
            return n_[2:] if n_.startswith("s_") else n_
        pb = [st.enter_context(nc.psum_tensor(f"pb{i}", [128, 512], F32)) for i in range(8)]
        pbb = [p.bitcast(BF16) for p in pb]

        identf = sb("identf", [128, 128]); identb = sb("identb", [128, 128], BF16)
        negm_p = sb("negm_p", [128, 128]); negm_s = sb("negm_s", [128, 128])
        rm_p = sb("rm_p", [32, 128]); rm_s = sb("rm_s", [32, 128]); rm128 = sb("rm128", [128, 128]); rmq = sb("rmq", [128, 256])
        seqmask = sb("seqmask", [128, 16]); pairmask = sb("pairmask", [128, 4])
        ones32 = sb("ones32", [32, 128]); onesb = sb("onesb", [128, 128], BF16)
        onecol = sb("onecol", [128, 1]); epscol = sb("epscol", [128, 1])
        for nm, t in [("ident", identf), ("negm_p", negm_p), ("negm_s", negm_s), ("rm_p", rm_p), ("rm_s", rm_s),
                      ("rm128", rm128), ("rmq", rmq), ("seqmask", seqmask), ("pairmask", pairmask)]:
            DMA(lambda e, nm=nm, t=t: e.dma_start(out=t[:], in_=I[nm]), w=[kn(t)])
        V(lambda e: e.tensor_copy(out=identb[:], in_=identf[:]), ["identf"], ["identb"])
        G(lambda e: e.memset(ones32[:], 1.0), w=["ones32"])
        G(lambda e: e.memset(onesb[:], 1.0), w=["onesb"])
        G(lambda e: e.memset(onecol[:], 1.0), w=["onecol"])
        G(lambda e: e.memset(epscol[:], EPS), w=["epscol"])

        def colvec(name, src, nt):
            t = sb(name, [128, nt])
            DMA(lambda e: e.dma_start(out=t[:], in_=src.rearrange("(j p) -> p j", p=128), allow_slow_non_contiguous=True), w=[name])
            return t
        normw = colvec("normw", I["norm_w"], 8)
        convb = colvec("convb", I["conv_b"], 32)
        ssdnw = colvec("ssdnw", I["ssd_norm_w"], 16)
        ds5 = colvec("ds5", I["D_s5"], 8)
        bglu = colvec("bglu", I["b_glu"], 8)
        convw = sb("convw", [128, 32, 4])
        for k in range(4):
            DMA(lambda e, k=k: e.dma_start(out=convw[:, :, k], in_=I["conv_w"][k].rearrange("(j p) -> p j", p=128), allow_slow_non_contiguous=True), w=["convw"])
        fnw = sb("fnw", [128, 1024])
        DMA(lambda e: e.dma_start(out=fnw[:], in_=I["final_norm_w"].partition_broadcast(128)), w=["fnw"])
        dtb = sb("dtb", [32, 1]); aneg = sb("aneg", [32, 1])
        DMA(lambda e: e.dma_start(out=dtb[:], in_=I["dt_bias"].rearrange("(h o) -> h o", o=1)), w=["dtb"])
        DMA(lambda e: e.dma_start(out=aneg[:], in_=I["A_log"].rearrange("(h o) -> h o", o=1)), w=["aneg"])
        A(lambda e: e.activation(out=aneg[:], in_=aneg[:], func=AF.Exp), ["aneg"], ["aneg"])
        V(lambda e: e.tensor_scalar(out=aneg[:], in0=aneg[:], scalar1=-1.0, scalar2=None, op0=ALU.mult), ["aneg"], ["aneg"])
        dexp = sb("dexp", [128, 16])
        dv = I["D_ssd"].rearrange("(pr h2) -> h2 pr", h2=2)
        for h2 in range(2):
            DMA(lambda e, h2=h2: e.dma_start(out=dexp[64 * h2:64 * h2 + 64, :], in_=dv[h2].partition_broadcast(64), allow_slow_non_contiguous=True), w=["dexp"])
        are = sb("are", [128, 32]); aim = sb("aim", [128, 32]); rdec = sb("rdec", [128, 32])
        aere = sb("aere", [128, 32]); aeim = sb("aeim", [128, 32])
        ecos = sb("ecos", [128, 32, 64]); esin = sb("esin", [128, 32, 64])
        wb5 = sb("wb5", [128, 8, 2, 128], BF16); wd5 = sb("wd5", [128, 8, 2, 128], BF16)
        halo = sb("halo", [128, 32, 3])
        STt = sb("STt", [128, 32, 64])
        SA = sb("SA", [128, 16, 128], BF16); SB = sb("SB", [128, 16, 128], BF16)
        XA = sb("XA", [128, 16, 128], BF16); XB = sb("XB", [128, 16, 128], BF16)
        sgc = sb("sgc", [128, 32, 2])
        for t_ in (halo, STt, SA, SB, XA, XB, sgc, wd5):
            G(lambda e, t_=t_: e.memset(t_[:], 0.0), w=([kn(t_)] if kn(t_) not in ("SA", "SB") else [f"{kn(t_)}{i}" for i in range(4)]))

        pst = contextlib.ExitStack()
        with pst:
            def psb(name, shape, dt=F32):
                return sb(name, shape, dt, stack=pst)
            iota = psb("iota", [128, 128])
            DMA(lambda e: e.dma_start(out=iota[:], in_=I["iota"]), w=["iota"])
            lre = psb("lre", [128, 32]); lim = psb("lim", [128, 32]); stp = psb("stp", [128, 32]); th = psb("th", [128, 32])
            DMA(lambda e: e.dma_start(out=lre[:], in_=I["lam_re"].rearrange("(gp g2) p -> (g2 p) gp", g2=2), allow_slow_non_contiguous=True), w=["lre"])
            DMA(lambda e: e.dma_start(out=lim[:], in_=I["lam_im"].rearrange("(gp g2) p -> (g2 p) gp", g2=2), allow_slow_non_contiguous=True), w=["lim"])
            ldv = I["log_dt"].rearrange("(gp g2) -> g2 gp", g2=2)
            for g2 in range(2):
                DMA(lambda e, g2=g2: e.dma_start(out=stp[64 * g2:64 * g2 + 64, :], in_=ldv[g2].partition_broadcast(64), allow_slow_non_contiguous=True), w=["stp"])
            A(lambda e: e.activation(out=stp[:], in_=stp[:], func=AF.Exp), ["stp"], ["stp"])
            V(lambda e: e.tensor_tensor(out=rdec[:], in0=lre[:], in1=stp[:], op=ALU.mult), ["lre", "stp"], ["rdec"])
            A(lambda e: e.activation(out=rdec[:], in_=rdec[:], func=AF.Exp), ["rdec"], ["rdec"])
            V(lambda e: e.tensor_tensor(out=th[:], in0=lim[:], in1=stp[:], op=ALU.mult), ["lim", "stp"], ["th"])
            V(lambda e: e.tensor_scalar(out=th[:], in0=th[:], scalar1=1.0 / (2.0 * math.pi), scalar2=None, op0=ALU.mult), ["th"], ["th"])
            tmpx = psb("tmpx", [128, 2048]); tmpf = psb("tmpf", [128, 2048]); tmpi = psb("tmpi", [128, 2048], I32)

            def sin_turns(out_ap, mk_x, n, off, rkeys, wkeys):
                V(lambda e: mk_x(e, tmpx[:, 0:n]), rkeys, ["tmpx"])
                if off:
                    V(lambda e: e.tensor_scalar(out=tmpx[:, 0:n], in0=tmpx[:, 0:n], scalar1=float(off), scalar2=None, op0=ALU.add), ["tmpx"], ["tmpx"])
                V(lambda e: e.tensor_copy(out=tmpi[:, 0:n], in_=tmpx[:, 0:n]), ["tmpx"], ["tmpi"])
                V(lambda e: e.tensor_copy(out=tmpf[:, 0:n], in_=tmpi[:, 0:n]), ["tmpi"], ["tmpf"])
                V(lambda e: e.tensor_tensor(out=tmpx[:, 0:n], in0=tmpx[:, 0:n], in1=tmpf[:, 0:n], op=ALU.subtract), ["tmpx", "tmpf"], ["tmpx"])
                A(lambda e: e.activation(out=out_ap, in_=tmpx[:, 0:n], func=AF.Sin, scale=SIN_SCALE), ["tmpx"], wkeys)
            cs1 = psb("cs1", [128, 32]); sn1 = psb("sn1", [128, 32])
            x1 = lambda e, dst: e.tensor_copy(out=dst, in_=th[:])
            sin_turns(sn1[:], x1, 32, 0.0, ["th"], ["sn1"])
            sin_turns(cs1[:], x1, 32, 0.25, ["th"], ["cs1"])
            V(lambda e: e.tensor_tensor(out=are[:], in0=rdec[:], in1=cs1[:], op=ALU.mult), ["rdec", "cs1"], ["are"])
            V(lambda e: e.tensor_tensor(out=aim[:], in0=rdec[:], in1=sn1[:], op=ALU.mult), ["rdec", "sn1"], ["aim"])
            x64 = lambda e, dst: e.tensor_scalar(out=dst, in0=th[:], scalar1=64.0, scalar2=None, op0=ALU.mult)
            sin_turns(sn1[:], x64, 32, 0.0, ["th"], ["sn1"])
            sin_turns(cs1[:], x64, 32, 0.25, ["th"], ["cs1"])
            V(lambda e: e.tensor_tensor(out=aere[:], in0=rdec[:], in1=cs1[:], op=ALU.mult), ["rdec", "cs1"], ["aere"])
            V(lambda e: e.tensor_tensor(out=aeim[:], in0=rdec[:], in1=sn1[:], op=ALU.mult), ["rdec", "sn1"], ["aeim"])
            xt_ = lambda e, dst: e.tensor_tensor(out=dst.rearrange("p (a b) -> p a b", b=64), in0=th[:].unsqueeze(2).broadcast_to([128, 32, 64]),
                                                 in1=iota[:, 0:64].unsqueeze(1).broadcast_to([128, 32, 64]), op=ALU.mult)
            sin_turns(esin[:].rearrange("p a b -> p (a b)"), xt_, 2048, 0.0, ["th", "iota"], ["esin"])
            sin_turns(ecos[:].rearrange("p a b -> p (a b)"), xt_, 2048, 0.25, ["th", "iota"], ["ecos"])
            gre = psb("gre", [128, 32]); gim = psb("gim", [128, 32]); den = psb("den", [128, 32])
            t1 = psb("t1s", [128, 32]); t2 = psb("t2s", [128, 32]); am1 = psb("am1", [128, 32])
            V(lambda e: e.tensor_scalar(out=am1[:], in0=are[:], scalar1=-1.0, scalar2=None, op0=ALU.add), ["are"], ["am1"])
            V(lambda e: e.tensor_tensor(out=den[:], in0=lre[:], in1=lre[:], op=ALU.mult), ["lre"], ["den"])
            V(lambda e: e.tensor_tensor(out=t1[:], in0=lim[:], in1=lim[:], op=ALU.mult), ["lim"], ["t1s"])
            V(lambda e: e.tensor_tensor(out=den[:], in0=den[:], in1=t1[:], op=ALU.add), ["den", "t1s"], ["den"])
            V(lambda e: e.reciprocal(out=den[:], in_=den[:]), ["den"], ["den"])
            V(lambda e: e.tensor_tensor(out=t1[:], in0=am1[:], in1=lre[:], op=ALU.mult), ["am1", "lre"], ["t1s"])
            V(lambda e: e.tensor_tensor(out=t2[:], in0=aim[:], in1=lim[:], op=ALU.mult), ["aim", "lim"], ["t2s"])
            V(lambda e: e.tensor_tensor(out=gre[:], in0=t1[:], in1=t2[:], op=ALU.add), ["t1s", "t2s"], ["gre"])
            V(lambda e: e.tensor_tensor(out=gre[:], in0=gre[:], in1=den[:], op=ALU.mult), ["gre", "den"], ["gre"])
            V(lambda e: e.tensor_tensor(out=t1[:], in0=aim[:], in1=lre[:], op=ALU.mult), ["aim", "lre"], ["t1s"])
            V(lambda e: e.tensor_tensor(out=t2[:], in0=am1[:], in1=lim[:], op=ALU.mult), ["am1", "lim"], ["t2s"])
            V(lambda e: e.tensor_tensor(out=gim[:], in0=t1[:], in1=t2[:], op=ALU.subtract), ["t1s", "t2s"], ["gim"])
            V(lambda e: e.tensor_tensor(out=gim[:], in0=gim[:], in1=den[:], op=ALU.mult), ["gim", "den"], ["gim"])
            bre = psb("bre", [128, 32, 16]); bim = psb("bim", [128, 32, 16]); bbr = psb("bbr", [128, 32, 16]); bbi = psb("bbi", [128, 32, 16]); tb = psb("tbs", [128, 32, 16])
            DMA(lambda e: e.dma_start(out=bre[:], in_=I["B_re"].rearrange("(gp g2) p k -> (g2 p) gp k", g2=2)), w=["bre"])
            DMA(lambda e: e.dma_start(out=bim[:], in_=I["B_im"].rearrange("(gp g2) p k -> (g2 p) gp k", g2=2)), w=["bim"])
            gb_ = lambda t: t[:].unsqueeze(2).broadcast_to([128, 32, 16])
            V(lambda e: e.tensor_tensor(out=bbr[:], in0=bre[:], in1=gb_(gre), op=ALU.mult), ["bre", "gre"], ["bbr"])
            V(lambda e: e.tensor_tensor(out=tb[:], in0=bim[:], in1=gb_(gim), op=ALU.mult), ["bim", "gim"], ["tbs"])
            V(lambda e: e.tensor_tensor(out=bbr[:], in0=bbr[:], in1=tb[:], op=ALU.subtract), ["bbr", "tbs"], ["bbr"])
            V(lambda e: e.tensor_tensor(out=bbi[:], in0=bim[:], in1=gb_(gre), op=ALU.mult), ["bim", "gre"], ["bbi"])
            V(lambda e: e.tensor_tensor(out=tb[:], in0=bre[:], in1=gb_(gim), op=ALU.mult), ["bre", "gim"], ["tbs"])
            V(lambda e: e.tensor_tensor(out=bbi[:], in0=bbi[:], in1=tb[:], op=ALU.add), ["bbi", "tbs"], ["bbi"])
            cin = psb("cin", [128, 4, 2, 64])
            cst = [psb("cstr", [128, 32, 16]), psb("csti", [128, 32, 16])]
            for ri, nm in enumerate(["C_re", "C_im"]):
                cv = I[nm].rearrange("(gh gq g2) k p -> gq k gh g2 p", gq=8, g2=2)
                for gq in range(8):
                    for gh in range(4):
                        DMA(lambda e, gq=gq, gh=gh, cv=cv: e.dma_start(out=cin[16 * gq:16 * gq + 16, gh, :, :], in_=cv[gq][:, gh, :, :]), w=["cin"])
                for gh in range(4):
                    T(lambda e, gh=gh: e.transpose(out=pb[7][:, 0:128], in_=cin[:, gh, :, :].rearrange("p a b -> p (a b)"), identity=identf[:]), ["cin", "identf"], ["pb7"])
                    V(lambda e, gh=gh, ri=ri: e.tensor_copy(out=cst[ri][:, 8 * gh:8 * gh + 8, :], in_=pb[7][:, 0:128].rearrange("p (a b) -> p a b", b=16)), ["pb7"], [kn(cst[ri])])
            zst = psb("zst", [128, 4, 2, 16])
            G(lambda e: e.memset(zst[:], 0.0), w=["zst"])
            for ft in range(8):
                for ri in range(2):
                    src = [bbr, bbi][ri]
                    for h2 in range(2):
                        V(lambda e, ft=ft, h2=h2, src=src: e.tensor_copy(out=zst[64 * h2:64 * h2 + 64, :, h2, :], in_=src[64 * h2:64 * h2 + 64, 4 * ft:4 * ft + 4, :]), [kn(src)], ["zst"])
                    T(lambda e: e.transpose(out=pb[6][:, 0:128], in_=zst[:].rearrange("p a b c -> p (a b c)"), identity=identf[:]), ["zst", "identf"], ["pb6"])
                    V(lambda e, ft=ft, ri=ri: e.tensor_copy(out=wb5[:, ft, ri, :], in_=pb[6][:, 0:128]), ["pb6"], ["wb5"])
                    for h2 in range(2):
                        V(lambda e, ft=ft, ri=ri, h2=h2: e.tensor_scalar(
                            out=wd5[64 * h2:64 * h2 + 64, ft, ri, :].rearrange("p (q g k) -> p q g k", g=2, k=16)[:, :, h2, :],
                            in0=cst[ri][64 * h2:64 * h2 + 64, 4 * ft:4 * ft + 4, :], scalar1=(1.0 if ri == 0 else -1.0), scalar2=None, op0=ALU.mult),
                          [kn(cst[ri])], ["wd5"])
            NST = 6
            stg = [psb(f"stg{i}", [128, 2048]) for i in range(NST)]
            stgb = [psb(f"stgb{i}", [128, 2048], BF16) for i in range(NST)]
            cnt = [0]

            def cast_rows(src, dst, ncols):
                for c0 in range(0, ncols, 2048):
                    w_ = min(2048, ncols - c0)
                    i = cnt[0] % NST
                    cnt[0] += 1
                    DMA(lambda e, i=i, c0=c0, w_=w_: e.dma_start(out=stg[i][:, 0:w_], in_=src[:, c0:c0 + w_]), w=[f"stg{i}"])
                    sel = (1, 2, 2, 2, 1, 0)[cnt[0] % 6]
                    if sel == 0:
                        G(lambda e, i=i, w_=w_: e.tensor_copy(out=stgb[i][:, 0:w_], in_=stg[i][:, 0:w_]), [f"stg{i}"], [f"stgb{i}"])
                    elif sel == 1:
                        A(lambda e, i=i, w_=w_: e.activation(out=stgb[i][:, 0:w_], in_=stg[i][:, 0:w_], func=AF.Copy), [f"stg{i}"], [f"stgb{i}"])
                    else:
                        V(lambda e, i=i, w_=w_: e.tensor_copy(out=stgb[i][:, 0:w_], in_=stg[i][:, 0:w_]), [f"stg{i}"], [f"stgb{i}"])
                    DMAH(lambda e, i=i, c0=c0, w_=w_: e.dma_start(out=dst[:, c0:c0 + w_], in_=stgb[i][:, 0:w_]), [f"stgb{i}"], ["ws"])
            for nm, rows in [("w_in", 1024), ("w_down_ssd", 2048), ("w_glu", 1024), ("w_down_s5", 1024), ("w_out", 1024)]:
                nco = PROJ if nm == "w_in" else 1024
                for r0 in range(0, rows, 128):
                    cast_rows(I[nm][r0:r0 + 128, :], WS[nm][r0:r0 + 128, :], nco)

        P.barrier()

        xt = sb("xt", [128, 1024]); ot = sb("ot", [128, 1024]); ss = sb("ss", [128, 1]); rstd = sb("rstd", [128, 1])
        ssB = sb("ssB", [128, 1]); rstdB = sb("rstdB", [128, 1])
        xsn = sb("xsn", [128, 1024], BF16); hT = sb("hT", [128, 8, 128], BF16)
        wbuf = [sb(f"wbuf{i}", [128, 8, 256], BF16) for i in range(4)]
        zs = sb("zs", [128, 16, 128], BF16)
        xc = sb("xc", [128, 32, 128], BF16)
        raw = [sb(f"raw{i}", [128, 176]) for i in range(2)]
        acc = [sb(f"acc{i}", [128, 128]) for i in range(2)]
        dts = sb("dts", [32, 128]); dAs = sb("dAs", [32, 128]); acs = sb("acs", [32, 128]); dec = sb("dec", [32, 128]); wdt = sb("wdt", [32, 128])
        tokm = sb("tokm", [128, 3, 32])
        u5_ = [sb(f"u5_{i}", [128, 8, 128], BF16) for i in range(2)]; z5s_ = [sb(f"z5s_{i}", [128, 8, 128], BF16) for i in range(2)]
        gas = sb("gas", [128, 8, 128], BF16); gbs_ = [sb(f"gbs_{i}", [128, 8, 128], BF16) for i in range(2)]
        Xd = sb("Xd", [128, 32, 64], BF16); Btok = sb("Btok", [128, 8, 128], BF16)
        rhsR = sb("rhsR", [32, 4, 128]); Egs = [sb("Eg0", [128, 4, 128])] * 2; Dm = sb("Dm", [128, 4, 128])
        Lgs = [sb(f"Lg{i}", [128, 4, 128], BF16) for i in range(2)]; Mg = sb("Mg", [128, 4, 128], BF16)
        CEall = sb("CEall", [128, 32, 128], BF16)
        dch = sb("dch", [128, 32, 16])
        S0qs = [sb(f"S0q{i}", [128, 4, 128]) for i in range(2)]; Snqs = [sb(f"Snq{i}", [128, 4, 128]) for i in range(2)]
        Bms = [sb("Bm0", [128, 8, 128], BF16)] * 2
        yv2 = [sb(f"yv2_{i}", [128, 2, 128]) for i in range(2)]; ysq2 = [sb(f"ysq2_{i}", [128, 2, 128], BF16) for i in range(2)]
        rstg2 = [sb(f"rstg2_{i}", [128, 128]) for i in range(2)]; ynT = sb("ynT", [128, 16, 128], BF16)
        yaT_ = [sb(f"yaT_{i}", [128, 8, 128], BF16) for i in range(2)]; ybT = sb("ybT", [128, 8, 128], BF16); mixT = sb("mixT", [128, 8, 128], BF16)
        u5m = sb("u5m", [128, 4, 128], BF16)
        bu = sb("bu", [128, 4, 2, 128]); sgm = sb("sgm", [128, 4, 2, 128]); rts = sb("rts", [128, 4, 128])
        bt1 = sb("bt1", [128, 4, 128]); bt2 = sb("bt2", [128, 4, 128])
        s5s = sb("s5s", [128, 4, 2, 128], BF16)
        cin_ = sb("cinn", [128, 4, 2, 16])
        ctm = sb("ctm", [128, 4, 2, 16])
        fin = sb("fin", [128, 32, 2, 16]); s0all = sb("s0all", [128, 32, 2, 16])
        y5t = sb("y5t", [128, 128])
        y5g = sb("y5g", [128, 8, 128], BF16); y5f = sb("y5f", [128, 8, 128], BF16); glus = sb("glus", [128, 128])
        s5io = sb("s5io", [128, 4, 2, 64]); s5tr = sb("s5tr", [128, 16, 8])
        cvo = sb("cvo", [48, 1024]); cst3 = sb("cst3", [128, 48])

        wcnt = [0]

        WSET = [(0, 1, 2)]

        def wload(src_ap):
            i = WSET[0][wcnt[0] % len(WSET[0])]
            wcnt[0] += 1
            kt = src_ap.shape[0] // 128
            nco = src_ap.shape[1]
            DMA(lambda e: e.dma_start(out=wbuf[i][:, 0:kt, 0:nco], in_=src_ap.rearrange("(j p) f -> p j f", p=128)), ["ws"], [f"wbuf{i}"])
            return wbuf[i], f"wbuf{i}"

        pslot = [0]

        def mm_slot():
            s = MMB[0][pslot[0] % 2]
            pslot[0] += 1
            return pb[s][:, 0:128], f"pb{s}"

        def proj_ft(wt, wk, c0, nrows=128):
            ps, pk = mm_slot()
            for j in range(8):
                T(lambda e, j=j: e.matmul(ps[0:nrows, :], lhsT=wt[:, j, c0:c0 + nrows], rhs=hT[:, j, :], start=(j == 0), stop=(j == 7)), [wk, "hT"], [pk])
            return ps, pk

        def dense_T(wname, kt, rhs_fn, rkeys_fn, evac):
            for cb in range(4):
                halves = [wload(WS[wname][128 * k0:128 * (k0 + 8), 256 * cb:256 * cb + 256]) for k0 in range(0, kt, 8)]
                for s in range(2):
                    m = 2 * cb + s
                    ps, pk = mm_slot()
                    for k in range(kt):
                        wt, wk = halves[k // 8]
                        T(lambda e, wt=wt, k=k, s=s, ps=ps: e.matmul(ps, lhsT=wt[:, k % 8, 128 * s:128 * s + 128], rhs=rhs_fn(k), start=(k == 0), stop=(k == kt - 1)), [wk] + rkeys_fn(k), [pk])
                    evac(m, ps, pk)

        def cmul(out_re, out_im, ar, ai, br, bi, keys_r, keys_w, t1_, t2_, tk):
            V(lambda e: e.tensor_tensor(out=t1_, in0=ar, in1=br, op=ALU.mult), keys_r, [tk[0]])
            V(lambda e: e.tensor_tensor(out=t2_, in0=ai, in1=bi, op=ALU.mult), keys_r, [tk[1]])
            V(lambda e: e.tensor_tensor(out=out_re, in0=t1_, in1=t2_, op=ALU.subtract), tk, keys_w[0:1])
            V(lambda e: e.tensor_tensor(out=t1_, in0=ar, in1=bi, op=ALU.mult), keys_r + keys_w[0:1], [tk[0]])
            V(lambda e: e.tensor_tensor(out=t2_, in0=ai, in1=br, op=ALU.mult), keys_r + keys_w[0:1], [tk[1]])
            V(lambda e: e.tensor_tensor(out=out_im, in0=t1_, in1=t2_, op=ALU.add), tk, keys_w[1:2])

        def merge(la, lb):
            out = []
            ia = ib = 0
            na, nb = len(la), len(lb)
            while ia < na or ib < nb:
                if ib >= nb or (ia < na and ia * nb <= ib * na):
                    out.append(la[ia]); ia += 1
                else:
                    out.append(lb[ib]); ib += 1
            return out

        def recA(ti, par):
            u5 = u5_[par]; z5s = z5s_[par]; gbs = gbs_[par]; yaT = yaT_[par]
            smp = (ti == 16)
            lastp = (ti == 15)
            xsrc = I["xs"] if smp else I["xp"][128 * ti:128 * ti + 128, :]
            ydst = O["y_s"] if smp else O["y_p"][128 * ti:128 * ti + 128, :]
            negm = negm_s if smp else negm_p
            rmk = rm_s if smp else rm_p
            nseq, L = (16, 8) if smp else (1, 128)
            TAG[0] = "A.T1"
            DMA(lambda e: e.dma_start(out=xt[:], in_=xsrc), w=["xt"])
            G(lambda e: e.memset(ss[:], 0.0), w=["ss"])
            A(lambda e: e.activation(out=xsn[:], in_=xt[:], func=AF.Square, accum_out=ss[:]), ["xt", "ss"], ["xsn", "ss"])
            A(lambda e: e.activation(out=rstd[:], in_=ss[:], func=AF.Sqrt, scale=1.0 / D, bias=epscol[:]), ["ss", "epscol"], ["rstd"])
            V(lambda e: e.reciprocal(out=rstd[:], in_=rstd[:]), ["rstd"], ["rstd"])
            V(lambda e: e.tensor_scalar(out=xsn[:], in0=xt[:], scalar1=rstd[:, 0:1], scalar2=None, op0=ALU.mult), ["xt", "rstd"], ["xsn"])
            for j in range(8):
                T(lambda e, j=j: e.transpose(out=pbb[5][:, 128 * j:128 * j + 128], in_=xsn[:, 128 * j:128 * j + 128], identity=identb[:]), ["xsn", "identb"], ["pb5"])
            for j in range(8):
                V(lambda e, j=j: e.tensor_scalar(out=hT[:, j, :], in0=pbb[5][:, 128 * j:128 * j + 128], scalar1=normw[:, j:j + 1], scalar2=None, op0=ALU.mult), ["pb5", "normw"], ["hT"])
            TAG[0] = "A.proj"
            blocks = [("xbc", c0, 256) for c0 in range(XBC0, DT0, 256)] + [("dt", DT0, 32)]
            blocks += [("z", c0, 256) for c0 in range(Z0, XBC0, 256)]
            for nm, b0 in (("u5", U0), ("z5", Z50), ("ga", GA0), ("gb", GB0)):
                blocks += [(nm, c0, 256) for c0 in range(b0, b0 + 1024, 256)]
            base = {"xbc": XBC0, "z": Z0, "u5": U0, "z5": Z50, "ga": GA0, "gb": GB0}
            deferred = []
            main_list = CUR[0]
            late_list = []
            for nm, c0, nco in blocks:
                if nm == "u5" and c0 == U0 and SPLITA[0] and not smp:
                    while deferred:
                        deferred.pop(0)()
                    CUR[0] = late_list
                wt, wk = wload(WS["w_in"][:, c0:c0 + nco])
                if nm == "dt":
                    while deferred:
                        deferred.pop(0)()
                    ps, pk = proj_ft(wt, wk, 0, 32)
                    A(lambda e, ps=ps: e.activation(out=dts[:], in_=ps[0:32, :], func=AF.Exp, bias=dtb[:]), [pk, "dtb"], ["dts"])
                    A(lambda e: e.activation(out=dts[:], in_=dts[:], func=AF.Ln, bias=onecol[0:32, :]), ["dts", "onecol"], ["dts"])
                    continue
                for s in range(2):
                    ft = (c0 - base[nm]) // 128 + s
                    ps, pk = proj_ft(wt, wk, 128 * s)
                    if nm == "z":
                        A(lambda e, ps=ps, ft=ft: e.activation(out=zs[:, ft, :], in_=ps, func=AF.Silu), [pk], [f"zs{ft}"])
                    elif nm == "u5":
                        A(lambda e, ps=ps, ft=ft: e.activation(out=u5[:, ft, :], in_=ps, func=AF.Copy), [pk], [f"u5_{par}_{ft}"])
                    elif nm == "z5":
                        A(lambda e, ps=ps, ft=ft: e.activation(out=z5s[:, ft, :], in_=ps, func=AF.Silu), [pk], [f"z5_{par}_{ft}"])
                    elif nm == "ga":
                        A(lambda e, ps=ps, ft=ft: e.activation(out=gas[:, ft, :], in_=ps, func=AF.Sigmoid), [pk], [f"ga{ft}"])
                    elif nm == "gb":
                        A(lambda e, ps=ps, ft=ft: e.activation(out=gbs[:, ft, :], in_=ps, func=AF.Sigmoid), [pk], [f"gb_{par}_{ft}"])
                    else:
                        r_ = raw[ft % 2]; rk = f"raw{ft % 2}"; ac = acc[ft % 2]; ak = f"acc{ft % 2}"
                        EV = V
                        if smp:
                            rv = r_[:, 0:176].rearrange("p (s r) -> p s r", r=11)
                            if ft % 8 == 0:
                                DMA(lambda e, ft=ft: e.dma_start(out=cvo[:], in_=I["conv0"].rearrange("s r c -> (s r) c")[:, 128 * ft:128 * ft + 1024]), w=["cvo"])
                            T(lambda e, ft=ft: e.transpose(out=pb[6][:, 0:48], in_=cvo[:, 128 * (ft % 8):128 * (ft % 8) + 128], identity=identf[0:48, 0:48]), ["cvo", "identf"], ["pb6"])
                            V(lambda e, rv=rv: e.tensor_copy(out=rv[:, :, 0:3], in_=pb[6][:, 0:48].rearrange("p (s r) -> p s r", r=3)), ["pb6"], [rk])
                        else:
                            rv = r_[:, 0:131].unsqueeze(1)
                            A(lambda e, rv=rv, ft=ft: e.activation(out=rv[:, 0, 0:3], in_=halo[:, ft, :], func=AF.Copy), ["halo"], [rk])
                        A(lambda e, rv=rv, ps=ps: e.activation(out=rv[:, :, 3:3 + L], in_=ps.rearrange("p (s l) -> p s l", l=L), func=AF.Copy), [pk], [rk])
                        while deferred:
                            deferred.pop(0)()
                        av = ac[:].rearrange("p (s l) -> p s l", l=L)
                        EV(lambda e, rv=rv, av=av, ft=ft: e.tensor_scalar(out=av, in0=rv[:, :, 0:L], scalar1=convw[:, ft, 0:1], scalar2=convb[:, ft:ft + 1], op0=ALU.mult, op1=ALU.add), [rk, "convw", "convb"], [ak])
                        for k in range(1, 4):
                            EV(lambda e, rv=rv, av=av, ft=ft, k=k: e.scalar_tensor_tensor(out=av, in0=rv[:, :, k:k + L], scalar=convw[:, ft, k:k + 1], in1=av, op0=ALU.mult, op1=ALU.add), [rk, ak, "convw"], [ak])
                        def late(ac=ac, ft=ft, rv=rv, ak=ak, rk=rk):
                            A(lambda e: e.activation(out=xc[:, ft, :], in_=ac[:], func=AF.Silu), [ak], [f"xc{ft}"])
                            if not smp:
                                A(lambda e: e.activation(out=halo[:, ft, :], in_=rv[:, 0, 128:131], func=AF.Copy), [rk], ["halo"])
                        deferred.append(late)
                        if smp or lastp:
                            nr = 3 * nseq
                            EV(lambda e, rv=rv, nr=nr: e.tensor_copy(out=cst3[:, 0:nr].rearrange("p (s r) -> p s r", r=3), in_=rv[:, :, L:L + 3]), [rk], ["cst3"])
                            T(lambda e, nr=nr: e.transpose(out=pb[6][0:nr, 128:256], in_=cst3[:, 0:nr], identity=identf[:]), ["cst3", "identf"], ["pb6"])
                            V(lambda e, ft=ft, nr=nr: e.tensor_copy(out=cvo[0:nr, 128 * (ft % 8):128 * (ft % 8) + 128], in_=pb[6][0:nr, 128:256]), ["pb6"], ["cvo"])
                            if ft % 8 == 7:
                                cdst = (O["conv_s"].rearrange("s r c -> (s r) c") if smp else O["conv_p"])[:, 128 * (ft - 7):128 * (ft - 7) + 1024]
                                DMAO(lambda e, cdst=cdst, nr=nr: e.dma_start(out=cdst, in_=cvo[0:nr, :]), ["cvo"], [])
            ssd_list = []
            if CUR[0] is late_list:
                CUR[0] = ssd_list
            TAG[0] = "A.dt"
            V(lambda e: e.tensor_scalar(out=dAs[:], in0=dts[:], scalar1=aneg[:, 0:1], scalar2=None, op0=ALU.mult), ["dts", "aneg"], ["dAs"])
            V(lambda e: e.tensor_tensor_scan(out=acs[:], data0=rmk[:], data1=dAs[:], initial=0.0, op0=ALU.mult, op1=ALU.add), [kn(rmk), "dAs"], ["acs"])
            a3 = acs[:].rearrange("h (s l) -> h s l", l=L)
            V(lambda e: e.tensor_tensor(out=dec[:].rearrange("h (s l) -> h s l", l=L), in0=a3[:, :, L - 1:L].broadcast_to([32, nseq, L]), in1=a3, op=ALU.subtract), ["acs"], ["dec"])
            A(lambda e: e.activation(out=dec[:], in_=dec[:], func=AF.Exp), ["dec"], ["dec"])
            V(lambda e: e.tensor_tensor(out=wdt[:], in0=dts[:], in1=dec[:], op=ALU.mult), ["dts", "dec"], ["wdt"])
            for i_, src in enumerate((acs, dts, wdt)):
                T(lambda e, i_=i_, src=src: e.transpose(out=pb[6][:, 256 + 32 * i_:256 + 32 * i_ + 32], in_=src[:], identity=identf[0:32, 0:32]), [kn(src), "identf"], ["pb6"])
            V(lambda e: e.tensor_copy(out=tokm[:].rearrange("p a b -> p (a b)"), in_=pb[6][:, 256:352]), ["pb6"], ["tokm"])
            TAG[0] = "A.X"
            for half in range(2):
                bk = f"pb{2 + half}"
                for j in range(8):
                    ft = 8 * half + j
                    T(lambda e, ft=ft, j=j, half=half: e.transpose(out=pbb[2 + half][:, 128 * j:128 * j + 128], in_=xc[:, ft, :], identity=identb[:]), [f"xc{ft}", "identb"], [bk])
                pv = pbb[2 + half][:, :].rearrange("p (a h2 q) -> p a h2 q", h2=2, q=64)
                prs = slice(8 * half, 8 * half + 8)
                dtT = tokm[:, 1, 16 * half:16 * half + 16].rearrange("p (a h2) -> p a h2", h2=2)
                wT = tokm[:, 2, 16 * half:16 * half + 16]
                V(lambda e, pv=pv, prs=prs, dtT=dtT: e.tensor_tensor(out=XA[:, prs, 0:64], in0=pv[:, :, 0, :], in1=dtT[:, :, 0:1].broadcast_to([128, 8, 64]), op=ALU.mult), [bk, "tokm"], ["XA"])
                V(lambda e, pv=pv, prs=prs, dtT=dtT: e.tensor_tensor(out=XB[:, prs, 64:128], in0=pv[:, :, 1, :], in1=dtT[:, :, 1:2].broadcast_to([128, 8, 64]), op=ALU.mult), [bk, "tokm"], ["XB"])
                V(lambda e, half=half, wT=wT: e.tensor_tensor(out=Xd[:, 16 * half:16 * half + 16, :], in0=pbb[2 + half][:, :].rearrange("p (h q) -> p h q", q=64), in1=wT.unsqueeze(2).broadcast_to([128, 16, 64]), op=ALU.mult), [bk, "tokm"], ["Xd"])
            for j in range(8):
                T(lambda e, j=j: e.transpose(out=pbb[6][:, 128 * j:128 * j + 128], in_=xc[:, 16 + j, :], identity=identb[:]), [f"xc{16 + j}", "identb"], ["pb6"])
            A(lambda e: e.activation(out=Btok[:].rearrange("p a b -> p (a b)"), in_=pbb[6][:, :], func=AF.Copy), ["pb6"], ["Btok"])
            def yevac(gq):
                b2 = gq % 2
                yv = yv2[b2]; ysq = ysq2[b2]; rstg = rstg2[b2]
                for j in range(2):
                    pr = 2 * gq + j
                    V(lambda e, pr=pr, j=j, yv=yv: e.scalar_tensor_tensor(out=yv[:, j, :], in0=xc[:, pr, :], scalar=dexp[:, pr:pr + 1], in1=yps(pr), op0=ALU.mult, op1=ALU.add), [f"xc{pr}", "dexp", ypk(pr)], [f"yv{b2}"])
                    V(lambda e, pr=pr, j=j, yv=yv: e.tensor_tensor(out=yv[:, j, :], in0=yv[:, j, :], in1=zs[:, pr, :], op=ALU.mult), [f"yv{b2}", f"zs{pr}"], [f"yv{b2}"])
                    A(lambda e, j=j, yv=yv, ysq=ysq: e.activation(out=ysq[:, j, :], in_=yv[:, j, :], func=AF.Square), [f"yv{b2}"], [f"ysq{b2}"])
                ps = pb[7][:, 128 * b2:128 * b2 + 128]
                pk = "pb7"
                for j in range(2):
                    T(lambda e, ps=ps, j=j, ysq=ysq: e.matmul(ps, lhsT=onesb[:], rhs=ysq[:, j, :], start=(j == 0), stop=(j == 1)), ["onesb", f"ysq{b2}"], [pk])
                A(lambda e, ps=ps, rstg=rstg: e.activation(out=rstg[:], in_=ps, func=AF.Sqrt, scale=1.0 / 256, bias=epscol[:]), [pk, "epscol"], [f"rstg{b2}"])
                V(lambda e, rstg=rstg: e.reciprocal(out=rstg[:], in_=rstg[:]), [f"rstg{b2}"], [f"rstg{b2}"])
                for j in range(2):
                    pr = 2 * gq + j
                    V(lambda e, pr=pr, j=j, yv=yv, rstg=rstg: e.scalar_tensor_tensor(out=ynT[:, pr, :], in0=yv[:, j, :], scalar=ssdnw[:, pr:pr + 1], in1=rstg[:], op0=ALU.mult, op1=ALU.mult), [f"yv{b2}", "ssdnw", f"rstg{b2}"], [f"ynT{pr}"])
            TAG[0] = "A.grp"
            it = 0
            for hf in ((0, 1) if smp else (None,)):
                grp_range = range(8) if hf is None else range(4 * hf, 4 * hf + 4)
                q4_range = range(1) if hf is None else range(2 * hf, 2 * hf + 2)
                if smp:
                    ypk = lambda pr: f"pb{2 + (pr % 8) // 4}"
                    yps = lambda pr: pb[2 + (pr % 8) // 4][:, 128 * (pr % 4):128 * (pr % 4) + 128]
                    yfirst = lambda pr: pr % 4 == 0
                else:
                    ypk = lambda pr: f"pb{2 + (pr // 2) % 2}"
                    yps = lambda pr: pb[2 + (pr // 2) % 2][:, 128 * (pr % 2):128 * (pr % 2) + 128]
                    yfirst = lambda pr: pr % 2 == 0
                def stage1(g):
                    Eg = Egs[g % 2]; Lg = Lgs[g % 2]; cbk = g % 2
                    V(lambda e, g=g: e.tensor_tensor(out=rhsR[:], in0=acs[:].unsqueeze(1).broadcast_to([32, 4, 128]), in1=identf[0:32, 4 * g:4 * g + 4].unsqueeze(2).broadcast_to([32, 4, 128]), op=ALU.mult), ["acs", "identf"], ["rhsR"])
                    T(lambda e: e.matmul(pb[6][:, :], lhsT=ones32[:], rhs=rhsR[:].rearrange("p a b -> p (a b)"), start=True, stop=True), ["ones32", "rhsR"], ["pb6"])
                    A(lambda e, Eg=Eg: e.activation(out=Eg[:].rearrange("p a b -> p (a b)"), in_=pb[6][:, :], func=AF.Exp), ["pb6"], ["Eg0"])
                    G(lambda e, g=g, Eg=Eg: e.tensor_tensor(out=CEall[:, 4 * g:4 * g + 4, :], in0=Eg[:], in1=xc[:, 24 + g, :].unsqueeze(1).broadcast_to([128, 4, 128]), op=ALU.mult), ["Eg0", f"xc{24 + g}"], ["CEall"])
                    for h4 in range(4):
                        V(lambda e, g=g, h4=h4: e.scalar_tensor_tensor(out=Dm[:, h4, :], in0=pb[6][:, 128 * h4:128 * h4 + 128], scalar=tokm[:, 0, 4 * g + h4:4 * g + h4 + 1], in1=negm[:], op0=ALU.subtract, op1=ALU.min), ["pb6", "tokm", kn(negm)], ["Dm"])
                    A(lambda e, Lg=Lg: e.activation(out=Lg[:].rearrange("p a b -> p (a b)"), in_=Dm[:].rearrange("p a b -> p (a b)"), func=AF.Exp), ["Dm"], [f"Lg{g % 2}"])
                    T(lambda e, g=g: e.matmul(pb[7][:, 256 + 128 * (g % 2):256 + 128 * (g % 2) + 128], lhsT=xc[:, 16 + g, :], rhs=xc[:, 24 + g, :], start=True, stop=True), [f"xc{16 + g}", f"xc{24 + g}"], ["pb7"])
                    V(lambda e, g=g, Eg=Eg: e.tensor_copy(out=dch[:, 4 * g:4 * g + 4, 0:nseq], in_=Eg[:].rearrange("p a (s l) -> p a s l", l=L)[:, :, :, L - 1]), ["Eg0"], ["dch"])
                def stage2(g):
                    Eg = Egs[g % 2]; Lg = Lgs[g % 2]; cbk = g % 2
                    V(lambda e, Lg=Lg, cbk=cbk: e.tensor_tensor(out=Mg[:], in0=Lg[:], in1=pb[7][:, 256 + 128 * cbk:256 + 128 * cbk + 128].unsqueeze(1).broadcast_to([128, 4, 128]), op=ALU.mult), [f"Lg{g % 2}", "pb7"], ["Mg"])
                    for j in range(2):
                        pr = 2 * g + j
                        T(lambda e, pr=pr, j=j: e.matmul(yps(pr), lhsT=XA[:, pr, :], rhs=Mg[:, 2 * j, :], start=yfirst(pr), stop=False, skip_group_check=True), ["XA", "Mg"], [ypk(pr)])
                        T(lambda e, pr=pr, j=j: e.matmul(yps(pr), lhsT=XB[:, pr, :], rhs=Mg[:, 2 * j + 1, :], start=False, stop=False, skip_group_check=True), ["XB", "Mg"], [ypk(pr)])
                    if not smp:
                        sak_ = [f"SA{i}" for i in range(4)] + [f"SB{i}" for i in range(4)]
                        for j in range(2):
                            pr = 2 * g + j
                            T(lambda e, pr=pr: e.matmul(yps(pr), lhsT=SA[:, pr, :], rhs=CEall[:, 2 * pr, :], start=False, stop=False, skip_group_check=True), sak_ + ["CEall"], [ypk(pr)])
                            T(lambda e, pr=pr, j=j: e.matmul(yps(pr), lhsT=SB[:, pr, :], rhs=CEall[:, 2 * pr + 1, :], start=False, stop=(j == 1), skip_group_check=True), sak_ + ["CEall"], [ypk(pr)])
                        yevac(g)

                grps_ = list(grp_range)
                stage1(grps_[0])
                for i_g, g_ in enumerate(grps_):
                    if i_g + 1 < len(grps_):
                        stage1(grps_[i_g + 1])
                    stage2(g_)
                if smp:
                    dchP = yv2[0][:].rearrange("p a b -> p (a b)").rearrange("p (pr s) -> p pr s", s=16)
                    dv_ = dch[:, 16 * hf:16 * hf + 16, :].rearrange("p (pr h2) s -> p pr h2 s", h2=2)
                    for h2 in range(2):
                        hs = slice(64 * h2, 64 * h2 + 64)
                        V(lambda e, hs=hs, h2=h2, dv_=dv_, hf=hf, dchP=dchP: e.tensor_copy(out=dchP[hs, 8 * hf:8 * hf + 8, :], in_=dv_[hs, :, h2, :]), ["dch"], ["yv0"])
                for s in range(nseq if smp else 0):
                    cs = slice(L * s, L * s + L)
                    last = (s == nseq - 1)
                    if smp:
                        Bm = Bms[0]; bmk = "Bm0"
                        G(lambda e, s=s, Bm=Bm: e.tensor_scalar(out=Bm[:].rearrange("p a b -> p (a b)"), in0=Btok[:].rearrange("p a b -> p (a b)"), scalar1=seqmask[:, s:s + 1], scalar2=None, op0=ALU.mult), ["Btok", "seqmask"], [bmk])
                    for q4 in q4_range:
                        prs_ = range(4 * q4, 4 * q4 + 4) if smp else range(16)
                        sak = [f"SA{q4}", f"SB{q4}"] if smp else [f"SA{i}" for i in range(4)] + [f"SB{i}" for i in range(4)]
                        if smp:
                            S0q = S0qs[it % 2]; s0k = f"S0q{it % 2}"; Snq = Snqs[it % 2]; snk = f"Snq{it % 2}"
                            tb_ = 7 if it % 2 == 0 else 0
                            nb_ = 6 if it % 2 == 0 else 1
                            it += 1
                            sview = I["ssm0"][s].rearrange("(pr h2) p n -> (h2 p) pr n", h2=2)[:, 4 * q4:4 * q4 + 4, :]
                            DMA(lambda e, sview=sview, S0q=S0q: e.dma_start(out=S0q[:], in_=sview), w=[s0k])
                            for j in range(4):
                                T(lambda e, j=j, S0q=S0q, tb_=tb_: e.transpose(out=pb[tb_][:, 128 * j:128 * j + 128], in_=S0q[:, j, :], identity=identf[:]), [s0k, "identf"], [f"pb{tb_}"])
                            pv = pb[tb_][:, :].rearrange("p (a h2 q) -> p a h2 q", h2=2, q=64)
                            A(lambda e, q4=q4, pv=pv: e.activation(out=SA[:, 4 * q4:4 * q4 + 4, 0:64], in_=pv[:, :, 0, :], func=AF.Copy), [f"pb{tb_}"], [f"SA{q4}"])
                            A(lambda e, q4=q4, pv=pv: e.activation(out=SB[:, 4 * q4:4 * q4 + 4, 64:128], in_=pv[:, :, 1, :], func=AF.Copy), [f"pb{tb_}"], [f"SB{q4}"])
                        for pr in prs_:
                            T(lambda e, pr=pr, cs=cs: e.matmul(yps(pr)[:, cs], lhsT=SA[:, pr, :], rhs=CEall[:, 2 * pr, cs], start=False, stop=False, skip_group_check=True), sak + ["CEall"], [ypk(pr)])
                            T(lambda e, pr=pr, cs=cs, last=last: e.matmul(yps(pr)[:, cs], lhsT=SB[:, pr, :], rhs=CEall[:, 2 * pr + 1, cs], start=False, stop=last, skip_group_check=True), sak + ["CEall"], [ypk(pr)])
                        if smp:
                            for j in range(4):
                                pr = 4 * q4 + j
                                T(lambda e, pr=pr, j=j, Bm=Bm, nb_=nb_: e.matmul(pb[nb_][:, 128 * j:128 * j + 128], lhsT=Xd[:, 2 * pr:2 * pr + 2, :].rearrange("p a b -> p (a b)"), rhs=Bm[:, pr // 2, :], start=True, stop=True), ["Xd", bmk], [f"pb{nb_}"])
                            for j in range(4):
                                pr = 4 * q4 + j
                                V(lambda e, pr=pr, j=j, s=s, S0q=S0q, Snq=Snq, nb_=nb_, dchP=dchP: e.scalar_tensor_tensor(out=Snq[:, j, :], in0=S0q[:, j, :], scalar=dchP[:, pr, s:s + 1], in1=pb[nb_][:, 128 * j:128 * j + 128], op0=ALU.mult, op1=ALU.add), [s0k, "yv0", f"pb{nb_}"], [snk])
                            oview = O["ssm_s"][s].rearrange("(pr h2) p n -> (h2 p) pr n", h2=2)[:, 4 * q4:4 * q4 + 4, :]
                            DMAO(lambda e, oview=oview, Snq=Snq: e.dma_start(out=oview, in_=Snq[:]), [snk], [])
                if smp:
                    for gq in grp_range:
                        yevac(gq)
            if smp:
                sv = STt[:].rearrange("p (pr h2) q -> p pr h2 q", h2=2)
                A(lambda e, sv=sv: e.activation(out=SA[:, :, 0:64], in_=sv[:, :, 0, :], func=AF.Copy), ["STt"], [f"SA{i}" for i in range(4)])
                A(lambda e, sv=sv: e.activation(out=SB[:, :, 64:128], in_=sv[:, :, 1, :], func=AF.Copy), ["STt"], [f"SB{i}" for i in range(4)])
            TAG[0] = "A.state"
            if not smp:
                V(lambda e: e.tensor_tensor(out=STt[:], in0=STt[:], in1=dch[:, :, 0:1].broadcast_to([128, 32, 64]), op=ALU.mult), ["STt", "dch"], ["STt"])
                for rnd in range(2):
                    for g in range(4 * rnd, 4 * rnd + 4):
                        bk_ = 2 + (g % 4) // 2
                        T(lambda e, g=g, bk_=bk_: e.matmul(pb[bk_][:, 256 * (g % 2):256 * (g % 2) + 256], lhsT=Btok[:, g, :], rhs=Xd[:, 4 * g:4 * g + 4, :].rearrange("p a b -> p (a b)"), start=True, stop=True), ["Btok", "Xd"], [f"pb{bk_}"])
                    for b2_ in range(2):
                        h0 = 16 * rnd + 8 * b2_
                        V(lambda e, h0=h0, b2_=b2_: e.tensor_tensor(out=STt[:, h0:h0 + 8, :], in0=STt[:, h0:h0 + 8, :], in1=pb[2 + b2_][:, :].rearrange("p (h q) -> p h q", q=64), op=ALU.add), ["STt", f"pb{2 + b2_}"], ["STt"])
                sv = STt[:].rearrange("p (pr h2) q -> p pr h2 q", h2=2)
                A(lambda e, sv=sv: e.activation(out=SA[:, :, 0:64], in_=sv[:, :, 0, :], func=AF.Copy), ["STt"], [f"SA{i}" for i in range(4)])
                A(lambda e, sv=sv: e.activation(out=SB[:, :, 64:128], in_=sv[:, :, 1, :], func=AF.Copy), ["STt"], [f"SB{i}" for i in range(4)])
                if lastp:
                    for q4 in range(4):
                        for j in range(4):
                            pr = 4 * q4 + j
                            T(lambda e, pr=pr, j=j: e.transpose(out=pb[6][:, 128 * j:128 * j + 128], in_=STt[:, 2 * pr:2 * pr + 2, :].rearrange("p a b -> p (a b)"), identity=identf[:]), ["STt", "identf"], ["pb6"])
                        Snq = Snqs[q4 % 2]; snk = f"Snq{q4 % 2}"
                        V(lambda e, Snq=Snq: e.tensor_copy(out=Snq[:].rearrange("p a b -> p (a b)"), in_=pb[6][:, :]), ["pb6"], [snk])
                        oview = O["ssm_p"].rearrange("(pr h2) p n -> (h2 p) pr n", h2=2)[:, 4 * q4:4 * q4 + 4, :]
                        DMAO(lambda e, oview=oview, Snq=Snq: e.dma_start(out=oview, in_=Snq[:]), [snk], [])
            if CUR[0] is ssd_list:
                main_list.extend(merge(late_list, ssd_list))
                CUR[0] = main_list
            TAG[0] = "A.dense"
            dense_T("w_down_ssd", 16, lambda k: ynT[:, k, :], lambda k: [f"ynT{k}"],
                    lambda m, ps, pk: V(lambda e: e.tensor_tensor(out=yaT[:, m, :], in0=ps, in1=gas[:, m, :], op=ALU.mult), [pk, f"ga{m}"], [f"yaT_{par}_{m}"]))


        def recB(ti, par):
            u5 = u5_[par]; z5s = z5s_[par]; gbs = gbs_[par]; yaT = yaT_[par]
            smp = (ti == 16)
            lastp = (ti == 15)
            xsrc = I["xs"] if smp else I["xp"][128 * ti:128 * ti + 128, :]
            ydst = O["y_s"] if smp else O["y_p"][128 * ti:128 * ti + 128, :]
            if smp:
                for ri, nm in enumerate(["re0", "im0"]):
                    sv_ = I[nm].rearrange("s (gh gq g2) p -> s gq gh (g2 p)", gq=8, g2=2)
                    for s in range(16):
                        DMA(lambda e, s=s, sv_=sv_: e.dma_start(out=s5io[8 * s:8 * s + 8, :, :, :].rearrange("p a b c -> p a (b c)"), in_=sv_[s]), w=["s5io"])
                    for gh in range(4):
                        T(lambda e, gh=gh: e.transpose(out=pb[5][:, 0:128], in_=s5io[:, gh, :, :].rearrange("p a b -> p (a b)"), identity=identf[:]), ["s5io", "identf"], ["pb5"])
                        V(lambda e, gh=gh, ri=ri: e.tensor_copy(out=s0all[:, 8 * gh:8 * gh + 8, ri, :], in_=pb[5][:, 0:128].rearrange("p (s g) -> p g s", g=8)), ["pb5"], ["s0all"])
            TAG[0] = "B.s5"
            nsp, Lx = (1, 128) if smp else (2, 64)
            Ls = 8 if smp else 64

            def vw(t_, r=None):
                v = t_[:].rearrange("p a b c -> p (a b c)").rearrange("p (r s q t) -> p r s q t", r=2, s=nsp, q=4)
                return v if r is None else v[:, r]

            def pv_(t_):
                return t_[:].rearrange("p a b -> p (a b)").rearrange("p (s q t) -> p s q t", s=nsp, q=4)

            def sub(ap):
                return ap.rearrange("p s q (n l) -> p s q n l", l=Ls) if smp else ap
            def s5_ft(ft):
                gsl = slice(4 * ft, 4 * ft + 4)
                if smp:
                    ecv = ecos[:, gsl, 0:Ls].unsqueeze(1).unsqueeze(3).broadcast_to([128, 1, 4, 16, Ls])
                    esv = esin[:, gsl, 0:Ls].unsqueeze(1).unsqueeze(3).broadcast_to([128, 1, 4, 16, Ls])
                else:
                    ecv = ecos[:, gsl, 0:Ls].unsqueeze(1).broadcast_to([128, 2, 4, Ls])
                    esv = esin[:, gsl, 0:Ls].unsqueeze(1).broadcast_to([128, 2, 4, Ls])
                def st_a():
                    for q in range(4):
                        A(lambda e, ft=ft, q=q: e.activation(out=u5m[:, q, :], in_=u5[:, ft, :], func=AF.Copy, scale=pairmask[:, q:q + 1]), [f"u5_{par}_{ft}", "pairmask"], ["u5m"])
                    for ri in range(2):
                        pbk = f"pb{4 + ri}"
                        for q in range(4):
                            T(lambda e, ft=ft, q=q, ri=ri: e.matmul(pb[4 + ri][:, 128 * q:128 * q + 128], lhsT=wb5[:, ft, ri, :], rhs=u5m[:, q, :], start=True, stop=True), ["wb5", "u5m"], [pbk])
                        A(lambda e, ri=ri: e.activation(out=vw(bu, ri), in_=pb[4 + ri][:, :].rearrange("p (q s t) -> p s q t", s=nsp, t=Lx), func=AF.Copy), [pbk], ["bu"])
                def st_b():
                    bre_, bim_ = sub(vw(bu, 0)), sub(vw(bu, 1))
                    V(lambda e, ecv=ecv, bre_=bre_: e.tensor_tensor(out=sub(pv_(bt1)), in0=ecv, in1=bre_, op=ALU.mult), ["ecos", "bu"], ["bt1"])
                    V(lambda e, esv=esv, bim_=bim_: e.tensor_tensor(out=sub(pv_(bt2)), in0=esv, in1=bim_, op=ALU.mult), ["esin", "bu"], ["bt2"])
                    V(lambda e: e.tensor_tensor(out=vw(sgm, 0), in0=pv_(bt1), in1=pv_(bt2), op=ALU.add), ["bt1", "bt2"], ["sgm"])
                    V(lambda e, ecv=ecv, bim_=bim_: e.tensor_tensor(out=sub(pv_(bt1)), in0=ecv, in1=bim_, op=ALU.mult), ["ecos", "bu", "sgm"], ["bt1"])
                    V(lambda e, esv=esv, bre_=bre_: e.tensor_tensor(out=sub(pv_(bt2)), in0=esv, in1=bre_, op=ALU.mult), ["esin", "bu", "sgm"], ["bt2"])
                    V(lambda e: e.tensor_tensor(out=vw(sgm, 1), in0=pv_(bt1), in1=pv_(bt2), op=ALU.subtract), ["bt1", "bt2"], ["sgm"])
                    rtv = rts[:].rearrange("p a b -> p (a b)")[:, 0:4 * Lx]
                    if smp:
                        V(lambda e, gsl=gsl: e.tensor_tensor(out=rts[:], in0=rdec[:, gsl].unsqueeze(2).broadcast_to([128, 4, 128]), in1=rm128[:].unsqueeze(1).broadcast_to([128, 4, 128]), op=ALU.mult), ["rdec", "rm128"], ["rts"])
                    else:
                        V(lambda e, gsl=gsl, rtv=rtv: e.tensor_tensor(out=rtv.rearrange("p (q t) -> p q t", t=64), in0=rdec[:, gsl].unsqueeze(2).broadcast_to([128, 4, 64]), in1=rmq[:].rearrange("p (q t) -> p q t", t=64), op=ALU.mult), ["rdec", "rmq"], ["rts"])
                    flat = lambda ap: ap.rearrange("p q t -> p (q t)")
                    if smp:
                        ab = lambda t: t[:, gsl].unsqueeze(2).broadcast_to([128, 4, 16])
                        cmul(cin_[:, :, 0, :], cin_[:, :, 1, :], ab(are), ab(aim), s0all[:, gsl, 0, :], s0all[:, gsl, 1, :],
                             ["are", "aim", "s0all"], ["cinr", "cini"], ctm[:, :, 0, :], ctm[:, :, 1, :], ["ctm0", "ctm1"])
                        for ri in range(2):
                            tgt = sub(vw(sgm, ri))[:, 0, :, :, 0]
                            V(lambda e, tgt=tgt, ri=ri: e.tensor_tensor(out=tgt, in0=tgt, in1=cin_[:, :, ri, :], op=ALU.add), ["sgm", "cinr", "cini"], ["sgm"])
                        for ri in range(2):
                            V(lambda e, ri=ri, rtv=rtv: e.tensor_tensor_scan(out=flat(vw(bu, ri)[:, 0]), data0=rtv, data1=flat(vw(sgm, ri)[:, 0]), initial=0.0, op0=ALU.mult, op1=ALU.add), ["rts", "sgm"], ["bu"])
                    else:
                        for sg in range(2):
                            src_re = sgc[:, gsl, 0:1] if sg == 0 else vw(bu, 0)[:, 0, :, 63:64]
                            src_im = sgc[:, gsl, 1:2] if sg == 0 else vw(bu, 1)[:, 0, :, 63:64]
                            ab = lambda t: t[:, gsl].unsqueeze(2)
                            cmul(cin_[:, :, 0, 0:1], cin_[:, :, 1, 0:1], ab(aere), ab(aeim), src_re, src_im,
                                 ["aere", "aeim", "sgc", "bu"], ["cinr", "cini"], ctm[:, :, 0, 0:1], ctm[:, :, 1, 0:1], ["ctm0", "ctm1"])
                            for ri in range(2):
                                tgt = vw(sgm, ri)[:, sg, :, 0:1]
                                V(lambda e, tgt=tgt, ri=ri: e.tensor_tensor(out=tgt, in0=tgt, in1=cin_[:, :, ri, 0:1], op=ALU.add), ["sgm", "cinr", "cini"], ["sgm"])
                            for ri in range(2):
                                V(lambda e, ri=ri, sg=sg, rtv=rtv: e.tensor_tensor_scan(out=flat(vw(bu, ri)[:, sg]), data0=rtv, data1=flat(vw(sgm, ri)[:, sg]), initial=0.0, op0=ALU.mult, op1=ALU.add), ["rts", "sgm"], ["bu"])
                        V(lambda e, gsl=gsl: e.tensor_copy(out=sgc[:, gsl, :], in_=vw(bu)[:, :, 1, :, 63].rearrange("p r q -> p q r")), ["bu"], ["sgc"])
                    sre_, sim_ = sub(vw(bu, 0)), sub(vw(bu, 1))
                    s5v = lambda r: s5s[:, :, r, :].rearrange("p q (s t) -> p s q t", s=nsp)
                    V(lambda e, ecv=ecv, sre_=sre_: e.tensor_tensor(out=sub(pv_(bt1)), in0=ecv, in1=sre_, op=ALU.mult), ["ecos", "bu"], ["bt1"])
                    V(lambda e, esv=esv, sim_=sim_: e.tensor_tensor(out=sub(pv_(bt2)), in0=esv, in1=sim_, op=ALU.mult), ["esin", "bu"], ["bt2"])
                    V(lambda e: e.tensor_tensor(out=s5v(0), in0=pv_(bt1), in1=pv_(bt2), op=ALU.subtract), ["bt1", "bt2"], ["s5s"])
                    if smp or lastp:
                        fsrc = lambda t: (sub(pv_(t))[:, 0, :, :, Ls - 1] if smp else pv_(t)[:, 1, :, 63:64])
                        nf = 16 if smp else 1
                        V(lambda e, gsl=gsl, fsrc=fsrc, nf=nf: e.tensor_tensor(out=fin[:, gsl, 0, 0:nf], in0=fsrc(bt1), in1=fsrc(bt2), op=ALU.subtract), ["bt1", "bt2"], ["fin"])
                    V(lambda e, ecv=ecv, sim_=sim_: e.tensor_tensor(out=sub(pv_(bt1)), in0=ecv, in1=sim_, op=ALU.mult), ["ecos", "bu", "s5s", "fin"], ["bt1"])
                    V(lambda e, esv=esv, sre_=sre_: e.tensor_tensor(out=sub(pv_(bt2)), in0=esv, in1=sre_, op=ALU.mult), ["esin", "bu", "s5s", "fin"], ["bt2"])
                    V(lambda e: e.tensor_tensor(out=s5v(1), in0=pv_(bt1), in1=pv_(bt2), op=ALU.add), ["bt1", "bt2"], ["s5s"])
                    if smp or lastp:
                        V(lambda e, gsl=gsl, fsrc=fsrc, nf=nf: e.tensor_tensor(out=fin[:, gsl, 1, 0:nf], in0=fsrc(bt1), in1=fsrc(bt2), op=ALU.add), ["bt1", "bt2"], ["fin"])
                def st_c():
                    ps4 = [pb[4][:, 128 * q:128 * q + 128] for q in range(4)]
                    for q in range(4):
                        for ri in range(2):
                            T(lambda e, ft=ft, q=q, ri=ri: e.matmul(ps4[q], lhsT=wd5[:, ft, ri, :], rhs=s5s[:, q, ri, :], start=(ri == 0), stop=(ri == 1)), ["wd5", "s5s"], ["pb4"])
                    for q in range(4):
                        rs_ = slice(32 * q, 32 * q + 32)
                        V(lambda e, ft=ft, q=q, rs_=rs_: e.scalar_tensor_tensor(out=y5t[rs_, :], in0=u5[rs_, ft, :], scalar=ds5[rs_, ft:ft + 1], in1=ps4[q][rs_, :], op0=ALU.mult, op1=ALU.add), [f"u5_{par}_{ft}", "ds5", "pb4"], ["y5t"])
                    A(lambda e, ft=ft: e.activation(out=y5g[:, ft, :], in_=y5t[:], func=AF.Gelu_apprx_tanh), ["y5t"], [f"y5g{ft}"])
                return st_a, st_b, st_c
            fts_ = [s5_ft(ft) for ft in range(8)]
            fts_[0][0]()
            for ft in range(8):
                fts_[ft][1]()
                if ft + 1 < 8:
                    fts_[ft + 1][0]()
                fts_[ft][2]()
            if lastp:
                for ri, nm in enumerate(["re_p", "im_p"]):
                    DMAO(lambda e, ri=ri, nm=nm: e.dma_start(out=O[nm].rearrange("(gp g2) p -> (g2 p) gp", g2=2), in_=fin[:, :, ri, 0], allow_slow_non_contiguous=True), ["fin"], [])
            if smp:
                for ri, nm in enumerate(["re_s", "im_s"]):
                    for gh in range(4):
                        V(lambda e, gh=gh, ri=ri: e.tensor_copy(out=s5tr[:], in_=fin[:, 8 * gh:8 * gh + 8, ri, :].rearrange("p g s -> p s g")), ["fin"], ["s5tr"])
                        T(lambda e: e.transpose(out=pb[5][:, 0:128], in_=s5tr[:].rearrange("p a b -> p (a b)"), identity=identf[:]), ["s5tr", "identf"], ["pb5"])
                        V(lambda e, gh=gh: e.tensor_copy(out=s5io[:, gh, :, :].rearrange("p a b -> p (a b)"), in_=pb[5][:, 0:128]), ["pb5"], ["s5io"])
                    ov = O[nm].rearrange("s (gh gq g2) p -> s gq gh (g2 p)", gq=8, g2=2)
                    for s in range(16):
                        DMAO(lambda e, s=s, ov=ov: e.dma_start(out=ov[s], in_=s5io[8 * s:8 * s + 8, :, :, :].rearrange("p a b c -> p a (b c)")), ["s5io"], [])

            TAG[0] = "B.glu"
            def glu_evac(m, ps, pk):
                A(lambda e: e.activation(out=glus[:], in_=ps, func=AF.Sigmoid, bias=bglu[:, m:m + 1]), [pk, "bglu"], ["glus"])
                V(lambda e: e.tensor_tensor(out=glus[:], in0=glus[:], in1=y5g[:, m, :], op=ALU.mult), ["glus", f"y5g{m}"], ["glus"])
                V(lambda e: e.tensor_tensor(out=y5f[:, m, :], in0=glus[:], in1=z5s[:, m, :], op=ALU.mult), ["glus", f"z5_{par}_{m}"], [f"y5f{m}"])
            dense_T("w_glu", 8, lambda k: y5g[:, k, :], lambda k: [f"y5g{k}"], glu_evac)
            dense_T("w_down_s5", 8, lambda k: y5f[:, k, :], lambda k: [f"y5f{k}"],
                    lambda m, ps, pk: V(lambda e: e.tensor_tensor(out=ybT[:, m, :], in0=ps, in1=gbs[:, m, :], op=ALU.mult), [pk, f"gb_{par}_{m}"], [f"ybT{m}"]))
            for m in range(8):
                V(lambda e, m=m: e.tensor_tensor(out=mixT[:, m, :], in0=yaT[:, m, :], in1=ybT[:, m, :], op=ALU.add), [f"yaT_{par}_{m}", f"ybT{m}"], [f"mixT{m}"])
            TAG[0] = "B.out"
            DMA(lambda e: e.dma_start(out=ot[:], in_=xsrc), w=["ot"])
            for cb in range(4):
                wt, wk = wload(WS["w_out"][:, 256 * cb:256 * cb + 256])
                hb = cb % 2
                ps = pb[4 + hb][:, 0:256]
                pks = [f"pb{4 + hb}"]
                for k in range(8):
                    T(lambda e, wt=wt, k=k, ps=ps: e.matmul(ps, lhsT=mixT[:, k, :], rhs=wt[:, k, 0:256], start=(k == 0), stop=(k == 7)), [wk, f"mixT{k}"], pks)
                V(lambda e, cb=cb, ps=ps: e.tensor_tensor(out=ot[:, 256 * cb:256 * cb + 256], in0=ps, in1=ot[:, 256 * cb:256 * cb + 256], op=ALU.add), pks + ["ot"], ["ot"])
            G(lambda e: e.memset(ssB[:], 0.0), w=["ssB"])
            A(lambda e: e.activation(out=y5f[:].rearrange("p a b -> p (a b)"), in_=ot[:], func=AF.Square, accum_out=ssB[:]), ["ot", "ssB"], [f"y5f{m}" for m in range(8)] + ["ssB"])
            A(lambda e: e.activation(out=rstdB[:], in_=ssB[:], func=AF.Sqrt, scale=1.0 / D, bias=epscol[:]), ["ssB", "epscol"], ["rstdB"])
            V(lambda e: e.reciprocal(out=rstdB[:], in_=rstdB[:]), ["rstdB"], ["rstdB"])
            V(lambda e: e.scalar_tensor_tensor(out=ot[:], in0=ot[:], scalar=rstdB[:, 0:1], in1=fnw[:], op0=ALU.mult, op1=ALU.mult), ["ot", "rstdB", "fnw"], ["ot"])
            DMAO(lambda e: e.dma_start(out=ydst, in_=ot[:]), ["ot"], [])


        def flush(lst):
            for (eng, fn, r, w, dma, tag) in lst:
                P.op(eng, fn, r, w, dma=dma)
                P.ops[-1]["tag"] = tag

        def record(fn_, ti, par, banks, wset):
            CUR[0] = []
            MMB[0] = banks
            WSET[0] = wset
            fn_(ti, par)
            lst = CUR[0]
            CUR[0] = None
            return lst

        tl = list(tiles) if tiles is not None else (list(range(8)) + [16] + list(range(8, 16)))
        if tl:
            flush(record(recA, tl[0], 0, (0, 1), (0, 1)))
            for n_, ti in enumerate(tl):
                lb = record(recB, ti, n_ % 2, (4, 5), (2, 3))
                if n_ + 1 < len(tl):
                    la = record(recA, tl[n_ + 1], (n_ + 1) % 2, (0, 1), (0, 1))
                    if not INTERLEAVE[0]:
                        flush(lb); flush(la)
                    else:
                        flush(merge(la, lb))
                else:
                    flush(lb)
        P.emit(nc)
    return nc


_NC = {}


def _shard(inputs, c):
    d = {}
    d["xp"] = np.ascontiguousarray(inputs["x_prompt"][c])
    d["xs"] = np.ascontiguousarray(inputs["x_sample"][16 * c:16 * c + 16].reshape(128, 1024))
    d["conv0"] = np.ascontiguousarray(inputs["state_conv"][0, 16 * c:16 * c + 16])
    d["ssm0"] = np.ascontiguousarray(inputs["state_ssm"][0, 16 * c:16 * c + 16])
    d["re0"] = np.ascontiguousarray(inputs["state_s5_re"][0, 16 * c:16 * c + 16])
    d["im0"] = np.ascontiguousarray(inputs["state_s5_im"][0, 16 * c:16 * c + 16])
    for k in ("norm_w", "w_in", "conv_w", "conv_b", "dt_bias", "A_log", "D_ssd", "ssd_norm_w", "w_down_ssd", "lam_re",
              "lam_im", "log_dt", "B_re", "B_im", "C_re", "C_im", "D_s5", "w_glu", "b_glu", "w_down_s5", "w_out"):
        d[k] = np.ascontiguousarray(inputs[k][0])
    d["final_norm_w"] = np.ascontiguousarray(inputs["final_norm_w"])
    return d


def kernel(**inputs):
    inputs = {k: np.asarray(v, dtype=np.float32) for k, v in inputs.items()}
    if "nc" not in _NC:
        _NC["nc"] = build()
    nc = _NC["nc"]
    consts = host_consts()
    in_maps = []
    for c in range(NCORES):
        d = _shard(inputs, c)
        d.update(consts)
        in_maps.append(d)
    res = run_bass_kernel_spmd(nc, in_maps, core_ids=list(range(NCORES))).results
    cat = lambda n: np.concatenate([r[n] for r in res], axis=0)
    y_p = np.stack([r["y_p"] for r in res], 0)
    y_s = cat("y_s").reshape(128, 8, 1024)
    conv_p = np.stack([r["conv_p"] for r in res], 0)[None]
    ssm_p = np.stack([r["ssm_p"] for r in res], 0)[None]
    re_p = np.stack([r["re_p"] for r in res], 0)[None]
    im_p = np.stack([r["im_p"] for r in res], 0)[None]
    conv_s = cat("conv_s")[None]
    ssm_s = cat("ssm_s")[None]
    re_s = cat("re_s")[None]
    im_s = cat("im_s")[None]
    return tuple(np.ascontiguousarray(a, dtype=np.float32) for a in
                 (y_p, y_s, conv_p, ssm_p, re_p, im_p, conv_s, ssm_s, re_s, im_s))
```
